# Optimizing a Trainium2 kernel written in Bass

```python
import math
import jax, jax.numpy as jnp
from jax import lax
import numpy as np

D_MODEL = 1024
BATCH = 8
SEQ = 2048
DEPTH = 4
DEC_BATCH = 32
DEC_SEQ = 8
PAST_LEN = 8192
PAGE_SIZE = 128

MIX_W = D_MODEL
SSD_W = MIX_W // 2
SSD_HD = 64
SSD_H = SSD_W // SSD_HD
SSD_N = 128
SSD_G = 2
SSD_CONV = 4
SSD_CHUNK = 128
SSD_CONV_DIM = SSD_W + 2 * SSD_G * SSD_N
POOL_W = MIX_W // 4
POOL_WINDOWS = (2, 4, 8, 16)
POOL_GC = POOL_W // len(POOL_WINDOWS)
POOL_KEEP = max(POOL_WINDOWS) - 1
NSA_W = MIX_W // 4
NSA_HD = 64
NSA_H = NSA_W // NSA_HD
CMP_STRIDE = 16
CMP_LEN = 2 * CMP_STRIDE
CMP_HID = 2 * NSA_HD
SLC_BLOCK = 64
N_SELECT = 16
WINDOW = 512
Q_BLOCK = 128
ROPE_DIM = NSA_HD // 4
ROPE_THETA = 500000.0
D_FF = 4 * D_MODEL
FFN_CONV = 3
RMS_EPS = 1e-6
OFF_Z = SSD_W
OFF_XBC = OFF_Z + SSD_CONV_DIM
OFF_DT = OFF_XBC + SSD_H
OFF_POOL = OFF_DT + POOL_W
OFF_Q = OFF_POOL + NSA_W
OFF_KV = OFF_Q + 6 * NSA_HD
IN_W = OFF_KV + 3 * NSA_H
IN_SPLITS = (OFF_Z, OFF_XBC, OFF_DT, OFF_POOL, OFF_Q, OFF_KV)

kernel_name = 'hymba_ssd_pool_nsa_decode_step'


def rmsnorm(x, w):
    xf = x.astype(jnp.float32)
    y = xf * lax.rsqrt(jnp.mean(xf * xf, axis=-1, keepdims=True) + RMS_EPS)
    return (y * w.astype(jnp.float32)).astype(x.dtype)


def rope(x, pos):
    half = ROPE_DIM // 2
    inv = 1.0 / (ROPE_THETA ** (jnp.arange(half, dtype=jnp.float32) / half))
    ang = pos.astype(jnp.float32)[:, None] * inv
    shape = (ang.shape[0],) + (1,) * (x.ndim - 3) + (half,)
    cos = jnp.cos(ang).reshape(shape)
    sin = jnp.sin(ang).reshape(shape)
    xf = x.astype(jnp.float32)
    x1 = xf[..., :half]
    x2 = xf[..., half:ROPE_DIM]
    out = jnp.concatenate([x1 * cos - x2 * sin, x2 * cos + x1 * sin, xf[..., ROPE_DIM:]], axis=-1)
    return out.astype(x.dtype)


def causal_dwconv(u, prev, w, b):
    k_w = w.shape[0]
    t = u.shape[1]
    ext = jnp.concatenate([prev, u], axis=1)
    out = ext[:, :t] * w[0]
    for k in range(1, k_w):
        out = out + ext[:, k:k + t] * w[k]
    return out + b, ext[:, t:]


def masked_softmax(s, mask):
    s = jnp.where(mask, s, -jnp.inf)
    m = jnp.max(s, axis=-1, keepdims=True)
    m = jnp.where(jnp.isfinite(m), m, 0.0)
    e = jnp.where(mask, jnp.exp(s - m), 0.0)
    return e / jnp.maximum(jnp.sum(e, axis=-1, keepdims=True), 1e-30)


def segsum(a):
    t = a.shape[-1]
    x = jnp.broadcast_to(a[..., :, None], a.shape + (t,))
    s = jnp.cumsum(jnp.where(jnp.tril(jnp.ones((t, t), bool), -1), x, 0.0), axis=-2)
    return jnp.where(jnp.tril(jnp.ones((t, t), bool)), s, -jnp.inf)


def ssd_chunked(x, dt, a, bg, cg, h0):
    b, t, h, p = x.shape
    q = min(SSD_CHUNK, t)
    nc = -(-t // q)
    pad = nc * q - t
    def padt(v):
        return jnp.pad(v, ((0, 0), (0, pad)) + ((0, 0),) * (v.ndim - 2))
    x, dt, bg, cg = padt(x), padt(dt), padt(bg), padt(cg)
    rep = h // SSD_G
    bh = jnp.repeat(bg, rep, axis=2).reshape(b, nc, q, h, SSD_N)
    ch = jnp.repeat(cg, rep, axis=2).reshape(b, nc, q, h, SSD_N)
    xc = (x * dt[..., None]).reshape(b, nc, q, h, p)
    ac = (dt * a).reshape(b, nc, q, h).transpose(0, 3, 1, 2)
    a_cum = jnp.cumsum(ac, axis=-1)
    decay_in = jnp.exp(segsum(ac))
    scores = jnp.einsum('bclhn,bcshn->bhcls', ch, bh) * decay_in
    y_diag = jnp.einsum('bhcls,bcshp->bclhp', scores, xc)
    decay_to_end = jnp.exp(a_cum[..., -1:] - a_cum)
    chunk_states = jnp.einsum('bclhn,bhcl,bclhp->bchpn', bh, decay_to_end, xc)
    states = jnp.concatenate([h0[:, None], chunk_states], axis=1)
    chunk_decay = jnp.exp(segsum(jnp.pad(a_cum[..., -1], ((0, 0), (0, 0), (1, 0)))))
    states = jnp.einsum('bhzc,bchpn->bzhpn', chunk_decay, states)
    y_off = jnp.einsum('bclhn,bchpn,bhcl->bclhp', ch, states[:, :-1], jnp.exp(a_cum))
    y = (y_diag + y_off).reshape(b, nc * q, h, p)[:, :t]
    return y, states[:, -1]


def ssd_mixer(z, xbc, dt_raw, conv_prev, ssm_prev, conv_w, conv_b, dt_bias, a_log, d_skip, norm_w):
    f32 = jnp.float32
    b, t, _ = z.shape
    xbc, conv_new = causal_dwconv(xbc, conv_prev, conv_w, conv_b)
    xbc = jax.nn.silu(xbc)
    xs, bm, cm = jnp.split(xbc, (SSD_W, SSD_W + SSD_G * SSD_N), axis=-1)
    xh = xs.reshape(b, t, SSD_H, SSD_HD).astype(f32)
    dt = jax.nn.softplus(dt_raw.astype(f32) + dt_bias.astype(f32))
    a = -jnp.exp(a_log.astype(f32))
    y, ssm_new = ssd_chunked(xh, dt, a, bm.reshape(b, t, SSD_G, SSD_N).astype(f32),
                             cm.reshape(b, t, SSD_G, SSD_N).astype(f32), ssm_prev.astype(f32))
    y = y + d_skip.astype(f32)[:, None] * xh
    y = y.reshape(b, t, SSD_W) * jax.nn.silu(z.astype(f32))
    y = rmsnorm(y, norm_w)
    return y.astype(z.dtype), conv_new, ssm_new.astype(ssm_prev.dtype)


def pool_mixer(u, prev, pos0, w_grp, scale):
    b, t, c = u.shape
    ext = jnp.concatenate([prev, u], axis=1).astype(jnp.float32)
    cs = jnp.concatenate([jnp.zeros((b, 1, c), jnp.float32), jnp.cumsum(ext, axis=1)], axis=1)
    pos = pos0 + jnp.arange(t)
    outs = []
    for g, w in enumerate(POOL_WINDOWS):
        sl = slice(g * POOL_GC, (g + 1) * POOL_GC)
        s = cs[:, POOL_KEEP + 1:POOL_KEEP + 1 + t, sl] - cs[:, POOL_KEEP + 1 - w:POOL_KEEP + 1 - w + t, sl]
        cnt = jnp.minimum(pos + 1, w).astype(jnp.float32)
        outs.append(s / cnt[None, :, None])
    pooled = jnp.concatenate(outs, axis=-1) - u.astype(jnp.float32)
    y = jnp.einsum('btgc,gcd->btgd', pooled.reshape(b, t, len(POOL_WINDOWS), POOL_GC), w_grp.astype(jnp.float32))
    y = y.reshape(b, t, c) * scale.astype(jnp.float32)
    return y.astype(u.dtype), ext[:, -POOL_KEEP:].astype(u.dtype)


def compress_blocks(rows, pe, w1, w2):
    b, l, d = rows.shape
    n_ch = -(-l // CMP_STRIDE)
    rows = jnp.pad(rows, ((0, 0), (0, n_ch * CMP_STRIDE - l), (0, 0)))
    ch = rows.reshape(b, n_ch, CMP_STRIDE, d)
    h = (jnp.einsum('bnld,lde->bne', ch[:, :-1] + pe[:CMP_STRIDE], w1[:CMP_STRIDE])
         + jnp.einsum('bnld,lde->bne', ch[:, 1:] + pe[CMP_STRIDE:], w1[CMP_STRIDE:]))
    return jax.nn.gelu(h) @ w2


def nsa_attend(q, gates, full, win_ext, pos0, pe_k, pe_v, w1_k, w1_v, w2_k, w2_v):
    f32 = jnp.float32
    b, t, h, d = q.shape
    q = q.astype(f32) * (d ** -0.5)
    gates = gates.astype(f32)
    full = full.astype(f32)
    win_ext = win_ext.astype(f32)
    lk = full.shape[1]
    ck = compress_blocks(full[:, :, 0], pe_k.astype(f32), w1_k.astype(f32), w2_k.astype(f32))
    cv = compress_blocks(full[:, :, 1], pe_v.astype(f32), w1_v.astype(f32), w2_v.astype(f32))
    n_cmp = ck.shape[1]
    cmp_end = jnp.arange(n_cmp) * CMP_STRIDE + (CMP_LEN - 1)
    n_slc = -(-lk // SLC_BLOCK)
    pad_k = n_slc * SLC_BLOCK - lk
    ks = jnp.pad(full[:, :, 2], ((0, 0), (0, pad_k), (0, 0)))
    vs = jnp.pad(full[:, :, 3], ((0, 0), (0, pad_k), (0, 0)))
    c_start = jnp.arange(n_cmp)[:, None] * CMP_STRIDE
    s_start = jnp.arange(n_slc)[None, :] * SLC_BLOCK
    overlap = ((c_start < s_start + SLC_BLOCK) & (c_start + CMP_LEN > s_start)).astype(f32)
    n_sel = min(N_SELECT, n_slc)
    blk = jnp.arange(n_slc)
    qb_len = min(Q_BLOCK, t)
    n_qb = -(-t // qb_len)
    pad_q = n_qb * qb_len - t
    qp = jnp.pad(q, ((0, 0), (0, pad_q), (0, 0), (0, 0)))
    gp = jnp.pad(gates, ((0, 0), (0, pad_q), (0, 0), (0, 0)))
    wk = jnp.pad(win_ext[:, :, 0], ((0, 0), (0, pad_q), (0, 0)))
    wv = jnp.pad(win_ext[:, :, 1], ((0, 0), (0, pad_q), (0, 0)))
    band = WINDOW + qb_len

    def query_block(args):
        qb, gb, s = args
        tpos = pos0 + s + jnp.arange(qb_len)
        sc = jnp.einsum('bqhd,bnd->bhqn', qb, ck)
        pc = masked_softmax(sc, (cmp_end[None, :] <= tpos[:, None])[None, None])
        o_cmp = jnp.einsum('bhqn,bnd->bqhd', pc, cv)
        imp = jnp.einsum('bhqn,nj->bqj', pc, overlap)
        cur = (tpos // SLC_BLOCK)[:, None]
        forced = (blk[None] == 0) | (blk[None] == cur) | (blk[None] == cur - 1)
        imp = jnp.where(forced[None], jnp.inf, imp)
        imp = jnp.where((blk[None] * SLC_BLOCK <= tpos[:, None])[None], imp, -jnp.inf)
        top_v, top_i = lax.top_k(imp, n_sel)
        tok = (top_i[..., None] * SLC_BLOCK + jnp.arange(SLC_BLOCK)).reshape(b, qb_len, n_sel * SLC_BLOCK)
        kg = jax.vmap(lambda kk, ii: kk[ii])(ks, tok)
        vg = jax.vmap(lambda vv, ii: vv[ii])(vs, tok)
        smask = (tok <= tpos[None, :, None]) & jnp.repeat(top_v > -jnp.inf, SLC_BLOCK, axis=-1)
        ss = jnp.einsum('bqhd,bqsd->bhqs', qb, kg)
        ps = masked_softmax(ss, smask[:, None])
        o_slc = jnp.einsum('bhqs,bqsd->bqhd', ps, vg)
        kw = lax.dynamic_slice_in_dim(wk, s, band, axis=1)
        vw = lax.dynamic_slice_in_dim(wv, s, band, axis=1)
        kpos = pos0 - WINDOW + s + jnp.arange(band)
        wmask = (kpos[None] >= 0) & (kpos[None] <= tpos[:, None]) & (kpos[None] > tpos[:, None] - WINDOW)
        sw = jnp.einsum('bqhd,bkd->bhqk', qb, kw)
        pw = masked_softmax(sw, wmask[None, None])
        o_win = jnp.einsum('bhqk,bkd->bqhd', pw, vw)
        return gb[..., 0:1] * o_cmp + gb[..., 1:2] * o_slc + gb[..., 2:3] * o_win

    qs = qp.reshape(b, n_qb, qb_len, h, d).swapaxes(0, 1)
    gs = gp.reshape(b, n_qb, qb_len, h, 3).swapaxes(0, 1)
    starts = jnp.arange(n_qb, dtype=jnp.int32) * qb_len
    out = lax.map(query_block, (qs, gs, starts))
    return out.swapaxes(0, 1).reshape(b, n_qb * qb_len, h, d)[:, :t]


def conv_ffn(h, prev, w_gate, w_val, conv_w, conv_b, w_down):
    gc, new_prev = causal_dwconv(h @ w_gate, prev, conv_w, conv_b)
    act = jax.nn.gelu(gc, approximate=True) * (h @ w_val)
    return act @ w_down, new_prev


def hybrid_layer(x, pos0, nsa_past, win_prefix, win_keep, conv_prev, ssm_prev, pool_prev, ffn_prev, p):
    b, t, _ = x.shape
    pos = pos0 + jnp.arange(t, dtype=jnp.int32)
    h = rmsnorm(x, p['norm_mix_pre'])
    proj = h @ p['w_in']
    z, xbc, dt_raw, u_pool, q, kv, g = jnp.split(proj, IN_SPLITS, axis=-1)
    y_ssd, conv_new, ssm_new = ssd_mixer(z, xbc, dt_raw, conv_prev, ssm_prev, p['ssd_conv_w'], p['ssd_conv_b'],
                                         p['ssd_dt_bias'], p['ssd_a_log'], p['ssd_d'], p['ssd_norm'])
    y_pool, pool_new = pool_mixer(u_pool, pool_prev, pos0, p['pool_w'], p['pool_scale'])
    q = rope(q.reshape(b, t, NSA_H, NSA_HD), pos)
    kv = kv.reshape(b, t, 6, NSA_HD)
    rows = jnp.stack([rope(kv[:, :, 0::2], pos), kv[:, :, 1::2]], axis=3).reshape(b, t, 6, NSA_HD)
    cache_rows = rows[:, :, :4]
    win_ext = jnp.concatenate([win_prefix, rows[:, :, 4:]], axis=1)
    full = cache_rows if nsa_past is None else jnp.concatenate([nsa_past, cache_rows], axis=1)
    gates = jax.nn.sigmoid(g.reshape(b, t, NSA_H, 3))
    y_nsa = nsa_attend(q, gates, full, win_ext, pos0, p['nsa_pe_k'], p['nsa_pe_v'], p['nsa_w1_k'],
                       p['nsa_w1_v'], p['nsa_w2_k'], p['nsa_w2_v']).astype(x.dtype)
    new_win = win_ext[:, -win_keep:]
    mix = jnp.concatenate([y_ssd, y_pool, y_nsa.reshape(b, t, NSA_W)], axis=-1) @ p['w_out']
    x = x + rmsnorm(mix, p['norm_mix_post'])
    f, ffn_new = conv_ffn(rmsnorm(x, p['norm_ffn_pre']), ffn_prev, p['ffn_w_gate'], p['ffn_w_val'],
                          p['ffn_conv_w'], p['ffn_conv_b'], p['ffn_w_down'])
    x = x + rmsnorm(f, p['norm_ffn_post'])
    return x, (cache_rows, new_win, conv_new, ssm_new, pool_new, ffn_new)


def setup_inputs(seed: int = 0) -> dict:
    key = jax.random.key(seed)
    keys = iter(jax.random.split(key, 48))
    f32 = jnp.float32

    def nrm(shape, scale):
        return jax.random.normal(next(keys), shape, f32) * scale

    def gain(shape):
        return 1.0 + nrm(shape, 0.05)

    n_pages = PAST_LEN // PAGE_SIZE
    n_phys = (5 * DEC_BATCH * n_pages + 3) // 4
    win_buf = min(WINDOW, PAST_LEN)
    perm = jax.random.permutation(next(keys), n_phys)
    page_table = perm[:DEC_BATCH * n_pages].reshape(DEC_BATCH, n_pages).astype(jnp.int32)
    u = jax.random.uniform(next(keys), (DEPTH, SSD_H), f32)
    dt0 = jnp.exp(u * (math.log(0.1) - math.log(0.001)) + math.log(0.001))
    ssd_dt_bias = dt0 + jnp.log(-jnp.expm1(-dt0))
    ssd_a_log = jnp.log(jax.random.uniform(next(keys), (DEPTH, SSD_H), f32, 1.0, 16.0))
    return {
        'x_prompt': nrm((BATCH, SEQ, D_MODEL), 1.0),
        'x_sample': nrm((DEC_BATCH, DEC_SEQ, D_MODEL), 1.0),
        'cache_nsa_kv': nrm((DEPTH, n_phys, PAGE_SIZE, 4, NSA_HD), 1.0),
        'page_table': page_table,
        'state_nsa_win': nrm((DEPTH, DEC_BATCH, win_buf, 2, NSA_HD), 1.0),
        'state_ssd_conv': nrm((DEPTH, DEC_BATCH, SSD_CONV - 1, SSD_CONV_DIM), 1.0),
        'state_ssm': nrm((DEPTH, DEC_BATCH, SSD_H, SSD_HD, SSD_N), 0.1),
        'state_pool': nrm((DEPTH, DEC_BATCH, POOL_KEEP, POOL_W), 1.0),
        'state_ffn_conv': nrm((DEPTH, DEC_BATCH, FFN_CONV - 1, D_FF), 1.0),
        'norm_mix_pre': gain((DEPTH, D_MODEL)),
        'w_in': nrm((DEPTH, D_MODEL, IN_W), D_MODEL ** -0.5),
        'ssd_conv_w': nrm((DEPTH, SSD_CONV, SSD_CONV_DIM), SSD_CONV ** -0.5),
        'ssd_conv_b': nrm((DEPTH, SSD_CONV_DIM), 0.02),
        'ssd_dt_bias': ssd_dt_bias,
        'ssd_a_log': ssd_a_log,
        'ssd_d': gain((DEPTH, SSD_H)),
        'ssd_norm': gain((DEPTH, SSD_W)),
        'pool_w': nrm((DEPTH, len(POOL_WINDOWS), POOL_GC, POOL_GC), POOL_GC ** -0.5),
        'pool_scale': 1.0 + nrm((DEPTH, POOL_W), 0.1),
        'nsa_pe_k': nrm((DEPTH, CMP_LEN, NSA_HD), 0.1),
        'nsa_pe_v': nrm((DEPTH, CMP_LEN, NSA_HD), 0.1),
        'nsa_w1_k': nrm((DEPTH, CMP_LEN, NSA_HD, CMP_HID), (CMP_LEN * NSA_HD) ** -0.5),
        'nsa_w1_v': nrm((DEPTH, CMP_LEN, NSA_HD, CMP_HID), (CMP_LEN * NSA_HD) ** -0.5),
        'nsa_w2_k': nrm((DEPTH, CMP_HID, NSA_HD), CMP_HID ** -0.5),
        'nsa_w2_v': nrm((DEPTH, CMP_HID, NSA_HD), CMP_HID ** -0.5),
        'w_out': nrm((DEPTH, MIX_W, D_MODEL), MIX_W ** -0.5),
        'norm_mix_post': gain((DEPTH, D_MODEL)),
        'norm_ffn_pre': gain((DEPTH, D_MODEL)),
        'ffn_w_gate': nrm((DEPTH, D_MODEL, D_FF), D_MODEL ** -0.5),
        'ffn_w_val': nrm((DEPTH, D_MODEL, D_FF), D_MODEL ** -0.5),
        'ffn_conv_w': nrm((DEPTH, FFN_CONV, D_FF), FFN_CONV ** -0.5),
        'ffn_conv_b': nrm((DEPTH, D_FF), 0.02),
        'ffn_w_down': nrm((DEPTH, D_FF, D_MODEL), D_FF ** -0.5),
        'norm_ffn_post': gain((DEPTH, D_MODEL)),
    }


def reference(x_prompt, x_sample, cache_nsa_kv, page_table, state_nsa_win, state_ssd_conv, state_ssm,
              state_pool, state_ffn_conv, norm_mix_pre, w_in, ssd_conv_w, ssd_conv_b, ssd_dt_bias, ssd_a_log,
              ssd_d, ssd_norm, pool_w, pool_scale, nsa_pe_k, nsa_pe_v, nsa_w1_k, nsa_w1_v, nsa_w2_k, nsa_w2_v,
              w_out, norm_mix_post, norm_ffn_pre, ffn_w_gate, ffn_w_val, ffn_conv_w, ffn_conv_b, ffn_w_down,
              norm_ffn_post):
    params = dict(norm_mix_pre=norm_mix_pre, w_in=w_in, ssd_conv_w=ssd_conv_w, ssd_conv_b=ssd_conv_b,
                  ssd_dt_bias=ssd_dt_bias, ssd_a_log=ssd_a_log, ssd_d=ssd_d, ssd_norm=ssd_norm, pool_w=pool_w,
                  pool_scale=pool_scale, nsa_pe_k=nsa_pe_k, nsa_pe_v=nsa_pe_v, nsa_w1_k=nsa_w1_k,
                  nsa_w1_v=nsa_w1_v, nsa_w2_k=nsa_w2_k, nsa_w2_v=nsa_w2_v, w_out=w_out,
                  norm_mix_post=norm_mix_post, norm_ffn_pre=norm_ffn_pre, ffn_w_gate=ffn_w_gate,
                  ffn_w_val=ffn_w_val, ffn_conv_w=ffn_conv_w, ffn_conv_b=ffn_conv_b, ffn_w_down=ffn_w_down,
                  norm_ffn_post=norm_ffn_post)
    bp, tp, _ = x_prompt.shape
    bs = x_sample.shape[0]
    past_len = page_table.shape[1] * PAGE_SIZE
    win_buf = state_nsa_win.shape[2]
    dty = x_prompt.dtype
    win0 = jnp.zeros((bp, WINDOW, 2, NSA_HD), dty)
    conv0 = jnp.zeros((bp, SSD_CONV - 1, SSD_CONV_DIM), dty)
    ssm0 = jnp.zeros((bp, SSD_H, SSD_HD, SSD_N), dty)
    pool0 = jnp.zeros((bp, POOL_KEEP, POOL_W), dty)
    ffn0 = jnp.zeros((bp, FFN_CONV - 1, D_FF), dty)
    h_p = x_prompt
    h_s = x_sample
    outs_p = []
    outs_s = []
    for l in range(DEPTH):
        p = {k: v[l] for k, v in params.items()}
        h_p, st_p = hybrid_layer(h_p, 0, None, win0, min(WINDOW, tp), conv0, ssm0, pool0, ffn0, p)
        past = cache_nsa_kv[l][page_table].reshape(bs, past_len, 4, NSA_HD)
        prefix = jnp.pad(state_nsa_win[l], ((0, 0), (WINDOW - win_buf, 0), (0, 0), (0, 0)))
        h_s, st_s = hybrid_layer(h_s, past_len, past, prefix, win_buf, state_ssd_conv[l], state_ssm[l],
                                 state_pool[l], state_ffn_conv[l], p)
        outs_p.append(st_p)
        outs_s.append(st_s)

    def stk(outs, i):
        return jnp.stack([o[i] for o in outs])

    return (h_p, h_s, stk(outs_p, 0), stk(outs_s, 0), stk(outs_p, 1), stk(outs_s, 1), stk(outs_p, 2),
            stk(outs_s, 2), stk(outs_p, 3), stk(outs_s, 3), stk(outs_p, 4), stk(outs_s, 4), stk(outs_p, 5),
            stk(outs_s, 5))
```

```python
import numpy as np
import contextlib
import concourse.bass as bass
import concourse.mybir as mybir
from concourse.bass_utils import run_bass_kernel_spmd

F32 = mybir.dt.float32
BF16 = mybir.dt.bfloat16
I32 = mybir.dt.int32
ALU = mybir.AluOpType
AF = mybir.ActivationFunctionType
AX = mybir.AxisListType

NDS = 8
SBUF_BYTES = 229376
SBUF_BASE = 16640


class Res:
    __slots__ = ("w", "r", "name")

    def __init__(self, name=""):
        self.w = None
        self.r = {}
        self.name = name


class V:
    __slots__ = ("ap", "res")

    def __init__(self, ap, res=()):
        self.ap = ap
        self.res = tuple(res)

    def __getitem__(self, key):
        return V(self.ap[key], self.res)

    def rr(self, s, **kw):
        return V(self.ap.rearrange(s, **kw), self.res)

    def bc(self, shape):
        return V(self.ap.to_broadcast(list(shape)), self.res)

    def us(self, axis):
        return V(self.ap.unsqueeze(axis), self.res)

    def bitcast(self, dt):
        return V(self.ap.bitcast(dt), self.res)

    def with_res(self, res):
        return V(self.ap, res)

    @property
    def shape(self):
        return self.ap.shape


def _resources(vs):
    out = []
    for v in vs:
        if isinstance(v, V):
            for r in v.res:
                if r not in out:
                    out.append(r)
    return out


class Sched:
    def __init__(self, nc, es, same_sync=True):
        self.nc = nc
        self.es = es
        self.eng = dict(pe=nc.tensor, act=nc.scalar, dve=nc.vector, pool=nc.gpsimd, sp=nc.sync)
        self.prog = {k: [] for k in self.eng}
        self.sem = {}
        self.val = {}
        for k in self.eng:
            self.sem[k] = es.enter_context(nc.semaphore("s_" + k))
            self.val[k] = 0
        self.dq = {}
        for q in ("sp", "pool", "act"):
            names = []
            for i in range(NDS):
                n = "d_%s%d" % (q, i)
                self.sem[n] = es.enter_context(nc.semaphore(n))
                self.val[n] = 0
                names.append(n)
            self.dq[q] = [names, 0]
        self.seen = {k: {} for k in self.eng}
        self.same_sync = same_sync
        self.ntens = 0
        self.sb_top = SBUF_BASE
        self.sb_peak = 0

    def sb(self, shape, dtype, name=None):
        self.ntens += 1
        name = "%s_%d" % (name or "t", self.ntens)
        esz = 2 if dtype == BF16 else 4
        nbytes = esz
        for d in shape[1:]:
            nbytes *= d
        off = (self.sb_top + 31) // 32 * 32
        self.sb_top = off + nbytes
        self.sb_peak = max(self.sb_peak, self.sb_top)
        assert self.sb_top <= SBUF_BYTES, "SBUF overflow: %s needs %d at %d" % (name, nbytes, off)
        t = self.nc.alloc_sbuf_tensor_at(name, list(shape), dtype, offset=off)
        return V(t[tuple(slice(None) for _ in shape)], (Res(name),))

    @contextlib.contextmanager
    def scope(self):
        self.barrier()
        top = self.sb_top
        try:
            yield
        finally:
            self.barrier()
            self.sb_top = top

    def ps(self, shape, dtype, name=None):
        self.ntens += 1
        name = "%s_%d" % (name or "p", self.ntens)
        t = self.es.enter_context(self.nc.psum_tensor(name, list(shape), dtype))
        return V(t[tuple(slice(None) for _ in shape)], (Res(name),))

    def op(self, e, fn, outs, ins, dma=False):
        need = {}
        seen = self.seen[e]

        def add(s, v):
            if s == e and not dma:
                if e == "pe" or (not self.same_sync and e != "pool"):
                    return
            if seen.get(s, 0) >= v:
                return
            if need.get(s, 0) < v:
                need[s] = v

        rin = _resources(ins)
        rout = _resources(outs)
        for r in rin:
            if r.w is not None:
                add(*r.w)
        for r in rout:
            if r.w is not None:
                add(*r.w)
            for s, v in r.r.items():
                add(s, v)
        if dma:
            names, i = self.dq[e]
            s = names[i % len(names)]
            self.dq[e][1] += 1
            if self.val[s] > 0:
                add(s, self.val[s])
            self.val[s] += 16
            ev = (s, self.val[s])
            inc = 16
        else:
            self.val[e] += 1
            s = e
            ev = (e, self.val[e])
            inc = 1
        for s2, v in need.items():
            seen[s2] = v
        self.prog[e].append((list(need.items()), fn, s, inc))
        for r in rin:
            if r.r.get(ev[0], 0) < ev[1]:
                r.r[ev[0]] = ev[1]
        for r in rout:
            r.w = ev
            r.r = {}

    def barrier(self):
        for e in self.eng:
            waits = []
            for s, v in self.val.items():
                if v > 0 and self.seen[e].get(s, 0) < v and s != e:
                    waits.append((s, v))
                    self.seen[e][s] = v
            if waits:
                self.prog[e].append((waits, None, None, 0))

    def emit(self):
        waits = [(s, v) for s, v in self.val.items() if v > 0 and s != "sp"]
        self.prog["sp"].append((waits, None, None, 0))
        sem = self.sem
        prog = self.prog

        def mk(ename):
            def body(eobj):
                for waits, fn, s, inc in prog[ename]:
                    for (ws, wv) in waits:
                        eobj.wait_ge(sem[ws], wv)
                    if fn is not None:
                        ins = fn(eobj)
                        ins.then_inc(sem[s], inc)
            return body

        with self.nc.Block() as block:
            block.tensor(mk("pe"))
            block.scalar(mk("act"))
            block.vector(mk("dve"))
            block.gpsimd(mk("pool"))
            block.sync(mk("sp"))

    def dma(self, q, out, in_, **kw):
        self.op(q, lambda e: e.dma_start(out=out.ap, in_=in_.ap, **kw), [out], [in_], dma=True)

    def mm(self, out, lhsT, rhs, start=True, stop=True):
        self.op("pe", lambda e: e.matmul(out.ap, lhsT.ap, rhs.ap, start=start, stop=stop), [out], [lhsT, rhs])

    def tr(self, out, in_, ident):
        self.op("pe", lambda e: e.transpose(out.ap, in_.ap, ident.ap), [out], [in_, ident])

    def act(self, out, in_, func, bias=None, scale=None, accum=None):
        ins = [in_] + [b for b in (bias, scale) if isinstance(b, V)]
        outs = [out] + ([accum] if accum is not None else [])

        def fn(e):
            kw = {}
            if bias is not None:
                kw["bias"] = bias.ap if isinstance(bias, V) else bias
            if scale is not None:
                kw["scale"] = scale.ap if isinstance(scale, V) else scale
            if accum is not None:
                kw["accum_out"] = accum.ap
            return e.activation(out=out.ap, in_=in_.ap, func=func, **kw)
        self.op("act", fn, outs, ins)

    def tt(self, e, out, in0, in1, op):
        self.op(e, lambda en: en.tensor_tensor(out.ap, in0.ap, in1.ap, op), [out], [in0, in1])

    def ts(self, e, out, in0, s1, s2=None, op0=ALU.mult, op1=None, accum=None):
        ins = [in0] + [b for b in (s1, s2) if isinstance(b, V)]
        outs = [out] + ([accum] if accum is not None else [])

        def fn(en):
            a1 = s1.ap if isinstance(s1, V) else s1
            a2 = s2.ap if isinstance(s2, V) else s2
            kw = {}
            if op1 is not None:
                kw["op1"] = op1
            if accum is not None:
                kw["accum_out"] = accum.ap
            return en.tensor_scalar(out.ap, in0.ap, a1, a2, op0, **kw)
        self.op(e, fn, outs, ins)

    def stt(self, out, in0, scalar, in1, op0, op1):
        ins = [in0, in1] + ([scalar] if isinstance(scalar, V) else [])

        def fn(en):
            sc = scalar.ap if isinstance(scalar, V) else scalar
            return en.scalar_tensor_tensor(out.ap, in0.ap, sc, in1.ap, op0, op1)
        self.op("dve", fn, [out], ins)

    def copy(self, e, out, in_):
        if e == "act":
            self.op("act", lambda en: en.copy(out.ap, in_.ap), [out], [in_])
        else:
            self.op(e, lambda en: en.tensor_copy(out.ap, in_.ap), [out], [in_])

    def memset(self, e, out, val):
        self.op(e, lambda en: en.memset(out.ap, val), [out], [])

    def reduce(self, out, in_, op, axis=AX.X):
        self.op("dve", lambda en: en.tensor_reduce(out.ap, in_.ap, axis, op), [out], [in_])

    def max8(self, out, in_):
        self.op("dve", lambda en: en.max(out.ap, in_.ap), [out], [in_])

    def match_replace(self, out, to_replace, values, imm):
        self.op("dve", lambda en: en.match_replace(out.ap, to_replace.ap, values.ap, imm), [out], [to_replace, values])

    def recip(self, out, in_):
        self.op("dve", lambda en: en.reciprocal(out.ap, in_.ap), [out], [in_])


class Ring:
    def __init__(self, S, n, shape, dtype, name, psum=False):
        self.t = [(S.ps if psum else S.sb)(shape, dtype, name) for _ in range(n)]
        self.i = 0

    def next(self):
        t = self.t[self.i % len(self.t)]
        self.i += 1
        return t

import ml_dtypes

D_MODEL = 1024
IN_W = 2452
OFF_XBC = 512
OFF_DT = 1536
OFF_POOL = 1544
OFF_Q = 1800
OFF_KV = 2056
NEG = -30000.0
RMS_EPS = 1e-6
NCORES = 8


def make_consts(cfg):
    T = cfg["T"]
    NT = T // 128
    NS = cfg["NS"]
    LS = NS * 8
    c = {}
    c["c_identb"] = np.eye(128).astype(ml_dtypes.bfloat16)
    c["c_identf"] = np.eye(128, dtype=np.float32)
    k = np.arange(128)
    c["c_incl"] = (k[:, None] <= k[None, :]).astype(np.float32)
    c["c_after"] = (k[:, None] > k[None, :]).astype(np.float32)
    c["c_diag"] = np.where(k[None, :] <= k[:, None], 0.0, NEG).astype(np.float32)
    c["c_far"] = np.where(k[None, :] > k[:, None], 0.0, NEG).astype(np.float32)
    half = 8
    inv = 1.0 / (500000.0 ** (np.arange(half, dtype=np.float32) / half))
    pos = np.arange(T, dtype=np.float32)
    ang = pos[:, None] * inv[None, :]
    cs = np.concatenate([np.cos(ang), np.sin(ang)], axis=1).astype(np.float32)
    c["c_cs_p"] = np.ascontiguousarray(cs.reshape(NT, 128, 16).transpose(1, 0, 2))
    r = np.arange(8)
    c["c_cmpdiag"] = np.where(k[:, None] >= 16 * (r[None, :] - 1) + 31, 0.0, NEG).astype(np.float32)
    n_slc = T // 64
    addc = np.zeros((128, NT, max(n_slc, 8)), np.float32)
    for ti in range(NT):
        tpos = 128 * ti + k
        cur = tpos // 64
        j = np.arange(n_slc)
        forced = (j[None, :] == 0) | (j[None, :] == cur[:, None]) | (j[None, :] == cur[:, None] - 1)
        causal = (j[None, :] * 64 <= tpos[:, None])
        a = np.where(forced, 1e30, 0.0)
        a = np.where(causal, a, -1e30)
        addc[:, ti, :n_slc] = a
    c["c_addc_p"] = addc
    n_cmp = T // 16 - 1
    nn = np.arange(128)
    jj = np.arange(max(n_slc, 8))
    ov = ((16 * nn[:, None] < 64 * jj[None, :] + 64) & (16 * nn[:, None] + 32 > 64 * jj[None, :]))
    ov = ov & (nn[:, None] < n_cmp)
    c["c_ov_p"] = ov.astype(ml_dtypes.bfloat16)
    rc = np.zeros((64, 4, 16), np.float32)
    for g, w in enumerate((2, 4, 8, 16)):
        rc[:, g, :] = 1.0 / np.minimum(np.arange(16) + 1, w)
    c["c_rc"] = rc
    c["c_ones"] = np.ones((128, 128), np.float32)
    if NS > 0:
        NPG = cfg["NPG"]
        past = NPG * 128
        WB = min(512, past)
        r = np.arange(LS)
        sq = r // 8
        same = sq[:, None] == sq[None, :]
        c["c_incl_s"] = ((r[:, None] <= r[None, :]) & same).astype(np.float32)
        c["c_after_s"] = ((r[:, None] > r[None, :]) & same).astype(np.float32)
        sm_ = np.zeros((LS, NS, 128), np.float32)
        rm_ = np.zeros((LS, NS), np.float32)
        for b in range(NS):
            sm_[b * 8:(b + 1) * 8, b, :] = 1.0
            rm_[b * 8:(b + 1) * 8, b] = 1.0
        c["c_seqmask_s"] = sm_
        c["c_rowmask_s"] = rm_
        pos_s = (past + (r % 8)).astype(np.float32)
        ang_s = pos_s[:, None] * inv[None, :]
        c["c_cs_s"] = np.concatenate([np.cos(ang_s), np.sin(ang_s)], axis=1).astype(np.float32)
        ri = np.arange(32) % 8
        c["c_R"] = (ri[:, None] == ri[None, :]).astype(np.float32)
        NSLC_S = past // 64 + 1
        NV = past // 16 - 1
        cur = past // 64
        j = np.arange(NSLC_S)
        forced = (j == 0) | (j == cur) | (j == cur - 1)
        c["c_addc_s"] = np.broadcast_to(np.where(forced, 1e30, 0.0).astype(np.float32)[None, :], (32, NSLC_S)).copy()
        NTC = (NV + 127) // 128
        nn2 = np.arange(NTC * 128)
        ov2 = ((16 * nn2[:, None] < 64 * j[None, :] + 64) & (16 * nn2[:, None] + 32 > 64 * j[None, :])) & (nn2[:, None] < NV)
        c["c_ov_s"] = np.ascontiguousarray(ov2.reshape(NTC, 128, NSLC_S).transpose(1, 0, 2)).astype(ml_dtypes.bfloat16)
        cc = np.arange(64)
        c["c_tokmask_s"] = np.where(cc[None, :] <= ri[:, None], 0.0, NEG).astype(np.float32)
        jw = np.arange(WB + 8)
        stored = jw[None, :] < WB
        valid = np.where(stored, jw[None, :] > ri[:, None] + WB - 512, (jw[None, :] - WB) <= ri[:, None])
        c["c_winmask_s"] = np.where(valid, 0.0, NEG).astype(np.float32)
        c["c_pidx"] = np.broadcast_to(np.arange(128, dtype=np.float32)[:, None], (128, NS * NPG)).copy()
    return c


class Obj:
    pass


def build(cfg):
    T = cfg["T"]
    NT = T // 128
    DEPTH = cfg["DEPTH"]
    NS = cfg["NS"]
    WK = min(512, T)
    NCMP = T // 16 - 1
    NSLC = T // 64
    consts = make_consts(cfg)
    STAGE = cfg.get("stage", 99)
    VAR = cfg.get("var", 0)
    SST = cfg.get("sst", 99)

    nc = bass.Bass("TRN2", target_bir_lowering=False)
    es = contextlib.ExitStack()
    outs = []
    with es:
        S = Sched(nc, es, same_sync=cfg.get("same_sync", True))

        def din(name, shape, dt=F32):
            return V(nc.dram_tensor(name, list(shape), dt, kind="ExternalInput").ap())

        def dout(name, shape, dt=F32):
            outs.append(name)
            return V(nc.dram_tensor(name, list(shape), dt, kind="ExternalOutput").ap())

        def dscr(name, shape, dt=F32):
            return nc.dram_tensor(name, list(shape), dt, kind="Internal").ap()

        x_p = din("x_p", [T, D_MODEL])
        W = {}
        for nm, shp in [("norm_mix_pre", [DEPTH, 1024]), ("w_in", [DEPTH, 1024, IN_W]), ("ssd_conv_w", [DEPTH, 4, 1024]),
                        ("ssd_conv_b", [DEPTH, 1024]), ("ssd_dt_bias", [DEPTH, 8]), ("ssd_a_log", [DEPTH, 8]),
                        ("ssd_d", [DEPTH, 8]), ("ssd_norm", [DEPTH, 512]), ("pool_w", [DEPTH, 4, 64, 64]),
                        ("pool_scale", [DEPTH, 256]), ("nsa_pe_k", [DEPTH, 32, 64]), ("nsa_pe_v", [DEPTH, 32, 64]),
                        ("nsa_w1_k", [DEPTH, 32, 64, 128]), ("nsa_w1_v", [DEPTH, 32, 64, 128]),
                        ("nsa_w2_k", [DEPTH, 128, 64]), ("nsa_w2_v", [DEPTH, 128, 64]), ("w_out", [DEPTH, 1024, 1024]),
                        ("norm_mix_post", [DEPTH, 1024]), ("norm_ffn_pre", [DEPTH, 1024]),
                        ("ffn_w_gate", [DEPTH, 1024, 4096]), ("ffn_w_val", [DEPTH, 1024, 4096]),
                        ("ffn_conv_w", [DEPTH, 3, 4096]), ("ffn_conv_b", [DEPTH, 4096]),
                        ("ffn_w_down", [DEPTH, 4096, 1024]), ("norm_ffn_post", [DEPTH, 1024])]:
            W[nm] = din(nm, shp)
        CD = {}
        for nm, arr in consts.items():
            CD[nm] = din(nm, list(arr.shape), BF16 if arr.dtype == ml_dtypes.bfloat16 else F32)

        y_p = dout("y_p", [T, D_MODEL])
        kv_p = dout("kv_p", [DEPTH, T, 256])
        win_p = dout("win_p", [DEPTH, WK, 128])
        conv_p = dout("conv_p", [DEPTH, 3, 1024])
        ssm_p = dout("ssm_p", [DEPTH, 8, 64, 128])
        pool_p = dout("pool_p", [DEPTH, 15, 256])
        ffn_p = dout("ffn_p", [DEPTH, 2, 4096])

        xcur_ap = dscr("xcur", [T, D_MODEL])
        xcur_res = [Res("xcur%d" % i) for i in range(NT)]

        def xcur_tile(i):
            return V(xcur_ap[i * 128:(i + 1) * 128, :], (xcur_res[i],))

        def cload(name, shape, dt=F32, src=None, q="sp"):
            t = S.sb(shape, dt, name)
            S.dma(q, t, src if src is not None else CD[name])
            return t

        identb = cload("c_identb", [128, 128], BF16)
        identf = cload("c_identf", [128, 128])
        incl = cload("c_incl", [128, 128])
        after = cload("c_after", [128, 128])
        diagm = cload("c_diag", [128, 128])
        farm = cload("c_far", [128, 128])
        cs_p = cload("c_cs_p", [128, NT, 16])
        cmpdiag = cload("c_cmpdiag", [128, 8])
        addc_p = cload("c_addc_p", [128, NT, max(NSLC, 8)])
        ov_p = cload("c_ov_p", [128, max(NSLC, 8)], BF16)
        rc_t = cload("c_rc", [64, 4, 16])
        ones = cload("c_ones", [128, 128])

        psf = Ring(S, 6, [128, 512], F32, "psf", psum=True)
        psb = Ring(S, 2, [128, 1024], BF16, "psb", psum=True)

        win_sb = S.sb([128, 8, IN_W], BF16, "win")
        wout_sb = S.sb([128, 8, 1024], BF16, "wout")
        gcol_pre = S.sb([128, 8], F32, "gpre")
        gcol_ffn = S.sb([128, 8], F32, "gffn")
        gpost_b = S.sb([128, 1024], F32, "gpost")
        gfpost_b = S.sb([128, 1024], F32, "gfpost")
        mixg = S.sb([128, 8], F32, "mixg")
        convw = S.sb([128, 8, 4], F32, "convw")
        convb = S.sb([128, 8], F32, "convb")
        dtb_b = S.sb([128, 8], F32, "dtb")
        a_b = S.sb([128, 8], F32, "ab")
        dsk_b = S.sb([128, 8], F32, "dsk")
        poolw_f = S.sb([64, 4, 64], F32, "poolwf")
        pscale_b = S.sb([64, 256], F32, "pscale")
        poolw_sb = S.sb([64, 4, 64], BF16, "poolw")
        CW = Obj()

        def load_cmp_weights(l):
            CW.peT_k = S.sb([64, 32], BF16, "pek")
            CW.peT_v = S.sb([64, 32], BF16, "pev")
            CW.w1k = S.sb([64, 32, 128], BF16, "w1k")
            CW.w1v = S.sb([64, 32, 128], BF16, "w1v")
            CW.w2k = S.sb([128, 64], BF16, "w2k")
            CW.w2v = S.sb([128, 64], BF16, "w2v")
            CW.cbias_k = S.sb([128, 1], F32, "cbk")
            CW.cbias_v = S.sb([128, 1], F32, "cbv")
            pef = small()
            S.dma("sp", pef[0:64, 0:32], W["nsa_pe_k"][l].rr("l d -> d l"), allow_slow_non_contiguous=True)
            S.copy("dve", CW.peT_k, pef[0:64, 0:32])
            pef = small()
            S.dma("sp", pef[0:64, 0:32], W["nsa_pe_v"][l].rr("l d -> d l"), allow_slow_non_contiguous=True)
            S.copy("dve", CW.peT_v, pef[0:64, 0:32])
            S.dma("pool", CW.w1k, W["nsa_w1_k"][l].rr("l d e -> d l e"))
            S.dma("pool", CW.w1v, W["nsa_w1_v"][l].rr("l d e -> d l e"))
            S.dma("pool", CW.w2k, W["nsa_w2_k"][l])
            S.dma("pool", CW.w2v, W["nsa_w2_v"][l])
            for (w1, peT, cb) in ((CW.w1k, CW.peT_k, CW.cbias_k), (CW.w1v, CW.peT_v, CW.cbias_v)):
                p = psf.next()
                for li in range(32):
                    S.mm(p[:, 0:1], w1[:, li, :], peT[:, li:li + 1], start=(li == 0), stop=(li == 31))
                S.copy("act", cb, p[:, 0:1])
        fconvw = S.sb([128, 32, 3], F32, "fconvw")
        fconvb = S.sb([128, 32], F32, "fconvb")

        def load_layer_weights(l):
            S.dma("pool", win_sb, W["w_in"][l].rr("(k p) n -> p k n", p=128))
            S.dma("pool", wout_sb, W["w_out"][l].rr("(k p) n -> p k n", p=128))
            S.dma("sp", gcol_pre, W["norm_mix_pre"][l].rr("(k p) -> p k", p=128), allow_slow_non_contiguous=True)
            S.dma("sp", gcol_ffn, W["norm_ffn_pre"][l].rr("(k p) -> p k", p=128), allow_slow_non_contiguous=True)
            S.dma("sp", gpost_b, V(W["norm_mix_post"].ap[l].partition_broadcast(128)))
            S.dma("sp", gfpost_b, V(W["norm_ffn_post"].ap[l].partition_broadcast(128)))
            S.memset("pool", mixg, 1.0)
            S.dma("sp", mixg[:, 0:4], W["ssd_norm"][l].rr("(k p) -> p k", p=128), allow_slow_non_contiguous=True)
            for t_ in range(4):
                S.dma("sp", convw[:, :, t_], W["ssd_conv_w"][l, t_].rr("(c p) -> p c", p=128), allow_slow_non_contiguous=True)
            S.dma("sp", convb, W["ssd_conv_b"][l].rr("(c p) -> p c", p=128), allow_slow_non_contiguous=True)
            S.dma("sp", dtb_b, V(W["ssd_dt_bias"].ap[l].partition_broadcast(128)))
            S.dma("sp", a_b, V(W["ssd_a_log"].ap[l].partition_broadcast(128)))
            S.act(a_b, a_b, AF.Exp)
            S.ts("dve", a_b, a_b, -1.0, None, ALU.mult)
            S.dma("sp", dsk_b, V(W["ssd_d"].ap[l].partition_broadcast(128)))
            S.dma("sp", poolw_f, W["pool_w"][l].rr("g c d -> c g d"))
            S.dma("sp", pscale_b, V(W["pool_scale"].ap[l].partition_broadcast(64)))
            S.tt("dve", poolw_sb, poolw_f, pscale_b.rr("p (g d) -> p g d", g=4), ALU.mult)
            for t_ in range(3):
                S.dma("sp", fconvw[:, :, t_], W["ffn_conv_w"][l, t_].rr("(c p) -> p c", p=128), allow_slow_non_contiguous=True)
            S.dma("sp", fconvb, W["ffn_conv_b"][l].rr("(c p) -> p c", p=128), allow_slow_non_contiguous=True)

        r_sm = Ring(S, 24, [128, 32], F32, "sm")
        NTB = min(4, NT)
        xmid_ap = dscr("xmid", [T, D_MODEL])
        xmid_res = [Res("xmid%d" % i) for i in range(NT)]

        def xmid_tile(i):
            return V(xmid_ap[i * 128:(i + 1) * 128, :], (xmid_res[i],))

        R = Obj()
        F = Obj()
        PP = Obj()

        def alloc_prompt_persist():
            PP.kcT = S.sb([64, T], BF16, "kcT")
            PP.vcT = S.sb([64, T], BF16, "vcT")
            PP.ksT = S.sb([64, T], BF16, "ksT")
            PP.kwT = S.sb([64, T], BF16, "kwT")
            PP.vs_tok = S.sb([128, NT, 64], BF16, "vstok")
            PP.vw_tok = S.sb([128, NT, 64], BF16, "vwtok")
            PP.geluT_k = S.sb([128, 128], BF16, "gelk")
            PP.geluT_v = S.sb([128, 128], BF16, "gelv")
            PP.ckT = S.sb([64, 128], BF16, "ckT")
            PP.cv = S.sb([128, 64], BF16, "cv")
            PP.ST = S.sb([128, 512], F32, "ST")
            PP.STbf = S.sb([128, 512], BF16, "STbf")
            PP.cprev = S.sb([128, 8, 3], F32, "cprev")
            PP.pprev = S.sb([64, 4, 15], F32, "pprev")
            PP.carry = S.sb([128, 32, 2], F32, "carry")

        def alloc_mixer_rings():
            R.x = Ring(S, 1, [128, 1024], F32, "xt")
            R.junk = Ring(S, 1, [128, 1024], BF16, "junk")
            R.xn = Ring(S, 1, [128, 1024], BF16, "xn")
            R.hT = Ring(S, 1, [128, 8, 128], BF16, "hT")
            R.ext = Ring(S, 1, [128, 8, 1, 131], F32, "ext")
            R.acc = Ring(S, 1, [128, 8, 128], F32, "cacc")
            R.actbf = Ring(S, 1, [128, 8, 128], BF16, "actbf")
            R.xsB = Ring(S, 1, [128, 768], BF16, "xsB")
            R.xc = Ring(S, 2, [128, 512], BF16, "xc")
            R.rhsD = Ring(S, 1, [128, 4, 128], F32, "rhsD")
            R.Dexp = Ring(S, 1, [128, 4, 128], F32, "Dexp")
            R.scm = Ring(S, 1, [128, 2, 128], F32, "scm")
            R.MT = Ring(S, 1, [128, 8, 128], BF16, "MT")
            R.y = Ring(S, 1, [128, 512], F32, "y")
            R.t512 = Ring(S, 3, [128, 512], F32, "t512")
            R.pext = Ring(S, 1, [64, 4, 1, 143], F32, "pext")
            R.ps2 = Ring(S, 1, [64, 4, 1, 142], F32, "ps2")
            R.ps4 = Ring(S, 1, [64, 3, 1, 140], F32, "ps4")
            R.ps8 = Ring(S, 1, [64, 2, 1, 136], F32, "ps8")
            R.ps16 = Ring(S, 1, [64, 1, 1, 128], F32, "ps16")
            R.pooled = Ring(S, 1, [64, 4, 128], BF16, "pooled")
            R.mix = Ring(S, 1, [128, 1024], BF16, "mix")
            R.mixT = Ring(S, 1, [128, 8, 128], BF16, "mixT")
            R.qf = Ring(S, 1, [128, 256], F32, "qf")
            R.rows = Ring(S, 1, [128, 384], F32, "rows")
            R.rt = Ring(S, 1, [128, 4, 4, 8], F32, "ropet")
            R.qbf = Ring(S, 1, [128, 256], BF16, "qbf")
            R.kvbf = Ring(S, 1, [128, 384], BF16, "kvbf")
            R.qT = Ring(S, 1, [64, 4, 128], BF16, "qT")
            R.gates = Ring(S, 2, [128, 12], F32, "gates")
            R.ssb = Ring(S, 1, [128, max(T, 1280)], F32, "ssb")
            R.ebf = Ring(S, 1, [128, max(T, 512)], BF16, "ebf")
            R.pT = Ring(S, 1, [128, max(NT, 5), 128], BF16, "pT")
            R.e32 = Ring(S, 2, [128, 128], F32, "e32")
            R.pn = Ring(S, 2, [128, 128], BF16, "pn")
            R.pnT = Ring(S, 2, [128, 128], BF16, "pnT")
            R.oacc = Ring(S, 1, [128, 256], F32, "oacc")
            R.imp = Ring(S, 3, [128, max(NSLC, 8)], F32, "imp")
            R.selm = Ring(S, 1, [128, max(NSLC, 8)], F32, "selm")
            R.xo = Ring(S, 1, [128, 1024], F32, "xo")

        def alloc_ffn_bufs(Ltot):
            F.h2T = S.sb([128, 8, Ltot], BF16, "h2T")
            F.actT = S.sb([128, 32, Ltot], BF16, "actT")
            F.wg = Ring(S, 2, [128, 8, 256], BF16, "wg")
            F.wv = Ring(S, 2, [128, 8, 256], BF16, "wv")
            F.wd = Ring(S, 2, [128, 2, 1024], BF16, "wd")
            F.gext = Ring(S, 2, [128, Ltot + 2], F32, "gext")
            F.gacc = Ring(S, 2, [128, Ltot], F32, "gacc")
            F.f_sb = S.sb([128, (Ltot + 127) // 128, 1024], F32, "fsb")
            F.xin = Ring(S, 1, [128, 1024], F32, "xin")
            F.junk = Ring(S, 1, [128, 1024], BF16, "fjunk")
            F.xn = Ring(S, 1, [128, 1024], BF16, "fxn")
            F.gts = Ring(S, 1, [128, 256], F32, "gts")

        def small():
            return r_sm.next()

        DBG = cfg.get("debug", False)

        def dbg(name, v, dt=F32):
            if not DBG:
                return
            shp = list(v.shape)
            o = dout("dbg_" + name, shp, dt)
            S.dma("sp", o, v)

        def rmsnorm_T(src, L, gcol, hT_dst, D=1024, B=None):
            B = B or R
            junk = B.junk.next()
            ss = small()
            S.act(junk[:L, :D], src, AF.Square, accum=ss[:L, 0:1])
            S.ts("dve", ss[:L, 1:2], ss[:L, 0:1], 1.0 / D, RMS_EPS, ALU.mult, ALU.add)
            S.act(ss[:L, 1:2], ss[:L, 1:2], AF.Sqrt)
            S.recip(ss[:L, 2:3], ss[:L, 1:2])
            xn = B.xn.next()
            S.act(xn[:L, :D], src, AF.Copy, scale=ss[:L, 2:3])
            nk = D // 128
            pT = psb.next()
            pTv = pT.rr("p (k l) -> p k l", k=8)
            for k in range(nk):
                S.tr(pTv[:, k, :L], xn[:L, k * 128:(k + 1) * 128], identb[:L, :L])
            S.tt("dve", hT_dst, pTv[:, 0:nk, :L], gcol.us(2).bc([128, nk, L]), ALU.mult)

        def rope(vw, H, cs, L):
            rt = R.rt.next()
            cosb = cs[:, 0:8].us(1).bc([L, H, 8])
            sinb = cs[:, 8:16].us(1).bc([L, H, 8])
            x1 = vw[:, :, 0:8]
            x2 = vw[:, :, 8:16]
            S.tt("pool", rt[:L, 0, 0:H, :], x1, cosb, ALU.mult)
            S.tt("pool", rt[:L, 1, 0:H, :], x2, sinb, ALU.mult)
            S.tt("pool", rt[:L, 2, 0:H, :], x2, cosb, ALU.mult)
            S.tt("pool", rt[:L, 3, 0:H, :], x1, sinb, ALU.mult)
            S.tt("pool", x1, rt[:L, 0, 0:H, :], rt[:L, 1, 0:H, :], ALU.subtract)
            S.tt("pool", x2, rt[:L, 2, 0:H, :], rt[:L, 3, 0:H, :], ALU.add)

        def softmax_rows(s_sb, L, nk, e_out, clamp=False):
            sm = small()
            S.reduce(sm[:L, 0:1], s_sb, ALU.max)
            if clamp:
                S.ts("dve", sm[:L, 1:2], sm[:L, 0:1], -1e4, -1.0, ALU.max, ALU.mult)
            else:
                S.ts("dve", sm[:L, 1:2], sm[:L, 0:1], -1.0, None, ALU.mult)
            S.act(e_out, s_sb, AF.Exp, bias=sm[:L, 1:2], accum=sm[:L, 2:3])
            return sm


        def mixer_tile_prompt(l, ti, xsrc):
            L = 128
            last_tile = (ti == NT - 1)
            if STAGE < 0:
                return
            xt = R.x.next()
            S.dma("sp", xt, xsrc)
            hT = R.hT.next()
            rmsnorm_T(xt, L, gcol_pre, hT)

            if STAGE < 1:
                return
            ext = R.ext.next()
            for half in range(2):
                p = psf.next()
                pv = p.rr("p (c l) -> p c l", c=4)
                for c4 in range(4):
                    c = half * 4 + c4
                    for k in range(8):
                        S.mm(pv[:, c4, :], win_sb[:, k, OFF_XBC + c * 128: OFF_XBC + (c + 1) * 128], hT[:, k, :],
                             start=(k == 0), stop=(k == 7))
                S.copy("act", ext[:, half * 4:(half + 1) * 4, 0, 3:131], pv)
            pext = R.pext.next()
            p = psf.next()
            pv = p.rr("p (c l) -> p c l", c=4)
            for g in range(4):
                for k in range(8):
                    S.mm(pv[0:64, g, :], win_sb[:, k, OFF_POOL + g * 64: OFF_POOL + (g + 1) * 64], hT[:, k, :],
                         start=(k == 0), stop=(k == 7))
            S.copy("act", pext[:, :, 0, 15:143], pv[0:64])
            z_ps = psf.next()
            for k in range(8):
                S.mm(z_ps, hT[:, k, :], win_sb[:, k, 0:512], start=(k == 0), stop=(k == 7))
            zs = R.t512.next()
            S.act(zs, z_ps, AF.Silu)
            q_ps = psf.next()
            for k in range(8):
                S.mm(q_ps[:, 0:256], hT[:, k, :], win_sb[:, k, OFF_Q:OFF_Q + 256], start=(k == 0), stop=(k == 7))
            kv_ps = psf.next()
            for k in range(8):
                S.mm(kv_ps[:, 0:396], hT[:, k, :], win_sb[:, k, OFF_KV:OFF_KV + 396], start=(k == 0), stop=(k == 7))
            if last_tile:
                tl = R.ssb.next()
                for nh in range(2):
                    p = psf.next()
                    for k in range(8):
                        S.mm(p, hT[:, k, :], win_sb[:, k, OFF_XBC + nh * 512: OFF_XBC + (nh + 1) * 512],
                             start=(k == 0), stop=(k == 7))
                    S.copy("act", tl[:, nh * 512:(nh + 1) * 512], p)
                p = psf.next()
                for k in range(8):
                    S.mm(p[:, 0:256], hT[:, k, :], win_sb[:, k, OFF_POOL:OFF_POOL + 256], start=(k == 0), stop=(k == 7))
                S.copy("act", tl[:, 1024:1280], p[:, 0:256])
                S.dma("sp", conv_p[l], tl[125:128, 0:1024])
                S.dma("sp", pool_p[l], tl[113:128, 1024:1280])

            if STAGE < 2:
                return
            qf = R.qf.next()
            S.act(qf, q_ps[:, 0:256], AF.Copy, scale=0.125)
            rows = R.rows.next()
            S.copy("act", rows, kv_ps[:, 0:384])
            gates = R.gates.next()
            S.act(gates, kv_ps[:, 384:396], AF.Sigmoid)
            if STAGE < 2.2:
                return
            cs = cs_p[:, ti, :]
            rope(qf.rr("p (h d) -> p h d", h=4), 4, cs, L)
            rope(rows.rr("p (j two d) -> p j two d", j=3, two=2)[:, :, 0, :], 3, cs, L)
            if STAGE < 2.4:
                return
            S.dma("sp", kv_p[l, ti * 128:(ti + 1) * 128, :], rows[:, 0:256])
            if (ti + 1) * 128 > T - WK:
                o0 = ti * 128 - (T - WK)
                S.dma("sp", win_p[l, o0:o0 + 128, :], rows[:, 256:384])
            qbf = R.qbf.next()
            S.copy("pool", qbf, qf)
            kvbf = R.kvbf.next()
            S.copy("pool", kvbf, rows)
            if STAGE < 2.6:
                return
            pT = psb.next()
            pTv = pT.rr("p (k l) -> p k l", k=8)
            for h in range(4):
                S.tr(pTv[0:64, h, :], qbf[:, h * 64:(h + 1) * 64], identb)
            for j, c0 in enumerate((0, 64, 128, 256)):
                S.tr(pTv[0:64, 4 + j, :], kvbf[:, c0:c0 + 64], identb)
            if STAGE < 2.8:
                return
            qT = R.qT.next()
            S.copy("dve", qT, pTv[0:64, 0:4, :])
            if STAGE < 2.85:
                return
            tsl = slice(ti * 128, (ti + 1) * 128)
            S.copy("dve", PP.kcT[:, tsl], pTv[0:64, 4, :])
            S.copy("dve", PP.vcT[:, tsl], pTv[0:64, 5, :])
            if STAGE < 2.9:
                return
            S.copy("dve", PP.ksT[:, tsl], pTv[0:64, 6, :])
            S.copy("dve", PP.kwT[:, tsl], pTv[0:64, 7, :])
            if STAGE < 2.95:
                return
            if VAR == 1:
                S.memset("dve", PP.vs_tok[:, ti, :], 0.0)
            elif VAR == 2:
                S.copy("dve", PP.vs_tok[:, ti, :], kvbf[:, 192:256])
            elif VAR == 3:
                S.copy("dve", PP.vs_tok[:, ti, :], kvbf[:, 128:192])
            elif VAR == 4:
                S.copy("dve", R.mix.next()[:, 0:64], kvbf[:, 192:256])
            elif VAR == 6:
                S.memset("dve", small()[:, 0:8], 0.0)
            elif VAR == 7:
                S.copy("dve", R.mix.next()[:, 0:128], kvbf[:, 128:256])
            elif VAR == 8:
                pass
            elif VAR == 5:
                S.copy("dve", PP.vs_tok[:, ti, :], qbf[:, 192:256])
            else:
                S.copy("dve", PP.vs_tok[:, ti, :], kvbf[:, 192:256])
                S.copy("dve", PP.vw_tok[:, ti, :], kvbf[:, 320:384])

            if STAGE < 3:
                return
            if ti == 0:
                S.memset("pool", ext[:, :, 0, 0:3], 0.0)
            else:
                S.copy("pool", ext[:, :, 0, 0:3], PP.cprev)
            S.copy("pool", PP.cprev, ext[:, :, 0, 128:131])
            acc = R.acc.next()
            for c in range(8):
                S.ts("dve", acc[:, c, :], ext[:, c, 0, 0:128], convw[:, c, 0:1], convb[:, c:c + 1], ALU.mult, ALU.add)
                for k in range(1, 4):
                    S.stt(acc[:, c, :], ext[:, c, 0, k:k + 128], convw[:, c, k:k + 1], acc[:, c, :], ALU.mult, ALU.add)
            abf = R.actbf.next()
            S.act(abf, acc, AF.Silu)
            pT = psb.next()
            pTv = pT.rr("p (k l) -> p k l", k=8)
            for c in range(6):
                S.tr(pTv[:, c, :], abf[:, c, :], identb)
            xsB = R.xsB.next()
            S.copy("dve", xsB, pTv[:, 0:6, :].rr("p k l -> p (k l)"))
            xs3 = xsB[:, 0:512].rr("p (h d) -> p h d", h=8)

            misc_ps = psf.next()
            for k in range(8):
                S.mm(misc_ps[:, 0:8], hT[:, k, :], win_sb[:, k, OFF_DT:OFF_DT + 8], start=(k == 0), stop=(k == 7))
            sm = small()
            dtp = sm[:, 0:8]
            S.tt("dve", dtp, misc_ps[:, 0:8], dtb_b, ALU.add)
            sm2 = small()
            S.ts("dve", sm2[:, 8:16], dtp, -1.0, None, ALU.mult)
            S.tt("dve", sm2[:, 0:8], dtp, sm2[:, 8:16], ALU.max)
            S.act(sm2[:, 0:8], sm2[:, 0:8], AF.Exp, scale=-1.0)
            S.act(sm2[:, 0:8], sm2[:, 0:8], AF.Ln, bias=1.0)
            S.ts("dve", sm2[:, 8:16], dtp, 0.0, None, ALU.max)
            dt = sm[:, 8:16]
            S.tt("dve", dt, sm2[:, 0:8], sm2[:, 8:16], ALU.add)
            sm3 = small()
            adt = sm3[:, 0:8]
            S.tt("dve", adt, dt, a_b, ALU.mult)
            xc = R.xc.next()
            S.tt("pool", xc.rr("p (h d) -> p h d", h=8), xs3, dt.us(2).bc([128, 8, 64]), ALU.mult)
            if ti == 0:
                dbg("dt", dt); dbg("adt", adt); dbg("xsB", xsB, BF16); dbg("xc", xc, BF16); dbg("acc", acc)
            S.mm(misc_ps[:, 8:16], incl, adt)
            S.mm(misc_ps[:, 16:24], after, adt)
            S.mm(misc_ps[:, 24:32], ones, adt)
            sm4 = small()
            S.act(sm4[:, 0:8], misc_ps[:, 8:16], AF.Exp)
            S.act(sm4[:, 8:16], misc_ps[:, 16:24], AF.Exp)
            S.act(sm3[:, 8:16], misc_ps[:, 24:32], AF.Exp)
            eacum = sm4[:, 0:8]
            dte = sm4[:, 8:16]
            elast = sm3[:, 8:16]
            sc_ps = psf.next()
            scv = sc_ps[:, 0:256].rr("p (g l) -> p g l", g=2)
            for g in range(2):
                S.mm(scv[:, g, :], abf[:, 4 + g, :], abf[:, 6 + g, :])
            scm = R.scm.next()
            S.tt("dve", scm, scv, incl.us(1).bc([128, 2, 128]), ALU.mult)
            MT = R.MT.next()
            for g in range(2):
                rhsD = R.rhsD.next()
                S.tt("pool", rhsD, incl.us(1).bc([128, 4, 128]), adt[:, g * 4:(g + 1) * 4].us(2).bc([128, 4, 128]), ALU.mult)
                p = psf.next()
                S.mm(p, after, rhsD.rr("p h l -> p (h l)"))
                Dexp = R.Dexp.next()
                S.act(Dexp.rr("p h l -> p (h l)"), p, AF.Exp)
                S.tt("pool" if g == 0 else "dve", MT[:, g * 4:(g + 1) * 4, :], Dexp,
                     scm[:, g, :].us(1).bc([128, 4, 128]), ALU.mult)
            if ti == 0:
                dbg("eacum", eacum); dbg("dte", dte); dbg("elast", elast); dbg("MT", MT, BF16); dbg("scm", scm)
            Y_ps = psf.next()
            for h in range(8):
                S.mm(Y_ps[:, h * 64:(h + 1) * 64], MT[:, h, :], xc[:, h * 64:(h + 1) * 64])
            y = R.y.next()
            t1 = R.t512.next()
            S.tt("pool", t1.rr("p (h d) -> p h d", h=8), xs3, dsk_b.us(2).bc([128, 8, 64]), ALU.mult)
            S.tt("dve", y, Y_ps, t1, ALU.add)
            if ti > 0:
                Yo_ps = psf.next()
                for g in range(2):
                    S.mm(Yo_ps[:, g * 256:(g + 1) * 256], abf[:, 6 + g, :], PP.STbf[:, g * 256:(g + 1) * 256])
                t2 = R.t512.next()
                S.tt("dve", t2.rr("p (h d) -> p h d", h=8), Yo_ps.rr("p (h d) -> p h d", h=8),
                     eacum.us(2).bc([128, 8, 64]), ALU.mult)
                S.tt("pool", y, y, t2, ALU.add)
            S.tt("dve", y, y, zs, ALU.mult)
            xcd = R.xc.next()
            S.tt("pool", xcd.rr("p (h d) -> p h d", h=8), xc.rr("p (h d) -> p h d", h=8),
                 dte.us(2).bc([128, 8, 64]), ALU.mult)
            Sn_ps = psf.next()
            for g in range(2):
                S.mm(Sn_ps[:, g * 256:(g + 1) * 256], xsB[:, 512 + g * 128: 512 + (g + 1) * 128], xcd[:, g * 256:(g + 1) * 256])
            if ti == 0:
                S.copy("act", PP.ST, Sn_ps)
            else:
                S.tt("pool", PP.ST.rr("p (h d) -> p h d", h=8), PP.ST.rr("p (h d) -> p h d", h=8),
                     elast.us(2).bc([128, 8, 64]), ALU.mult)
                S.tt("dve", PP.ST, PP.ST, Sn_ps, ALU.add)
            S.copy("pool", PP.STbf, PP.ST)
            if ti == 0:
                dbg("y", y); dbg("ST0", PP.ST); dbg("zs", zs)
            mix = R.mix.next()
            junk = R.junk.next()
            ss = small()
            S.act(junk[:, 0:512], y, AF.Square, accum=ss[:, 0:1])
            S.ts("dve", ss[:, 1:2], ss[:, 0:1], 1.0 / 512, RMS_EPS, ALU.mult, ALU.add)
            S.act(ss[:, 1:2], ss[:, 1:2], AF.Sqrt)
            S.recip(ss[:, 2:3], ss[:, 1:2])
            S.act(mix[:, 0:512], y, AF.Copy, scale=ss[:, 2:3])

            if STAGE < 4:
                return
            if ti == 0:
                S.memset("pool", pext[:, :, 0, 0:15], 0.0)
            else:
                S.copy("pool", pext[:, :, 0, 0:15], PP.pprev)
            S.copy("pool", PP.pprev, pext[:, :, 0, 128:143])
            s2 = R.ps2.next()
            s4 = R.ps4.next()
            s8 = R.ps8.next()
            s16 = R.ps16.next()
            S.tt("pool", s2, pext[:, :, :, 1:143], pext[:, :, :, 0:142], ALU.add)
            S.tt("pool", s4, s2[:, 1:4, :, 2:142], s2[:, 1:4, :, 0:140], ALU.add)
            S.tt("pool", s8, s4[:, 1:3, :, 4:140], s4[:, 1:3, :, 0:136], ALU.add)
            S.tt("pool", s16, s8[:, 1:2, :, 8:136], s8[:, 1:2, :, 0:128], ALU.add)
            pooled = R.pooled.next()
            srcs = [(s2, 0, 14), (s4, 0, 12), (s8, 0, 8), (s16, 0, 0)]
            for g, (sx, gi, off) in enumerate(srcs):
                S.stt(pooled[:, g, :], sx[:, gi, 0, off:off + 128], 1.0 / (2 ** (g + 1)), pext[:, g, 0, 15:143],
                      ALU.mult, ALU.subtract)
            if ti == 0:
                for g, (sx, gi, off) in enumerate(srcs):
                    tmp = small()
                    S.tt("dve", tmp[0:64, 0:16], sx[:, gi, 0, off:off + 16], rc_t[:, g, :], ALU.mult)
                    S.tt("dve", pooled[:, g, 0:16], tmp[0:64, 0:16], pext[:, g, 0, 15:31], ALU.subtract)
            yp_ps = psf.next()
            for g in range(4):
                S.mm(yp_ps[:, g * 64:(g + 1) * 64], pooled[:, g, :], poolw_sb[:, g, :])
            S.copy("act", mix[:, 512:768], yp_ps[:, 0:256])

            if STAGE < 5:
                return
            n0 = max(0, 8 * ti - 1)
            n1 = min(8 * ti + 6, NCMP - 1)
            nb = n1 - n0 + 1
            ncur = n1 + 1
            if nb > 0:
                for (srcT, w1, cb, gel) in ((PP.kcT, CW.w1k, CW.cbias_k, PP.geluT_k), (PP.vcT, CW.w1v, CW.cbias_v, PP.geluT_v)):
                    p = psf.next()
                    for li in range(32):
                        S.mm(p[:, 0:nb], w1[:, li, :], srcT[:, 16 * n0 + li: 16 * n1 + li + 1: 16],
                             start=(li == 0), stop=(li == 31))
                    S.act(gel[:, n0:n1 + 1], p[:, 0:nb], AF.Gelu_apprx_tanh, bias=cb)
                p = psf.next()
                S.mm(p[0:64, 0:nb], CW.w2k, PP.geluT_k[:, n0:n1 + 1])
                S.copy("act", PP.ckT[:, n0:n1 + 1], p[0:64, 0:nb])
                p = psf.next()
                S.mm(p[0:ncur, 0:64], PP.geluT_v[:, 0:ncur], CW.w2v)
                S.copy("act", PP.cv[0:ncur, :], p[0:ncur, 0:64])

            if STAGE < 6:
                return
            oacc = R.oacc.next()
            use_sel = (ti >= 8)
            nblk = 2 * ti + 2
            impacc = R.imp.next() if use_sel else None
            for h in range(4):
                s_ps = psf.next()
                S.mm(s_ps[:, 0:ncur], qT[:, h, :], PP.ckT[:, 0:ncur])
                ssb = R.ssb.next()
                c0 = max(0, 8 * ti - 1)
                r0 = c0 - (8 * ti - 1)
                if c0 > 0:
                    S.copy("act", ssb[:, 0:c0], s_ps[:, 0:c0])
                S.tt("dve", ssb[:, c0:ncur], s_ps[:, c0:ncur], cmpdiag[:, r0:r0 + (ncur - c0)], ALU.add)
                e32 = R.e32.next()
                sm = softmax_rows(ssb[:, 0:ncur], L, ncur, e32[:, 0:ncur], clamp=True)
                S.ts("dve", sm[:, 3:4], sm[:, 2:3], 1e-30, None, ALU.max)
                S.recip(sm[:, 4:5], sm[:, 3:4])
                pn = R.pn.next()
                S.ts("dve", pn[:, 0:ncur], e32[:, 0:ncur], sm[:, 4:5], None, ALU.mult)
                pT = psb.next()
                S.tr(pT[0:ncur, 0:128], pn[:, 0:ncur], identb)
                pnT = R.pnT.next()
                S.copy("dve", pnT[0:ncur, :], pT[0:ncur, 0:128])
                o_ps = psf.next()
                S.mm(o_ps[:, 0:64], pnT[0:ncur, :], PP.cv[0:ncur, :])
                S.ts("dve", oacc[:, h * 64:(h + 1) * 64], o_ps[:, 0:64], gates[:, 3 * h:3 * h + 1], None, ALU.mult)
                if use_sel:
                    i_ps = psf.next()
                    S.mm(i_ps[:, 0:nblk], pnT[0:ncur, :], ov_p[0:ncur, 0:nblk])
                    if h == 0:
                        S.tt("dve", impacc[:, 0:nblk], i_ps[:, 0:nblk], addc_p[:, ti, 0:nblk], ALU.add)
                    else:
                        S.tt("dve", impacc[:, 0:nblk], impacc[:, 0:nblk], i_ps[:, 0:nblk], ALU.add)
            selm = None
            if use_sel:
                imp = impacc
                m8 = small()
                wk = R.imp.next()
                S.max8(m8[:, 0:8], imp[:, 0:nblk])
                S.match_replace(wk[:, 0:nblk], m8[:, 0:8], imp[:, 0:nblk], -3e38)
                S.max8(m8[:, 8:16], wk[:, 0:nblk])
                selm = R.selm.next()
                S.ts("dve", selm[:, 0:nblk], imp[:, 0:nblk], m8[:, 15:16], NEG, ALU.is_lt, ALU.mult)
            for h in range(4):
                for br in (1, 2):
                    if br == 1:
                        kt0 = 0
                        kT_, v_ = PP.ksT, PP.vs_tok
                    else:
                        kt0 = max(0, ti - 4)
                        kT_, v_ = PP.kwT, PP.vw_tok
                    ntl = ti - kt0 + 1
                    nk = ntl * 128
                    ssb = R.ssb.next()
                    k0 = 0
                    while k0 < nk:
                        w = min(512, nk - k0)
                        s_ps = psf.next()
                        S.mm(s_ps[:, 0:w], qT[:, h, :], kT_[:, kt0 * 128 + k0: kt0 * 128 + k0 + w])
                        if br == 1 and use_sel:
                            S.tt("dve", ssb[:, k0:k0 + w].rr("p (j c) -> p j c", c=64),
                                 s_ps[:, 0:w].rr("p (j c) -> p j c", c=64),
                                 selm[:, k0 // 64:(k0 + w) // 64].us(2).bc([128, w // 64, 64]), ALU.add)
                        else:
                            S.copy("act", ssb[:, k0:k0 + w], s_ps[:, 0:w])
                        k0 += w
                    S.tt("pool", ssb[:, nk - 128:nk], ssb[:, nk - 128:nk], diagm, ALU.add)
                    if br == 2 and ti - 4 >= 0:
                        S.tt("pool", ssb[:, 0:128], ssb[:, 0:128], farm, ALU.add)
                    ebf = R.ebf.next()
                    sm = softmax_rows(ssb[:, 0:nk], L, nk, ebf[:, 0:nk])
                    S.recip(sm[:, 3:4], sm[:, 2:3])
                    S.tt("dve", sm[:, 4:5], sm[:, 3:4], gates[:, 3 * h + br:3 * h + br + 1], ALU.mult)
                    pTs = R.pT.next()
                    for b0 in range(0, ntl, 8):
                        nb8 = min(8, ntl - b0)
                        pT = psb.next()
                        pTv = pT.rr("p (k l) -> p k l", k=8)
                        for j in range(nb8):
                            S.tr(pTv[:, j, :], ebf[:, (b0 + j) * 128:(b0 + j + 1) * 128], identb)
                        S.copy("dve", pTs[:, b0:b0 + nb8, :], pTv[:, 0:nb8, :])
                    o_ps = psf.next()
                    for j in range(ntl):
                        S.mm(o_ps[:, 0:64], pTs[:, j, :], v_[:, kt0 + j, :], start=(j == 0), stop=(j == ntl - 1))
                    S.stt(oacc[:, h * 64:(h + 1) * 64], o_ps[:, 0:64], sm[:, 4:5], oacc[:, h * 64:(h + 1) * 64],
                          ALU.mult, ALU.add)
            S.copy("pool", mix[:, 768:1024], oacc)

            if STAGE < 7:
                return
            pT = psb.next()
            pTv = pT.rr("p (k l) -> p k l", k=8)
            for k in range(8):
                S.tr(pTv[:, k, :], mix[:, k * 128:(k + 1) * 128], identb)
            mixT = R.mixT.next()
            S.tt("dve", mixT, pTv, mixg.us(2).bc([128, 8, 128]), ALU.mult)
            mo = [psf.next(), psf.next()]
            ss = small()
            junk = R.junk.next()
            for nh in range(2):
                for k in range(8):
                    S.mm(mo[nh], mixT[:, k, :], wout_sb[:, k, nh * 512:(nh + 1) * 512], start=(k == 0), stop=(k == 7))
                S.act(junk[:, nh * 512:(nh + 1) * 512], mo[nh], AF.Square, accum=ss[:, nh:nh + 1])
            S.tt("dve", ss[:, 2:3], ss[:, 0:1], ss[:, 1:2], ALU.add)
            S.ts("dve", ss[:, 3:4], ss[:, 2:3], 1.0 / 1024, RMS_EPS, ALU.mult, ALU.add)
            S.act(ss[:, 3:4], ss[:, 3:4], AF.Sqrt)
            S.recip(ss[:, 4:5], ss[:, 3:4])
            xm = R.xo.next()
            for nh in range(2):
                sl = slice(nh * 512, (nh + 1) * 512)
                tmp = R.t512.next()
                S.stt(tmp, mo[nh], ss[:, 4:5], gpost_b[:, sl], ALU.mult, ALU.mult)
                S.tt("pool", xm[:, sl], tmp, xt[:, sl], ALU.add)
            S.dma("sp", xmid_tile(ti), xm)

        def ffn_block_prompt(l, blk, ntb, xdst_fn):
            Ltot = ntb * 128
            first_blk = (blk == 0)
            last_blk = (blk * NTB + ntb == NT)
            h2T, actT, f_sb = F.h2T, F.actT, F.f_sb
            carry = PP.carry
            for j in range(ntb):
                xin = F.xin.next()
                S.dma("sp", xin, xmid_tile(blk * NTB + j))
                rmsnorm_T(xin, 128, gcol_ffn, h2T[:, :, j * 128:(j + 1) * 128], B=F)
            for s in range(16):
                wg = F.wg.next()
                wv = F.wv.next()
                S.dma("pool", wg, W["ffn_w_gate"][l, :, s * 256:(s + 1) * 256].rr("(k p) n -> p k n", p=128))
                S.dma("pool", wv, W["ffn_w_val"][l, :, s * 256:(s + 1) * 256].rr("(k p) n -> p k n", p=128))
                if last_blk:
                    p = psf.next()
                    for k in range(8):
                        S.mm(p[:, 0:256], h2T[:, k, Ltot - 128:Ltot], wg[:, k, :], start=(k == 0), stop=(k == 7))
                    gts = F.gts.next()
                    S.copy("act", gts, p[:, 0:256])
                    S.dma("sp", ffn_p[l, :, s * 256:(s + 1) * 256], gts[126:128, :])
                for c2 in range(2):
                    c = s * 2 + c2
                    g_ps = psf.next()
                    v_ps = psf.next()
                    for k in range(8):
                        S.mm(g_ps[:, 0:Ltot], wg[:, k, c2 * 128:(c2 + 1) * 128], h2T[:, k, 0:Ltot], start=(k == 0), stop=(k == 7))
                    for k in range(8):
                        S.mm(v_ps[:, 0:Ltot], wv[:, k, c2 * 128:(c2 + 1) * 128], h2T[:, k, 0:Ltot], start=(k == 0), stop=(k == 7))
                    gext = F.gext.next()
                    S.copy("act", gext[:, 2:2 + Ltot], g_ps[:, 0:Ltot])
                    if first_blk:
                        S.memset("pool", gext[:, 0:2], 0.0)
                    else:
                        S.copy("pool", gext[:, 0:2], carry[:, c, :])
                    S.copy("pool", carry[:, c, :], gext[:, Ltot:Ltot + 2])
                    gacc = F.gacc.next()
                    S.ts("pool", gacc[:, 0:Ltot], gext[:, 0:Ltot], fconvw[:, c, 0:1], fconvb[:, c:c + 1], ALU.mult, ALU.add)
                    S.stt(gacc[:, 0:Ltot], gext[:, 1:1 + Ltot], fconvw[:, c, 1:2], gacc[:, 0:Ltot], ALU.mult, ALU.add)
                    S.stt(gacc[:, 0:Ltot], gext[:, 2:2 + Ltot], fconvw[:, c, 2:3], gacc[:, 0:Ltot], ALU.mult, ALU.add)
                    S.act(gacc[:, 0:Ltot], gacc[:, 0:Ltot], AF.Gelu_apprx_tanh)
                    S.tt("dve", actT[:, c, 0:Ltot], gacc[:, 0:Ltot], v_ps[:, 0:Ltot], ALU.mult)
            for s in range(16):
                wd = F.wd.next()
                S.dma("pool", wd, W["ffn_w_down"][l, s * 256:(s + 1) * 256, :].rr("(c p) n -> p c n", p=128))
                for j in range(ntb):
                    for nh in range(2):
                        p = psf.next()
                        for c2 in range(2):
                            S.mm(p, actT[:, s * 2 + c2, j * 128:(j + 1) * 128], wd[:, c2, nh * 512:(nh + 1) * 512],
                                 start=(c2 == 0), stop=(c2 == 1))
                        dst = f_sb[:, j, nh * 512:(nh + 1) * 512]
                        if s == 0:
                            S.copy("act", dst, p)
                        else:
                            S.tt("dve", dst, dst, p, ALU.add)
            for j in range(ntb):
                junk = F.junk.next()
                ss = small()
                S.act(junk, f_sb[:, j, :], AF.Square, accum=ss[:, 0:1])
                S.ts("dve", ss[:, 1:2], ss[:, 0:1], 1.0 / 1024, RMS_EPS, ALU.mult, ALU.add)
                S.act(ss[:, 1:2], ss[:, 1:2], AF.Sqrt)
                S.recip(ss[:, 2:3], ss[:, 1:2])
                xin = F.xin.next()
                S.dma("sp", xin, xmid_tile(blk * NTB + j))
                S.stt(f_sb[:, j, :], f_sb[:, j, :], ss[:, 2:3], gfpost_b, ALU.mult, ALU.mult)
                S.tt("pool", f_sb[:, j, :], f_sb[:, j, :], xin, ALU.add)
                S.dma("sp", xdst_fn(blk * NTB + j), f_sb[:, j, :])

        def ssm_out_prompt(l):
            so = S.sb([64, 1024], F32, "ssmo")
            for hh in range(2):
                p = psf.next()
                pv = p.rr("p (h n) -> p h n", h=4)
                for h4 in range(4):
                    h = hh * 4 + h4
                    S.tr(pv[0:64, h4, :], PP.ST[:, h * 64:(h + 1) * 64], identf)
                S.copy("act", so[:, hh * 512:(hh + 1) * 512], p[0:64, :])
            S.dma("sp", ssm_p[l].rr("h p n -> p h n"), so.rr("p (h n) -> p h n", h=8))

        if NS > 0:
            NPG = cfg["NPG"]
            NPHYS = cfg["NPHYS"]
            PAST = NPG * 128
            WB = min(512, PAST)
            LS = NS * 8
            NV = PAST // 16 - 1
            NTC = (NV + 127) // 128
            NSLC_S = PAST // 64 + 1
            NKS = PAST + 64
            NTW = (WB + 8 + 127) // 128
            x_s = din("x_s", [LS, D_MODEL])
            cache = din("cache", [DEPTH, NPHYS * 128, 256])
            pt_d = din("pt", [NS * NPG], I32)
            st_win = din("st_win", [DEPTH, NS, WB, 128])
            st_conv = din("st_conv", [DEPTH, NS * 3, 1024])
            st_ssm = din("st_ssm", [DEPTH, NS, 8, 64, 128])
            st_pool = din("st_pool", [DEPTH, NS * 15, 256])
            st_ffn = din("st_ffn", [DEPTH, NS * 2, 4096])
            y_s = dout("y_s", [LS, D_MODEL])
            kv_s = dout("kv_s", [DEPTH, LS, 256])
            win_s = dout("win_s", [DEPTH, NS, WB, 128])
            conv_s = dout("conv_s", [DEPTH, NS * 3, 1024])
            ssm_s = dout("ssm_s", [DEPTH, NS, 8, 64, 128])
            pool_s = dout("pool_s", [DEPTH, NS * 15, 256])
            ffn_s = dout("ffn_s", [DEPTH, NS * 2, 4096])
            xs_ap = dscr("xs_cur", [LS, D_MODEL])
            xs_res = Res("xs_cur")
            xs_cur = V(xs_ap[:, :], (xs_res,))

            incl_s = cload("c_incl_s", [LS, LS])
            after_s = cload("c_after_s", [LS, LS])
            seqmask_s = cload("c_seqmask_s", [LS, NS, 128])
            rowmask_s = cload("c_rowmask_s", [LS, NS])
            cs_s = cload("c_cs_s", [LS, 16])
            Rm = cload("c_R", [32, 32])
            addc_s = cload("c_addc_s", [32, NSLC_S])
            ov_s = cload("c_ov_s", [128, NTC, NSLC_S], BF16)
            tokmask_s = cload("c_tokmask_s", [32, 64])
            winmask_s = cload("c_winmask_s", [32, WB + 8])
            ptfk = S.sb([128, NS * NPG], F32, "ptfk")
            cache_flat = V(cache.ap.rearrange("d r c -> (d r) c"))
            with S.scope():
                ptb = S.sb([128, NS * NPG], I32, "ptb")
                pidx = cload("c_pidx", [128, NS * NPG])
                S.dma("sp", ptb, V(pt_d.ap.partition_broadcast(128)))
                S.copy("dve", ptfk, ptb)
                S.ts("dve", ptfk, ptfk, 128.0, None, ALU.mult)
                S.tt("dve", ptfk, ptfk, pidx, ALU.add)

        def sample_layer(l, last):
            L = LS
            xsrc = x_s if l == 0 else xs_cur
            xdst = y_s if last else xs_cur
            with S.scope():
                idx = S.sb([128, NS * NPG], I32, "s_idx")
                ptf2 = S.sb([128, NS * NPG], F32, "s_ptf2")
                S.ts("dve", ptf2, ptfk, float(l * NPHYS * 128), None, ALU.add)
                S.copy("dve", idx, ptf2)
                xt = S.sb([128, 1024], F32, "s_xt")
                mix = S.sb([128, 1024], BF16, "s_mix")
                qT_s = S.sb([64, 4, L], BF16, "s_qT")
                knew = S.sb([64, 4, L], BF16, "s_knew")
                kvbf = S.sb([128, 384], BF16, "s_kvbf")
                gates = S.sb([128, 12], F32, "s_gates")
                xmid_s = S.sb([128, 1024], F32, "s_xmid")
                junkS = Obj()
                junkS.junk = Ring(S, 1, [128, 1024], BF16, "s_junk")
                junkS.xn = Ring(S, 1, [128, 1024], BF16, "s_xn")
                S.dma("sp", xt[:L], xsrc)
                with S.scope():
                    hT = S.sb([128, 8, L], BF16, "s_hT")
                    rmsnorm_T(xt[:L], L, gcol_pre, hT, B=junkS)
                    if SST < 0.2:
                        return
                    ext = S.sb([128, 8, NS, 11], F32, "s_ext")
                    pext = S.sb([64, 4, NS, 23], F32, "s_pext")
                    tl = S.sb([128, 1280], F32, "s_tail")
                    p = psf.next()
                    pv = p[:, 0:8 * L].rr("p (c l) -> p c l", c=8)
                    for c in range(8):
                        for k in range(8):
                            S.mm(pv[:, c, :], win_sb[:, k, OFF_XBC + c * 128: OFF_XBC + (c + 1) * 128], hT[:, k, :],
                                 start=(k == 0), stop=(k == 7))
                    S.copy("act", ext[:, :, :, 3:11], pv.rr("p c (b i) -> p c b i", b=NS))
                    p = psf.next()
                    pv = p[:, 0:4 * L].rr("p (c l) -> p c l", c=4)
                    for g in range(4):
                        for k in range(8):
                            S.mm(pv[0:64, g, :], win_sb[:, k, OFF_POOL + g * 64: OFF_POOL + (g + 1) * 64], hT[:, k, :],
                                 start=(k == 0), stop=(k == 7))
                    S.copy("act", pext[:, :, :, 15:23], pv[0:64].rr("p c (b i) -> p c b i", b=NS))
                    if SST < 0.4:
                        return
                    z_ps = psf.next()
                    for k in range(8):
                        S.mm(z_ps[:L], hT[:, k, :], win_sb[:, k, 0:512], start=(k == 0), stop=(k == 7))
                    zs = S.sb([128, 512], F32, "s_zs")
                    S.act(zs[:L], z_ps[:L], AF.Silu)
                    q_ps = psf.next()
                    for k in range(8):
                        S.mm(q_ps[:L, 0:256], hT[:, k, :], win_sb[:, k, OFF_Q:OFF_Q + 256], start=(k == 0), stop=(k == 7))
                    kv_ps = psf.next()
                    for k in range(8):
                        S.mm(kv_ps[:L, 0:396], hT[:, k, :], win_sb[:, k, OFF_KV:OFF_KV + 396], start=(k == 0), stop=(k == 7))
                    if SST < 0.5:
                        return
                    qf = S.sb([128, 256], F32, "s_qf")
                    rows = S.sb([128, 384], F32, "s_rows")
                    S.act(qf[:L], q_ps[:L, 0:256], AF.Copy, scale=0.125)
                    S.copy("act", rows[:L], kv_ps[:L, 0:384])
                    S.act(gates[:L], kv_ps[:L, 384:396], AF.Sigmoid)
                    for nh in range(2):
                        p = psf.next()
                        for k in range(8):
                            S.mm(p[:L], hT[:, k, :], win_sb[:, k, OFF_XBC + nh * 512: OFF_XBC + (nh + 1) * 512],
                                 start=(k == 0), stop=(k == 7))
                        S.copy("act", tl[:L, nh * 512:(nh + 1) * 512], p[:L])
                    p = psf.next()
                    for k in range(8):
                        S.mm(p[:L, 0:256], hT[:, k, :], win_sb[:, k, OFF_POOL:OFF_POOL + 256], start=(k == 0), stop=(k == 7))
                    S.copy("act", tl[:L, 1024:1280], p[:L, 0:256])
                    for b in range(NS):
                        S.dma("sp", conv_s[l, b * 3:(b + 1) * 3, :], tl[b * 8 + 5:b * 8 + 8, 0:1024])
                        S.dma("sp", pool_s[l, b * 15 + 7:b * 15 + 15, :], tl[b * 8:b * 8 + 8, 1024:1280])
                    if SST < 0.8:
                        return
                    rt = S.sb([128, 4, 4, 8], F32, "s_rt")
                    R.rt = Ring(S, 1, [128, 4, 4, 8], F32, "s_rt2")
                    rope(qf[:L].rr("p (h d) -> p h d", h=4), 4, cs_s, L)
                    rope(rows[:L].rr("p (j two d) -> p j two d", j=3, two=2)[:, :, 0, :], 3, cs_s, L)
                    if SST < 0.85:
                        return
                    S.dma("sp", kv_s[l], rows[:L, 0:256])
                    for b in range(NS):
                        S.dma("sp", win_s[l, b, WB - 8:WB, :], rows[b * 8:(b + 1) * 8, 256:384])
                    qbf = S.sb([128, 256], BF16, "s_qbf")
                    S.copy("pool", qbf[:L], qf[:L])
                    S.copy("pool", kvbf[:L], rows[:L])
                    if SST < 0.9:
                        return
                    pT = psb.next()
                    pTv = pT.rr("p (k l) -> p k l", k=8)[:, :, 0:L]
                    for h in range(4):
                        S.tr(pTv[0:64, h, :], qbf[:L, h * 64:(h + 1) * 64], identb[:L, :L])
                    for j, c0 in enumerate((0, 64, 128, 256)):
                        S.tr(pTv[0:64, 4 + j, :], kvbf[:L, c0:c0 + 64], identb[:L, :L])
                    if SST < 0.95:
                        return
                    if VAR == 1:
                        S.copy("dve", qT_s, pTv[0:64, 0:4, :])
                    elif VAR == 2:
                        for h in range(4):
                            S.copy("dve", qT_s[:, h, :], pTv[0:64, h, :])
                            S.copy("dve", knew[:, h, :], pTv[0:64, 4 + h, :])
                    elif VAR == 3:
                        S.memset("dve", qT_s, 0.0)
                    elif VAR == 4:
                        S.memset("dve", small()[:, 0:8], 0.0)
                    elif VAR == 5:
                        S.memset("pool", small()[:, 0:8], 0.0)
                    elif VAR == 6:
                        pass
                    else:
                        S.copy("dve", qT_s, pTv[0:64, 0:4, :])
                        S.copy("dve", knew, pTv[0:64, 4:8, :])
                    if SST < 1:
                        return
                    cst = S.sb([NS * 3, 1024], F32, "s_cst")
                    S.dma("sp", cst, st_conv[l])
                    p = psf.next()
                    pv = p[:, 0:8 * NS * 3].rr("p (c r) -> p c r", c=8)
                    for c in range(8):
                        S.mm(pv[:, c, :], cst[:, c * 128:(c + 1) * 128], identf[:NS * 3, :NS * 3])
                    S.copy("act", ext[:, :, :, 0:3], pv.rr("p c (b t) -> p c b t", b=NS))
                    if SST < 1.2:
                        return
                    acc = S.sb([128, 8, NS, 8], F32, "s_acc")
                    for c in range(8):
                        S.ts("dve", acc[:, c], ext[:, c, :, 0:8], convw[:, c, 0:1], convb[:, c:c + 1], ALU.mult, ALU.add)
                        for k in range(1, 4):
                            S.stt(acc[:, c], ext[:, c, :, k:k + 8], convw[:, c, k:k + 1], acc[:, c], ALU.mult, ALU.add)
                    abf = S.sb([128, 8, L], BF16, "s_abf")
                    S.act(abf, acc.rr("p c b i -> p c (b i)"), AF.Silu)
                    dbg("s_acc", acc.rr("p c b i -> p c (b i)")); dbg("s_ext", ext.rr("p c b t -> p c (b t)"))
                    pT = psb.next()
                    pTv = pT[:, 0:768].rr("p (k l) -> p k l", k=6)
                    for c in range(6):
                        S.tr(pTv[:L, c, :], abf[:, c, :], identb)
                    xsB = S.sb([128, 768], BF16, "s_xsB")
                    S.copy("dve", xsB[:L], pTv[:L].rr("p k l -> p (k l)"))
                    xs3 = xsB[:L, 0:512].rr("p (h d) -> p h d", h=8)
                    if SST < 1.4:
                        return
                    misc_ps = psf.next()
                    for k in range(8):
                        S.mm(misc_ps[:L, 0:8], hT[:, k, :], win_sb[:, k, OFF_DT:OFF_DT + 8], start=(k == 0), stop=(k == 7))
                    sm = small()
                    dtp = sm[:L, 0:8]
                    S.tt("dve", dtp, misc_ps[:L, 0:8], dtb_b[:L], ALU.add)
                    sm2 = small()
                    S.ts("dve", sm2[:L, 8:16], dtp, -1.0, None, ALU.mult)
                    S.tt("dve", sm2[:L, 0:8], dtp, sm2[:L, 8:16], ALU.max)
                    S.act(sm2[:L, 0:8], sm2[:L, 0:8], AF.Exp, scale=-1.0)
                    S.act(sm2[:L, 0:8], sm2[:L, 0:8], AF.Ln, bias=1.0)
                    S.ts("dve", sm2[:L, 8:16], dtp, 0.0, None, ALU.max)
                    dt = sm[:L, 8:16]
                    S.tt("dve", dt, sm2[:L, 0:8], sm2[:L, 8:16], ALU.add)
                    sm3 = small()
                    adt = sm3[:L, 0:8]
                    S.tt("dve", adt, dt, a_b[:L], ALU.mult)
                    xc = S.sb([128, 512], BF16, "s_xc")
                    S.tt("pool", xc[:L].rr("p (h d) -> p h d", h=8), xs3, dt.us(2).bc([L, 8, 64]), ALU.mult)
                    dbg("s_dt", dt); dbg("s_xsB", xsB[:L], BF16); dbg("s_xc", xc[:L], BF16)
                    S.mm(misc_ps[:L, 8:16], incl_s, adt)
                    S.mm(misc_ps[:L, 16:24], after_s, adt)
                    for b in range(NS):
                        S.mm(misc_ps[:, 24 + 8 * b:32 + 8 * b], seqmask_s[:, b, :], adt)
                    sm4 = small()
                    S.act(sm4[:L, 0:8], misc_ps[:L, 8:16], AF.Exp)
                    S.act(sm4[:L, 8:16], misc_ps[:L, 16:24], AF.Exp)
                    elast = S.sb([128, NS, 8], F32, "s_elast")
                    S.act(elast.rr("p b h -> p (b h)"), misc_ps[:, 24:24 + 8 * NS], AF.Exp)
                    eacum = sm4[:L, 0:8]
                    dte = sm4[:L, 8:16]
                    if SST < 1.6:
                        return
                    sc_ps = psf.next()
                    scv = sc_ps[:L, 0:2 * L].rr("p (g l) -> p g l", g=2)
                    for g in range(2):
                        S.mm(scv[:, g, :], abf[:, 4 + g, :], abf[:, 6 + g, :])
                    scm = S.sb([128, 2, L], F32, "s_scm")
                    S.tt("dve", scm[:L], scv, incl_s.us(1).bc([L, 2, L]), ALU.mult)
                    if SST < 1.65:
                        return
                    rhsD = S.sb([128, 8, L], F32, "s_rhsD")
                    S.tt("pool", rhsD[:L], incl_s.us(1).bc([L, 8, L]), adt.us(2).bc([L, 8, L]), ALU.mult)
                    p = psf.next()
                    S.mm(p[:L, 0:8 * L], after_s, rhsD[:L].rr("p h l -> p (h l)"))
                    Dexp = S.sb([128, 8, L], F32, "s_Dexp")
                    S.act(Dexp[:L].rr("p h l -> p (h l)"), p[:L, 0:8 * L], AF.Exp)
                    if SST < 1.7:
                        return
                    MT = S.sb([128, 8, L], BF16, "s_MT")
                    for g in range(2):
                        S.tt("dve", MT[:L, g * 4:(g + 1) * 4, :], Dexp[:L, g * 4:(g + 1) * 4, :],
                             scm[:L, g, :].us(1).bc([L, 4, L]), ALU.mult)
                    if SST < 1.75:
                        return
                    Y_ps = psf.next()
                    for h in range(8):
                        S.mm(Y_ps[:L, h * 64:(h + 1) * 64], MT[:L, h, :], xc[:L, h * 64:(h + 1) * 64])
                    if SST < 1.78:
                        return
                    y = S.sb([128, 512], F32, "s_y")
                    t1 = S.sb([128, 512], F32, "s_t1")
                    t2 = S.sb([128, 512], F32, "s_t2")
                    S.tt("pool", t1[:L].rr("p (h d) -> p h d", h=8), xs3, dsk_b[:L].us(2).bc([L, 8, 64]), ALU.mult)
                    if SST < 1.785:
                        return
                    S.tt("dve", y[:L], Y_ps[:L], t1[:L], ALU.add)
                    if SST < 1.8:
                        return
                    STb = S.sb([128, 512], BF16, "s_STb")
                    sA = S.sb([64, 8, 128], F32, "s_sA")
                    sB = S.sb([128, 4, 128], F32, "s_sB")
                    sBb = S.sb([128, 4, 128], BF16, "s_sBb")
                    snew = S.sb([64, 8, 128], F32, "s_snew")
                    stmp = S.sb([64, 4, 128], F32, "s_stmp")
                    xcd = S.sb([128, 512], BF16, "s_xcd")
                    for b in range(NS):
                        S.dma("sp", sA, st_ssm[l, b].rr("h p n -> p h n"))
                        S.dma("sp", sB, st_ssm[l, b].rr("h p n -> (h p) n").rr("(hp q) n -> q hp n", q=128))
                        S.copy("dve", sBb, sB)
                        pT = psb.next()
                        pTv = pT[:, 0:512].rr("p (k l) -> p k l", k=4)
                        for hp in range(4):
                            S.tr(pTv[:, hp, :], sBb[:, hp, :], identb)
                        S.copy("dve", STb, pT[:, 0:512])
                        if SST < 1.82:
                            return
                        Yo_ps = psf.next()
                        for g in range(2):
                            S.mm(Yo_ps[:L, g * 256:(g + 1) * 256], abf[:, 6 + g, :], STb[:, g * 256:(g + 1) * 256])
                        smb = small()
                        S.ts("dve", smb[:L, 0:8], eacum, rowmask_s[:, b:b + 1], None, ALU.mult)
                        S.ts("dve", smb[:L, 8:16], dte, rowmask_s[:, b:b + 1], None, ALU.mult)
                        S.tt("dve", t2[:L].rr("p (h d) -> p h d", h=8), Yo_ps[:L].rr("p (h d) -> p h d", h=8),
                             smb[:L, 0:8].us(2).bc([L, 8, 64]), ALU.mult)
                        S.tt("pool", y[:L], y[:L], t2[:L], ALU.add)
                        if SST < 1.84:
                            return
                        S.tt("pool", xcd[:L].rr("p (h d) -> p h d", h=8), xc[:L].rr("p (h d) -> p h d", h=8),
                             smb[:L, 8:16].us(2).bc([L, 8, 64]), ALU.mult)
                        for hh in range(2):
                            Sn_ps = psf.next()
                            for h4 in range(4):
                                h = hh * 4 + h4
                                S.mm(Sn_ps[0:64, h4 * 128:(h4 + 1) * 128], xcd[:L, h * 64:(h + 1) * 64],
                                     xsB[:L, 512 + hh * 128: 512 + (hh + 1) * 128])
                            S.tt("pool", stmp, sA[:, hh * 4:(hh + 1) * 4, :],
                                 elast[0:64, b, hh * 4:(hh + 1) * 4].us(2).bc([64, 4, 128]), ALU.mult)
                            S.tt("dve", snew[:, hh * 4:(hh + 1) * 4, :], stmp, Sn_ps[0:64].rr("p (h n) -> p h n", h=4), ALU.add)
                        S.dma("sp", ssm_s[l, b].rr("h p n -> p h n"), snew)
                    if SST < 2:
                        return
                    S.tt("dve", y[:L], y[:L], zs[:L], ALU.mult)
                    ss = small()
                    junk = junkS.junk.next()
                    S.act(junk[:L, 0:512], y[:L], AF.Square, accum=ss[:L, 0:1])
                    S.ts("dve", ss[:L, 1:2], ss[:L, 0:1], 1.0 / 512, RMS_EPS, ALU.mult, ALU.add)
                    S.act(ss[:L, 1:2], ss[:L, 1:2], AF.Sqrt)
                    S.recip(ss[:L, 2:3], ss[:L, 1:2])
                    S.act(mix[:L, 0:512], y[:L], AF.Copy, scale=ss[:L, 2:3])
                    pst = S.sb([NS * 15, 256], F32, "s_pst")
                    S.dma("sp", pst, st_pool[l])
                    for b in range(NS):
                        S.dma("sp", pool_s[l, b * 15:b * 15 + 7, :], pst[b * 15 + 8:b * 15 + 15, :])
                    p = psf.next()
                    pv = p[:, 0:4 * NS * 15].rr("p (c r) -> p c r", c=4)
                    for g in range(4):
                        S.mm(pv[0:64, g, :], pst[:, g * 64:(g + 1) * 64], identf[:NS * 15, :NS * 15])
                    S.copy("act", pext[:, :, :, 0:15], pv[0:64].rr("p c (b t) -> p c b t", b=NS))
                    s2 = S.sb([64, 4, NS, 22], F32, "s_s2")
                    s4 = S.sb([64, 3, NS, 20], F32, "s_s4")
                    s8 = S.sb([64, 2, NS, 16], F32, "s_s8")
                    s16 = S.sb([64, 1, NS, 8], F32, "s_s16")
                    S.tt("pool", s2, pext[:, :, :, 1:23], pext[:, :, :, 0:22], ALU.add)
                    S.tt("pool", s4, s2[:, 1:4, :, 2:22], s2[:, 1:4, :, 0:20], ALU.add)
                    S.tt("pool", s8, s4[:, 1:3, :, 4:20], s4[:, 1:3, :, 0:16], ALU.add)
                    S.tt("pool", s16, s8[:, 1:2, :, 8:16], s8[:, 1:2, :, 0:8], ALU.add)
                    pooled = S.sb([64, 4, NS, 8], BF16, "s_pooled")
                    srcs = [(s2, 0, 14), (s4, 0, 12), (s8, 0, 8), (s16, 0, 0)]
                    for g, (sx, gi, off) in enumerate(srcs):
                        S.stt(pooled[:, g], sx[:, gi, :, off:off + 8], 1.0 / (2 ** (g + 1)), pext[:, g, :, 15:23],
                              ALU.mult, ALU.subtract)
                    yp_ps = psf.next()
                    for g in range(4):
                        S.mm(yp_ps[:L, g * 64:(g + 1) * 64], pooled[:, g].rr("p b i -> p (b i)"), poolw_sb[:, g, :])
                    S.copy("act", mix[:L, 512:768], yp_ps[:L, 0:256])
                if SST < 3:
                    return
                with S.scope():
                    load_cmp_weights(l)
                    ksT = S.sb([64, NKS], BF16, "s_ksT")
                    vs = S.sb([128, NPG + 1, 64], BF16, "s_vs")
                    kwT = S.sb([64, WB + 8], BF16, "s_kwT")
                    vw = S.sb([128, NTW, 64], BF16, "s_vw")
                    gel_k = S.sb([128, NTC * 128], BF16, "s_gelk")
                    gel_v = S.sb([128, NTC * 128], BF16, "s_gelv")
                    ckT = S.sb([64, NTC * 128], BF16, "s_ckT")
                    cvs = S.sb([128, NTC, 64], BF16, "s_cv")
                    qTb = S.sb([64, 4, 8], BF16, "s_qTb")
                    g_rows = S.sb([32, 3], F32, "s_grows")
                    oacc = S.sb([32, 64], F32, "s_oacc")
                    oaccb = S.sb([32, 64], BF16, "s_oaccb")
                    for b in range(NS):
                        S.memset("dve", ksT[:, PAST:NKS], 0.0)
                        S.memset("dve", vs[:, NPG, :], 0.0)
                        S.memset("dve", vw[:, NTW - 1, :], 0.0)
                        with S.scope():
                            cT = S.sb([64, 2, PAST], BF16, "s_cT")
                            r_pg = Ring(S, 3, [128, 256], F32, "s_pg")
                            r_pgb = Ring(S, 2, [128, 256], BF16, "s_pgb")
                            for pg in range(NPG):
                                pgt = r_pg.next()
                                j = b * NPG + pg

                                def fn(e, pgt=pgt, j=j, idx=idx):
                                    return e.indirect_dma_start(
                                        out=pgt.ap, out_offset=None, in_=cache_flat.ap,
                                        in_offset=bass.IndirectOffsetOnAxis(ap=idx.ap[:, j:j + 1], axis=0))
                                S.op("pool", fn, [pgt], [idx], dma=True)
                                pgb = r_pgb.next()
                                S.copy("pool", pgb, pgt)
                                pT = psb.next()
                                pTv = pT[0:64, 0:384].rr("p (k l) -> p k l", k=3)
                                for jj in range(3):
                                    S.tr(pTv[:, jj, :], pgb[:, jj * 64:(jj + 1) * 64], identb)
                                S.copy("dve", cT[:, :, pg * 128:(pg + 1) * 128], pTv[:, 0:2, :])
                                S.copy("dve", ksT[:, pg * 128:(pg + 1) * 128], pTv[:, 2, :])
                                S.copy("pool", vs[:, pg, :], pgb[:, 192:256])
                            for (ci, w1, cb, gel) in ((0, CW.w1k, CW.cbias_k, gel_k), (1, CW.w1v, CW.cbias_v, gel_v)):
                                n0 = 0
                                while n0 < NV:
                                    nb = min(512, NV - n0)
                                    p = psf.next()
                                    for li in range(32):
                                        S.mm(p[:, 0:nb], w1[:, li, :], cT[:, ci, 16 * n0 + li: 16 * (n0 + nb - 1) + li + 1: 16],
                                             start=(li == 0), stop=(li == 31))
                                    S.act(gel[:, n0:n0 + nb], p[:, 0:nb], AF.Gelu_apprx_tanh, bias=cb)
                                    n0 += nb
                        if SST < 4:
                            return
                        n0 = 0
                        while n0 < NV:
                            nb = min(512, NV - n0)
                            p = psf.next()
                            S.mm(p[0:64, 0:nb], CW.w2k, gel_k[:, n0:n0 + nb])
                            S.copy("act", ckT[:, n0:n0 + nb], p[0:64, 0:nb])
                            n0 += nb
                        for t in range(NTC):
                            rws = min(128, NV - t * 128)
                            p = psf.next()
                            S.mm(p[0:rws, 0:64], gel_v[:, t * 128:t * 128 + rws], CW.w2v)
                            S.copy("act", cvs[0:rws, t, :], p[0:rws, 0:64])
                        S.copy("dve", ksT[:, PAST:PAST + 8], knew[:, 2, b * 8:(b + 1) * 8])
                        S.dma("sp", vs[0:8, NPG, :], kvbf[b * 8:(b + 1) * 8, 192:256])
                        with S.scope():
                            r_wt = Ring(S, 2, [128, 128], F32, "s_wt")
                            r_wtb = Ring(S, 2, [128, 128], BF16, "s_wtb")
                            for t in range(WB // 128):
                                wt = r_wt.next()
                                S.dma("sp", wt, st_win[l, b, t * 128:(t + 1) * 128, :])
                                r0 = 8 if t == 0 else 0
                                S.dma("sp", win_s[l, b, t * 128 - 8 + r0:t * 128 + 120, :], wt[r0:128, :])
                                wtb = r_wtb.next()
                                S.copy("pool", wtb, wt)
                                pT = psb.next()
                                S.tr(pT[0:64, 0:128], wtb[:, 0:64], identb)
                                S.copy("dve", kwT[:, t * 128:(t + 1) * 128], pT[0:64, 0:128])
                                S.copy("pool", vw[:, t, :], wtb[:, 64:128])
                        S.copy("dve", kwT[:, WB:WB + 8], knew[:, 3, b * 8:(b + 1) * 8])
                        S.dma("sp", vw[0:8, WB // 128, :], kvbf[b * 8:(b + 1) * 8, 320:384])
                        S.copy("dve", qTb, qT_s[:, :, b * 8:(b + 1) * 8])
                        for h in range(4):
                            S.dma("sp", g_rows[h * 8:(h + 1) * 8, :], gates[b * 8:(b + 1) * 8, 3 * h:3 * h + 3])
                        qTf = qTb.rr("p h i -> p (h i)")
                        with S.scope():
                            M = 32
                            ssb = S.sb([32, 512], F32, "s_ssb")
                            ebf = S.sb([32, NKS], BF16, "s_ebf")
                            pTs = S.sb([128, NPG + 1, 32], BF16, "s_pTs")
                            e32 = S.sb([32, NTC * 128], F32, "s_e32")
                            pn = S.sb([32, NTC * 128], BF16, "s_pn")
                            pnT = S.sb([128, NTC, 32], BF16, "s_pnT")
                            impr = S.sb([32, NSLC_S], F32, "s_impr")
                            imp = S.sb([32, NSLC_S], F32, "s_imp")
                            wk = S.sb([32, NSLC_S], F32, "s_wk")
                            selm = S.sb([32, NSLC_S], F32, "s_selm")
                            mxc = S.sb([32, 32], F32, "s_mxc")
                            n0 = 0
                            while n0 < NV:
                                nb = min(512, NV - n0)
                                p = psf.next()
                                S.mm(p[0:M, 0:nb], qTf, ckT[:, n0:n0 + nb])
                                S.copy("act", e32[:, n0:n0 + nb], p[0:M, 0:nb])
                                n0 += nb
                            sm = softmax_rows(e32[:, 0:NV], M, NV, e32[:, 0:NV], clamp=True)
                            S.ts("dve", sm[:M, 3:4], sm[:M, 2:3], 1e-30, None, ALU.max)
                            S.recip(sm[:M, 4:5], sm[:M, 3:4])
                            S.ts("dve", pn[:, 0:NV], e32[:, 0:NV], sm[:M, 4:5], None, ALU.mult)
                            pT = psb.next()
                            pTv = pT[:, 0:NTC * 32].rr("p (k l) -> p k l", k=NTC)
                            for t in range(NTC):
                                rws = min(128, NV - t * 128)
                                S.tr(pTv[0:rws, t, :], pn[:, t * 128:t * 128 + rws], identb[:M, :M])
                            if NV % 128 != 0:
                                S.memset("dve", pnT[:, NTC - 1, :], 0.0)
                            for t in range(NTC):
                                rws = min(128, NV - t * 128)
                                S.copy("dve", pnT[0:rws, t, :], pTv[0:rws, t, :])
                            o_ps = psf.next()
                            for t in range(NTC):
                                rws = min(128, NV - t * 128)
                                S.mm(o_ps[0:M, 0:64], pnT[0:rws, t, :], cvs[0:rws, t, :], start=(t == 0), stop=(t == NTC - 1))
                            S.ts("dve", oacc, o_ps[0:M, 0:64], g_rows[:, 0:1], None, ALU.mult)
                            if SST < 5:
                                return
                            i_ps = psf.next()
                            for t in range(NTC):
                                rws = min(128, NV - t * 128)
                                S.mm(i_ps[0:M, 0:NSLC_S], pnT[0:rws, t, :], ov_s[0:rws, t, :], start=(t == 0), stop=(t == NTC - 1))
                            S.copy("act", impr, i_ps[0:M, 0:NSLC_S])
                            i2_ps = psf.next()
                            S.mm(i2_ps[0:M, 0:NSLC_S], Rm, impr)
                            S.tt("dve", imp, i2_ps[0:M, 0:NSLC_S], addc_s, ALU.add)
                            m8 = small()
                            if NSLC_S > 16:
                                S.max8(m8[:M, 0:8], imp)
                                S.match_replace(wk, m8[:M, 0:8], imp, -3e38)
                                S.max8(m8[:M, 8:16], wk)
                                S.ts("dve", selm, imp, m8[:M, 15:16], NEG, ALU.is_lt, ALU.mult)
                            else:
                                S.memset("dve", selm, 0.0)
                            nch = (NKS + 511) // 512
                            for pas in range(2):
                                for c in range(nch):
                                    k0 = c * 512
                                    w = min(512, NKS - k0)
                                    p = psf.next()
                                    S.mm(p[0:M, 0:w], qTf, ksT[:, k0:k0 + w])
                                    S.tt("dve", ssb[:, 0:w].rr("p (j c) -> p j c", c=64), p[0:M, 0:w].rr("p (j c) -> p j c", c=64),
                                         selm[:, k0 // 64:(k0 + w) // 64].us(2).bc([M, w // 64, 64]), ALU.add)
                                    if c == nch - 1:
                                        S.tt("dve", ssb[:, w - 64:w], ssb[:, w - 64:w], tokmask_s, ALU.add)
                                    if pas == 0:
                                        S.reduce(mxc[:, c:c + 1], ssb[:, 0:w], ALU.max)
                                    else:
                                        S.act(ebf[:, k0:k0 + w], ssb[:, 0:w], AF.Exp, bias=sm1[:M, 1:2], accum=mxc[:, c:c + 1])
                                if pas == 0:
                                    sm1 = small()
                                    S.reduce(sm1[:M, 0:1], mxc[:, 0:nch], ALU.max)
                                    S.ts("dve", sm1[:M, 1:2], sm1[:M, 0:1], -1.0, None, ALU.mult)
                                else:
                                    S.reduce(sm1[:M, 2:3], mxc[:, 0:nch], ALU.add)
                            S.recip(sm1[:M, 3:4], sm1[:M, 2:3])
                            S.tt("dve", sm1[:M, 4:5], sm1[:M, 3:4], g_rows[:, 1:2], ALU.mult)
                            ntl = NPG + 1
                            for b0 in range(0, ntl, 32):
                                nb8 = min(32, ntl - b0)
                                pT = psb.next()
                                pTv = pT.rr("p (k l) -> p k l", k=32)
                                for jx in range(nb8):
                                    t = b0 + jx
                                    rws = min(128, NKS - t * 128)
                                    S.tr(pTv[0:rws, jx, :], ebf[:, t * 128:t * 128 + rws], identb[:M, :M])
                                nfull = nb8 if (b0 + nb8 < ntl) else nb8 - 1
                                if nfull > 0:
                                    S.copy("dve", pTs[:, b0:b0 + nfull, :], pTv[:, 0:nfull, :])
                                if nfull < nb8:
                                    S.copy("dve", pTs[0:64, ntl - 1, :], pTv[0:64, nb8 - 1, :])
                            o_ps = psf.next()
                            for t in range(ntl):
                                rws = min(128, NKS - t * 128)
                                S.mm(o_ps[0:M, 0:64], pTs[0:rws, t, :], vs[0:rws, t, :], start=(t == 0), stop=(t == ntl - 1))
                            S.stt(oacc, o_ps[0:M, 0:64], sm1[:M, 4:5], oacc, ALU.mult, ALU.add)
                            if SST < 6:
                                return
                            NKW = WB + 8
                            wsb = S.sb([32, NKW], F32, "s_wsb")
                            k0 = 0
                            while k0 < NKW:
                                w = min(512, NKW - k0)
                                p = psf.next()
                                S.mm(p[0:M, 0:w], qTf, kwT[:, k0:k0 + w])
                                S.tt("dve", wsb[:, k0:k0 + w], p[0:M, 0:w], winmask_s[:, k0:k0 + w], ALU.add)
                                k0 += w
                            sm = softmax_rows(wsb, M, NKW, ebf[:, 0:NKW])
                            S.recip(sm[:M, 3:4], sm[:M, 2:3])
                            S.tt("dve", sm[:M, 4:5], sm[:M, 3:4], g_rows[:, 2:3], ALU.mult)
                            pT = psb.next()
                            pTv = pT.rr("p (k l) -> p k l", k=32)
                            for t in range(NTW):
                                rws = min(128, NKW - t * 128)
                                S.tr(pTv[0:rws, t, :], ebf[:, t * 128:t * 128 + rws], identb[:M, :M])
                            for t in range(NTW):
                                rws = min(128, NKW - t * 128)
                                S.copy("dve", pTs[0:rws, t, :], pTv[0:rws, t, :])
                            o_ps = psf.next()
                            for t in range(NTW):
                                rws = min(128, NKW - t * 128)
                                S.mm(o_ps[0:M, 0:64], pTs[0:rws, t, :], vw[0:rws, t, :], start=(t == 0), stop=(t == NTW - 1))
                            S.stt(oacc, o_ps[0:M, 0:64], sm[:M, 4:5], oacc, ALU.mult, ALU.add)
                            S.copy("dve", oaccb, oacc)
                            for h in range(4):
                                S.dma("sp", mix[b * 8:(b + 1) * 8, 768 + h * 64:768 + (h + 1) * 64], oaccb[h * 8:(h + 1) * 8, :])
                if SST < 7:
                    return
                with S.scope():
                    pT = psb.next()
                    pTv = pT[:, 0:8 * L].rr("p (k l) -> p k l", k=8)
                    for k in range(8):
                        S.tr(pTv[:, k, :], mix[:L, k * 128:(k + 1) * 128], identb[:L, :L])
                    mixT = S.sb([128, 8, L], BF16, "s_mixT")
                    S.tt("dve", mixT, pTv, mixg.us(2).bc([128, 8, L]), ALU.mult)
                    mo = [psf.next(), psf.next()]
                    ss = small()
                    junk = junkS.junk.next()
                    for nh in range(2):
                        for k in range(8):
                            S.mm(mo[nh][:L], mixT[:, k, :], wout_sb[:, k, nh * 512:(nh + 1) * 512], start=(k == 0), stop=(k == 7))
                        S.act(junk[:L, nh * 512:(nh + 1) * 512], mo[nh][:L], AF.Square, accum=ss[:L, nh:nh + 1])
                    S.tt("dve", ss[:L, 2:3], ss[:L, 0:1], ss[:L, 1:2], ALU.add)
                    S.ts("dve", ss[:L, 3:4], ss[:L, 2:3], 1.0 / 1024, RMS_EPS, ALU.mult, ALU.add)
                    S.act(ss[:L, 3:4], ss[:L, 3:4], AF.Sqrt)
                    S.recip(ss[:L, 4:5], ss[:L, 3:4])
                    tmp = S.sb([128, 512], F32, "s_tmp")
                    for nh in range(2):
                        sl = slice(nh * 512, (nh + 1) * 512)
                        S.stt(tmp[:L], mo[nh][:L], ss[:L, 4:5], gpost_b[:L, sl], ALU.mult, ALU.mult)
                        S.tt("pool", xmid_s[:L, sl], tmp[:L], xt[:L, sl], ALU.add)
                if SST < 8:
                    return
                with S.scope():
                    alloc_ffn_bufs(128)
                    h2T, actT, f_sb = F.h2T, F.actT, F.f_sb
                    rmsnorm_T(xmid_s[:L], L, gcol_ffn, h2T[:, :, 0:L], B=F)
                    fst = S.sb([NS * 2, 4096], F32, "s_fst")
                    S.dma("sp", fst, st_ffn[l])
                    carry_s = S.sb([128, 32, NS, 2], F32, "s_carry")
                    for q4 in range(4):
                        p = psf.next()
                        pv = p[:, 0:8 * NS * 2].rr("p (c r) -> p c r", c=8)
                        for c8 in range(8):
                            c = q4 * 8 + c8
                            S.mm(pv[:, c8, :], fst[:, c * 128:(c + 1) * 128], identf[:NS * 2, :NS * 2])
                        S.copy("act", carry_s[:, q4 * 8:(q4 + 1) * 8], pv.rr("p c (b t) -> p c b t", b=NS))
                    gtl = S.sb([32, 4096], F32, "s_gtl")
                    r_gx = Ring(S, 2, [128, NS, 10], F32, "s_gx")
                    r_ga = Ring(S, 2, [128, NS, 8], F32, "s_ga")
                    for s in range(16):
                        wg = F.wg.next()
                        wv = F.wv.next()
                        S.dma("pool", wg, W["ffn_w_gate"][l, :, s * 256:(s + 1) * 256].rr("(k p) n -> p k n", p=128))
                        S.dma("pool", wv, W["ffn_w_val"][l, :, s * 256:(s + 1) * 256].rr("(k p) n -> p k n", p=128))
                        p = psf.next()
                        for k in range(8):
                            S.mm(p[:L, 0:256], h2T[:, k, 0:L], wg[:, k, :], start=(k == 0), stop=(k == 7))
                        S.copy("act", gtl[:, s * 256:(s + 1) * 256], p[:L, 0:256])
                        for c2 in range(2):
                            c = s * 2 + c2
                            g_ps = psf.next()
                            v_ps = psf.next()
                            for k in range(8):
                                S.mm(g_ps[:, 0:L], wg[:, k, c2 * 128:(c2 + 1) * 128], h2T[:, k, 0:L], start=(k == 0), stop=(k == 7))
                            for k in range(8):
                                S.mm(v_ps[:, 0:L], wv[:, k, c2 * 128:(c2 + 1) * 128], h2T[:, k, 0:L], start=(k == 0), stop=(k == 7))
                            gx = r_gx.next()
                            S.copy("act", gx[:, :, 2:10], g_ps[:, 0:L].rr("p (b i) -> p b i", b=NS))
                            S.copy("pool", gx[:, :, 0:2], carry_s[:, c])
                            ga = r_ga.next()
                            S.ts("pool", ga, gx[:, :, 0:8], fconvw[:, c, 0:1], fconvb[:, c:c + 1], ALU.mult, ALU.add)
                            S.stt(ga, gx[:, :, 1:9], fconvw[:, c, 1:2], ga, ALU.mult, ALU.add)
                            S.stt(ga, gx[:, :, 2:10], fconvw[:, c, 2:3], ga, ALU.mult, ALU.add)
                            S.act(ga, ga, AF.Gelu_apprx_tanh)
                            S.tt("dve", actT[:, c, 0:L], ga.rr("p b i -> p (b i)"), v_ps[:, 0:L], ALU.mult)
                    for b in range(NS):
                        S.dma("sp", ffn_s[l, b * 2:(b + 1) * 2, :], gtl[b * 8 + 6:b * 8 + 8, :])
                    for s in range(16):
                        wd = F.wd.next()
                        S.dma("pool", wd, W["ffn_w_down"][l, s * 256:(s + 1) * 256, :].rr("(c p) n -> p c n", p=128))
                        for nh in range(2):
                            p = psf.next()
                            for c2 in range(2):
                                S.mm(p[:L], actT[:, s * 2 + c2, 0:L], wd[:, c2, nh * 512:(nh + 1) * 512],
                                     start=(c2 == 0), stop=(c2 == 1))
                            dst = f_sb[:L, 0, nh * 512:(nh + 1) * 512]
                            if s == 0:
                                S.copy("act", dst, p[:L])
                            else:
                                S.tt("dve", dst, dst, p[:L], ALU.add)
                    junk = F.junk.next()
                    ss = small()
                    S.act(junk[:L], f_sb[:L, 0, :], AF.Square, accum=ss[:L, 0:1])
                    S.ts("dve", ss[:L, 1:2], ss[:L, 0:1], 1.0 / 1024, RMS_EPS, ALU.mult, ALU.add)
                    S.act(ss[:L, 1:2], ss[:L, 1:2], AF.Sqrt)
                    S.recip(ss[:L, 2:3], ss[:L, 1:2])
                    S.stt(f_sb[:L, 0, :], f_sb[:L, 0, :], ss[:L, 2:3], gfpost_b[:L], ALU.mult, ALU.mult)
                    S.tt("pool", f_sb[:L, 0, :], f_sb[:L, 0, :], xmid_s[:L], ALU.add)
                    S.dma("sp", xdst, f_sb[:L, 0, :])

        for l in range(DEPTH):
            load_layer_weights(l)
            last = (l == DEPTH - 1)

            def xdst(i, last=last):
                if last:
                    return y_p[i * 128:(i + 1) * 128, :]
                return xcur_tile(i)
            with S.scope():
                alloc_prompt_persist()
                nblk_ = (NT + NTB - 1) // NTB if not cfg.get("skip_prompt") else 0
                for blk in range(nblk_):
                    t0 = blk * NTB
                    ntb = min(NTB, NT - t0)
                    with S.scope():
                        alloc_mixer_rings()
                        load_cmp_weights(l)
                        for ti in range(t0, t0 + ntb):
                            xsrc = x_p[ti * 128:(ti + 1) * 128, :] if l == 0 else xcur_tile(ti)
                            mixer_tile_prompt(l, ti, xsrc)
                    if STAGE >= 8:
                        with S.scope():
                            alloc_ffn_bufs(ntb * 128)
                            ffn_block_prompt(l, blk, ntb, xdst)
                if STAGE >= 3 and not cfg.get("skip_prompt"):
                    with S.scope():
                        ssm_out_prompt(l)
            if NS > 0 and STAGE >= 9:
                sample_layer(l, last)
        print("SBUF peak bytes", S.sb_peak, "instr", {k: len(v) for k, v in S.prog.items()})
        S.emit()
    return nc, outs, consts


_CACHE = {}


def kernel(**inp):
    x_prompt = np.asarray(inp["x_prompt"], np.float32)
    x_sample = np.asarray(inp["x_sample"], np.float32)
    B, T, D = x_prompt.shape
    BS, TS, _ = x_sample.shape
    DEPTH = inp["w_in"].shape[0]
    cache_kv = np.asarray(inp["cache_nsa_kv"], np.float32)
    NPHYS = cache_kv.shape[1]
    page_table = np.asarray(inp["page_table"], np.int32)
    NPG = page_table.shape[1]
    n_cores = NCORES
    assert B == n_cores and BS % n_cores == 0 and TS == 8
    NS = BS // n_cores
    cfg = dict(T=T, DEPTH=DEPTH, NS=NS, NPG=NPG, NPHYS=NPHYS)
    key = (T, DEPTH, NS, NPG, NPHYS)
    if key not in _CACHE:
        _CACHE[key] = build(cfg)
    nc, outs, consts = _CACHE[key]
    WB = inp["state_nsa_win"].shape[2]
    cache2 = np.ascontiguousarray(cache_kv.reshape(DEPTH, NPHYS * 128, 256))
    st_win = np.asarray(inp["state_nsa_win"], np.float32).reshape(DEPTH, BS, WB, 128)
    st_conv = np.asarray(inp["state_ssd_conv"], np.float32)
    st_ssm = np.asarray(inp["state_ssm"], np.float32)
    st_pool = np.asarray(inp["state_pool"], np.float32)
    st_ffn = np.asarray(inp["state_ffn_conv"], np.float32)
    wnames = ["norm_mix_pre", "w_in", "ssd_conv_w", "ssd_conv_b", "ssd_dt_bias", "ssd_a_log", "ssd_d", "ssd_norm",
              "pool_w", "pool_scale", "nsa_pe_k", "nsa_pe_v", "nsa_w1_k", "nsa_w1_v", "nsa_w2_k", "nsa_w2_v", "w_out",
              "norm_mix_post", "norm_ffn_pre", "ffn_w_gate", "ffn_w_val", "ffn_conv_w", "ffn_conv_b", "ffn_w_down",
              "norm_ffn_post"]
    shared = {n: np.ascontiguousarray(np.asarray(inp[n], np.float32)) for n in wnames}
    shared.update(consts)
    shared["cache"] = cache2
    in_maps = []
    for c in range(n_cores):
        sl = slice(c * NS, (c + 1) * NS)
        m = dict(shared)
        m["x_p"] = np.ascontiguousarray(x_prompt[c])
        m["x_s"] = np.ascontiguousarray(x_sample[sl].reshape(NS * 8, D))
        m["pt"] = np.ascontiguousarray(page_table[sl].reshape(-1))
        m["st_win"] = np.ascontiguousarray(st_win[:, sl])
        m["st_conv"] = np.ascontiguousarray(st_conv[:, sl].reshape(DEPTH, NS * 3, 1024))
        m["st_ssm"] = np.ascontiguousarray(st_ssm[:, sl])
        m["st_pool"] = np.ascontiguousarray(st_pool[:, sl].reshape(DEPTH, NS * 15, 256))
        m["st_ffn"] = np.ascontiguousarray(st_ffn[:, sl].reshape(DEPTH, NS * 2, 4096))
        in_maps.append(m)
    res = run_bass_kernel_spmd(nc, in_maps, core_ids=list(range(n_cores))).results
    WK = min(512, T)

    def g(name):
        return [np.asarray(r[name], np.float32) for r in res]

    y_p = np.stack(g("y_p"), 0)
    y_s = np.concatenate([a.reshape(NS, 8, D) for a in g("y_s")], 0)
    kv_p = np.stack([a.reshape(DEPTH, T, 4, 64) for a in g("kv_p")], 1)
    kv_s = np.concatenate([a.reshape(DEPTH, NS, 8, 4, 64) for a in g("kv_s")], 1)
    win_p = np.stack([a.reshape(DEPTH, WK, 2, 64) for a in g("win_p")], 1)
    win_s = np.concatenate([a.reshape(DEPTH, NS, WB, 2, 64) for a in g("win_s")], 1)
    conv_p = np.stack(g("conv_p"), 1)
    conv_s = np.concatenate([a.reshape(DEPTH, NS, 3, 1024) for a in g("conv_s")], 1)
    ssm_p = np.stack(g("ssm_p"), 1)
    ssm_s = np.concatenate(g("ssm_s"), 1)
    pool_p = np.stack(g("pool_p"), 1)
    pool_s = np.concatenate([a.reshape(DEPTH, NS, 15, 256) for a in g("pool_s")], 1)
    ffn_p = np.stack(g("ffn_p"), 1)
    ffn_s = np.concatenate([a.reshape(DEPTH, NS, 2, 4096) for a in g("ffn_s")], 1)
    return (y_p, y_s, kv_p, kv_s, win_p, win_s, conv_p, conv_s, ssm_p, ssm_s, pool_p, pool_s, ffn_p, ffn_s)
```

```python
import numpy as np
import contextlib
import concourse.bass as bass
import concourse.mybir as mybir
from concourse.bass_utils import run_bass_kernel_spmd

F32 = mybir.dt.float32
BF16 = mybir.dt.bfloat16
I32 = mybir.dt.int32
ALU = mybir.AluOpType
AF = mybir.ActivationFunctionType
AX = mybir.AxisListType

NDS = 8
SBUF_BYTES = 229376
SBUF_BASE = 16640


class Res:
    __slots__ = ("w", "r", "name")

    def __init__(self, name=""):
        self.w = None
        self.r = {}
        self.name = name


class V:
    __slots__ = ("ap", "res")

    def __init__(self, ap, res=()):
        self.ap = ap
        self.res = tuple(res)

    def __getitem__(self, key):
        return V(self.ap[key], self.res)

    def rr(self, s, **kw):
        return V(self.ap.rearrange(s, **kw), self.res)

    def bc(self, shape):
        return V(self.ap.to_broadcast(list(shape)), self.res)

    def us(self, axis):
        return V(self.ap.unsqueeze(axis), self.res)

    def bitcast(self, dt):
        return V(self.ap.bitcast(dt), self.res)

    def with_res(self, res):
        return V(self.ap, res)

    @property
    def shape(self):
        return self.ap.shape


def _resources(vs):
    out = []
    for v in vs:
        if isinstance(v, V):
            for r in v.res:
                if r not in out:
                    out.append(r)
    return out


class Sched:
    def __init__(self, nc, es, same_sync=True):
        self.nc = nc
        self.es = es
        self.eng = dict(pe=nc.tensor, act=nc.scalar, dve=nc.vector, pool=nc.gpsimd, sp=nc.sync)
        self.prog = {k: [] for k in self.eng}
        self.sem = {}
        self.val = {}
        for k in self.eng:
            self.sem[k] = es.enter_context(nc.semaphore("s_" + k))
            self.val[k] = 0
        self.dq = {}
        for q in ("sp", "pool", "act"):
            names = []
            for i in range(NDS):
                n = "d_%s%d" % (q, i)
                self.sem[n] = es.enter_context(nc.semaphore(n))
                self.val[n] = 0
                names.append(n)
            self.dq[q] = [names, 0]
        self.seen = {k: {} for k in self.eng}
        self.same_sync = same_sync
        self.ntens = 0
        self.sb_top = SBUF_BASE
        self.sb_peak = 0

    def sb(self, shape, dtype, name=None):
        self.ntens += 1
        name = "%s_%d" % (name or "t", self.ntens)
        esz = 2 if dtype == BF16 else 4
        nbytes = esz
        for d in shape[1:]:
            nbytes *= d
        off = (self.sb_top + 31) // 32 * 32
        self.sb_top = off + nbytes
        self.sb_peak = max(self.sb_peak, self.sb_top)
        assert self.sb_top <= SBUF_BYTES, "SBUF overflow: %s needs %d at %d" % (name, nbytes, off)
        t = self.nc.alloc_sbuf_tensor_at(name, list(shape), dtype, offset=off)
        return V(t[tuple(slice(None) for _ in shape)], (Res(name),))

    @contextlib.contextmanager
    def scope(self):
        self.barrier()
        top = self.sb_top
        try:
            yield
        finally:
            self.barrier()
            self.sb_top = top

    def ps(self, shape, dtype, name=None):
        self.ntens += 1
        name = "%s_%d" % (name or "p", self.ntens)
        t = self.es.enter_context(self.nc.psum_tensor(name, list(shape), dtype))
        return V(t[tuple(slice(None) for _ in shape)], (Res(name),))

    def op(self, e, fn, outs, ins, dma=False):
        need = {}
        seen = self.seen[e]

        def add(s, v):
            if s == e and not dma:
                if e == "pe" or (not self.same_sync and e != "pool"):
                    return
            if seen.get(s, 0) >= v:
                return
            if need.get(s, 0) < v:
                need[s] = v

        rin = _resources(ins)
        rout = _resources(outs)
        for r in rin:
            if r.w is not None:
                add(*r.w)
        for r in rout:
            if r.w is not None:
                add(*r.w)
            for s, v in r.r.items():
                add(s, v)
        if dma:
            names, i = self.dq[e]
            s = names[i % len(names)]
            self.dq[e][1] += 1
            if self.val[s] > 0:
                add(s, self.val[s])
            self.val[s] += 16
            ev = (s, self.val[s])
            inc = 16
        else:
            self.val[e] += 1
            s = e
            ev = (e, self.val[e])
            inc = 1
        for s2, v in need.items():
            seen[s2] = v
        self.prog[e].append((list(need.items()), fn, s, inc))
        for r in rin:
            if r.r.get(ev[0], 0) < ev[1]:
                r.r[ev[0]] = ev[1]
        for r in rout:
            r.w = ev
            r.r = {}

    def barrier(self):
        for e in self.eng:
            waits = []
            for s, v in self.val.items():
                if v > 0 and self.seen[e].get(s, 0) < v and s != e:
                    waits.append((s, v))
                    self.seen[e][s] = v
            if waits:
                self.prog[e].append((waits, None, None, 0))

    def emit(self):
        waits = [(s, v) for s, v in self.val.items() if v > 0 and s != "sp"]
        self.prog["sp"].append((waits, None, None, 0))
        sem = self.sem
        prog = self.prog

        def mk(ename):
            def body(eobj):
                for waits, fn, s, inc in prog[ename]:
                    for (ws, wv) in waits:
                        eobj.wait_ge(sem[ws], wv)
                    if fn is not None:
                        ins = fn(eobj)
                        ins.then_inc(sem[s], inc)
            return body

        with self.nc.Block() as block:
            block.tensor(mk("pe"))
            block.scalar(mk("act"))
            block.vector(mk("dve"))
            block.gpsimd(mk("pool"))
            block.sync(mk("sp"))

    def dma(self, q, out, in_, **kw):
        self.op(q, lambda e: e.dma_start(out=out.ap, in_=in_.ap, **kw), [out], [in_], dma=True)

    def mm(self, out, lhsT, rhs, start=True, stop=True):
        self.op("pe", lambda e: e.matmul(out.ap, lhsT.ap, rhs.ap, start=start, stop=stop), [out], [lhsT, rhs])

    def tr(self, out, in_, ident):
        self.op("pe", lambda e: e.transpose(out.ap, in_.ap, ident.ap), [out], [in_, ident])

    def act(self, out, in_, func, bias=None, scale=None, accum=None):
        ins = [in_] + [b for b in (bias, scale) if isinstance(b, V)]
        outs = [out] + ([accum] if accum is not None else [])

        def fn(e):
            kw = {}
            if bias is not None:
                kw["bias"] = bias.ap if isinstance(bias, V) else bias
            if scale is not None:
                kw["scale"] = scale.ap if isinstance(scale, V) else scale
            if accum is not None:
                kw["accum_out"] = accum.ap
            return e.activation(out=out.ap, in_=in_.ap, func=func, **kw)
        self.op("act", fn, outs, ins)

    def tt(self, e, out, in0, in1, op):
        self.op(e, lambda en: en.tensor_tensor(out.ap, in0.ap, in1.ap, op), [out], [in0, in1])

    def ts(self, e, out, in0, s1, s2=None, op0=ALU.mult, op1=None, accum=None):
        ins = [in0] + [b for b in (s1, s2) if isinstance(b, V)]
        outs = [out] + ([accum] if accum is not None else [])

        def fn(en):
            a1 = s1.ap if isinstance(s1, V) else s1
            a2 = s2.ap if isinstance(s2, V) else s2
            kw = {}
            if op1 is not None:
                kw["op1"] = op1
            if accum is not None:
                kw["accum_out"] = accum.ap
            return en.tensor_scalar(out.ap, in0.ap, a1, a2, op0, **kw)
        self.op(e, fn, outs, ins)

    def stt(self, out, in0, scalar, in1, op0, op1):
        ins = [in0, in1] + ([scalar] if isinstance(scalar, V) else [])

        def fn(en):
            sc = scalar.ap if isinstance(scalar, V) else scalar
            return en.scalar_tensor_tensor(out.ap, in0.ap, sc, in1.ap, op0, op1)
        self.op("dve", fn, [out], ins)

    def copy(self, e, out, in_):
        if e == "act":
            self.op("act", lambda en: en.copy(out.ap, in_.ap), [out], [in_])
        else:
            self.op(e, lambda en: en.tensor_copy(out.ap, in_.ap), [out], [in_])

    def memset(self, e, out, val):
        self.op(e, lambda en: en.memset(out.ap, val), [out], [])

    def reduce(self, out, in_, op, axis=AX.X):
        self.op("dve", lambda en: en.tensor_reduce(out.ap, in_.ap, axis, op), [out], [in_])

    def max8(self, out, in_):
        self.op("dve", lambda en: en.max(out.ap, in_.ap), [out], [in_])

    def match_replace(self, out, to_replace, values, imm):
        self.op("dve", lambda en: en.match_replace(out.ap, to_replace.ap, values.ap, imm), [out], [to_replace, values])

    def recip(self, out, in_):
        self.op("dve", lambda en: en.reciprocal(out.ap, in_.ap), [out], [in_])


class Ring:
    def __init__(self, S, n, shape, dtype, name, psum=False):
        self.t = [(S.ps if psum else S.sb)(shape, dtype, name) for _ in range(n)]
        self.i = 0

    def next(self):
        t = self.t[self.i % len(self.t)]
        self.i += 1
        return t

import ml_dtypes

D_MODEL = 1024
IN_W = 2452
OFF_XBC = 512
OFF_DT = 1536
OFF_POOL = 1544
OFF_Q = 1800
OFF_KV = 2056
NEG = -30000.0
RMS_EPS = 1e-6
NCORES = 8


def make_consts(cfg):
    T = cfg["T"]
    NT = T // 128
    NS = cfg["NS"]
    LS = NS * 8
    c = {}
    c["c_identb"] = np.eye(128).astype(ml_dtypes.bfloat16)
    c["c_identf"] = np.eye(128, dtype=np.float32)
    k = np.arange(128)
    c["c_incl"] = (k[:, None] <= k[None, :]).astype(np.float32)
    c["c_after"] = (k[:, None] > k[None, :]).astype(np.float32)
    c["c_diag"] = np.where(k[None, :] <= k[:, None], 0.0, NEG).astype(np.float32)
    c["c_far"] = np.where(k[None, :] > k[:, None], 0.0, NEG).astype(np.float32)
    half = 8
    inv = 1.0 / (500000.0 ** (np.arange(half, dtype=np.float32) / half))
    pos = np.arange(T, dtype=np.float32)
    ang = pos[:, None] * inv[None, :]
    cs = np.concatenate([np.cos(ang), np.sin(ang)], axis=1).astype(np.float32)
    c["c_cs_p"] = np.ascontiguousarray(cs.reshape(NT, 128, 16).transpose(1, 0, 2))
    r = np.arange(8)
    c["c_cmpdiag"] = np.where(k[:, None] >= 16 * (r[None, :] - 1) + 31, 0.0, NEG).astype(np.float32)
    n_slc = T // 64
    addc = np.zeros((128, NT, max(n_slc, 8)), np.float32)
    for ti in range(NT):
        tpos = 128 * ti + k
        cur = tpos // 64
        j = np.arange(n_slc)
        forced = (j[None, :] == 0) | (j[None, :] == cur[:, None]) | (j[None, :] == cur[:, None] - 1)
        causal = (j[None, :] * 64 <= tpos[:, None])
        a = np.where(forced, 1e30, 0.0)
        a = np.where(causal, a, -1e30)
        addc[:, ti, :n_slc] = a
    c["c_addc_p"] = addc
    n_cmp = T // 16 - 1
    nn = np.arange(128)
    jj = np.arange(max(n_slc, 8))
    ov = ((16 * nn[:, None] < 64 * jj[None, :] + 64) & (16 * nn[:, None] + 32 > 64 * jj[None, :]))
    ov = ov & (nn[:, None] < n_cmp)
    c["c_ov_p"] = ov.astype(ml_dtypes.bfloat16)
    rc = np.zeros((64, 4, 16), np.float32)
    for g, w in enumerate((2, 4, 8, 16)):
        rc[:, g, :] = 1.0 / np.minimum(np.arange(16) + 1, w)
    c["c_rc"] = rc
    c["c_ones"] = np.ones((128, 128), np.float32)
    if NS > 0:
        NPG = cfg["NPG"]
        past = NPG * 128
        WB = min(512, past)
        r = np.arange(LS)
        sq = r // 8
        same = sq[:, None] == sq[None, :]
        c["c_incl_s"] = ((r[:, None] <= r[None, :]) & same).astype(np.float32)
        c["c_after_s"] = ((r[:, None] > r[None, :]) & same).astype(np.float32)
        sm_ = np.zeros((LS, NS, 128), np.float32)
        rm_ = np.zeros((LS, NS), np.float32)
        for b in range(NS):
            sm_[b * 8:(b + 1) * 8, b, :] = 1.0
            rm_[b * 8:(b + 1) * 8, b] = 1.0
        c["c_seqmask_s"] = sm_
        c["c_rowmask_s"] = rm_
        pos_s = (past + (r % 8)).astype(np.float32)
        ang_s = pos_s[:, None] * inv[None, :]
        c["c_cs_s"] = np.concatenate([np.cos(ang_s), np.sin(ang_s)], axis=1).astype(np.float32)
        ri = np.arange(32) % 8
        c["c_R"] = (ri[:, None] == ri[None, :]).astype(np.float32)
        NSLC_S = past // 64 + 1
        NV = past // 16 - 1
        cur = past // 64
        j = np.arange(NSLC_S)
        forced = (j == 0) | (j == cur) | (j == cur - 1)
        c["c_addc_s"] = np.broadcast_to(np.where(forced, 1e30, 0.0).astype(np.float32)[None, :], (32, NSLC_S)).copy()
        NTC = (NV + 127) // 128
        nn2 = np.arange(NTC * 128)
        ov2 = ((16 * nn2[:, None] < 64 * j[None, :] + 64) & (16 * nn2[:, None] + 32 > 64 * j[None, :])) & (nn2[:, None] < NV)
        c["c_ov_s"] = np.ascontiguousarray(ov2.reshape(NTC, 128, NSLC_S).transpose(1, 0, 2)).astype(ml_dtypes.bfloat16)
        cc = np.arange(64)
        c["c_tokmask_s"] = np.where(cc[None, :] <= ri[:, None], 0.0, NEG).astype(np.float32)
        jw = np.arange(WB + 8)
        stored = jw[None, :] < WB
        valid = np.where(stored, jw[None, :] > ri[:, None] + WB - 512, (jw[None, :] - WB) <= ri[:, None])
        c["c_winmask_s"] = np.where(valid, 0.0, NEG).astype(np.float32)
        c["c_pidx"] = np.broadcast_to(np.arange(128, dtype=np.float32)[:, None], (128, NS * NPG)).copy()
    return c


class Obj:
    pass


def build(cfg):
    T = cfg["T"]
    NT = T // 128
    DEPTH = cfg["DEPTH"]
    NS = cfg["NS"]
    WK = min(512, T)
    NCMP = T // 16 - 1
    NSLC = T // 64
    consts = make_consts(cfg)
    STAGE = cfg.get("stage", 99)
    VAR = cfg.get("var", 0)
    SST = cfg.get("sst", 99)

    nc = bass.Bass("TRN2", target_bir_lowering=False)
    es = contextlib.ExitStack()
    outs = []
    with es:
        S = Sched(nc, es, same_sync=cfg.get("same_sync", True))

        def din(name, shape, dt=F32):
            return V(nc.dram_tensor(name, list(shape), dt, kind="ExternalInput").ap())

        def dout(name, shape, dt=F32):
            outs.append(name)
            return V(nc.dram_tensor(name, list(shape), dt, kind="ExternalOutput").ap())

        def dscr(name, shape, dt=F32):
            return nc.dram_tensor(name, list(shape), dt, kind="Internal").ap()

        x_p = din("x_p", [T, D_MODEL])
        W = {}
        for nm, shp in [("norm_mix_pre", [DEPTH, 1024]), ("w_in", [DEPTH, 1024, IN_W]), ("ssd_conv_w", [DEPTH, 4, 1024]),
                        ("ssd_conv_b", [DEPTH, 1024]), ("ssd_dt_bias", [DEPTH, 8]), ("ssd_a_log", [DEPTH, 8]),
                        ("ssd_d", [DEPTH, 8]), ("ssd_norm", [DEPTH, 512]), ("pool_w", [DEPTH, 4, 64, 64]),
                        ("pool_scale", [DEPTH, 256]), ("nsa_pe_k", [DEPTH, 32, 64]), ("nsa_pe_v", [DEPTH, 32, 64]),
                        ("nsa_w1_k", [DEPTH, 32, 64, 128]), ("nsa_w1_v", [DEPTH, 32, 64, 128]),
                        ("nsa_w2_k", [DEPTH, 128, 64]), ("nsa_w2_v", [DEPTH, 128, 64]), ("w_out", [DEPTH, 1024, 1024]),
                        ("norm_mix_post", [DEPTH, 1024]), ("norm_ffn_pre", [DEPTH, 1024]),
                        ("ffn_w_gate", [DEPTH, 1024, 4096]), ("ffn_w_val", [DEPTH, 1024, 4096]),
                        ("ffn_conv_w", [DEPTH, 3, 4096]), ("ffn_conv_b", [DEPTH, 4096]),
                        ("ffn_w_down", [DEPTH, 4096, 1024]), ("norm_ffn_post", [DEPTH, 1024])]:
            W[nm] = din(nm, shp)
        CD = {}
        for nm, arr in consts.items():
            CD[nm] = din(nm, list(arr.shape), BF16 if arr.dtype == ml_dtypes.bfloat16 else F32)

        y_p = dout("y_p", [T, D_MODEL])
        kv_p = dout("kv_p", [DEPTH, T, 256])
        win_p = dout("win_p", [DEPTH, WK, 128])
        conv_p = dout("conv_p", [DEPTH, 3, 1024])
        ssm_p = dout("ssm_p", [DEPTH, 8, 64, 128])
        pool_p = dout("pool_p", [DEPTH, 15, 256])
        ffn_p = dout("ffn_p", [DEPTH, 2, 4096])

        xcur_ap = dscr("xcur", [T, D_MODEL])
        xcur_res = [Res("xcur%d" % i) for i in range(NT)]

        def xcur_tile(i):
            return V(xcur_ap[i * 128:(i + 1) * 128, :], (xcur_res[i],))

        def cload(name, shape, dt=F32, src=None, q="sp"):
            t = S.sb(shape, dt, name)
            S.dma(q, t, src if src is not None else CD[name])
            return t

        identb = cload("c_identb", [128, 128], BF16)
        identf = cload("c_identf", [128, 128])
        incl = cload("c_incl", [128, 128])
        after = cload("c_after", [128, 128])
        diagm = cload("c_diag", [128, 128])
        farm = cload("c_far", [128, 128])
        cs_p = cload("c_cs_p", [128, NT, 16])
        cmpdiag = cload("c_cmpdiag", [128, 8])
        addc_p = cload("c_addc_p", [128, NT, max(NSLC, 8)])
        ov_p = cload("c_ov_p", [128, max(NSLC, 8)], BF16)
        rc_t = cload("c_rc", [64, 4, 16])
        ones = cload("c_ones", [128, 128])

        psf = Ring(S, 6, [128, 512], F32, "psf", psum=True)
        psb = Ring(S, 2, [128, 1024], BF16, "psb", psum=True)

        win_sb = S.sb([128, 8, IN_W], BF16, "win")
        wout_sb = S.sb([128, 8, 1024], BF16, "wout")
        gcol_pre = S.sb([128, 8], F32, "gpre")
        gcol_ffn = S.sb([128, 8], F32, "gffn")
        gpost_b = S.sb([128, 1024], F32, "gpost")
        gfpost_b = S.sb([128, 1024], F32, "gfpost")
        mixg = S.sb([128, 8], F32, "mixg")
        convw = S.sb([128, 8, 4], F32, "convw")
        convb = S.sb([128, 8], F32, "convb")
        dtb_b = S.sb([128, 8], F32, "dtb")
        a_b = S.sb([128, 8], F32, "ab")
        dsk_b = S.sb([128, 8], F32, "dsk")
        poolw_f = S.sb([64, 4, 64], F32, "poolwf")
        pscale_b = S.sb([64, 256], F32, "pscale")
        poolw_sb = S.sb([64, 4, 64], BF16, "poolw")
        CW = Obj()

        def load_cmp_weights(l):
            CW.peT_k = S.sb([64, 32], BF16, "pek")
            CW.peT_v = S.sb([64, 32], BF16, "pev")
            CW.w1k = S.sb([64, 32, 128], BF16, "w1k")
            CW.w1v = S.sb([64, 32, 128], BF16, "w1v")
            CW.w2k = S.sb([128, 64], BF16, "w2k")
            CW.w2v = S.sb([128, 64], BF16, "w2v")
            CW.cbias_k = S.sb([128, 1], F32, "cbk")
            CW.cbias_v = S.sb([128, 1], F32, "cbv")
            pef = small()
            S.dma("sp", pef[0:64, 0:32], W["nsa_pe_k"][l].rr("l d -> d l"), allow_slow_non_contiguous=True)
            S.copy("dve", CW.peT_k, pef[0:64, 0:32])
            pef = small()
            S.dma("sp", pef[0:64, 0:32], W["nsa_pe_v"][l].rr("l d -> d l"), allow_slow_non_contiguous=True)
            S.copy("dve", CW.peT_v, pef[0:64, 0:32])
            S.dma("pool", CW.w1k, W["nsa_w1_k"][l].rr("l d e -> d l e"))
            S.dma("pool", CW.w1v, W["nsa_w1_v"][l].rr("l d e -> d l e"))
            S.dma("pool", CW.w2k, W["nsa_w2_k"][l])
            S.dma("pool", CW.w2v, W["nsa_w2_v"][l])
            for (w1, peT, cb) in ((CW.w1k, CW.peT_k, CW.cbias_k), (CW.w1v, CW.peT_v, CW.cbias_v)):
                p = psf.next()
                for li in range(32):
                    S.mm(p[:, 0:1], w1[:, li, :], peT[:, li:li + 1], start=(li == 0), stop=(li == 31))
                S.copy("act", cb, p[:, 0:1])
        fconvw = S.sb([128, 32, 3], F32, "fconvw")
        fconvb = S.sb([128, 32], F32, "fconvb")

        def load_layer_weights(l):
            S.dma("pool", win_sb, W["w_in"][l].rr("(k p) n -> p k n", p=128))
            S.dma("pool", wout_sb, W["w_out"][l].rr("(k p) n -> p k n", p=128))
            S.dma("sp", gcol_pre, W["norm_mix_pre"][l].rr("(k p) -> p k", p=128), allow_slow_non_contiguous=True)
            S.dma("sp", gcol_ffn, W["norm_ffn_pre"][l].rr("(k p) -> p k", p=128), allow_slow_non_contiguous=True)
            S.dma("sp", gpost_b, V(W["norm_mix_post"].ap[l].partition_broadcast(128)))
            S.dma("sp", gfpost_b, V(W["norm_ffn_post"].ap[l].partition_broadcast(128)))
            S.memset("pool", mixg, 1.0)
            S.dma("sp", mixg[:, 0:4], W["ssd_norm"][l].rr("(k p) -> p k", p=128), allow_slow_non_contiguous=True)
            for t_ in range(4):
                S.dma("sp", convw[:, :, t_], W["ssd_conv_w"][l, t_].rr("(c p) -> p c", p=128), allow_slow_non_contiguous=True)
            S.dma("sp", convb, W["ssd_conv_b"][l].rr("(c p) -> p c", p=128), allow_slow_non_contiguous=True)
            S.dma("sp", dtb_b, V(W["ssd_dt_bias"].ap[l].partition_broadcast(128)))
            S.dma("sp", a_b, V(W["ssd_a_log"].ap[l].partition_broadcast(128)))
            S.act(a_b, a_b, AF.Exp)
            S.ts("dve", a_b, a_b, -1.0, None, ALU.mult)
            S.dma("sp", dsk_b, V(W["ssd_d"].ap[l].partition_broadcast(128)))
            S.dma("sp", poolw_f, W["pool_w"][l].rr("g c d -> c g d"))
            S.dma("sp", pscale_b, V(W["pool_scale"].ap[l].partition_broadcast(64)))
            S.tt("dve", poolw_sb, poolw_f, pscale_b.rr("p (g d) -> p g d", g=4), ALU.mult)
            for t_ in range(3):
                S.dma("sp", fconvw[:, :, t_], W["ffn_conv_w"][l, t_].rr("(c p) -> p c", p=128), allow_slow_non_contiguous=True)
            S.dma("sp", fconvb, W["ffn_conv_b"][l].rr("(c p) -> p c", p=128), allow_slow_non_contiguous=True)

        r_sm = Ring(S, 24, [128, 32], F32, "sm")
        NTB = min(4, NT)
        xmid_ap = dscr("xmid", [T, D_MODEL])
        xmid_res = [Res("xmid%d" % i) for i in range(NT)]

        def xmid_tile(i):
            return V(xmid_ap[i * 128:(i + 1) * 128, :], (xmid_res[i],))

        wg_s_ap = dscr("wg_s", [16, 128, 8, 256], BF16)
        wv_s_ap = dscr("wv_s", [16, 128, 8, 256], BF16)
        wd_s_ap = dscr("wd_s", [16, 128, 2, 1024], BF16)
        wg_s = [V(wg_s_ap[i], (Res("wgs%d" % i),)) for i in range(16)]
        wv_s = [V(wv_s_ap[i], (Res("wvs%d" % i),)) for i in range(16)]
        wd_s = [V(wd_s_ap[i], (Res("wds%d" % i),)) for i in range(16)]

        def precast_ffn(l):
            for s_ in range(16):
                S.dma("pool", wg_s[s_], W["ffn_w_gate"][l, :, s_ * 256:(s_ + 1) * 256].rr("(k p) n -> p k n", p=128))
                S.dma("pool", wv_s[s_], W["ffn_w_val"][l, :, s_ * 256:(s_ + 1) * 256].rr("(k p) n -> p k n", p=128))
            for s_ in range(16):
                S.dma("pool", wd_s[s_], W["ffn_w_down"][l, s_ * 256:(s_ + 1) * 256, :].rr("(c p) n -> p c n", p=128))

        R = Obj()
        F = Obj()
        PP = Obj()

        def alloc_prompt_persist():
            PP.kcT = S.sb([64, T], BF16, "kcT")
            PP.vcT = S.sb([64, T], BF16, "vcT")
            PP.ksT = S.sb([64, T], BF16, "ksT")
            PP.kwT = S.sb([64, T], BF16, "kwT")
            PP.vs_tok = S.sb([128, NT, 64], BF16, "vstok")
            PP.vw_tok = S.sb([128, NT, 64], BF16, "vwtok")
            PP.geluT_k = S.sb([128, 128], BF16, "gelk")
            PP.geluT_v = S.sb([128, 128], BF16, "gelv")
            PP.ckT = S.sb([64, 128], BF16, "ckT")
            PP.cv = S.sb([128, 64], BF16, "cv")
            PP.ST = S.sb([128, 512], F32, "ST")
            PP.STbf = S.sb([128, 512], BF16, "STbf")
            PP.cprev = S.sb([128, 8, 3], F32, "cprev")
            PP.pprev = S.sb([64, 4, 15], F32, "pprev")
            PP.carry = S.sb([128, 32, 2], F32, "carry")

        def alloc_mixer_rings():
            R.x = Ring(S, 1, [128, 1024], F32, "xt")
            R.junk = Ring(S, 1, [128, 1024], BF16, "junk")
            R.xn = Ring(S, 1, [128, 1024], BF16, "xn")
            R.hT = Ring(S, 1, [128, 8, 128], BF16, "hT")
            R.ext = Ring(S, 1, [128, 8, 1, 131], F32, "ext")
            R.acc = Ring(S, 1, [128, 8, 128], F32, "cacc")
            R.actbf = Ring(S, 1, [128, 8, 128], BF16, "actbf")
            R.xsB = Ring(S, 1, [128, 768], BF16, "xsB")
            R.xc = Ring(S, 2, [128, 512], BF16, "xc")
            R.rhsD = Ring(S, 1, [128, 4, 128], F32, "rhsD")
            R.Dexp = Ring(S, 1, [128, 4, 128], F32, "Dexp")
            R.scm = Ring(S, 1, [128, 2, 128], F32, "scm")
            R.MT = Ring(S, 1, [128, 8, 128], BF16, "MT")
            R.y = Ring(S, 1, [128, 512], F32, "y")
            R.t512 = Ring(S, 3, [128, 512], F32, "t512")
            R.pext = Ring(S, 1, [64, 4, 1, 143], F32, "pext")
            R.ps2 = Ring(S, 1, [64, 4, 1, 142], F32, "ps2")
            R.ps4 = Ring(S, 1, [64, 3, 1, 140], F32, "ps4")
            R.ps8 = Ring(S, 1, [64, 2, 1, 136], F32, "ps8")
            R.ps16 = Ring(S, 1, [64, 1, 1, 128], F32, "ps16")
            R.pooled = Ring(S, 1, [64, 4, 128], BF16, "pooled")
            R.mix = Ring(S, 1, [128, 1024], BF16, "mix")
            R.mixT = Ring(S, 1, [128, 8, 128], BF16, "mixT")
            R.qf = Ring(S, 1, [128, 256], F32, "qf")
            R.rows = Ring(S, 1, [128, 384], F32, "rows")
            R.rt = Ring(S, 1, [128, 4, 4, 8], F32, "ropet")
            R.qbf = Ring(S, 1, [128, 256], BF16, "qbf")
            R.kvbf = Ring(S, 1, [128, 384], BF16, "kvbf")
            R.qT = Ring(S, 1, [64, 4, 128], BF16, "qT")
            R.gates = Ring(S, 2, [128, 12], F32, "gates")
            R.ssb = Ring(S, 1, [128, max(T, 1280)], F32, "ssb")
            R.ebf = Ring(S, 1, [128, max(T, 512)], BF16, "ebf")
            R.pT = Ring(S, 1, [128, max(NT, 5), 128], BF16, "pT")
            R.e32 = Ring(S, 2, [128, 128], F32, "e32")
            R.pn = Ring(S, 2, [128, 128], BF16, "pn")
            R.pnT = Ring(S, 2, [128, 128], BF16, "pnT")
            R.oacc = Ring(S, 1, [128, 256], F32, "oacc")
            R.imp = Ring(S, 3, [128, max(NSLC, 8)], F32, "imp")
            R.selm = Ring(S, 1, [128, max(NSLC, 8)], F32, "selm")
            R.xo = Ring(S, 1, [128, 1024], F32, "xo")

        def alloc_ffn_bufs(Ltot):
            F.h2T = S.sb([128, 8, Ltot], BF16, "h2T")
            F.actT = S.sb([128, 32, Ltot], BF16, "actT")
            F.wg = Ring(S, 2, [128, 8, 256], BF16, "wg")
            F.wv = Ring(S, 2, [128, 8, 256], BF16, "wv")
            F.wd = Ring(S, 2, [128, 2, 1024], BF16, "wd")
            F.gext = Ring(S, 2, [128, Ltot + 2], F32, "gext")
            F.gacc = Ring(S, 2, [128, Ltot], F32, "gacc")
            F.f_sb = S.sb([128, (Ltot + 127) // 128, 1024], F32, "fsb")
            F.xin = Ring(S, 1, [128, 1024], F32, "xin")
            F.junk = Ring(S, 1, [128, 1024], BF16, "fjunk")
            F.xn = Ring(S, 1, [128, 1024], BF16, "fxn")
            F.gts = Ring(S, 1, [128, 256], F32, "gts")

        def small():
            return r_sm.next()

        DBG = cfg.get("debug", False)

        def dbg(name, v, dt=F32):
            if not DBG:
                return
            shp = list(v.shape)
            o = dout("dbg_" + name, shp, dt)
            S.dma("sp", o, v)

        def rmsnorm_T(src, L, gcol, hT_dst, D=1024, B=None):
            B = B or R
            junk = B.junk.next()
            ss = small()
            S.act(junk[:L, :D], src, AF.Square, accum=ss[:L, 0:1])
            S.ts("dve", ss[:L, 1:2], ss[:L, 0:1], 1.0 / D, RMS_EPS, ALU.mult, ALU.add)
            S.act(ss[:L, 1:2], ss[:L, 1:2], AF.Sqrt)
            S.recip(ss[:L, 2:3], ss[:L, 1:2])
            xn = B.xn.next()
            S.act(xn[:L, :D], src, AF.Copy, scale=ss[:L, 2:3])
            nk = D // 128
            pT = psb.next()
            pTv = pT.rr("p (k l) -> p k l", k=8)
            for k in range(nk):
                S.tr(pTv[:, k, :L], xn[:L, k * 128:(k + 1) * 128], identb[:L, :L])
            S.tt("dve", hT_dst, pTv[:, 0:nk, :L], gcol.us(2).bc([128, nk, L]), ALU.mult)

        def rope(vw, H, cs, L):
            rt = R.rt.next()
            cosb = cs[:, 0:8].us(1).bc([L, H, 8])
            sinb = cs[:, 8:16].us(1).bc([L, H, 8])
            x1 = vw[:, :, 0:8]
            x2 = vw[:, :, 8:16]
            S.tt("pool", rt[:L, 0, 0:H, :], x1, cosb, ALU.mult)
            S.tt("pool", rt[:L, 1, 0:H, :], x2, sinb, ALU.mult)
            S.tt("pool", rt[:L, 2, 0:H, :], x2, cosb, ALU.mult)
            S.tt("pool", rt[:L, 3, 0:H, :], x1, sinb, ALU.mult)
            S.tt("pool", x1, rt[:L, 0, 0:H, :], rt[:L, 1, 0:H, :], ALU.subtract)
            S.tt("pool", x2, rt[:L, 2, 0:H, :], rt[:L, 3, 0:H, :], ALU.add)

        def softmax_rows(s_sb, L, nk, e_out, clamp=False):
            sm = small()
            S.reduce(sm[:L, 0:1], s_sb, ALU.max)
            if clamp:
                S.ts("dve", sm[:L, 1:2], sm[:L, 0:1], -1e4, -1.0, ALU.max, ALU.mult)
            else:
                S.ts("dve", sm[:L, 1:2], sm[:L, 0:1], -1.0, None, ALU.mult)
            S.act(e_out, s_sb, AF.Exp, bias=sm[:L, 1:2], accum=sm[:L, 2:3])
            return sm


        def mixer_tile_prompt(l, ti, xsrc):
            L = 128
            last_tile = (ti == NT - 1)
            if STAGE < 0:
                return
            xt = R.x.next()
            S.dma("sp", xt, xsrc)
            hT = R.hT.next()
            rmsnorm_T(xt, L, gcol_pre, hT)

            if STAGE < 1:
                return
            ext = R.ext.next()
            for half in range(2):
                p = psf.next()
                pv = p.rr("p (c l) -> p c l", c=4)
                for c4 in range(4):
                    c = half * 4 + c4
                    for k in range(8):
                        S.mm(pv[:, c4, :], win_sb[:, k, OFF_XBC + c * 128: OFF_XBC + (c + 1) * 128], hT[:, k, :],
                             start=(k == 0), stop=(k == 7))
                S.copy("act", ext[:, half * 4:(half + 1) * 4, 0, 3:131], pv)
            pext = R.pext.next()
            p = psf.next()
            pv = p.rr("p (c l) -> p c l", c=4)
            for g in range(4):
                for k in range(8):
                    S.mm(pv[0:64, g, :], win_sb[:, k, OFF_POOL + g * 64: OFF_POOL + (g + 1) * 64], hT[:, k, :],
                         start=(k == 0), stop=(k == 7))
            S.copy("act", pext[:, :, 0, 15:143], pv[0:64])
            z_ps = psf.next()
            for k in range(8):
                S.mm(z_ps, hT[:, k, :], win_sb[:, k, 0:512], start=(k == 0), stop=(k == 7))
            zs = R.t512.next()
            S.act(zs, z_ps, AF.Silu)
            q_ps = psf.next()
            for k in range(8):
                S.mm(q_ps[:, 0:256], hT[:, k, :], win_sb[:, k, OFF_Q:OFF_Q + 256], start=(k == 0), stop=(k == 7))
            kv_ps = psf.next()
            for k in range(8):
                S.mm(kv_ps[:, 0:396], hT[:, k, :], win_sb[:, k, OFF_KV:OFF_KV + 396], start=(k == 0), stop=(k == 7))
            if last_tile:
                tl = R.ssb.next()
                for nh in range(2):
                    p = psf.next()
                    for k in range(8):
                        S.mm(p, hT[:, k, :], win_sb[:, k, OFF_XBC + nh * 512: OFF_XBC + (nh + 1) * 512],
                             start=(k == 0), stop=(k == 7))
                    S.copy("act", tl[:, nh * 512:(nh + 1) * 512], p)
                p = psf.next()
                for k in range(8):
                    S.mm(p[:, 0:256], hT[:, k, :], win_sb[:, k, OFF_POOL:OFF_POOL + 256], start=(k == 0), stop=(k == 7))
                S.copy("act", tl[:, 1024:1280], p[:, 0:256])
                S.dma("sp", conv_p[l], tl[125:128, 0:1024])
                S.dma("sp", pool_p[l], tl[113:128, 1024:1280])

            if STAGE < 2:
                return
            qf = R.qf.next()
            S.act(qf, q_ps[:, 0:256], AF.Copy, scale=0.125)
            rows = R.rows.next()
            S.copy("act", rows, kv_ps[:, 0:384])
            gates = R.gates.next()
            S.act(gates, kv_ps[:, 384:396], AF.Sigmoid)
            if STAGE < 2.2:
                return
            cs = cs_p[:, ti, :]
            rope(qf.rr("p (h d) -> p h d", h=4), 4, cs, L)
            rope(rows.rr("p (j two d) -> p j two d", j=3, two=2)[:, :, 0, :], 3, cs, L)
            if STAGE < 2.4:
                return
            S.dma("sp", kv_p[l, ti * 128:(ti + 1) * 128, :], rows[:, 0:256])
            if (ti + 1) * 128 > T - WK:
                o0 = ti * 128 - (T - WK)
                S.dma("sp", win_p[l, o0:o0 + 128, :], rows[:, 256:384])
            qbf = R.qbf.next()
            S.copy("pool", qbf, qf)
            kvbf = R.kvbf.next()
            S.copy("pool", kvbf, rows)
            if STAGE < 2.6:
                return
            pT = psb.next()
            pTv = pT.rr("p (k l) -> p k l", k=8)
            for h in range(4):
                S.tr(pTv[0:64, h, :], qbf[:, h * 64:(h + 1) * 64], identb)
            for j, c0 in enumerate((0, 64, 128, 256)):
                S.tr(pTv[0:64, 4 + j, :], kvbf[:, c0:c0 + 64], identb)
            if STAGE < 2.8:
                return
            qT = R.qT.next()
            S.copy("dve", qT, pTv[0:64, 0:4, :])
            if STAGE < 2.85:
                return
            tsl = slice(ti * 128, (ti + 1) * 128)
            S.copy("dve", PP.kcT[:, tsl], pTv[0:64, 4, :])
            S.copy("dve", PP.vcT[:, tsl], pTv[0:64, 5, :])
            if STAGE < 2.9:
                return
            S.copy("dve", PP.ksT[:, tsl], pTv[0:64, 6, :])
            S.copy("dve", PP.kwT[:, tsl], pTv[0:64, 7, :])
            if STAGE < 2.95:
                return
            if VAR == 1:
                S.memset("dve", PP.vs_tok[:, ti, :], 0.0)
            elif VAR == 2:
                S.copy("dve", PP.vs_tok[:, ti, :], kvbf[:, 192:256])
            elif VAR == 3:
                S.copy("dve", PP.vs_tok[:, ti, :], kvbf[:, 128:192])
            elif VAR == 4:
                S.copy("dve", R.mix.next()[:, 0:64], kvbf[:, 192:256])
            elif VAR == 6:
                S.memset("dve", small()[:, 0:8], 0.0)
            elif VAR == 7:
                S.copy("dve", R.mix.next()[:, 0:128], kvbf[:, 128:256])
            elif VAR == 8:
                pass
            elif VAR == 5:
                S.copy("dve", PP.vs_tok[:, ti, :], qbf[:, 192:256])
            else:
                S.copy("dve", PP.vs_tok[:, ti, :], kvbf[:, 192:256])
                S.copy("dve", PP.vw_tok[:, ti, :], kvbf[:, 320:384])

            if STAGE < 3:
                return
            if ti == 0:
                S.memset("pool", ext[:, :, 0, 0:3], 0.0)
            else:
                S.copy("pool", ext[:, :, 0, 0:3], PP.cprev)
            S.copy("pool", PP.cprev, ext[:, :, 0, 128:131])
            acc = R.acc.next()
            for c in range(8):
                S.ts("dve", acc[:, c, :], ext[:, c, 0, 0:128], convw[:, c, 0:1], convb[:, c:c + 1], ALU.mult, ALU.add)
                for k in range(1, 4):
                    S.stt(acc[:, c, :], ext[:, c, 0, k:k + 128], convw[:, c, k:k + 1], acc[:, c, :], ALU.mult, ALU.add)
            abf = R.actbf.next()
            S.act(abf, acc, AF.Silu)
            pT = psb.next()
            pTv = pT.rr("p (k l) -> p k l", k=8)
            for c in range(6):
                S.tr(pTv[:, c, :], abf[:, c, :], identb)
            xsB = R.xsB.next()
            S.copy("dve", xsB, pTv[:, 0:6, :].rr("p k l -> p (k l)"))
            xs3 = xsB[:, 0:512].rr("p (h d) -> p h d", h=8)

            misc_ps = psf.next()
            for k in range(8):
                S.mm(misc_ps[:, 0:8], hT[:, k, :], win_sb[:, k, OFF_DT:OFF_DT + 8], start=(k == 0), stop=(k == 7))
            sm = small()
            dtp = sm[:, 0:8]
            S.tt("dve", dtp, misc_ps[:, 0:8], dtb_b, ALU.add)
            sm2 = small()
            S.ts("dve", sm2[:, 8:16], dtp, -1.0, None, ALU.mult)
            S.tt("dve", sm2[:, 0:8], dtp, sm2[:, 8:16], ALU.max)
            S.act(sm2[:, 0:8], sm2[:, 0:8], AF.Exp, scale=-1.0)
            S.act(sm2[:, 0:8], sm2[:, 0:8], AF.Ln, bias=1.0)
            S.ts("dve", sm2[:, 8:16], dtp, 0.0, None, ALU.max)
            dt = sm[:, 8:16]
            S.tt("dve", dt, sm2[:, 0:8], sm2[:, 8:16], ALU.add)
            sm3 = small()
            adt = sm3[:, 0:8]
            S.tt("dve", adt, dt, a_b, ALU.mult)
            xc = R.xc.next()
            S.tt("pool", xc.rr("p (h d) -> p h d", h=8), xs3, dt.us(2).bc([128, 8, 64]), ALU.mult)
            if ti == 0:
                dbg("dt", dt); dbg("adt", adt); dbg("xsB", xsB, BF16); dbg("xc", xc, BF16); dbg("acc", acc)
            S.mm(misc_ps[:, 8:16], incl, adt)
            S.mm(misc_ps[:, 16:24], after, adt)
            S.mm(misc_ps[:, 24:32], ones, adt)
            sm4 = small()
            S.act(sm4[:, 0:8], misc_ps[:, 8:16], AF.Exp)
            S.act(sm4[:, 8:16], misc_ps[:, 16:24], AF.Exp)
            S.act(sm3[:, 8:16], misc_ps[:, 24:32], AF.Exp)
            eacum = sm4[:, 0:8]
            dte = sm4[:, 8:16]
            elast = sm3[:, 8:16]
            sc_ps = psf.next()
            scv = sc_ps[:, 0:256].rr("p (g l) -> p g l", g=2)
            for g in range(2):
                S.mm(scv[:, g, :], abf[:, 4 + g, :], abf[:, 6 + g, :])
            scm = R.scm.next()
            S.tt("dve", scm, scv, incl.us(1).bc([128, 2, 128]), ALU.mult)
            MT = R.MT.next()
            for g in range(2):
                rhsD = R.rhsD.next()
                S.tt("pool", rhsD, incl.us(1).bc([128, 4, 128]), adt[:, g * 4:(g + 1) * 4].us(2).bc([128, 4, 128]), ALU.mult)
                p = psf.next()
                S.mm(p, after, rhsD.rr("p h l -> p (h l)"))
                Dexp = R.Dexp.next()
                S.act(Dexp.rr("p h l -> p (h l)"), p, AF.Exp)
                S.tt("pool" if g == 0 else "dve", MT[:, g * 4:(g + 1) * 4, :], Dexp,
                     scm[:, g, :].us(1).bc([128, 4, 128]), ALU.mult)
            if ti == 0:
                dbg("eacum", eacum); dbg("dte", dte); dbg("elast", elast); dbg("MT", MT, BF16); dbg("scm", scm)
            Y_ps = psf.next()
            for h in range(8):
                S.mm(Y_ps[:, h * 64:(h + 1) * 64], MT[:, h, :], xc[:, h * 64:(h + 1) * 64])
            y = R.y.next()
            t1 = R.t512.next()
            S.tt("pool", t1.rr("p (h d) -> p h d", h=8), xs3, dsk_b.us(2).bc([128, 8, 64]), ALU.mult)
            S.tt("dve", y, Y_ps, t1, ALU.add)
            if ti > 0:
                Yo_ps = psf.next()
                for g in range(2):
                    S.mm(Yo_ps[:, g * 256:(g + 1) * 256], abf[:, 6 + g, :], PP.STbf[:, g * 256:(g + 1) * 256])
                t2 = R.t512.next()
                S.tt("dve", t2.rr("p (h d) -> p h d", h=8), Yo_ps.rr("p (h d) -> p h d", h=8),
                     eacum.us(2).bc([128, 8, 64]), ALU.mult)
                S.tt("pool", y, y, t2, ALU.add)
            S.tt("dve", y, y, zs, ALU.mult)
            xcd = R.xc.next()
            S.tt("pool", xcd.rr("p (h d) -> p h d", h=8), xc.rr("p (h d) -> p h d", h=8),
                 dte.us(2).bc([128, 8, 64]), ALU.mult)
            Sn_ps = psf.next()
            for g in range(2):
                S.mm(Sn_ps[:, g * 256:(g + 1) * 256], xsB[:, 512 + g * 128: 512 + (g + 1) * 128], xcd[:, g * 256:(g + 1) * 256])
            if ti == 0:
                S.copy("act", PP.ST, Sn_ps)
            else:
                S.tt("pool", PP.ST.rr("p (h d) -> p h d", h=8), PP.ST.rr("p (h d) -> p h d", h=8),
                     elast.us(2).bc([128, 8, 64]), ALU.mult)
                S.tt("dve", PP.ST, PP.ST, Sn_ps, ALU.add)
            S.copy("pool", PP.STbf, PP.ST)
            if ti == 0:
                dbg("y", y); dbg("ST0", PP.ST); dbg("zs", zs)
            mix = R.mix.next()
            junk = R.junk.next()
            ss = small()
            S.act(junk[:, 0:512], y, AF.Square, accum=ss[:, 0:1])
            S.ts("dve", ss[:, 1:2], ss[:, 0:1], 1.0 / 512, RMS_EPS, ALU.mult, ALU.add)
            S.act(ss[:, 1:2], ss[:, 1:2], AF.Sqrt)
            S.recip(ss[:, 2:3], ss[:, 1:2])
            S.act(mix[:, 0:512], y, AF.Copy, scale=ss[:, 2:3])

            if STAGE < 4:
                return
            if ti == 0:
                S.memset("pool", pext[:, :, 0, 0:15], 0.0)
            else:
                S.copy("pool", pext[:, :, 0, 0:15], PP.pprev)
            S.copy("pool", PP.pprev, pext[:, :, 0, 128:143])
            s2 = R.ps2.next()
            s4 = R.ps4.next()
            s8 = R.ps8.next()
            s16 = R.ps16.next()
            S.tt("pool", s2, pext[:, :, :, 1:143], pext[:, :, :, 0:142], ALU.add)
            S.tt("pool", s4, s2[:, 1:4, :, 2:142], s2[:, 1:4, :, 0:140], ALU.add)
            S.tt("pool", s8, s4[:, 1:3, :, 4:140], s4[:, 1:3, :, 0:136], ALU.add)
            S.tt("pool", s16, s8[:, 1:2, :, 8:136], s8[:, 1:2, :, 0:128], ALU.add)
            pooled = R.pooled.next()
            srcs = [(s2, 0, 14), (s4, 0, 12), (s8, 0, 8), (s16, 0, 0)]
            for g, (sx, gi, off) in enumerate(srcs):
                S.stt(pooled[:, g, :], sx[:, gi, 0, off:off + 128], 1.0 / (2 ** (g + 1)), pext[:, g, 0, 15:143],
                      ALU.mult, ALU.subtract)
            if ti == 0:
                for g, (sx, gi, off) in enumerate(srcs):
                    tmp = small()
                    S.tt("dve", tmp[0:64, 0:16], sx[:, gi, 0, off:off + 16], rc_t[:, g, :], ALU.mult)
                    S.tt("dve", pooled[:, g, 0:16], tmp[0:64, 0:16], pext[:, g, 0, 15:31], ALU.subtract)
            yp_ps = psf.next()
            for g in range(4):
                S.mm(yp_ps[:, g * 64:(g + 1) * 64], pooled[:, g, :], poolw_sb[:, g, :])
            S.copy("act", mix[:, 512:768], yp_ps[:, 0:256])

            if STAGE < 5:
                return
            n0 = max(0, 8 * ti - 1)
            n1 = min(8 * ti + 6, NCMP - 1)
            nb = n1 - n0 + 1
            ncur = n1 + 1
            if nb > 0:
                for (srcT, w1, cb, gel) in ((PP.kcT, CW.w1k, CW.cbias_k, PP.geluT_k), (PP.vcT, CW.w1v, CW.cbias_v, PP.geluT_v)):
                    p = psf.next()
                    for li in range(32):
                        S.mm(p[:, 0:nb], w1[:, li, :], srcT[:, 16 * n0 + li: 16 * n1 + li + 1: 16],
                             start=(li == 0), stop=(li == 31))
                    S.act(gel[:, n0:n1 + 1], p[:, 0:nb], AF.Gelu_apprx_tanh, bias=cb)
                p = psf.next()
                S.mm(p[0:64, 0:nb], CW.w2k, PP.geluT_k[:, n0:n1 + 1])
                S.copy("act", PP.ckT[:, n0:n1 + 1], p[0:64, 0:nb])
                p = psf.next()
                S.mm(p[0:ncur, 0:64], PP.geluT_v[:, 0:ncur], CW.w2v)
                S.copy("act", PP.cv[0:ncur, :], p[0:ncur, 0:64])

            if STAGE < 6:
                return
            oacc = R.oacc.next()
            use_sel = (ti >= 8)
            nblk = 2 * ti + 2
            impacc = R.imp.next() if use_sel else None
            for h in range(4):
                s_ps = psf.next()
                S.mm(s_ps[:, 0:ncur], qT[:, h, :], PP.ckT[:, 0:ncur])
                ssb = R.ssb.next()
                c0 = max(0, 8 * ti - 1)
                r0 = c0 - (8 * ti - 1)
                if c0 > 0:
                    S.copy("act", ssb[:, 0:c0], s_ps[:, 0:c0])
                S.tt("dve", ssb[:, c0:ncur], s_ps[:, c0:ncur], cmpdiag[:, r0:r0 + (ncur - c0)], ALU.add)
                e32 = R.e32.next()
                sm = softmax_rows(ssb[:, 0:ncur], L, ncur, e32[:, 0:ncur], clamp=True)
                S.ts("dve", sm[:, 3:4], sm[:, 2:3], 1e-30, None, ALU.max)
                S.recip(sm[:, 4:5], sm[:, 3:4])
                pn = R.pn.next()
                S.ts("dve", pn[:, 0:ncur], e32[:, 0:ncur], sm[:, 4:5], None, ALU.mult)
                pT = psb.next()
                S.tr(pT[0:ncur, 0:128], pn[:, 0:ncur], identb)
                pnT = R.pnT.next()
                S.copy("dve", pnT[0:ncur, :], pT[0:ncur, 0:128])
                o_ps = psf.next()
                S.mm(o_ps[:, 0:64], pnT[0:ncur, :], PP.cv[0:ncur, :])
                S.ts("dve", oacc[:, h * 64:(h + 1) * 64], o_ps[:, 0:64], gates[:, 3 * h:3 * h + 1], None, ALU.mult)
                if use_sel:
                    i_ps = psf.next()
                    S.mm(i_ps[:, 0:nblk], pnT[0:ncur, :], ov_p[0:ncur, 0:nblk])
                    if h == 0:
                        S.tt("dve", impacc[:, 0:nblk], i_ps[:, 0:nblk], addc_p[:, ti, 0:nblk], ALU.add)
                    else:
                        S.tt("dve", impacc[:, 0:nblk], impacc[:, 0:nblk], i_ps[:, 0:nblk], ALU.add)
            selm = None
            if use_sel:
                imp = impacc
                m8 = small()
                wk = R.imp.next()
                S.max8(m8[:, 0:8], imp[:, 0:nblk])
                S.match_replace(wk[:, 0:nblk], m8[:, 0:8], imp[:, 0:nblk], -3e38)
                S.max8(m8[:, 8:16], wk[:, 0:nblk])
                selm = R.selm.next()
                S.ts("dve", selm[:, 0:nblk], imp[:, 0:nblk], m8[:, 15:16], NEG, ALU.is_lt, ALU.mult)
            for h in range(4):
                for br in (1, 2):
                    if br == 1:
                        kt0 = 0
                        kT_, v_ = PP.ksT, PP.vs_tok
                    else:
                        kt0 = max(0, ti - 4)
                        kT_, v_ = PP.kwT, PP.vw_tok
                    ntl = ti - kt0 + 1
                    nk = ntl * 128
                    ssb = R.ssb.next()
                    k0 = 0
                    while k0 < nk:
                        w = min(512, nk - k0)
                        s_ps = psf.next()
                        S.mm(s_ps[:, 0:w], qT[:, h, :], kT_[:, kt0 * 128 + k0: kt0 * 128 + k0 + w])
                        if br == 1 and use_sel:
                            S.tt("dve", ssb[:, k0:k0 + w].rr("p (j c) -> p j c", c=64),
                                 s_ps[:, 0:w].rr("p (j c) -> p j c", c=64),
                                 selm[:, k0 // 64:(k0 + w) // 64].us(2).bc([128, w // 64, 64]), ALU.add)
                        else:
                            S.copy("act", ssb[:, k0:k0 + w], s_ps[:, 0:w])
                        k0 += w
                    S.tt("pool", ssb[:, nk - 128:nk], ssb[:, nk - 128:nk], diagm, ALU.add)
                    if br == 2 and ti - 4 >= 0:
                        S.tt("pool", ssb[:, 0:128], ssb[:, 0:128], farm, ALU.add)
                    ebf = R.ebf.next()
                    sm = softmax_rows(ssb[:, 0:nk], L, nk, ebf[:, 0:nk])
                    S.recip(sm[:, 3:4], sm[:, 2:3])
                    S.tt("dve", sm[:, 4:5], sm[:, 3:4], gates[:, 3 * h + br:3 * h + br + 1], ALU.mult)
                    pTs = R.pT.next()
                    for b0 in range(0, ntl, 8):
                        nb8 = min(8, ntl - b0)
                        pT = psb.next()
                        pTv = pT.rr("p (k l) -> p k l", k=8)
                        for j in range(nb8):
                            S.tr(pTv[:, j, :], ebf[:, (b0 + j) * 128:(b0 + j + 1) * 128], identb)
                        S.copy("dve", pTs[:, b0:b0 + nb8, :], pTv[:, 0:nb8, :])
                    o_ps = psf.next()
                    for j in range(ntl):
                        S.mm(o_ps[:, 0:64], pTs[:, j, :], v_[:, kt0 + j, :], start=(j == 0), stop=(j == ntl - 1))
                    S.stt(oacc[:, h * 64:(h + 1) * 64], o_ps[:, 0:64], sm[:, 4:5], oacc[:, h * 64:(h + 1) * 64],
                          ALU.mult, ALU.add)
            S.copy("pool", mix[:, 768:1024], oacc)

            if STAGE < 7:
                return
            pT = psb.next()
            pTv = pT.rr("p (k l) -> p k l", k=8)
            for k in range(8):
                S.tr(pTv[:, k, :], mix[:, k * 128:(k + 1) * 128], identb)
            mixT = R.mixT.next()
            S.tt("dve", mixT, pTv, mixg.us(2).bc([128, 8, 128]), ALU.mult)
            mo = [psf.next(), psf.next()]
            ss = small()
            junk = R.junk.next()
            for nh in range(2):
                for k in range(8):
                    S.mm(mo[nh], mixT[:, k, :], wout_sb[:, k, nh * 512:(nh + 1) * 512], start=(k == 0), stop=(k == 7))
                S.act(junk[:, nh * 512:(nh + 1) * 512], mo[nh], AF.Square, accum=ss[:, nh:nh + 1])
            S.tt("dve", ss[:, 2:3], ss[:, 0:1], ss[:, 1:2], ALU.add)
            S.ts("dve", ss[:, 3:4], ss[:, 2:3], 1.0 / 1024, RMS_EPS, ALU.mult, ALU.add)
            S.act(ss[:, 3:4], ss[:, 3:4], AF.Sqrt)
            S.recip(ss[:, 4:5], ss[:, 3:4])
            xm = R.xo.next()
            for nh in range(2):
                sl = slice(nh * 512, (nh + 1) * 512)
                tmp = R.t512.next()
                S.stt(tmp, mo[nh], ss[:, 4:5], gpost_b[:, sl], ALU.mult, ALU.mult)
                S.tt("pool", xm[:, sl], tmp, xt[:, sl], ALU.add)
            S.dma("sp", xmid_tile(ti), xm)

        def ffn_block_prompt(l, blk, ntb, xdst_fn):
            Ltot = ntb * 128
            first_blk = (blk == 0)
            last_blk = (blk * NTB + ntb == NT)
            h2T, actT, f_sb = F.h2T, F.actT, F.f_sb
            carry = PP.carry
            for j in range(ntb):
                xin = F.xin.next()
                S.dma("sp", xin, xmid_tile(blk * NTB + j))
                rmsnorm_T(xin, 128, gcol_ffn, h2T[:, :, j * 128:(j + 1) * 128], B=F)
            for s in range(16):
                wg = F.wg.next()
                wv = F.wv.next()
                S.dma("sp", wg, wg_s[s])
                S.dma("sp", wv, wv_s[s])
                if last_blk:
                    p = psf.next()
                    for k in range(8):
                        S.mm(p[:, 0:256], h2T[:, k, Ltot - 128:Ltot], wg[:, k, :], start=(k == 0), stop=(k == 7))
                    gts = F.gts.next()
                    S.copy("act", gts, p[:, 0:256])
                    S.dma("sp", ffn_p[l, :, s * 256:(s + 1) * 256], gts[126:128, :])
                for c2 in range(2):
                    c = s * 2 + c2
                    g_ps = psf.next()
                    v_ps = psf.next()
                    for k in range(8):
                        S.mm(g_ps[:, 0:Ltot], wg[:, k, c2 * 128:(c2 + 1) * 128], h2T[:, k, 0:Ltot], start=(k == 0), stop=(k == 7))
                    for k in range(8):
                        S.mm(v_ps[:, 0:Ltot], wv[:, k, c2 * 128:(c2 + 1) * 128], h2T[:, k, 0:Ltot], start=(k == 0), stop=(k == 7))
                    gext = F.gext.next()
                    S.copy("act", gext[:, 2:2 + Ltot], g_ps[:, 0:Ltot])
                    if first_blk:
                        S.memset("pool", gext[:, 0:2], 0.0)
                    else:
                        S.copy("pool", gext[:, 0:2], carry[:, c, :])
                    S.copy("pool", carry[:, c, :], gext[:, Ltot:Ltot + 2])
                    gacc = F.gacc.next()
                    S.ts("pool", gacc[:, 0:Ltot], gext[:, 0:Ltot], fconvw[:, c, 0:1], fconvb[:, c:c + 1], ALU.mult, ALU.add)
                    S.stt(gacc[:, 0:Ltot], gext[:, 1:1 + Ltot], fconvw[:, c, 1:2], gacc[:, 0:Ltot], ALU.mult, ALU.add)
                    S.stt(gacc[:, 0:Ltot], gext[:, 2:2 + Ltot], fconvw[:, c, 2:3], gacc[:, 0:Ltot], ALU.mult, ALU.add)
                    S.act(gacc[:, 0:Ltot], gacc[:, 0:Ltot], AF.Gelu_apprx_tanh)
                    S.tt("dve", actT[:, c, 0:Ltot], gacc[:, 0:Ltot], v_ps[:, 0:Ltot], ALU.mult)
            for s in range(16):
                wd = F.wd.next()
                S.dma("sp", wd, wd_s[s])
                for j in range(ntb):
                    for nh in range(2):
                        p = psf.next()
                        for c2 in range(2):
                            S.mm(p, actT[:, s * 2 + c2, j * 128:(j + 1) * 128], wd[:, c2, nh * 512:(nh + 1) * 512],
                                 start=(c2 == 0), stop=(c2 == 1))
                        dst = f_sb[:, j, nh * 512:(nh + 1) * 512]
                        if s == 0:
                            S.copy("act", dst, p)
                        else:
                            S.tt("dve", dst, dst, p, ALU.add)
            for j in range(ntb):
                junk = F.junk.next()
                ss = small()
                S.act(junk, f_sb[:, j, :], AF.Square, accum=ss[:, 0:1])
                S.ts("dve", ss[:, 1:2], ss[:, 0:1], 1.0 / 1024, RMS_EPS, ALU.mult, ALU.add)
                S.act(ss[:, 1:2], ss[:, 1:2], AF.Sqrt)
                S.recip(ss[:, 2:3], ss[:, 1:2])
                xin = F.xin.next()
                S.dma("sp", xin, xmid_tile(blk * NTB + j))
                S.stt(f_sb[:, j, :], f_sb[:, j, :], ss[:, 2:3], gfpost_b, ALU.mult, ALU.mult)
                S.tt("pool", f_sb[:, j, :], f_sb[:, j, :], xin, ALU.add)
                S.dma("sp", xdst_fn(blk * NTB + j), f_sb[:, j, :])

        def ssm_out_prompt(l):
            so = S.sb([64, 1024], F32, "ssmo")
            for hh in range(2):
                p = psf.next()
                pv = p.rr("p (h n) -> p h n", h=4)
                for h4 in range(4):
                    h = hh * 4 + h4
                    S.tr(pv[0:64, h4, :], PP.ST[:, h * 64:(h + 1) * 64], identf)
                S.copy("act", so[:, hh * 512:(hh + 1) * 512], p[0:64, :])
            S.dma("sp", ssm_p[l].rr("h p n -> p h n"), so.rr("p (h n) -> p h n", h=8))

        if NS > 0:
            NPG = cfg["NPG"]
            NPHYS = cfg["NPHYS"]
            PAST = NPG * 128
            WB = min(512, PAST)
            LS = NS * 8
            NV = PAST // 16 - 1
            NTC = (NV + 127) // 128
            NSLC_S = PAST // 64 + 1
            NKS = PAST + 64
            NTW = (WB + 8 + 127) // 128
            x_s = din("x_s", [LS, D_MODEL])
            cache = din("cache", [DEPTH, NPHYS * 128, 256])
            pt_d = din("pt", [NS * NPG], I32)
            st_win = din("st_win", [DEPTH, NS, WB, 128])
            st_conv = din("st_conv", [DEPTH, NS * 3, 1024])
            st_ssm = din("st_ssm", [DEPTH, NS, 8, 64, 128])
            st_pool = din("st_pool", [DEPTH, NS * 15, 256])
            st_ffn = din("st_ffn", [DEPTH, NS * 2, 4096])
            y_s = dout("y_s", [LS, D_MODEL])
            kv_s = dout("kv_s", [DEPTH, LS, 256])
            win_s = dout("win_s", [DEPTH, NS, WB, 128])
            conv_s = dout("conv_s", [DEPTH, NS * 3, 1024])
            ssm_s = dout("ssm_s", [DEPTH, NS, 8, 64, 128])
            pool_s = dout("pool_s", [DEPTH, NS * 15, 256])
            ffn_s = dout("ffn_s", [DEPTH, NS * 2, 4096])
            xs_ap = dscr("xs_cur", [LS, D_MODEL])
            xs_res = Res("xs_cur")
            xs_cur = V(xs_ap[:, :], (xs_res,))

            incl_s = cload("c_incl_s", [LS, LS])
            after_s = cload("c_after_s", [LS, LS])
            seqmask_s = cload("c_seqmask_s", [LS, NS, 128])
            rowmask_s = cload("c_rowmask_s", [LS, NS])
            cs_s = cload("c_cs_s", [LS, 16])
            Rm = cload("c_R", [32, 32])
            addc_s = cload("c_addc_s", [32, NSLC_S])
            ov_s = cload("c_ov_s", [128, NTC, NSLC_S], BF16)
            tokmask_s = cload("c_tokmask_s", [32, 64])
            winmask_s = cload("c_winmask_s", [32, WB + 8])
            ptfk = S.sb([128, NS * NPG], F32, "ptfk")
            cache_flat = V(cache.ap.rearrange("d r c -> (d r) c"))
            with S.scope():
                ptb = S.sb([128, NS * NPG], I32, "ptb")
                pidx = cload("c_pidx", [128, NS * NPG])
                S.dma("sp", ptb, V(pt_d.ap.partition_broadcast(128)))
                S.copy("dve", ptfk, ptb)
                S.ts("dve", ptfk, ptfk, 128.0, None, ALU.mult)
                S.tt("dve", ptfk, ptfk, pidx, ALU.add)

        def sample_layer(l, last):
            L = LS
            xsrc = x_s if l == 0 else xs_cur
            xdst = y_s if last else xs_cur
            with S.scope():
                idx = S.sb([128, NS * NPG], I32, "s_idx")
                ptf2 = S.sb([128, NS * NPG], F32, "s_ptf2")
                S.ts("dve", ptf2, ptfk, float(l * NPHYS * 128), None, ALU.add)
                S.copy("dve", idx, ptf2)
                xt = S.sb([128, 1024], F32, "s_xt")
                mix = S.sb([128, 1024], BF16, "s_mix")
                qT_s = S.sb([64, 4, L], BF16, "s_qT")
                knew = S.sb([64, 4, L], BF16, "s_knew")
                kvbf = S.sb([128, 384], BF16, "s_kvbf")
                gates = S.sb([128, 12], F32, "s_gates")
                xmid_s = S.sb([128, 1024], F32, "s_xmid")
                junkS = Obj()
                junkS.junk = Ring(S, 1, [128, 1024], BF16, "s_junk")
                junkS.xn = Ring(S, 1, [128, 1024], BF16, "s_xn")
                S.dma("sp", xt[:L], xsrc)
                with S.scope():
                    hT = S.sb([128, 8, L], BF16, "s_hT")
                    rmsnorm_T(xt[:L], L, gcol_pre, hT, B=junkS)
                    if SST < 0.2:
                        return
                    ext = S.sb([128, 8, NS, 11], F32, "s_ext")
                    pext = S.sb([64, 4, NS, 23], F32, "s_pext")
                    tl = S.sb([128, 1280], F32, "s_tail")
                    p = psf.next()
                    pv = p[:, 0:8 * L].rr("p (c l) -> p c l", c=8)
                    for c in range(8):
                        for k in range(8):
                            S.mm(pv[:, c, :], win_sb[:, k, OFF_XBC + c * 128: OFF_XBC + (c + 1) * 128], hT[:, k, :],
                                 start=(k == 0), stop=(k == 7))
                    S.copy("act", ext[:, :, :, 3:11], pv.rr("p c (b i) -> p c b i", b=NS))
                    p = psf.next()
                    pv = p[:, 0:4 * L].rr("p (c l) -> p c l", c=4)
                    for g in range(4):
                        for k in range(8):
                            S.mm(pv[0:64, g, :], win_sb[:, k, OFF_POOL + g * 64: OFF_POOL + (g + 1) * 64], hT[:, k, :],
                                 start=(k == 0), stop=(k == 7))
                    S.copy("act", pext[:, :, :, 15:23], pv[0:64].rr("p c (b i) -> p c b i", b=NS))
                    if SST < 0.4:
                        return
                    z_ps = psf.next()
                    for k in range(8):
                        S.mm(z_ps[:L], hT[:, k, :], win_sb[:, k, 0:512], start=(k == 0), stop=(k == 7))
                    zs = S.sb([128, 512], F32, "s_zs")
                    S.act(zs[:L], z_ps[:L], AF.Silu)
                    q_ps = psf.next()
                    for k in range(8):
                        S.mm(q_ps[:L, 0:256], hT[:, k, :], win_sb[:, k, OFF_Q:OFF_Q + 256], start=(k == 0), stop=(k == 7))
                    kv_ps = psf.next()
                    for k in range(8):
                        S.mm(kv_ps[:L, 0:396], hT[:, k, :], win_sb[:, k, OFF_KV:OFF_KV + 396], start=(k == 0), stop=(k == 7))
                    if SST < 0.5:
                        return
                    qf = S.sb([128, 256], F32, "s_qf")
                    rows = S.sb([128, 384], F32, "s_rows")
                    S.act(qf[:L], q_ps[:L, 0:256], AF.Copy, scale=0.125)
                    S.copy("act", rows[:L], kv_ps[:L, 0:384])
                    S.act(gates[:L], kv_ps[:L, 384:396], AF.Sigmoid)
                    for nh in range(2):
                        p = psf.next()
                        for k in range(8):
                            S.mm(p[:L], hT[:, k, :], win_sb[:, k, OFF_XBC + nh * 512: OFF_XBC + (nh + 1) * 512],
                                 start=(k == 0), stop=(k == 7))
                        S.copy("act", tl[:L, nh * 512:(nh + 1) * 512], p[:L])
                    p = psf.next()
                    for k in range(8):
                        S.mm(p[:L, 0:256], hT[:, k, :], win_sb[:, k, OFF_POOL:OFF_POOL + 256], start=(k == 0), stop=(k == 7))
                    S.copy("act", tl[:L, 1024:1280], p[:L, 0:256])
                    for b in range(NS):
                        S.dma("sp", conv_s[l, b * 3:(b + 1) * 3, :], tl[b * 8 + 5:b * 8 + 8, 0:1024])
                        S.dma("sp", pool_s[l, b * 15 + 7:b * 15 + 15, :], tl[b * 8:b * 8 + 8, 1024:1280])
                    if SST < 0.8:
                        return
                    rt = S.sb([128, 4, 4, 8], F32, "s_rt")
                    R.rt = Ring(S, 1, [128, 4, 4, 8], F32, "s_rt2")
                    rope(qf[:L].rr("p (h d) -> p h d", h=4), 4, cs_s, L)
                    rope(rows[:L].rr("p (j two d) -> p j two d", j=3, two=2)[:, :, 0, :], 3, cs_s, L)
                    if SST < 0.85:
                        return
                    S.dma("sp", kv_s[l], rows[:L, 0:256])
                    for b in range(NS):
                        S.dma("sp", win_s[l, b, WB - 8:WB, :], rows[b * 8:(b + 1) * 8, 256:384])
                    qbf = S.sb([128, 256], BF16, "s_qbf")
                    S.copy("pool", qbf[:L], qf[:L])
                    S.copy("pool", kvbf[:L], rows[:L])
                    if SST < 0.9:
                        return
                    pT = psb.next()
                    pTv = pT.rr("p (k l) -> p k l", k=8)[:, :, 0:L]
                    for h in range(4):
                        S.tr(pTv[0:64, h, :], qbf[:L, h * 64:(h + 1) * 64], identb[:L, :L])
                    for j, c0 in enumerate((0, 64, 128, 256)):
                        S.tr(pTv[0:64, 4 + j, :], kvbf[:L, c0:c0 + 64], identb[:L, :L])
                    if SST < 0.95:
                        return
                    if VAR == 1:
                        S.copy("dve", qT_s, pTv[0:64, 0:4, :])
                    elif VAR == 2:
                        for h in range(4):
                            S.copy("dve", qT_s[:, h, :], pTv[0:64, h, :])
                            S.copy("dve", knew[:, h, :], pTv[0:64, 4 + h, :])
                    elif VAR == 3:
                        S.memset("dve", qT_s, 0.0)
                    elif VAR == 4:
                        S.memset("dve", small()[:, 0:8], 0.0)
                    elif VAR == 5:
                        S.memset("pool", small()[:, 0:8], 0.0)
                    elif VAR == 6:
                        pass
                    else:
                        S.copy("dve", qT_s, pTv[0:64, 0:4, :])
                        S.copy("dve", knew, pTv[0:64, 4:8, :])
                    if SST < 1:
                        return
                    cst = S.sb([NS * 3, 1024], F32, "s_cst")
                    S.dma("sp", cst, st_conv[l])
                    p = psf.next()
                    pv = p[:, 0:8 * NS * 3].rr("p (c r) -> p c r", c=8)
                    for c in range(8):
                        S.mm(pv[:, c, :], cst[:, c * 128:(c + 1) * 128], identf[:NS * 3, :NS * 3])
                    S.copy("act", ext[:, :, :, 0:3], pv.rr("p c (b t) -> p c b t", b=NS))
                    if SST < 1.2:
                        return
                    acc = S.sb([128, 8, NS, 8], F32, "s_acc")
                    for c in range(8):
                        S.ts("dve", acc[:, c], ext[:, c, :, 0:8], convw[:, c, 0:1], convb[:, c:c + 1], ALU.mult, ALU.add)
                        for k in range(1, 4):
                            S.stt(acc[:, c], ext[:, c, :, k:k + 8], convw[:, c, k:k + 1], acc[:, c], ALU.mult, ALU.add)
                    abf = S.sb([128, 8, L], BF16, "s_abf")
                    S.act(abf, acc.rr("p c b i -> p c (b i)"), AF.Silu)
                    dbg("s_acc", acc.rr("p c b i -> p c (b i)")); dbg("s_ext", ext.rr("p c b t -> p c (b t)"))
                    pT = psb.next()
                    pTv = pT[:, 0:768].rr("p (k l) -> p k l", k=6)
                    for c in range(6):
                        S.tr(pTv[:L, c, :], abf[:, c, :], identb)
                    xsB = S.sb([128, 768], BF16, "s_xsB")
                    S.copy("dve", xsB[:L], pTv[:L].rr("p k l -> p (k l)"))
                    xs3 = xsB[:L, 0:512].rr("p (h d) -> p h d", h=8)
                    if SST < 1.4:
                        return
                    misc_ps = psf.next()
                    for k in range(8):
                        S.mm(misc_ps[:L, 0:8], hT[:, k, :], win_sb[:, k, OFF_DT:OFF_DT + 8], start=(k == 0), stop=(k == 7))
                    sm = small()
                    dtp = sm[:L, 0:8]
                    S.tt("dve", dtp, misc_ps[:L, 0:8], dtb_b[:L], ALU.add)
                    sm2 = small()
                    S.ts("dve", sm2[:L, 8:16], dtp, -1.0, None, ALU.mult)
                    S.tt("dve", sm2[:L, 0:8], dtp, sm2[:L, 8:16], ALU.max)
                    S.act(sm2[:L, 0:8], sm2[:L, 0:8], AF.Exp, scale=-1.0)
                    S.act(sm2[:L, 0:8], sm2[:L, 0:8], AF.Ln, bias=1.0)
                    S.ts("dve", sm2[:L, 8:16], dtp, 0.0, None, ALU.max)
                    dt = sm[:L, 8:16]
                    S.tt("dve", dt, sm2[:L, 0:8], sm2[:L, 8:16], ALU.add)
                    sm3 = small()
                    adt = sm3[:L, 0:8]
                    S.tt("dve", adt, dt, a_b[:L], ALU.mult)
                    xc = S.sb([128, 512], BF16, "s_xc")
                    S.tt("pool", xc[:L].rr("p (h d) -> p h d", h=8), xs3, dt.us(2).bc([L, 8, 64]), ALU.mult)
                    dbg("s_dt", dt); dbg("s_xsB", xsB[:L], BF16); dbg("s_xc", xc[:L], BF16)
                    S.mm(misc_ps[:L, 8:16], incl_s, adt)
                    S.mm(misc_ps[:L, 16:24], after_s, adt)
                    for b in range(NS):
                        S.mm(misc_ps[:, 24 + 8 * b:32 + 8 * b], seqmask_s[:, b, :], adt)
                    sm4 = small()
                    S.act(sm4[:L, 0:8], misc_ps[:L, 8:16], AF.Exp)
                    S.act(sm4[:L, 8:16], misc_ps[:L, 16:24], AF.Exp)
                    elast = S.sb([128, NS, 8], F32, "s_elast")
                    S.act(elast.rr("p b h -> p (b h)"), misc_ps[:, 24:24 + 8 * NS], AF.Exp)
                    eacum = sm4[:L, 0:8]
                    dte = sm4[:L, 8:16]
                    if SST < 1.6:
                        return
                    sc_ps = psf.next()
                    scv = sc_ps[:L, 0:2 * L].rr("p (g l) -> p g l", g=2)
                    for g in range(2):
                        S.mm(scv[:, g, :], abf[:, 4 + g, :], abf[:, 6 + g, :])
                    scm = S.sb([128, 2, L], F32, "s_scm")
                    S.tt("dve", scm[:L], scv, incl_s.us(1).bc([L, 2, L]), ALU.mult)
                    if SST < 1.65:
                        return
                    rhsD = S.sb([128, 8, L], F32, "s_rhsD")
                    S.tt("pool", rhsD[:L], incl_s.us(1).bc([L, 8, L]), adt.us(2).bc([L, 8, L]), ALU.mult)
                    p = psf.next()
                    S.mm(p[:L, 0:8 * L], after_s, rhsD[:L].rr("p h l -> p (h l)"))
                    Dexp = S.sb([128, 8, L], F32, "s_Dexp")
                    S.act(Dexp[:L].rr("p h l -> p (h l)"), p[:L, 0:8 * L], AF.Exp)
                    if SST < 1.7:
                        return
                    MT = S.sb([128, 8, L], BF16, "s_MT")
                    for g in range(2):
                        S.tt("dve", MT[:L, g * 4:(g + 1) * 4, :], Dexp[:L, g * 4:(g + 1) * 4, :],
                             scm[:L, g, :].us(1).bc([L, 4, L]), ALU.mult)
                    if SST < 1.75:
                        return
                    Y_ps = psf.next()
                    for h in range(8):
                        S.mm(Y_ps[:L, h * 64:(h + 1) * 64], MT[:L, h, :], xc[:L, h * 64:(h + 1) * 64])
                    if SST < 1.78:
                        return
                    y = S.sb([128, 512], F32, "s_y")
                    t1 = S.sb([128, 512], F32, "s_t1")
                    t2 = S.sb([128, 512], F32, "s_t2")
                    S.tt("pool", t1[:L].rr("p (h d) -> p h d", h=8), xs3, dsk_b[:L].us(2).bc([L, 8, 64]), ALU.mult)
                    if SST < 1.785:
                        return
                    S.tt("dve", y[:L], Y_ps[:L], t1[:L], ALU.add)
                    if SST < 1.8:
                        return
                    STb = S.sb([128, 512], BF16, "s_STb")
                    sA = S.sb([64, 8, 128], F32, "s_sA")
                    sB = S.sb([128, 4, 128], F32, "s_sB")
                    sBb = S.sb([128, 4, 128], BF16, "s_sBb")
                    snew = S.sb([64, 8, 128], F32, "s_snew")
                    stmp = S.sb([64, 4, 128], F32, "s_stmp")
                    xcd = S.sb([128, 512], BF16, "s_xcd")
                    for b in range(NS):
                        S.dma("sp", sA, st_ssm[l, b].rr("h p n -> p h n"))
                        S.dma("sp", sB, st_ssm[l, b].rr("h p n -> (h p) n").rr("(hp q) n -> q hp n", q=128))
                        S.copy("dve", sBb, sB)
                        pT = psb.next()
                        pTv = pT[:, 0:512].rr("p (k l) -> p k l", k=4)
                        for hp in range(4):
                            S.tr(pTv[:, hp, :], sBb[:, hp, :], identb)
                        S.copy("dve", STb, pT[:, 0:512])
                        if SST < 1.82:
                            return
                        Yo_ps = psf.next()
                        for g in range(2):
                            S.mm(Yo_ps[:L, g * 256:(g + 1) * 256], abf[:, 6 + g, :], STb[:, g * 256:(g + 1) * 256])
                        smb = small()
                        S.ts("dve", smb[:L, 0:8], eacum, rowmask_s[:, b:b + 1], None, ALU.mult)
                        S.ts("dve", smb[:L, 8:16], dte, rowmask_s[:, b:b + 1], None, ALU.mult)
                        S.tt("dve", t2[:L].rr("p (h d) -> p h d", h=8), Yo_ps[:L].rr("p (h d) -> p h d", h=8),
                             smb[:L, 0:8].us(2).bc([L, 8, 64]), ALU.mult)
                        S.tt("pool", y[:L], y[:L], t2[:L], ALU.add)
                        if SST < 1.84:
                            return
                        S.tt("pool", xcd[:L].rr("p (h d) -> p h d", h=8), xc[:L].rr("p (h d) -> p h d", h=8),
                             smb[:L, 8:16].us(2).bc([L, 8, 64]), ALU.mult)
                        for hh in range(2):
                            Sn_ps = psf.next()
                            for h4 in range(4):
                                h = hh * 4 + h4
                                S.mm(Sn_ps[0:64, h4 * 128:(h4 + 1) * 128], xcd[:L, h * 64:(h + 1) * 64],
                                     xsB[:L, 512 + hh * 128: 512 + (hh + 1) * 128])
                            S.tt("pool", stmp, sA[:, hh * 4:(hh + 1) * 4, :],
                                 elast[0:64, b, hh * 4:(hh + 1) * 4].us(2).bc([64, 4, 128]), ALU.mult)
                            S.tt("dve", snew[:, hh * 4:(hh + 1) * 4, :], stmp, Sn_ps[0:64].rr("p (h n) -> p h n", h=4), ALU.add)
                        S.dma("sp", ssm_s[l, b].rr("h p n -> p h n"), snew)
                    if SST < 2:
                        return
                    S.tt("dve", y[:L], y[:L], zs[:L], ALU.mult)
                    ss = small()
                    junk = junkS.junk.next()
                    S.act(junk[:L, 0:512], y[:L], AF.Square, accum=ss[:L, 0:1])
                    S.ts("dve", ss[:L, 1:2], ss[:L, 0:1], 1.0 / 512, RMS_EPS, ALU.mult, ALU.add)
                    S.act(ss[:L, 1:2], ss[:L, 1:2], AF.Sqrt)
                    S.recip(ss[:L, 2:3], ss[:L, 1:2])
                    S.act(mix[:L, 0:512], y[:L], AF.Copy, scale=ss[:L, 2:3])
                    pst = S.sb([NS * 15, 256], F32, "s_pst")
                    S.dma("sp", pst, st_pool[l])
                    for b in range(NS):
                        S.dma("sp", pool_s[l, b * 15:b * 15 + 7, :], pst[b * 15 + 8:b * 15 + 15, :])
                    p = psf.next()
                    pv = p[:, 0:4 * NS * 15].rr("p (c r) -> p c r", c=4)
                    for g in range(4):
                        S.mm(pv[0:64, g, :], pst[:, g * 64:(g + 1) * 64], identf[:NS * 15, :NS * 15])
                    S.copy("act", pext[:, :, :, 0:15], pv[0:64].rr("p c (b t) -> p c b t", b=NS))
                    s2 = S.sb([64, 4, NS, 22], F32, "s_s2")
                    s4 = S.sb([64, 3, NS, 20], F32, "s_s4")
                    s8 = S.sb([64, 2, NS, 16], F32, "s_s8")
                    s16 = S.sb([64, 1, NS, 8], F32, "s_s16")
                    S.tt("pool", s2, pext[:, :, :, 1:23], pext[:, :, :, 0:22], ALU.add)
                    S.tt("pool", s4, s2[:, 1:4, :, 2:22], s2[:, 1:4, :, 0:20], ALU.add)
                    S.tt("pool", s8, s4[:, 1:3, :, 4:20], s4[:, 1:3, :, 0:16], ALU.add)
                    S.tt("pool", s16, s8[:, 1:2, :, 8:16], s8[:, 1:2, :, 0:8], ALU.add)
                    pooled = S.sb([64, 4, NS, 8], BF16, "s_pooled")
                    srcs = [(s2, 0, 14), (s4, 0, 12), (s8, 0, 8), (s16, 0, 0)]
                    for g, (sx, gi, off) in enumerate(srcs):
                        S.stt(pooled[:, g], sx[:, gi, :, off:off + 8], 1.0 / (2 ** (g + 1)), pext[:, g, :, 15:23],
                              ALU.mult, ALU.subtract)
                    yp_ps = psf.next()
                    for g in range(4):
                        S.mm(yp_ps[:L, g * 64:(g + 1) * 64], pooled[:, g].rr("p b i -> p (b i)"), poolw_sb[:, g, :])
                    S.copy("act", mix[:L, 512:768], yp_ps[:L, 0:256])
                if SST < 3:
                    return
                with S.scope():
                    load_cmp_weights(l)
                    ksT = S.sb([64, NKS], BF16, "s_ksT")
                    vs = S.sb([128, NPG + 1, 64], BF16, "s_vs")
                    kwT = S.sb([64, WB + 8], BF16, "s_kwT")
                    vw = S.sb([128, NTW, 64], BF16, "s_vw")
                    gel_k = S.sb([128, NTC * 128], BF16, "s_gelk")
                    gel_v = S.sb([128, NTC * 128], BF16, "s_gelv")
                    ckT = S.sb([64, NTC * 128], BF16, "s_ckT")
                    cvs = S.sb([128, NTC, 64], BF16, "s_cv")
                    qTb = S.sb([64, 4, 8], BF16, "s_qTb")
                    g_rows = S.sb([32, 3], F32, "s_grows")
                    oacc = S.sb([32, 64], F32, "s_oacc")
                    oaccb = S.sb([32, 64], BF16, "s_oaccb")
                    for b in range(NS):
                        S.memset("dve", ksT[:, PAST:NKS], 0.0)
                        S.memset("dve", vs[:, NPG, :], 0.0)
                        S.memset("dve", vw[:, NTW - 1, :], 0.0)
                        with S.scope():
                            cT = S.sb([64, 2, PAST], BF16, "s_cT")
                            r_pg = Ring(S, 4, [128, 256], F32, "s_pg")
                            r_pgb = Ring(S, 2, [128, 256], BF16, "s_pgb")
                            pgts = {}

                            def issue_page(pg):
                                pgt = r_pg.next()
                                j = b * NPG + pg

                                def fn(e, pgt=pgt, j=j):
                                    return e.indirect_dma_start(
                                        out=pgt.ap, out_offset=None, in_=cache_flat.ap,
                                        in_offset=bass.IndirectOffsetOnAxis(ap=idx.ap[:, j:j + 1], axis=0))
                                S.op("pool", fn, [pgt], [idx], dma=True)
                                pgts[pg] = pgt
                            PF = 3
                            for pg in range(min(PF, NPG)):
                                issue_page(pg)
                            for pg in range(NPG):
                                pgt = pgts.pop(pg)
                                pgb = r_pgb.next()
                                S.copy("act", pgb, pgt)
                                if pg + PF < NPG:
                                    issue_page(pg + PF)
                                pT = psb.next()
                                pTv = pT[0:64, 0:384].rr("p (k l) -> p k l", k=3)
                                for jj in range(3):
                                    S.tr(pTv[:, jj, :], pgb[:, jj * 64:(jj + 1) * 64], identb)
                                S.copy("dve", cT[:, :, pg * 128:(pg + 1) * 128], pTv[:, 0:2, :])
                                S.copy("dve", ksT[:, pg * 128:(pg + 1) * 128], pTv[:, 2, :])
                                S.copy("act", vs[:, pg, :], pgb[:, 192:256])
                            for (ci, w1, cb, gel) in ((0, CW.w1k, CW.cbias_k, gel_k), (1, CW.w1v, CW.cbias_v, gel_v)):
                                n0 = 0
                                while n0 < NV:
                                    nb = min(512, NV - n0)
                                    p = psf.next()
                                    for li in range(32):
                                        S.mm(p[:, 0:nb], w1[:, li, :], cT[:, ci, 16 * n0 + li: 16 * (n0 + nb - 1) + li + 1: 16],
                                             start=(li == 0), stop=(li == 31))
                                    S.act(gel[:, n0:n0 + nb], p[:, 0:nb], AF.Gelu_apprx_tanh, bias=cb)
                                    n0 += nb
                        if SST < 4:
                            return
                        n0 = 0
                        while n0 < NV:
                            nb = min(512, NV - n0)
                            p = psf.next()
                            S.mm(p[0:64, 0:nb], CW.w2k, gel_k[:, n0:n0 + nb])
                            S.copy("act", ckT[:, n0:n0 + nb], p[0:64, 0:nb])
                            n0 += nb
                        for t in range(NTC):
                            rws = min(128, NV - t * 128)
                            p = psf.next()
                            S.mm(p[0:rws, 0:64], gel_v[:, t * 128:t * 128 + rws], CW.w2v)
                            S.copy("act", cvs[0:rws, t, :], p[0:rws, 0:64])
                        S.copy("dve", ksT[:, PAST:PAST + 8], knew[:, 2, b * 8:(b + 1) * 8])
                        S.dma("sp", vs[0:8, NPG, :], kvbf[b * 8:(b + 1) * 8, 192:256])
                        with S.scope():
                            r_wt = Ring(S, 2, [128, 128], F32, "s_wt")
                            r_wtb = Ring(S, 2, [128, 128], BF16, "s_wtb")
                            for t in range(WB // 128):
                                wt = r_wt.next()
                                S.dma("sp", wt, st_win[l, b, t * 128:(t + 1) * 128, :])
                                r0 = 8 if t == 0 else 0
                                S.dma("sp", win_s[l, b, t * 128 - 8 + r0:t * 128 + 120, :], wt[r0:128, :])
                                wtb = r_wtb.next()
                                S.copy("act", wtb, wt)
                                pT = psb.next()
                                S.tr(pT[0:64, 0:128], wtb[:, 0:64], identb)
                                S.copy("dve", kwT[:, t * 128:(t + 1) * 128], pT[0:64, 0:128])
                                S.copy("act", vw[:, t, :], wtb[:, 64:128])
                        S.copy("dve", kwT[:, WB:WB + 8], knew[:, 3, b * 8:(b + 1) * 8])
                        S.dma("sp", vw[0:8, WB // 128, :], kvbf[b * 8:(b + 1) * 8, 320:384])
                        S.copy("dve", qTb, qT_s[:, :, b * 8:(b + 1) * 8])
                        for h in range(4):
                            S.dma("sp", g_rows[h * 8:(h + 1) * 8, :], gates[b * 8:(b + 1) * 8, 3 * h:3 * h + 3])
                        qTf = qTb.rr("p h i -> p (h i)")
                        with S.scope():
                            M = 32
                            ssb = S.sb([32, 512], F32, "s_ssb")
                            ebf = S.sb([32, NKS], BF16, "s_ebf")
                            pTs = S.sb([128, NPG + 1, 32], BF16, "s_pTs")
                            e32 = S.sb([32, NTC * 128], F32, "s_e32")
                            pn = S.sb([32, NTC * 128], BF16, "s_pn")
                            pnT = S.sb([128, NTC, 32], BF16, "s_pnT")
                            impr = S.sb([32, NSLC_S], F32, "s_impr")
                            imp = S.sb([32, NSLC_S], F32, "s_imp")
                            wk = S.sb([32, NSLC_S], F32, "s_wk")
                            selm = S.sb([32, NSLC_S], F32, "s_selm")
                            mxc = S.sb([32, 32], F32, "s_mxc")
                            n0 = 0
                            while n0 < NV:
                                nb = min(512, NV - n0)
                                p = psf.next()
                                S.mm(p[0:M, 0:nb], qTf, ckT[:, n0:n0 + nb])
                                S.copy("act", e32[:, n0:n0 + nb], p[0:M, 0:nb])
                                n0 += nb
                            sm = softmax_rows(e32[:, 0:NV], M, NV, e32[:, 0:NV], clamp=True)
                            S.ts("dve", sm[:M, 3:4], sm[:M, 2:3], 1e-30, None, ALU.max)
                            S.recip(sm[:M, 4:5], sm[:M, 3:4])
                            S.ts("dve", pn[:, 0:NV], e32[:, 0:NV], sm[:M, 4:5], None, ALU.mult)
                            pT = psb.next()
                            pTv = pT[:, 0:NTC * 32].rr("p (k l) -> p k l", k=NTC)
                            for t in range(NTC):
                                rws = min(128, NV - t * 128)
                                S.tr(pTv[0:rws, t, :], pn[:, t * 128:t * 128 + rws], identb[:M, :M])
                            if NV % 128 != 0:
                                S.memset("dve", pnT[:, NTC - 1, :], 0.0)
                            for t in range(NTC):
                                rws = min(128, NV - t * 128)
                                S.copy("dve", pnT[0:rws, t, :], pTv[0:rws, t, :])
                            o_ps = psf.next()
                            for t in range(NTC):
                                rws = min(128, NV - t * 128)
                                S.mm(o_ps[0:M, 0:64], pnT[0:rws, t, :], cvs[0:rws, t, :], start=(t == 0), stop=(t == NTC - 1))
                            S.ts("dve", oacc, o_ps[0:M, 0:64], g_rows[:, 0:1], None, ALU.mult)
                            if SST < 5:
                                return
                            i_ps = psf.next()
                            for t in range(NTC):
                                rws = min(128, NV - t * 128)
                                S.mm(i_ps[0:M, 0:NSLC_S], pnT[0:rws, t, :], ov_s[0:rws, t, :], start=(t == 0), stop=(t == NTC - 1))
                            S.copy("act", impr, i_ps[0:M, 0:NSLC_S])
                            i2_ps = psf.next()
                            S.mm(i2_ps[0:M, 0:NSLC_S], Rm, impr)
                            S.tt("dve", imp, i2_ps[0:M, 0:NSLC_S], addc_s, ALU.add)
                            m8 = small()
                            if NSLC_S > 16:
                                S.max8(m8[:M, 0:8], imp)
                                S.match_replace(wk, m8[:M, 0:8], imp, -3e38)
                                S.max8(m8[:M, 8:16], wk)
                                S.ts("dve", selm, imp, m8[:M, 15:16], NEG, ALU.is_lt, ALU.mult)
                            else:
                                S.memset("dve", selm, 0.0)
                            nch = (NKS + 511) // 512
                            for pas in range(2):
                                for c in range(nch):
                                    k0 = c * 512
                                    w = min(512, NKS - k0)
                                    p = psf.next()
                                    S.mm(p[0:M, 0:w], qTf, ksT[:, k0:k0 + w])
                                    S.tt("dve", ssb[:, 0:w].rr("p (j c) -> p j c", c=64), p[0:M, 0:w].rr("p (j c) -> p j c", c=64),
                                         selm[:, k0 // 64:(k0 + w) // 64].us(2).bc([M, w // 64, 64]), ALU.add)
                                    if c == nch - 1:
                                        S.tt("dve", ssb[:, w - 64:w], ssb[:, w - 64:w], tokmask_s, ALU.add)
                                    if pas == 0:
                                        S.reduce(mxc[:, c:c + 1], ssb[:, 0:w], ALU.max)
                                    else:
                                        S.act(ebf[:, k0:k0 + w], ssb[:, 0:w], AF.Exp, bias=sm1[:M, 1:2], accum=mxc[:, c:c + 1])
                                if pas == 0:
                                    sm1 = small()
                                    S.reduce(sm1[:M, 0:1], mxc[:, 0:nch], ALU.max)
                                    S.ts("dve", sm1[:M, 1:2], sm1[:M, 0:1], -1.0, None, ALU.mult)
                                else:
                                    S.reduce(sm1[:M, 2:3], mxc[:, 0:nch], ALU.add)
                            S.recip(sm1[:M, 3:4], sm1[:M, 2:3])
                            S.tt("dve", sm1[:M, 4:5], sm1[:M, 3:4], g_rows[:, 1:2], ALU.mult)
                            ntl = NPG + 1
                            for b0 in range(0, ntl, 32):
                                nb8 = min(32, ntl - b0)
                                pT = psb.next()
                                pTv = pT.rr("p (k l) -> p k l", k=32)
                                for jx in range(nb8):
                                    t = b0 + jx
                                    rws = min(128, NKS - t * 128)
                                    S.tr(pTv[0:rws, jx, :], ebf[:, t * 128:t * 128 + rws], identb[:M, :M])
                                nfull = nb8 if (b0 + nb8 < ntl) else nb8 - 1
                                if nfull > 0:
                                    S.copy("dve", pTs[:, b0:b0 + nfull, :], pTv[:, 0:nfull, :])
                                if nfull < nb8:
                                    S.copy("dve", pTs[0:64, ntl - 1, :], pTv[0:64, nb8 - 1, :])
                            o_ps = psf.next()
                            for t in range(ntl):
                                rws = min(128, NKS - t * 128)
                                S.mm(o_ps[0:M, 0:64], pTs[0:rws, t, :], vs[0:rws, t, :], start=(t == 0), stop=(t == ntl - 1))
                            S.stt(oacc, o_ps[0:M, 0:64], sm1[:M, 4:5], oacc, ALU.mult, ALU.add)
                            if SST < 6:
                                return
                            NKW = WB + 8
                            wsb = S.sb([32, NKW], F32, "s_wsb")
                            k0 = 0
                            while k0 < NKW:
                                w = min(512, NKW - k0)
                                p = psf.next()
                                S.mm(p[0:M, 0:w], qTf, kwT[:, k0:k0 + w])
                                S.tt("dve", wsb[:, k0:k0 + w], p[0:M, 0:w], winmask_s[:, k0:k0 + w], ALU.add)
                                k0 += w
                            sm = softmax_rows(wsb, M, NKW, ebf[:, 0:NKW])
                            S.recip(sm[:M, 3:4], sm[:M, 2:3])
                            S.tt("dve", sm[:M, 4:5], sm[:M, 3:4], g_rows[:, 2:3], ALU.mult)
                            pT = psb.next()
                            pTv = pT.rr("p (k l) -> p k l", k=32)
                            for t in range(NTW):
                                rws = min(128, NKW - t * 128)
                                S.tr(pTv[0:rws, t, :], ebf[:, t * 128:t * 128 + rws], identb[:M, :M])
                            for t in range(NTW):
                                rws = min(128, NKW - t * 128)
                                S.copy("dve", pTs[0:rws, t, :], pTv[0:rws, t, :])
                            o_ps = psf.next()
                            for t in range(NTW):
                                rws = min(128, NKW - t * 128)
                                S.mm(o_ps[0:M, 0:64], pTs[0:rws, t, :], vw[0:rws, t, :], start=(t == 0), stop=(t == NTW - 1))
                            S.stt(oacc, o_ps[0:M, 0:64], sm[:M, 4:5], oacc, ALU.mult, ALU.add)
                            S.copy("dve", oaccb, oacc)
                            for h in range(4):
                                S.dma("sp", mix[b * 8:(b + 1) * 8, 768 + h * 64:768 + (h + 1) * 64], oaccb[h * 8:(h + 1) * 8, :])
                if SST < 7:
                    return
                with S.scope():
                    pT = psb.next()
                    pTv = pT[:, 0:8 * L].rr("p (k l) -> p k l", k=8)
                    for k in range(8):
                        S.tr(pTv[:, k, :], mix[:L, k * 128:(k + 1) * 128], identb[:L, :L])
                    mixT = S.sb([128, 8, L], BF16, "s_mixT")
                    S.tt("dve", mixT, pTv, mixg.us(2).bc([128, 8, L]), ALU.mult)
                    mo = [psf.next(), psf.next()]
                    ss = small()
                    junk = junkS.junk.next()
                    for nh in range(2):
                        for k in range(8):
                            S.mm(mo[nh][:L], mixT[:, k, :], wout_sb[:, k, nh * 512:(nh + 1) * 512], start=(k == 0), stop=(k == 7))
                        S.act(junk[:L, nh * 512:(nh + 1) * 512], mo[nh][:L], AF.Square, accum=ss[:L, nh:nh + 1])
                    S.tt("dve", ss[:L, 2:3], ss[:L, 0:1], ss[:L, 1:2], ALU.add)
                    S.ts("dve", ss[:L, 3:4], ss[:L, 2:3], 1.0 / 1024, RMS_EPS, ALU.mult, ALU.add)
                    S.act(ss[:L, 3:4], ss[:L, 3:4], AF.Sqrt)
                    S.recip(ss[:L, 4:5], ss[:L, 3:4])
                    tmp = S.sb([128, 512], F32, "s_tmp")
                    for nh in range(2):
                        sl = slice(nh * 512, (nh + 1) * 512)
                        S.stt(tmp[:L], mo[nh][:L], ss[:L, 4:5], gpost_b[:L, sl], ALU.mult, ALU.mult)
                        S.tt("pool", xmid_s[:L, sl], tmp[:L], xt[:L, sl], ALU.add)
                if SST < 8:
                    return
                with S.scope():
                    alloc_ffn_bufs(128)
                    h2T, actT, f_sb = F.h2T, F.actT, F.f_sb
                    rmsnorm_T(xmid_s[:L], L, gcol_ffn, h2T[:, :, 0:L], B=F)
                    fst = S.sb([NS * 2, 4096], F32, "s_fst")
                    S.dma("sp", fst, st_ffn[l])
                    carry_s = S.sb([128, 32, NS, 2], F32, "s_carry")
                    for q4 in range(4):
                        p = psf.next()
                        pv = p[:, 0:8 * NS * 2].rr("p (c r) -> p c r", c=8)
                        for c8 in range(8):
                            c = q4 * 8 + c8
                            S.mm(pv[:, c8, :], fst[:, c * 128:(c + 1) * 128], identf[:NS * 2, :NS * 2])
                        S.copy("act", carry_s[:, q4 * 8:(q4 + 1) * 8], pv.rr("p c (b t) -> p c b t", b=NS))
                    gtl = S.sb([32, 4096], F32, "s_gtl")
                    r_gx = Ring(S, 2, [128, NS, 10], F32, "s_gx")
                    r_ga = Ring(S, 2, [128, NS, 8], F32, "s_ga")
                    for s in range(16):
                        wg = F.wg.next()
                        wv = F.wv.next()
                        S.dma("sp", wg, wg_s[s])
                        S.dma("sp", wv, wv_s[s])
                        p = psf.next()
                        for k in range(8):
                            S.mm(p[:L, 0:256], h2T[:, k, 0:L], wg[:, k, :], start=(k == 0), stop=(k == 7))
                        S.copy("act", gtl[:, s * 256:(s + 1) * 256], p[:L, 0:256])
                        for c2 in range(2):
                            c = s * 2 + c2
                            g_ps = psf.next()
                            v_ps = psf.next()
                            for k in range(8):
                                S.mm(g_ps[:, 0:L], wg[:, k, c2 * 128:(c2 + 1) * 128], h2T[:, k, 0:L], start=(k == 0), stop=(k == 7))
                            for k in range(8):
                                S.mm(v_ps[:, 0:L], wv[:, k, c2 * 128:(c2 + 1) * 128], h2T[:, k, 0:L], start=(k == 0), stop=(k == 7))
                            gx = r_gx.next()
                            S.copy("act", gx[:, :, 2:10], g_ps[:, 0:L].rr("p (b i) -> p b i", b=NS))
                            S.copy("pool", gx[:, :, 0:2], carry_s[:, c])
                            ga = r_ga.next()
                            S.ts("pool", ga, gx[:, :, 0:8], fconvw[:, c, 0:1], fconvb[:, c:c + 1], ALU.mult, ALU.add)
                            S.stt(ga, gx[:, :, 1:9], fconvw[:, c, 1:2], ga, ALU.mult, ALU.add)
                            S.stt(ga, gx[:, :, 2:10], fconvw[:, c, 2:3], ga, ALU.mult, ALU.add)
                            S.act(ga, ga, AF.Gelu_apprx_tanh)
                            S.tt("dve", actT[:, c, 0:L], ga.rr("p b i -> p (b i)"), v_ps[:, 0:L], ALU.mult)
                    for b in range(NS):
                        S.dma("sp", ffn_s[l, b * 2:(b + 1) * 2, :], gtl[b * 8 + 6:b * 8 + 8, :])
                    for s in range(16):
                        wd = F.wd.next()
                        S.dma("sp", wd, wd_s[s])
                        for nh in range(2):
                            p = psf.next()
                            for c2 in range(2):
                                S.mm(p[:L], actT[:, s * 2 + c2, 0:L], wd[:, c2, nh * 512:(nh + 1) * 512],
                                     start=(c2 == 0), stop=(c2 == 1))
                            dst = f_sb[:L, 0, nh * 512:(nh + 1) * 512]
                            if s == 0:
                                S.copy("act", dst, p[:L])
                            else:
                                S.tt("dve", dst, dst, p[:L], ALU.add)
                    junk = F.junk.next()
                    ss = small()
                    S.act(junk[:L], f_sb[:L, 0, :], AF.Square, accum=ss[:L, 0:1])
                    S.ts("dve", ss[:L, 1:2], ss[:L, 0:1], 1.0 / 1024, RMS_EPS, ALU.mult, ALU.add)
                    S.act(ss[:L, 1:2], ss[:L, 1:2], AF.Sqrt)
                    S.recip(ss[:L, 2:3], ss[:L, 1:2])
                    S.stt(f_sb[:L, 0, :], f_sb[:L, 0, :], ss[:L, 2:3], gfpost_b[:L], ALU.mult, ALU.mult)
                    S.tt("pool", f_sb[:L, 0, :], f_sb[:L, 0, :], xmid_s[:L], ALU.add)
                    S.dma("sp", xdst, f_sb[:L, 0, :])

        for l in range(DEPTH):
            load_layer_weights(l)
            precast_ffn(l)
            last = (l == DEPTH - 1)

            def xdst(i, last=last):
                if last:
                    return y_p[i * 128:(i + 1) * 128, :]
                return xcur_tile(i)
            with S.scope():
                alloc_prompt_persist()
                nblk_ = (NT + NTB - 1) // NTB if not cfg.get("skip_prompt") else 0
                for blk in range(nblk_):
                    t0 = blk * NTB
                    ntb = min(NTB, NT - t0)
                    with S.scope():
                        alloc_mixer_rings()
                        load_cmp_weights(l)
                        for ti in range(t0, t0 + ntb):
                            xsrc = x_p[ti * 128:(ti + 1) * 128, :] if l == 0 else xcur_tile(ti)
                            mixer_tile_prompt(l, ti, xsrc)
                    if STAGE >= 8:
                        with S.scope():
                            alloc_ffn_bufs(ntb * 128)
                            ffn_block_prompt(l, blk, ntb, xdst)
                if STAGE >= 3 and not cfg.get("skip_prompt"):
                    with S.scope():
                        ssm_out_prompt(l)
            if NS > 0 and STAGE >= 9:
                sample_layer(l, last)
        print("SBUF peak bytes", S.sb_peak, "instr", {k: len(v) for k, v in S.prog.items()})
        S.emit()
    return nc, outs, consts


_CACHE = {}


def kernel(**inp):
    x_prompt = np.asarray(inp["x_prompt"], np.float32)
    x_sample = np.asarray(inp["x_sample"], np.float32)
    B, T, D = x_prompt.shape
    BS, TS, _ = x_sample.shape
    DEPTH = inp["w_in"].shape[0]
    cache_kv = np.asarray(inp["cache_nsa_kv"], np.float32)
    NPHYS = cache_kv.shape[1]
    page_table = np.asarray(inp["page_table"], np.int32)
    NPG = page_table.shape[1]
    n_cores = NCORES
    assert B == n_cores and BS % n_cores == 0 and TS == 8
    NS = BS // n_cores
    cfg = dict(T=T, DEPTH=DEPTH, NS=NS, NPG=NPG, NPHYS=NPHYS)
    key = (T, DEPTH, NS, NPG, NPHYS)
    if key not in _CACHE:
        _CACHE[key] = build(cfg)
    nc, outs, consts = _CACHE[key]
    WB = inp["state_nsa_win"].shape[2]
    cache2 = np.ascontiguousarray(cache_kv.reshape(DEPTH, NPHYS * 128, 256))
    st_win = np.asarray(inp["state_nsa_win"], np.float32).reshape(DEPTH, BS, WB, 128)
    st_conv = np.asarray(inp["state_ssd_conv"], np.float32)
    st_ssm = np.asarray(inp["state_ssm"], np.float32)
    st_pool = np.asarray(inp["state_pool"], np.float32)
    st_ffn = np.asarray(inp["state_ffn_conv"], np.float32)
    wnames = ["norm_mix_pre", "w_in", "ssd_conv_w", "ssd_conv_b", "ssd_dt_bias", "ssd_a_log", "ssd_d", "ssd_norm",
              "pool_w", "pool_scale", "nsa_pe_k", "nsa_pe_v", "nsa_w1_k", "nsa_w1_v", "nsa_w2_k", "nsa_w2_v", "w_out",
              "norm_mix_post", "norm_ffn_pre", "ffn_w_gate", "ffn_w_val", "ffn_conv_w", "ffn_conv_b", "ffn_w_down",
              "norm_ffn_post"]
    shared = {n: np.ascontiguousarray(np.asarray(inp[n], np.float32)) for n in wnames}
    shared.update(consts)
    shared["cache"] = cache2
    in_maps = []
    for c in range(n_cores):
        sl = slice(c * NS, (c + 1) * NS)
        m = dict(shared)
        m["x_p"] = np.ascontiguousarray(x_prompt[c])
        m["x_s"] = np.ascontiguousarray(x_sample[sl].reshape(NS * 8, D))
        m["pt"] = np.ascontiguousarray(page_table[sl].reshape(-1))
        m["st_win"] = np.ascontiguousarray(st_win[:, sl])
        m["st_conv"] = np.ascontiguousarray(st_conv[:, sl].reshape(DEPTH, NS * 3, 1024))
        m["st_ssm"] = np.ascontiguousarray(st_ssm[:, sl])
        m["st_pool"] = np.ascontiguousarray(st_pool[:, sl].reshape(DEPTH, NS * 15, 256))
        m["st_ffn"] = np.ascontiguousarray(st_ffn[:, sl].reshape(DEPTH, NS * 2, 4096))
        in_maps.append(m)
    res = run_bass_kernel_spmd(nc, in_maps, core_ids=list(range(n_cores))).results
    WK = min(512, T)

    def g(name):
        return [np.asarray(r[name], np.float32) for r in res]

    y_p = np.stack(g("y_p"), 0)
    y_s = np.concatenate([a.reshape(NS, 8, D) for a in g("y_s")], 0)
    kv_p = np.stack([a.reshape(DEPTH, T, 4, 64) for a in g("kv_p")], 1)
    kv_s = np.concatenate([a.reshape(DEPTH, NS, 8, 4, 64) for a in g("kv_s")], 1)
    win_p = np.stack([a.reshape(DEPTH, WK, 2, 64) for a in g("win_p")], 1)
    win_s = np.concatenate([a.reshape(DEPTH, NS, WB, 2, 64) for a in g("win_s")], 1)
    conv_p = np.stack(g("conv_p"), 1)
    conv_s = np.concatenate([a.reshape(DEPTH, NS, 3, 1024) for a in g("conv_s")], 1)
    ssm_p = np.stack(g("ssm_p"), 1)
    ssm_s = np.concatenate(g("ssm_s"), 1)
    pool_p = np.stack(g("pool_p"), 1)
    pool_s = np.concatenate([a.reshape(DEPTH, NS, 15, 256) for a in g("pool_s")], 1)
    ffn_p = np.stack(g("ffn_p"), 1)
    ffn_s = np.concatenate([a.reshape(DEPTH, NS, 2, 4096) for a in g("ffn_s")], 1)
    return (y_p, y_s, kv_p, kv_s, win_p, win_s, conv_p, conv_s, ssm_p, ssm_s, pool_p, pool_s, ffn_p, ffn_s)
```

```python
import numpy as np
import contextlib
import concourse.bass as bass
import concourse.mybir as mybir
from concourse.bass_utils import run_bass_kernel_spmd

F32 = mybir.dt.float32
BF16 = mybir.dt.bfloat16
I32 = mybir.dt.int32
ALU = mybir.AluOpType
AF = mybir.ActivationFunctionType
AX = mybir.AxisListType

NDS = 8
SBUF_BYTES = 229376
SBUF_BASE = 16640


class Res:
    __slots__ = ("w", "r", "name")

    def __init__(self, name=""):
        self.w = None
        self.r = {}
        self.name = name


class V:
    __slots__ = ("ap", "res")

    def __init__(self, ap, res=()):
        self.ap = ap
        self.res = tuple(res)

    def __getitem__(self, key):
        return V(self.ap[key], self.res)

    def rr(self, s, **kw):
        return V(self.ap.rearrange(s, **kw), self.res)

    def bc(self, shape):
        return V(self.ap.to_broadcast(list(shape)), self.res)

    def us(self, axis):
        return V(self.ap.unsqueeze(axis), self.res)

    def bitcast(self, dt):
        return V(self.ap.bitcast(dt), self.res)

    def with_res(self, res):
        return V(self.ap, res)

    @property
    def shape(self):
        return self.ap.shape


def _resources(vs):
    out = []
    for v in vs:
        if isinstance(v, V):
            for r in v.res:
                if r not in out:
                    out.append(r)
    return out


class Sched:
    def __init__(self, nc, es, same_sync=True):
        self.nc = nc
        self.es = es
        self.eng = dict(pe=nc.tensor, act=nc.scalar, dve=nc.vector, pool=nc.gpsimd, sp=nc.sync)
        self.prog = {k: [] for k in self.eng}
        self.sem = {}
        self.val = {}
        for k in self.eng:
            self.sem[k] = es.enter_context(nc.semaphore("s_" + k))
            self.val[k] = 0
        self.dq = {}
        for q in ("sp", "pool", "act"):
            names = []
            for i in range(NDS):
                n = "d_%s%d" % (q, i)
                self.sem[n] = es.enter_context(nc.semaphore(n))
                self.val[n] = 0
                names.append(n)
            self.dq[q] = [names, 0]
        self.seen = {k: {} for k in self.eng}
        self.same_sync = same_sync
        self.ntens = 0
        self.sb_top = SBUF_BASE
        self.sb_peak = 0

    def sb(self, shape, dtype, name=None):
        self.ntens += 1
        name = "%s_%d" % (name or "t", self.ntens)
        esz = 2 if dtype == BF16 else 4
        nbytes = esz
        for d in shape[1:]:
            nbytes *= d
        off = (self.sb_top + 31) // 32 * 32
        self.sb_top = off + nbytes
        self.sb_peak = max(self.sb_peak, self.sb_top)
        assert self.sb_top <= SBUF_BYTES, "SBUF overflow: %s needs %d at %d" % (name, nbytes, off)
        t = self.nc.alloc_sbuf_tensor_at(name, list(shape), dtype, offset=off)
        return V(t[tuple(slice(None) for _ in shape)], (Res(name),))

    @contextlib.contextmanager
    def scope(self):
        self.barrier()
        top = self.sb_top
        try:
            yield
        finally:
            self.barrier()
            self.sb_top = top

    def ps(self, shape, dtype, name=None):
        self.ntens += 1
        name = "%s_%d" % (name or "p", self.ntens)
        t = self.es.enter_context(self.nc.psum_tensor(name, list(shape), dtype))
        return V(t[tuple(slice(None) for _ in shape)], (Res(name),))

    def op(self, e, fn, outs, ins, dma=False):
        need = {}
        seen = self.seen[e]

        def add(s, v):
            if s == e and not dma:
                if e == "pe" or (not self.same_sync and e != "pool"):
                    return
            if seen.get(s, 0) >= v:
                return
            if need.get(s, 0) < v:
                need[s] = v

        rin = _resources(ins)
        rout = _resources(outs)
        for r in rin:
            if r.w is not None:
                add(*r.w)
        for r in rout:
            if r.w is not None:
                add(*r.w)
            for s, v in r.r.items():
                add(s, v)
        if dma:
            names, i = self.dq[e]
            s = names[i % len(names)]
            self.dq[e][1] += 1
            if self.val[s] > 0:
                add(s, self.val[s])
            self.val[s] += 16
            ev = (s, self.val[s])
            inc = 16
        else:
            self.val[e] += 1
            s = e
            ev = (e, self.val[e])
            inc = 1
        for s2, v in need.items():
            seen[s2] = v
        self.prog[e].append((list(need.items()), fn, s, inc))
        for r in rin:
            if r.r.get(ev[0], 0) < ev[1]:
                r.r[ev[0]] = ev[1]
        for r in rout:
            r.w = ev
            r.r = {}

    def barrier(self):
        for e in self.eng:
            waits = []
            for s, v in self.val.items():
                if v > 0 and self.seen[e].get(s, 0) < v and s != e:
                    waits.append((s, v))
                    self.seen[e][s] = v
            if waits:
                self.prog[e].append((waits, None, None, 0))

    def emit(self):
        waits = [(s, v) for s, v in self.val.items() if v > 0 and s != "sp"]
        self.prog["sp"].append((waits, None, None, 0))
        sem = self.sem
        prog = self.prog

        def mk(ename):
            def body(eobj):
                for waits, fn, s, inc in prog[ename]:
                    for (ws, wv) in waits:
                        eobj.wait_ge(sem[ws], wv)
                    if fn is not None:
                        ins = fn(eobj)
                        ins.then_inc(sem[s], inc)
            return body

        with self.nc.Block() as block:
            block.tensor(mk("pe"))
            block.scalar(mk("act"))
            block.vector(mk("dve"))
            block.gpsimd(mk("pool"))
            block.sync(mk("sp"))

    def dma(self, q, out, in_, **kw):
        self.op(q, lambda e: e.dma_start(out=out.ap, in_=in_.ap, **kw), [out], [in_], dma=True)

    def mm(self, out, lhsT, rhs, start=True, stop=True):
        self.op("pe", lambda e: e.matmul(out.ap, lhsT.ap, rhs.ap, start=start, stop=stop), [out], [lhsT, rhs])

    def tr(self, out, in_, ident):
        self.op("pe", lambda e: e.transpose(out.ap, in_.ap, ident.ap), [out], [in_, ident])

    def act(self, out, in_, func, bias=None, scale=None, accum=None):
        ins = [in_] + [b for b in (bias, scale) if isinstance(b, V)]
        outs = [out] + ([accum] if accum is not None else [])

        def fn(e):
            kw = {}
            if bias is not None:
                kw["bias"] = bias.ap if isinstance(bias, V) else bias
            if scale is not None:
                kw["scale"] = scale.ap if isinstance(scale, V) else scale
            if accum is not None:
                kw["accum_out"] = accum.ap
            return e.activation(out=out.ap, in_=in_.ap, func=func, **kw)
        self.op("act", fn, outs, ins)

    def tt(self, e, out, in0, in1, op):
        self.op(e, lambda en: en.tensor_tensor(out.ap, in0.ap, in1.ap, op), [out], [in0, in1])

    def ts(self, e, out, in0, s1, s2=None, op0=ALU.mult, op1=None, accum=None):
        ins = [in0] + [b for b in (s1, s2) if isinstance(b, V)]
        outs = [out] + ([accum] if accum is not None else [])

        def fn(en):
            a1 = s1.ap if isinstance(s1, V) else s1
            a2 = s2.ap if isinstance(s2, V) else s2
            kw = {}
            if op1 is not None:
                kw["op1"] = op1
            if accum is not None:
                kw["accum_out"] = accum.ap
            return en.tensor_scalar(out.ap, in0.ap, a1, a2, op0, **kw)
        self.op(e, fn, outs, ins)

    def stt(self, out, in0, scalar, in1, op0, op1):
        ins = [in0, in1] + ([scalar] if isinstance(scalar, V) else [])

        def fn(en):
            sc = scalar.ap if isinstance(scalar, V) else scalar
            return en.scalar_tensor_tensor(out.ap, in0.ap, sc, in1.ap, op0, op1)
        self.op("dve", fn, [out], ins)

    def copy(self, e, out, in_):
        if e == "act":
            self.op("act", lambda en: en.copy(out.ap, in_.ap), [out], [in_])
        else:
            self.op(e, lambda en: en.tensor_copy(out.ap, in_.ap), [out], [in_])

    def memset(self, e, out, val):
        self.op(e, lambda en: en.memset(out.ap, val), [out], [])

    def reduce(self, out, in_, op, axis=AX.X):
        self.op("dve", lambda en: en.tensor_reduce(out.ap, in_.ap, axis, op), [out], [in_])

    def max8(self, out, in_):
        self.op("dve", lambda en: en.max(out.ap, in_.ap), [out], [in_])

    def match_replace(self, out, to_replace, values, imm):
        self.op("dve", lambda en: en.match_replace(out.ap, to_replace.ap, values.ap, imm), [out], [to_replace, values])

    def recip(self, out, in_):
        self.op("dve", lambda en: en.reciprocal(out.ap, in_.ap), [out], [in_])


class Ring:
    def __init__(self, S, n, shape, dtype, name, psum=False):
        self.t = [(S.ps if psum else S.sb)(shape, dtype, name) for _ in range(n)]
        self.i = 0

    def next(self):
        t = self.t[self.i % len(self.t)]
        self.i += 1
        return t

import ml_dtypes

D_MODEL = 1024
IN_W = 2452
OFF_XBC = 512
OFF_DT = 1536
OFF_POOL = 1544
OFF_Q = 1800
OFF_KV = 2056
NEG = -30000.0
RMS_EPS = 1e-6
NCORES = 8


def make_consts(cfg):
    T = cfg["T"]
    NT = T // 128
    NS = cfg["NS"]
    LS = NS * 8
    c = {}
    c["c_identb"] = np.eye(128).astype(ml_dtypes.bfloat16)
    c["c_identf"] = np.eye(128, dtype=np.float32)
    k = np.arange(128)
    c["c_incl"] = (k[:, None] <= k[None, :]).astype(np.float32)
    c["c_after"] = (k[:, None] > k[None, :]).astype(np.float32)
    c["c_diag"] = np.where(k[None, :] <= k[:, None], 0.0, NEG).astype(np.float32)
    c["c_far"] = np.where(k[None, :] > k[:, None], 0.0, NEG).astype(np.float32)
    half = 8
    inv = 1.0 / (500000.0 ** (np.arange(half, dtype=np.float32) / half))
    pos = np.arange(T, dtype=np.float32)
    ang = pos[:, None] * inv[None, :]
    cs = np.concatenate([np.cos(ang), np.sin(ang)], axis=1).astype(np.float32)
    c["c_cs_p"] = np.ascontiguousarray(cs.reshape(NT, 128, 16).transpose(1, 0, 2))
    r = np.arange(8)
    c["c_cmpdiag"] = np.where(k[:, None] >= 16 * (r[None, :] - 1) + 31, 0.0, NEG).astype(np.float32)
    n_slc = T // 64
    addc = np.zeros((128, NT, max(n_slc, 8)), np.float32)
    for ti in range(NT):
        tpos = 128 * ti + k
        cur = tpos // 64
        j = np.arange(n_slc)
        forced = (j[None, :] == 0) | (j[None, :] == cur[:, None]) | (j[None, :] == cur[:, None] - 1)
        causal = (j[None, :] * 64 <= tpos[:, None])
        a = np.where(forced, 1e30, 0.0)
        a = np.where(causal, a, -1e30)
        addc[:, ti, :n_slc] = a
    c["c_addc_p"] = addc
    n_cmp = T // 16 - 1
    nn = np.arange(128)
    jj = np.arange(max(n_slc, 8))
    ov = ((16 * nn[:, None] < 64 * jj[None, :] + 64) & (16 * nn[:, None] + 32 > 64 * jj[None, :]))
    ov = ov & (nn[:, None] < n_cmp)
    c["c_ov_p"] = ov.astype(ml_dtypes.bfloat16)
    rc = np.zeros((64, 4, 16), np.float32)
    for g, w in enumerate((2, 4, 8, 16)):
        rc[:, g, :] = 1.0 / np.minimum(np.arange(16) + 1, w)
    c["c_rc"] = rc
    c["c_ones"] = np.ones((128, 128), np.float32)
    if NS > 0:
        NPG = cfg["NPG"]
        past = NPG * 128
        WB = min(512, past)
        r = np.arange(LS)
        sq = r // 8
        same = sq[:, None] == sq[None, :]
        c["c_incl_s"] = ((r[:, None] <= r[None, :]) & same).astype(np.float32)
        c["c_after_s"] = ((r[:, None] > r[None, :]) & same).astype(np.float32)
        sm_ = np.zeros((LS, NS, 128), np.float32)
        rm_ = np.zeros((LS, NS), np.float32)
        for b in range(NS):
            sm_[b * 8:(b + 1) * 8, b, :] = 1.0
            rm_[b * 8:(b + 1) * 8, b] = 1.0
        c["c_seqmask_s"] = sm_
        c["c_rowmask_s"] = rm_
        pos_s = (past + (r % 8)).astype(np.float32)
        ang_s = pos_s[:, None] * inv[None, :]
        c["c_cs_s"] = np.concatenate([np.cos(ang_s), np.sin(ang_s)], axis=1).astype(np.float32)
        ri = np.arange(32) % 8
        c["c_R"] = (ri[:, None] == ri[None, :]).astype(np.float32)
        NSLC_S = past // 64 + 1
        NV = past // 16 - 1
        cur = past // 64
        j = np.arange(NSLC_S)
        forced = (j == 0) | (j == cur) | (j == cur - 1)
        c["c_addc_s"] = np.broadcast_to(np.where(forced, 1e30, 0.0).astype(np.float32)[None, :], (32, NSLC_S)).copy()
        NTC = (NV + 127) // 128
        nn2 = np.arange(NTC * 128)
        ov2 = ((16 * nn2[:, None] < 64 * j[None, :] + 64) & (16 * nn2[:, None] + 32 > 64 * j[None, :])) & (nn2[:, None] < NV)
        c["c_ov_s"] = np.ascontiguousarray(ov2.reshape(NTC, 128, NSLC_S).transpose(1, 0, 2)).astype(ml_dtypes.bfloat16)
        cc = np.arange(64)
        c["c_tokmask_s"] = np.where(cc[None, :] <= ri[:, None], 0.0, NEG).astype(np.float32)
        jw = np.arange(WB + 8)
        stored = jw[None, :] < WB
        valid = np.where(stored, jw[None, :] > ri[:, None] + WB - 512, (jw[None, :] - WB) <= ri[:, None])
        c["c_winmask_s"] = np.where(valid, 0.0, NEG).astype(np.float32)
        c["c_pidx"] = np.broadcast_to(np.arange(128, dtype=np.float32)[:, None], (128, NS * NPG)).copy()
    return c


class Obj:
    pass


def build(cfg):
    T = cfg["T"]
    NT = T // 128
    DEPTH = cfg["DEPTH"]
    NS = cfg["NS"]
    WK = min(512, T)
    NCMP = T // 16 - 1
    NSLC = T // 64
    consts = make_consts(cfg)
    STAGE = cfg.get("stage", 99)
    VAR = cfg.get("var", 0)
    SST = cfg.get("sst", 99)

    nc = bass.Bass("TRN2", target_bir_lowering=False)
    es = contextlib.ExitStack()
    outs = []
    with es:
        S = Sched(nc, es, same_sync=cfg.get("same_sync", True))

        def din(name, shape, dt=F32):
            return V(nc.dram_tensor(name, list(shape), dt, kind="ExternalInput").ap())

        def dout(name, shape, dt=F32):
            outs.append(name)
            return V(nc.dram_tensor(name, list(shape), dt, kind="ExternalOutput").ap())

        def dscr(name, shape, dt=F32):
            return nc.dram_tensor(name, list(shape), dt, kind="Internal").ap()

        x_p = din("x_p", [T, D_MODEL])
        W = {}
        for nm, shp in [("norm_mix_pre", [DEPTH, 1024]), ("w_in", [DEPTH, 1024, IN_W]), ("ssd_conv_w", [DEPTH, 4, 1024]),
                        ("ssd_conv_b", [DEPTH, 1024]), ("ssd_dt_bias", [DEPTH, 8]), ("ssd_a_log", [DEPTH, 8]),
                        ("ssd_d", [DEPTH, 8]), ("ssd_norm", [DEPTH, 512]), ("pool_w", [DEPTH, 4, 64, 64]),
                        ("pool_scale", [DEPTH, 256]), ("nsa_pe_k", [DEPTH, 32, 64]), ("nsa_pe_v", [DEPTH, 32, 64]),
                        ("nsa_w1_k", [DEPTH, 32, 64, 128]), ("nsa_w1_v", [DEPTH, 32, 64, 128]),
                        ("nsa_w2_k", [DEPTH, 128, 64]), ("nsa_w2_v", [DEPTH, 128, 64]), ("w_out", [DEPTH, 1024, 1024]),
                        ("norm_mix_post", [DEPTH, 1024]), ("norm_ffn_pre", [DEPTH, 1024]),
                        ("ffn_w_gate", [DEPTH, 1024, 4096]), ("ffn_w_val", [DEPTH, 1024, 4096]),
                        ("ffn_conv_w", [DEPTH, 3, 4096]), ("ffn_conv_b", [DEPTH, 4096]),
                        ("ffn_w_down", [DEPTH, 4096, 1024]), ("norm_ffn_post", [DEPTH, 1024])]:
            W[nm] = din(nm, shp)
        CD = {}
        for nm, arr in consts.items():
            CD[nm] = din(nm, list(arr.shape), BF16 if arr.dtype == ml_dtypes.bfloat16 else F32)

        y_p = dout("y_p", [T, D_MODEL])
        kv_p = dout("kv_p", [DEPTH, T, 256])
        win_p = dout("win_p", [DEPTH, WK, 128])
        conv_p = dout("conv_p", [DEPTH, 3, 1024])
        ssm_p = dout("ssm_p", [DEPTH, 8, 64, 128])
        pool_p = dout("pool_p", [DEPTH, 15, 256])
        ffn_p = dout("ffn_p", [DEPTH, 2, 4096])

        xcur_ap = dscr("xcur", [T, D_MODEL])
        xcur_res = [Res("xcur%d" % i) for i in range(NT)]

        def xcur_tile(i):
            return V(xcur_ap[i * 128:(i + 1) * 128, :], (xcur_res[i],))

        def cload(name, shape, dt=F32, src=None, q="sp"):
            t = S.sb(shape, dt, name)
            S.dma(q, t, src if src is not None else CD[name])
            return t

        identb = cload("c_identb", [128, 128], BF16)
        identf = cload("c_identf", [128, 128])
        incl = cload("c_incl", [128, 128])
        after = cload("c_after", [128, 128])
        diagm = cload("c_diag", [128, 128])
        farm = cload("c_far", [128, 128])
        cs_p = cload("c_cs_p", [128, NT, 16])
        cmpdiag = cload("c_cmpdiag", [128, 8])
        addc_p = cload("c_addc_p", [128, NT, max(NSLC, 8)])
        ov_p = cload("c_ov_p", [128, max(NSLC, 8)], BF16)
        rc_t = cload("c_rc", [64, 4, 16])
        ones = cload("c_ones", [128, 128])

        psf = Ring(S, 6, [128, 512], F32, "psf", psum=True)
        psb = Ring(S, 2, [128, 1024], BF16, "psb", psum=True)

        win_sb = S.sb([128, 8, IN_W], BF16, "win")
        wout_sb = S.sb([128, 8, 1024], BF16, "wout")
        gcol_pre = S.sb([128, 8], F32, "gpre")
        gcol_ffn = S.sb([128, 8], F32, "gffn")
        gpost_b = S.sb([128, 1024], F32, "gpost")
        gfpost_b = S.sb([128, 1024], F32, "gfpost")
        mixg = S.sb([128, 8], F32, "mixg")
        convw = S.sb([128, 8, 4], F32, "convw")
        convb = S.sb([128, 8], F32, "convb")
        dtb_b = S.sb([128, 8], F32, "dtb")
        a_b = S.sb([128, 8], F32, "ab")
        dsk_b = S.sb([128, 8], F32, "dsk")
        poolw_f = S.sb([64, 4, 64], F32, "poolwf")
        pscale_b = S.sb([64, 256], F32, "pscale")
        poolw_sb = S.sb([64, 4, 64], BF16, "poolw")
        CW = Obj()

        def load_cmp_weights(l):
            CW.peT_k = S.sb([64, 32], BF16, "pek")
            CW.peT_v = S.sb([64, 32], BF16, "pev")
            CW.w1k = S.sb([64, 32, 128], BF16, "w1k")
            CW.w1v = S.sb([64, 32, 128], BF16, "w1v")
            CW.w2k = S.sb([128, 64], BF16, "w2k")
            CW.w2v = S.sb([128, 64], BF16, "w2v")
            CW.cbias_k = S.sb([128, 1], F32, "cbk")
            CW.cbias_v = S.sb([128, 1], F32, "cbv")
            pef = small()
            S.dma("sp", pef[0:64, 0:32], W["nsa_pe_k"][l].rr("l d -> d l"), allow_slow_non_contiguous=True)
            S.copy("dve", CW.peT_k, pef[0:64, 0:32])
            pef = small()
            S.dma("sp", pef[0:64, 0:32], W["nsa_pe_v"][l].rr("l d -> d l"), allow_slow_non_contiguous=True)
            S.copy("dve", CW.peT_v, pef[0:64, 0:32])
            S.dma("pool", CW.w1k, W["nsa_w1_k"][l].rr("l d e -> d l e"))
            S.dma("pool", CW.w1v, W["nsa_w1_v"][l].rr("l d e -> d l e"))
            S.dma("pool", CW.w2k, W["nsa_w2_k"][l])
            S.dma("pool", CW.w2v, W["nsa_w2_v"][l])
            for (w1, peT, cb) in ((CW.w1k, CW.peT_k, CW.cbias_k), (CW.w1v, CW.peT_v, CW.cbias_v)):
                p = psf.next()
                for li in range(32):
                    S.mm(p[:, 0:1], w1[:, li, :], peT[:, li:li + 1], start=(li == 0), stop=(li == 31))
                S.copy("act", cb, p[:, 0:1])
        fconvw = S.sb([128, 32, 3], F32, "fconvw")
        fconvb = S.sb([128, 32], F32, "fconvb")

        def load_layer_weights(l):
            S.dma("sp", win_sb, winpre)
            S.dma("act", wout_sb, woutpre)
            S.dma("sp", gcol_pre, W["norm_mix_pre"][l].rr("(k p) -> p k", p=128), allow_slow_non_contiguous=True)
            S.dma("sp", gcol_ffn, W["norm_ffn_pre"][l].rr("(k p) -> p k", p=128), allow_slow_non_contiguous=True)
            S.dma("sp", gpost_b, V(W["norm_mix_post"].ap[l].partition_broadcast(128)))
            S.dma("sp", gfpost_b, V(W["norm_ffn_post"].ap[l].partition_broadcast(128)))
            S.memset("pool", mixg, 1.0)
            S.dma("sp", mixg[:, 0:4], W["ssd_norm"][l].rr("(k p) -> p k", p=128), allow_slow_non_contiguous=True)
            for t_ in range(4):
                S.dma("sp", convw[:, :, t_], W["ssd_conv_w"][l, t_].rr("(c p) -> p c", p=128), allow_slow_non_contiguous=True)
            S.dma("sp", convb, W["ssd_conv_b"][l].rr("(c p) -> p c", p=128), allow_slow_non_contiguous=True)
            S.dma("sp", dtb_b, V(W["ssd_dt_bias"].ap[l].partition_broadcast(128)))
            S.dma("sp", a_b, V(W["ssd_a_log"].ap[l].partition_broadcast(128)))
            S.act(a_b, a_b, AF.Exp)
            S.ts("dve", a_b, a_b, -1.0, None, ALU.mult)
            S.dma("sp", dsk_b, V(W["ssd_d"].ap[l].partition_broadcast(128)))
            S.dma("sp", poolw_f, W["pool_w"][l].rr("g c d -> c g d"))
            S.dma("sp", pscale_b, V(W["pool_scale"].ap[l].partition_broadcast(64)))
            S.tt("dve", poolw_sb, poolw_f, pscale_b.rr("p (g d) -> p g d", g=4), ALU.mult)
            for t_ in range(3):
                S.dma("sp", fconvw[:, :, t_], W["ffn_conv_w"][l, t_].rr("(c p) -> p c", p=128), allow_slow_non_contiguous=True)
            S.dma("sp", fconvb, W["ffn_conv_b"][l].rr("(c p) -> p c", p=128), allow_slow_non_contiguous=True)

        r_sm = Ring(S, 24, [128, 32], F32, "sm")
        NTB = min(4, NT)
        xmid_ap = dscr("xmid", [T, D_MODEL])
        xmid_res = [Res("xmid%d" % i) for i in range(NT)]

        def xmid_tile(i):
            return V(xmid_ap[i * 128:(i + 1) * 128, :], (xmid_res[i],))

        wg_s_ap = dscr("wg_s", [16, 128, 8, 256], BF16)
        wv_s_ap = dscr("wv_s", [16, 128, 8, 256], BF16)
        wd_s_ap = dscr("wd_s", [16, 128, 2, 1024], BF16)
        wg_s = [V(wg_s_ap[i], (Res("wgs%d" % i),)) for i in range(16)]
        wv_s = [V(wv_s_ap[i], (Res("wvs%d" % i),)) for i in range(16)]
        wd_s = [V(wd_s_ap[i], (Res("wds%d" % i),)) for i in range(16)]

        win_s_ap = dscr("winpre_s", [128, 8, IN_W], BF16)
        wout_s_ap = dscr("woutpre_s", [128, 8, 1024], BF16)
        winpre = V(win_s_ap, (Res("winpre"),))
        woutpre = V(wout_s_ap, (Res("woutpre"),))

        def precast_in(l):
            S.dma("pool", winpre, W["w_in"][l].rr("(k p) n -> p k n", p=128))
            S.dma("pool", woutpre, W["w_out"][l].rr("(k p) n -> p k n", p=128))

        def precast_ffn(l):
            for s_ in range(16):
                S.dma("pool", wg_s[s_], W["ffn_w_gate"][l, :, s_ * 256:(s_ + 1) * 256].rr("(k p) n -> p k n", p=128))
                S.dma("pool", wv_s[s_], W["ffn_w_val"][l, :, s_ * 256:(s_ + 1) * 256].rr("(k p) n -> p k n", p=128))
            for s_ in range(16):
                S.dma("pool", wd_s[s_], W["ffn_w_down"][l, s_ * 256:(s_ + 1) * 256, :].rr("(c p) n -> p c n", p=128))

        R = Obj()
        F = Obj()
        PP = Obj()

        def alloc_prompt_persist():
            PP.kcT = S.sb([64, T], BF16, "kcT")
            PP.vcT = S.sb([64, T], BF16, "vcT")
            PP.ksT = S.sb([64, T], BF16, "ksT")
            PP.kwT = S.sb([64, T], BF16, "kwT")
            PP.vs_tok = S.sb([128, NT, 64], BF16, "vstok")
            PP.vw_tok = S.sb([128, NT, 64], BF16, "vwtok")
            PP.geluT_k = S.sb([128, 128], BF16, "gelk")
            PP.geluT_v = S.sb([128, 128], BF16, "gelv")
            PP.ckT = S.sb([64, 128], BF16, "ckT")
            PP.cv = S.sb([128, 64], BF16, "cv")
            PP.ST = S.sb([128, 512], F32, "ST")
            PP.STbf = S.sb([128, 512], BF16, "STbf")
            PP.cprev = S.sb([128, 8, 3], F32, "cprev")
            PP.pprev = S.sb([64, 4, 15], F32, "pprev")
            PP.carry = S.sb([128, 32, 2], F32, "carry")

        def alloc_mixer_rings():
            R.x = Ring(S, 1, [128, 1024], F32, "xt")
            R.junk = Ring(S, 1, [128, 1024], BF16, "junk")
            R.xn = Ring(S, 1, [128, 1024], BF16, "xn")
            R.hT = Ring(S, 1, [128, 8, 128], BF16, "hT")
            R.ext = Ring(S, 1, [128, 8, 1, 131], F32, "ext")
            R.acc = Ring(S, 1, [128, 8, 128], F32, "cacc")
            R.actbf = Ring(S, 1, [128, 8, 128], BF16, "actbf")
            R.xsB = Ring(S, 1, [128, 768], BF16, "xsB")
            R.xc = Ring(S, 2, [128, 512], BF16, "xc")
            R.rhsD = Ring(S, 1, [128, 4, 128], F32, "rhsD")
            R.Dexp = Ring(S, 1, [128, 4, 128], F32, "Dexp")
            R.scm = Ring(S, 1, [128, 2, 128], F32, "scm")
            R.MT = Ring(S, 1, [128, 8, 128], BF16, "MT")
            R.y = Ring(S, 1, [128, 512], F32, "y")
            R.t512 = Ring(S, 3, [128, 512], F32, "t512")
            R.pext = Ring(S, 1, [64, 4, 1, 143], F32, "pext")
            R.ps2 = Ring(S, 1, [64, 4, 1, 142], F32, "ps2")
            R.ps4 = Ring(S, 1, [64, 3, 1, 140], F32, "ps4")
            R.ps8 = Ring(S, 1, [64, 2, 1, 136], F32, "ps8")
            R.ps16 = Ring(S, 1, [64, 1, 1, 128], F32, "ps16")
            R.pooled = Ring(S, 1, [64, 4, 128], BF16, "pooled")
            R.mix = Ring(S, 1, [128, 1024], BF16, "mix")
            R.mixT = Ring(S, 1, [128, 8, 128], BF16, "mixT")
            R.qf = Ring(S, 1, [128, 256], F32, "qf")
            R.rows = Ring(S, 1, [128, 384], F32, "rows")
            R.rt = Ring(S, 1, [128, 4, 4, 8], F32, "ropet")
            R.qbf = Ring(S, 1, [128, 256], BF16, "qbf")
            R.kvbf = Ring(S, 1, [128, 384], BF16, "kvbf")
            R.qT = Ring(S, 1, [64, 4, 128], BF16, "qT")
            R.gates = Ring(S, 2, [128, 12], F32, "gates")
            R.ssb = Ring(S, 1, [128, max(T, 1280)], F32, "ssb")
            R.ebf = Ring(S, 1, [128, max(T, 512)], BF16, "ebf")
            R.pT = Ring(S, 1, [128, max(NT, 5), 128], BF16, "pT")
            R.e32 = Ring(S, 2, [128, 128], F32, "e32")
            R.pn = Ring(S, 2, [128, 128], BF16, "pn")
            R.pnT = Ring(S, 2, [128, 128], BF16, "pnT")
            R.oacc = Ring(S, 1, [128, 256], F32, "oacc")
            R.imp = Ring(S, 3, [128, max(NSLC, 8)], F32, "imp")
            R.selm = Ring(S, 1, [128, max(NSLC, 8)], F32, "selm")
            R.xo = Ring(S, 1, [128, 1024], F32, "xo")

        def alloc_ffn_bufs(Ltot):
            F.h2T = S.sb([128, 8, Ltot], BF16, "h2T")
            F.actT = S.sb([128, 32, Ltot], BF16, "actT")
            F.wg = Ring(S, 2, [128, 8, 256], BF16, "wg")
            F.wv = Ring(S, 2, [128, 8, 256], BF16, "wv")
            F.wd = Ring(S, 2, [128, 2, 1024], BF16, "wd")
            F.gext = Ring(S, 2, [128, Ltot + 2], F32, "gext")
            F.gacc = Ring(S, 2, [128, Ltot], F32, "gacc")
            F.f_sb = S.sb([128, (Ltot + 127) // 128, 1024], F32, "fsb")
            F.xin = Ring(S, 1, [128, 1024], F32, "xin")
            F.junk = Ring(S, 1, [128, 1024], BF16, "fjunk")
            F.xn = Ring(S, 1, [128, 1024], BF16, "fxn")
            F.gts = Ring(S, 1, [128, 256], F32, "gts")

        def small():
            return r_sm.next()

        DBG = cfg.get("debug", False)

        def dbg(name, v, dt=F32):
            if not DBG:
                return
            shp = list(v.shape)
            o = dout("dbg_" + name, shp, dt)
            S.dma("sp", o, v)

        def rmsnorm_T(src, L, gcol, hT_dst, D=1024, B=None):
            B = B or R
            junk = B.junk.next()
            ss = small()
            S.act(junk[:L, :D], src, AF.Square, accum=ss[:L, 0:1])
            S.ts("dve", ss[:L, 1:2], ss[:L, 0:1], 1.0 / D, RMS_EPS, ALU.mult, ALU.add)
            S.act(ss[:L, 1:2], ss[:L, 1:2], AF.Sqrt)
            S.recip(ss[:L, 2:3], ss[:L, 1:2])
            xn = B.xn.next()
            S.act(xn[:L, :D], src, AF.Copy, scale=ss[:L, 2:3])
            nk = D // 128
            pT = psb.next()
            pTv = pT.rr("p (k l) -> p k l", k=8)
            for k in range(nk):
                S.tr(pTv[:, k, :L], xn[:L, k * 128:(k + 1) * 128], identb[:L, :L])
            S.tt("dve", hT_dst, pTv[:, 0:nk, :L], gcol.us(2).bc([128, nk, L]), ALU.mult)

        def rope(vw, H, cs, L):
            rt = R.rt.next()
            cosb = cs[:, 0:8].us(1).bc([L, H, 8])
            sinb = cs[:, 8:16].us(1).bc([L, H, 8])
            x1 = vw[:, :, 0:8]
            x2 = vw[:, :, 8:16]
            S.tt("pool", rt[:L, 0, 0:H, :], x1, cosb, ALU.mult)
            S.tt("pool", rt[:L, 1, 0:H, :], x2, sinb, ALU.mult)
            S.tt("pool", rt[:L, 2, 0:H, :], x2, cosb, ALU.mult)
            S.tt("pool", rt[:L, 3, 0:H, :], x1, sinb, ALU.mult)
            S.tt("pool", x1, rt[:L, 0, 0:H, :], rt[:L, 1, 0:H, :], ALU.subtract)
            S.tt("pool", x2, rt[:L, 2, 0:H, :], rt[:L, 3, 0:H, :], ALU.add)

        def softmax_rows(s_sb, L, nk, e_out, clamp=False):
            sm = small()
            S.reduce(sm[:L, 0:1], s_sb, ALU.max)
            if clamp:
                S.ts("dve", sm[:L, 1:2], sm[:L, 0:1], -1e4, -1.0, ALU.max, ALU.mult)
            else:
                S.ts("dve", sm[:L, 1:2], sm[:L, 0:1], -1.0, None, ALU.mult)
            S.act(e_out, s_sb, AF.Exp, bias=sm[:L, 1:2], accum=sm[:L, 2:3])
            return sm


        def mixer_tile_prompt(l, ti, xsrc):
            L = 128
            last_tile = (ti == NT - 1)
            if STAGE < 0:
                return
            xt = R.x.next()
            S.dma("sp", xt, xsrc)
            hT = R.hT.next()
            rmsnorm_T(xt, L, gcol_pre, hT)

            if STAGE < 1:
                return
            ext = R.ext.next()
            for half in range(2):
                p = psf.next()
                pv = p.rr("p (c l) -> p c l", c=4)
                for c4 in range(4):
                    c = half * 4 + c4
                    for k in range(8):
                        S.mm(pv[:, c4, :], win_sb[:, k, OFF_XBC + c * 128: OFF_XBC + (c + 1) * 128], hT[:, k, :],
                             start=(k == 0), stop=(k == 7))
                S.copy("act", ext[:, half * 4:(half + 1) * 4, 0, 3:131], pv)
            pext = R.pext.next()
            p = psf.next()
            pv = p.rr("p (c l) -> p c l", c=4)
            for g in range(4):
                for k in range(8):
                    S.mm(pv[0:64, g, :], win_sb[:, k, OFF_POOL + g * 64: OFF_POOL + (g + 1) * 64], hT[:, k, :],
                         start=(k == 0), stop=(k == 7))
            S.copy("act", pext[:, :, 0, 15:143], pv[0:64])
            z_ps = psf.next()
            for k in range(8):
                S.mm(z_ps, hT[:, k, :], win_sb[:, k, 0:512], start=(k == 0), stop=(k == 7))
            zs = R.t512.next()
            S.act(zs, z_ps, AF.Silu)
            q_ps = psf.next()
            for k in range(8):
                S.mm(q_ps[:, 0:256], hT[:, k, :], win_sb[:, k, OFF_Q:OFF_Q + 256], start=(k == 0), stop=(k == 7))
            kv_ps = psf.next()
            for k in range(8):
                S.mm(kv_ps[:, 0:396], hT[:, k, :], win_sb[:, k, OFF_KV:OFF_KV + 396], start=(k == 0), stop=(k == 7))
            if last_tile:
                tl = R.ssb.next()
                for nh in range(2):
                    p = psf.next()
                    for k in range(8):
                        S.mm(p, hT[:, k, :], win_sb[:, k, OFF_XBC + nh * 512: OFF_XBC + (nh + 1) * 512],
                             start=(k == 0), stop=(k == 7))
                    S.copy("act", tl[:, nh * 512:(nh + 1) * 512], p)
                p = psf.next()
                for k in range(8):
                    S.mm(p[:, 0:256], hT[:, k, :], win_sb[:, k, OFF_POOL:OFF_POOL + 256], start=(k == 0), stop=(k == 7))
                S.copy("act", tl[:, 1024:1280], p[:, 0:256])
                S.dma("sp", conv_p[l], tl[125:128, 0:1024])
                S.dma("sp", pool_p[l], tl[113:128, 1024:1280])

            if STAGE < 2:
                return
            qf = R.qf.next()
            S.act(qf, q_ps[:, 0:256], AF.Copy, scale=0.125)
            rows = R.rows.next()
            S.copy("act", rows, kv_ps[:, 0:384])
            gates = R.gates.next()
            S.act(gates, kv_ps[:, 384:396], AF.Sigmoid)
            if STAGE < 2.2:
                return
            cs = cs_p[:, ti, :]
            rope(qf.rr("p (h d) -> p h d", h=4), 4, cs, L)
            rope(rows.rr("p (j two d) -> p j two d", j=3, two=2)[:, :, 0, :], 3, cs, L)
            if STAGE < 2.4:
                return
            S.dma("sp", kv_p[l, ti * 128:(ti + 1) * 128, :], rows[:, 0:256])
            if (ti + 1) * 128 > T - WK:
                o0 = ti * 128 - (T - WK)
                S.dma("sp", win_p[l, o0:o0 + 128, :], rows[:, 256:384])
            qbf = R.qbf.next()
            S.copy("pool", qbf, qf)
            kvbf = R.kvbf.next()
            S.copy("pool", kvbf, rows)
            if STAGE < 2.6:
                return
            pT = psb.next()
            pTv = pT.rr("p (k l) -> p k l", k=8)
            for h in range(4):
                S.tr(pTv[0:64, h, :], qbf[:, h * 64:(h + 1) * 64], identb)
            for j, c0 in enumerate((0, 64, 128, 256)):
                S.tr(pTv[0:64, 4 + j, :], kvbf[:, c0:c0 + 64], identb)
            if STAGE < 2.8:
                return
            qT = R.qT.next()
            S.copy("dve", qT, pTv[0:64, 0:4, :])
            if STAGE < 2.85:
                return
            tsl = slice(ti * 128, (ti + 1) * 128)
            S.copy("dve", PP.kcT[:, tsl], pTv[0:64, 4, :])
            S.copy("dve", PP.vcT[:, tsl], pTv[0:64, 5, :])
            if STAGE < 2.9:
                return
            S.copy("dve", PP.ksT[:, tsl], pTv[0:64, 6, :])
            S.copy("dve", PP.kwT[:, tsl], pTv[0:64, 7, :])
            if STAGE < 2.95:
                return
            if VAR == 1:
                S.memset("dve", PP.vs_tok[:, ti, :], 0.0)
            elif VAR == 2:
                S.copy("dve", PP.vs_tok[:, ti, :], kvbf[:, 192:256])
            elif VAR == 3:
                S.copy("dve", PP.vs_tok[:, ti, :], kvbf[:, 128:192])
            elif VAR == 4:
                S.copy("dve", R.mix.next()[:, 0:64], kvbf[:, 192:256])
            elif VAR == 6:
                S.memset("dve", small()[:, 0:8], 0.0)
            elif VAR == 7:
                S.copy("dve", R.mix.next()[:, 0:128], kvbf[:, 128:256])
            elif VAR == 8:
                pass
            elif VAR == 5:
                S.copy("dve", PP.vs_tok[:, ti, :], qbf[:, 192:256])
            else:
                S.copy("dve", PP.vs_tok[:, ti, :], kvbf[:, 192:256])
                S.copy("dve", PP.vw_tok[:, ti, :], kvbf[:, 320:384])

            if STAGE < 3:
                return
            if ti == 0:
                S.memset("pool", ext[:, :, 0, 0:3], 0.0)
            else:
                S.copy("pool", ext[:, :, 0, 0:3], PP.cprev)
            S.copy("pool", PP.cprev, ext[:, :, 0, 128:131])
            acc = R.acc.next()
            for c in range(8):
                S.ts("dve", acc[:, c, :], ext[:, c, 0, 0:128], convw[:, c, 0:1], convb[:, c:c + 1], ALU.mult, ALU.add)
                for k in range(1, 4):
                    S.stt(acc[:, c, :], ext[:, c, 0, k:k + 128], convw[:, c, k:k + 1], acc[:, c, :], ALU.mult, ALU.add)
            abf = R.actbf.next()
            S.act(abf, acc, AF.Silu)
            pT = psb.next()
            pTv = pT.rr("p (k l) -> p k l", k=8)
            for c in range(6):
                S.tr(pTv[:, c, :], abf[:, c, :], identb)
            xsB = R.xsB.next()
            S.copy("dve", xsB, pTv[:, 0:6, :].rr("p k l -> p (k l)"))
            xs3 = xsB[:, 0:512].rr("p (h d) -> p h d", h=8)

            misc_ps = psf.next()
            for k in range(8):
                S.mm(misc_ps[:, 0:8], hT[:, k, :], win_sb[:, k, OFF_DT:OFF_DT + 8], start=(k == 0), stop=(k == 7))
            sm = small()
            dtp = sm[:, 0:8]
            S.tt("dve", dtp, misc_ps[:, 0:8], dtb_b, ALU.add)
            sm2 = small()
            S.ts("dve", sm2[:, 8:16], dtp, -1.0, None, ALU.mult)
            S.tt("dve", sm2[:, 0:8], dtp, sm2[:, 8:16], ALU.max)
            S.act(sm2[:, 0:8], sm2[:, 0:8], AF.Exp, scale=-1.0)
            S.act(sm2[:, 0:8], sm2[:, 0:8], AF.Ln, bias=1.0)
            S.ts("dve", sm2[:, 8:16], dtp, 0.0, None, ALU.max)
            dt = sm[:, 8:16]
            S.tt("dve", dt, sm2[:, 0:8], sm2[:, 8:16], ALU.add)
            sm3 = small()
            adt = sm3[:, 0:8]
            S.tt("dve", adt, dt, a_b, ALU.mult)
            xc = R.xc.next()
            S.tt("pool", xc.rr("p (h d) -> p h d", h=8), xs3, dt.us(2).bc([128, 8, 64]), ALU.mult)
            if ti == 0:
                dbg("dt", dt); dbg("adt", adt); dbg("xsB", xsB, BF16); dbg("xc", xc, BF16); dbg("acc", acc)
            S.mm(misc_ps[:, 8:16], incl, adt)
            S.mm(misc_ps[:, 16:24], after, adt)
            S.mm(misc_ps[:, 24:32], ones, adt)
            sm4 = small()
            S.act(sm4[:, 0:8], misc_ps[:, 8:16], AF.Exp)
            S.act(sm4[:, 8:16], misc_ps[:, 16:24], AF.Exp)
            S.act(sm3[:, 8:16], misc_ps[:, 24:32], AF.Exp)
            eacum = sm4[:, 0:8]
            dte = sm4[:, 8:16]
            elast = sm3[:, 8:16]
            sc_ps = psf.next()
            scv = sc_ps[:, 0:256].rr("p (g l) -> p g l", g=2)
            for g in range(2):
                S.mm(scv[:, g, :], abf[:, 4 + g, :], abf[:, 6 + g, :])
            scm = R.scm.next()
            S.tt("dve", scm, scv, incl.us(1).bc([128, 2, 128]), ALU.mult)
            MT = R.MT.next()
            for g in range(2):
                rhsD = R.rhsD.next()
                S.tt("pool", rhsD, incl.us(1).bc([128, 4, 128]), adt[:, g * 4:(g + 1) * 4].us(2).bc([128, 4, 128]), ALU.mult)
                p = psf.next()
                S.mm(p, after, rhsD.rr("p h l -> p (h l)"))
                Dexp = R.Dexp.next()
                S.act(Dexp.rr("p h l -> p (h l)"), p, AF.Exp)
                S.tt("pool" if g == 0 else "dve", MT[:, g * 4:(g + 1) * 4, :], Dexp,
                     scm[:, g, :].us(1).bc([128, 4, 128]), ALU.mult)
            if ti == 0:
                dbg("eacum", eacum); dbg("dte", dte); dbg("elast", elast); dbg("MT", MT, BF16); dbg("scm", scm)
            Y_ps = psf.next()
            for h in range(8):
                S.mm(Y_ps[:, h * 64:(h + 1) * 64], MT[:, h, :], xc[:, h * 64:(h + 1) * 64])
            y = R.y.next()
            t1 = R.t512.next()
            S.tt("pool", t1.rr("p (h d) -> p h d", h=8), xs3, dsk_b.us(2).bc([128, 8, 64]), ALU.mult)
            S.tt("dve", y, Y_ps, t1, ALU.add)
            if ti > 0:
                Yo_ps = psf.next()
                for g in range(2):
                    S.mm(Yo_ps[:, g * 256:(g + 1) * 256], abf[:, 6 + g, :], PP.STbf[:, g * 256:(g + 1) * 256])
                t2 = R.t512.next()
                S.tt("dve", t2.rr("p (h d) -> p h d", h=8), Yo_ps.rr("p (h d) -> p h d", h=8),
                     eacum.us(2).bc([128, 8, 64]), ALU.mult)
                S.tt("pool", y, y, t2, ALU.add)
            S.tt("dve", y, y, zs, ALU.mult)
            xcd = R.xc.next()
            S.tt("pool", xcd.rr("p (h d) -> p h d", h=8), xc.rr("p (h d) -> p h d", h=8),
                 dte.us(2).bc([128, 8, 64]), ALU.mult)
            Sn_ps = psf.next()
            for g in range(2):
                S.mm(Sn_ps[:, g * 256:(g + 1) * 256], xsB[:, 512 + g * 128: 512 + (g + 1) * 128], xcd[:, g * 256:(g + 1) * 256])
            if ti == 0:
                S.copy("act", PP.ST, Sn_ps)
            else:
                S.tt("pool", PP.ST.rr("p (h d) -> p h d", h=8), PP.ST.rr("p (h d) -> p h d", h=8),
                     elast.us(2).bc([128, 8, 64]), ALU.mult)
                S.tt("dve", PP.ST, PP.ST, Sn_ps, ALU.add)
            S.copy("pool", PP.STbf, PP.ST)
            if ti == 0:
                dbg("y", y); dbg("ST0", PP.ST); dbg("zs", zs)
            mix = R.mix.next()
            junk = R.junk.next()
            ss = small()
            S.act(junk[:, 0:512], y, AF.Square, accum=ss[:, 0:1])
            S.ts("dve", ss[:, 1:2], ss[:, 0:1], 1.0 / 512, RMS_EPS, ALU.mult, ALU.add)
            S.act(ss[:, 1:2], ss[:, 1:2], AF.Sqrt)
            S.recip(ss[:, 2:3], ss[:, 1:2])
            S.act(mix[:, 0:512], y, AF.Copy, scale=ss[:, 2:3])

            if STAGE < 4:
                return
            if ti == 0:
                S.memset("pool", pext[:, :, 0, 0:15], 0.0)
            else:
                S.copy("pool", pext[:, :, 0, 0:15], PP.pprev)
            S.copy("pool", PP.pprev, pext[:, :, 0, 128:143])
            s2 = R.ps2.next()
            s4 = R.ps4.next()
            s8 = R.ps8.next()
            s16 = R.ps16.next()
            S.tt("pool", s2, pext[:, :, :, 1:143], pext[:, :, :, 0:142], ALU.add)
            S.tt("pool", s4, s2[:, 1:4, :, 2:142], s2[:, 1:4, :, 0:140], ALU.add)
            S.tt("pool", s8, s4[:, 1:3, :, 4:140], s4[:, 1:3, :, 0:136], ALU.add)
            S.tt("pool", s16, s8[:, 1:2, :, 8:136], s8[:, 1:2, :, 0:128], ALU.add)
            pooled = R.pooled.next()
            srcs = [(s2, 0, 14), (s4, 0, 12), (s8, 0, 8), (s16, 0, 0)]
            for g, (sx, gi, off) in enumerate(srcs):
                S.stt(pooled[:, g, :], sx[:, gi, 0, off:off + 128], 1.0 / (2 ** (g + 1)), pext[:, g, 0, 15:143],
                      ALU.mult, ALU.subtract)
            if ti == 0:
                for g, (sx, gi, off) in enumerate(srcs):
                    tmp = small()
                    S.tt("dve", tmp[0:64, 0:16], sx[:, gi, 0, off:off + 16], rc_t[:, g, :], ALU.mult)
                    S.tt("dve", pooled[:, g, 0:16], tmp[0:64, 0:16], pext[:, g, 0, 15:31], ALU.subtract)
            yp_ps = psf.next()
            for g in range(4):
                S.mm(yp_ps[:, g * 64:(g + 1) * 64], pooled[:, g, :], poolw_sb[:, g, :])
            S.copy("act", mix[:, 512:768], yp_ps[:, 0:256])

            if STAGE < 5:
                return
            n0 = max(0, 8 * ti - 1)
            n1 = min(8 * ti + 6, NCMP - 1)
            nb = n1 - n0 + 1
            ncur = n1 + 1
            if nb > 0:
                for (srcT, w1, cb, gel) in ((PP.kcT, CW.w1k, CW.cbias_k, PP.geluT_k), (PP.vcT, CW.w1v, CW.cbias_v, PP.geluT_v)):
                    p = psf.next()
                    for li in range(32):
                        S.mm(p[:, 0:nb], w1[:, li, :], srcT[:, 16 * n0 + li: 16 * n1 + li + 1: 16],
                             start=(li == 0), stop=(li == 31))
                    S.act(gel[:, n0:n1 + 1], p[:, 0:nb], AF.Gelu_apprx_tanh, bias=cb)
                p = psf.next()
                S.mm(p[0:64, 0:nb], CW.w2k, PP.geluT_k[:, n0:n1 + 1])
                S.copy("act", PP.ckT[:, n0:n1 + 1], p[0:64, 0:nb])
                p = psf.next()
                S.mm(p[0:ncur, 0:64], PP.geluT_v[:, 0:ncur], CW.w2v)
                S.copy("act", PP.cv[0:ncur, :], p[0:ncur, 0:64])

            if STAGE < 6:
                return
            oacc = R.oacc.next()
            use_sel = (ti >= 8)
            nblk = 2 * ti + 2
            impacc = R.imp.next() if use_sel else None
            for h in range(4):
                s_ps = psf.next()
                S.mm(s_ps[:, 0:ncur], qT[:, h, :], PP.ckT[:, 0:ncur])
                ssb = R.ssb.next()
                c0 = max(0, 8 * ti - 1)
                r0 = c0 - (8 * ti - 1)
                if c0 > 0:
                    S.copy("act", ssb[:, 0:c0], s_ps[:, 0:c0])
                S.tt("dve", ssb[:, c0:ncur], s_ps[:, c0:ncur], cmpdiag[:, r0:r0 + (ncur - c0)], ALU.add)
                e32 = R.e32.next()
                sm = softmax_rows(ssb[:, 0:ncur], L, ncur, e32[:, 0:ncur], clamp=True)
                S.ts("dve", sm[:, 3:4], sm[:, 2:3], 1e-30, None, ALU.max)
                S.recip(sm[:, 4:5], sm[:, 3:4])
                pn = R.pn.next()
                S.ts("dve", pn[:, 0:ncur], e32[:, 0:ncur], sm[:, 4:5], None, ALU.mult)
                pT = psb.next()
                S.tr(pT[0:ncur, 0:128], pn[:, 0:ncur], identb)
                pnT = R.pnT.next()
                S.copy("dve", pnT[0:ncur, :], pT[0:ncur, 0:128])
                o_ps = psf.next()
                S.mm(o_ps[:, 0:64], pnT[0:ncur, :], PP.cv[0:ncur, :])
                S.ts("dve", oacc[:, h * 64:(h + 1) * 64], o_ps[:, 0:64], gates[:, 3 * h:3 * h + 1], None, ALU.mult)
                if use_sel:
                    i_ps = psf.next()
                    S.mm(i_ps[:, 0:nblk], pnT[0:ncur, :], ov_p[0:ncur, 0:nblk])
                    if h == 0:
                        S.tt("dve", impacc[:, 0:nblk], i_ps[:, 0:nblk], addc_p[:, ti, 0:nblk], ALU.add)
                    else:
                        S.tt("dve", impacc[:, 0:nblk], impacc[:, 0:nblk], i_ps[:, 0:nblk], ALU.add)
            selm = None
            if use_sel:
                imp = impacc
                m8 = small()
                wk = R.imp.next()
                S.max8(m8[:, 0:8], imp[:, 0:nblk])
                S.match_replace(wk[:, 0:nblk], m8[:, 0:8], imp[:, 0:nblk], -3e38)
                S.max8(m8[:, 8:16], wk[:, 0:nblk])
                selm = R.selm.next()
                S.ts("dve", selm[:, 0:nblk], imp[:, 0:nblk], m8[:, 15:16], NEG, ALU.is_lt, ALU.mult)
            for h in range(4):
                for br in (1, 2):
                    if br == 1:
                        kt0 = 0
                        kT_, v_ = PP.ksT, PP.vs_tok
                    else:
                        kt0 = max(0, ti - 4)
                        kT_, v_ = PP.kwT, PP.vw_tok
                    ntl = ti - kt0 + 1
                    nk = ntl * 128
                    ssb = R.ssb.next()
                    k0 = 0
                    while k0 < nk:
                        w = min(512, nk - k0)
                        s_ps = psf.next()
                        S.mm(s_ps[:, 0:w], qT[:, h, :], kT_[:, kt0 * 128 + k0: kt0 * 128 + k0 + w])
                        if br == 1 and use_sel:
                            S.tt("dve", ssb[:, k0:k0 + w].rr("p (j c) -> p j c", c=64),
                                 s_ps[:, 0:w].rr("p (j c) -> p j c", c=64),
                                 selm[:, k0 // 64:(k0 + w) // 64].us(2).bc([128, w // 64, 64]), ALU.add)
                        else:
                            S.copy("act", ssb[:, k0:k0 + w], s_ps[:, 0:w])
                        k0 += w
                    S.tt("pool", ssb[:, nk - 128:nk], ssb[:, nk - 128:nk], diagm, ALU.add)
                    if br == 2 and ti - 4 >= 0:
                        S.tt("pool", ssb[:, 0:128], ssb[:, 0:128], farm, ALU.add)
                    ebf = R.ebf.next()
                    sm = softmax_rows(ssb[:, 0:nk], L, nk, ebf[:, 0:nk])
                    S.recip(sm[:, 3:4], sm[:, 2:3])
                    S.tt("dve", sm[:, 4:5], sm[:, 3:4], gates[:, 3 * h + br:3 * h + br + 1], ALU.mult)
                    pTs = R.pT.next()
                    for b0 in range(0, ntl, 8):
                        nb8 = min(8, ntl - b0)
                        pT = psb.next()
                        pTv = pT.rr("p (k l) -> p k l", k=8)
                        for j in range(nb8):
                            S.tr(pTv[:, j, :], ebf[:, (b0 + j) * 128:(b0 + j + 1) * 128], identb)
                        S.copy("dve", pTs[:, b0:b0 + nb8, :], pTv[:, 0:nb8, :])
                    o_ps = psf.next()
                    for j in range(ntl):
                        S.mm(o_ps[:, 0:64], pTs[:, j, :], v_[:, kt0 + j, :], start=(j == 0), stop=(j == ntl - 1))
                    S.stt(oacc[:, h * 64:(h + 1) * 64], o_ps[:, 0:64], sm[:, 4:5], oacc[:, h * 64:(h + 1) * 64],
                          ALU.mult, ALU.add)
            S.copy("pool", mix[:, 768:1024], oacc)

            if STAGE < 7:
                return
            pT = psb.next()
            pTv = pT.rr("p (k l) -> p k l", k=8)
            for k in range(8):
                S.tr(pTv[:, k, :], mix[:, k * 128:(k + 1) * 128], identb)
            mixT = R.mixT.next()
            S.tt("dve", mixT, pTv, mixg.us(2).bc([128, 8, 128]), ALU.mult)
            mo = [psf.next(), psf.next()]
            ss = small()
            junk = R.junk.next()
            for nh in range(2):
                for k in range(8):
                    S.mm(mo[nh], mixT[:, k, :], wout_sb[:, k, nh * 512:(nh + 1) * 512], start=(k == 0), stop=(k == 7))
                S.act(junk[:, nh * 512:(nh + 1) * 512], mo[nh], AF.Square, accum=ss[:, nh:nh + 1])
            S.tt("dve", ss[:, 2:3], ss[:, 0:1], ss[:, 1:2], ALU.add)
            S.ts("dve", ss[:, 3:4], ss[:, 2:3], 1.0 / 1024, RMS_EPS, ALU.mult, ALU.add)
            S.act(ss[:, 3:4], ss[:, 3:4], AF.Sqrt)
            S.recip(ss[:, 4:5], ss[:, 3:4])
            xm = R.xo.next()
            for nh in range(2):
                sl = slice(nh * 512, (nh + 1) * 512)
                tmp = R.t512.next()
                S.stt(tmp, mo[nh], ss[:, 4:5], gpost_b[:, sl], ALU.mult, ALU.mult)
                S.tt("pool", xm[:, sl], tmp, xt[:, sl], ALU.add)
            S.dma("sp", xmid_tile(ti), xm)

        def ffn_block_prompt(l, blk, ntb, xdst_fn):
            Ltot = ntb * 128
            first_blk = (blk == 0)
            last_blk = (blk * NTB + ntb == NT)
            h2T, actT, f_sb = F.h2T, F.actT, F.f_sb
            carry = PP.carry
            for j in range(ntb):
                xin = F.xin.next()
                S.dma("sp", xin, xmid_tile(blk * NTB + j))
                rmsnorm_T(xin, 128, gcol_ffn, h2T[:, :, j * 128:(j + 1) * 128], B=F)
            for s in range(16):
                wg = F.wg.next()
                wv = F.wv.next()
                S.dma("sp", wg, wg_s[s])
                S.dma("sp", wv, wv_s[s])
                if last_blk:
                    p = psf.next()
                    for k in range(8):
                        S.mm(p[:, 0:256], h2T[:, k, Ltot - 128:Ltot], wg[:, k, :], start=(k == 0), stop=(k == 7))
                    gts = F.gts.next()
                    S.copy("act", gts, p[:, 0:256])
                    S.dma("sp", ffn_p[l, :, s * 256:(s + 1) * 256], gts[126:128, :])
                for c2 in range(2):
                    c = s * 2 + c2
                    g_ps = psf.next()
                    v_ps = psf.next()
                    for k in range(8):
                        S.mm(g_ps[:, 0:Ltot], wg[:, k, c2 * 128:(c2 + 1) * 128], h2T[:, k, 0:Ltot], start=(k == 0), stop=(k == 7))
                    for k in range(8):
                        S.mm(v_ps[:, 0:Ltot], wv[:, k, c2 * 128:(c2 + 1) * 128], h2T[:, k, 0:Ltot], start=(k == 0), stop=(k == 7))
                    gext = F.gext.next()
                    S.copy("act", gext[:, 2:2 + Ltot], g_ps[:, 0:Ltot])
                    if first_blk:
                        S.memset("pool", gext[:, 0:2], 0.0)
                    else:
                        S.copy("pool", gext[:, 0:2], carry[:, c, :])
                    S.copy("pool", carry[:, c, :], gext[:, Ltot:Ltot + 2])
                    gacc = F.gacc.next()
                    S.ts("pool", gacc[:, 0:Ltot], gext[:, 0:Ltot], fconvw[:, c, 0:1], fconvb[:, c:c + 1], ALU.mult, ALU.add)
                    S.stt(gacc[:, 0:Ltot], gext[:, 1:1 + Ltot], fconvw[:, c, 1:2], gacc[:, 0:Ltot], ALU.mult, ALU.add)
                    S.stt(gacc[:, 0:Ltot], gext[:, 2:2 + Ltot], fconvw[:, c, 2:3], gacc[:, 0:Ltot], ALU.mult, ALU.add)
                    S.act(gacc[:, 0:Ltot], gacc[:, 0:Ltot], AF.Gelu_apprx_tanh)
                    S.tt("dve", actT[:, c, 0:Ltot], gacc[:, 0:Ltot], v_ps[:, 0:Ltot], ALU.mult)
            for s in range(16):
                wd = F.wd.next()
                S.dma("sp", wd, wd_s[s])
                for j in range(ntb):
                    for nh in range(2):
                        p = psf.next()
                        for c2 in range(2):
                            S.mm(p, actT[:, s * 2 + c2, j * 128:(j + 1) * 128], wd[:, c2, nh * 512:(nh + 1) * 512],
                                 start=(c2 == 0), stop=(c2 == 1))
                        dst = f_sb[:, j, nh * 512:(nh + 1) * 512]
                        if s == 0:
                            S.copy("act", dst, p)
                        else:
                            S.tt("dve", dst, dst, p, ALU.add)
            for j in range(ntb):
                junk = F.junk.next()
                ss = small()
                S.act(junk, f_sb[:, j, :], AF.Square, accum=ss[:, 0:1])
                S.ts("dve", ss[:, 1:2], ss[:, 0:1], 1.0 / 1024, RMS_EPS, ALU.mult, ALU.add)
                S.act(ss[:, 1:2], ss[:, 1:2], AF.Sqrt)
                S.recip(ss[:, 2:3], ss[:, 1:2])
                xin = F.xin.next()
                S.dma("sp", xin, xmid_tile(blk * NTB + j))
                S.stt(f_sb[:, j, :], f_sb[:, j, :], ss[:, 2:3], gfpost_b, ALU.mult, ALU.mult)
                S.tt("pool", f_sb[:, j, :], f_sb[:, j, :], xin, ALU.add)
                S.dma("sp", xdst_fn(blk * NTB + j), f_sb[:, j, :])

        def ssm_out_prompt(l):
            so = S.sb([64, 1024], F32, "ssmo")
            for hh in range(2):
                p = psf.next()
                pv = p.rr("p (h n) -> p h n", h=4)
                for h4 in range(4):
                    h = hh * 4 + h4
                    S.tr(pv[0:64, h4, :], PP.ST[:, h * 64:(h + 1) * 64], identf)
                S.copy("act", so[:, hh * 512:(hh + 1) * 512], p[0:64, :])
            S.dma("sp", ssm_p[l].rr("h p n -> p h n"), so.rr("p (h n) -> p h n", h=8))

        if NS > 0:
            NPG = cfg["NPG"]
            NPHYS = cfg["NPHYS"]
            PAST = NPG * 128
            WB = min(512, PAST)
            LS = NS * 8
            NV = PAST // 16 - 1
            NTC = (NV + 127) // 128
            NSLC_S = PAST // 64 + 1
            NKS = PAST + 64
            NTW = (WB + 8 + 127) // 128
            x_s = din("x_s", [LS, D_MODEL])
            cache = din("cache", [DEPTH, NPHYS * 128, 256])
            pt_d = din("pt", [NS * NPG], I32)
            st_win = din("st_win", [DEPTH, NS, WB, 128])
            st_conv = din("st_conv", [DEPTH, NS * 3, 1024])
            st_ssm = din("st_ssm", [DEPTH, NS, 8, 64, 128])
            st_pool = din("st_pool", [DEPTH, NS * 15, 256])
            st_ffn = din("st_ffn", [DEPTH, NS * 2, 4096])
            y_s = dout("y_s", [LS, D_MODEL])
            kv_s = dout("kv_s", [DEPTH, LS, 256])
            win_s = dout("win_s", [DEPTH, NS, WB, 128])
            conv_s = dout("conv_s", [DEPTH, NS * 3, 1024])
            ssm_s = dout("ssm_s", [DEPTH, NS, 8, 64, 128])
            pool_s = dout("pool_s", [DEPTH, NS * 15, 256])
            ffn_s = dout("ffn_s", [DEPTH, NS * 2, 4096])
            xs_ap = dscr("xs_cur", [LS, D_MODEL])
            xs_res = Res("xs_cur")
            xs_cur = V(xs_ap[:, :], (xs_res,))

            incl_s = cload("c_incl_s", [LS, LS])
            after_s = cload("c_after_s", [LS, LS])
            seqmask_s = cload("c_seqmask_s", [LS, NS, 128])
            rowmask_s = cload("c_rowmask_s", [LS, NS])
            cs_s = cload("c_cs_s", [LS, 16])
            Rm = cload("c_R", [32, 32])
            addc_s = cload("c_addc_s", [32, NSLC_S])
            ov_s = cload("c_ov_s", [128, NTC, NSLC_S], BF16)
            tokmask_s = cload("c_tokmask_s", [32, 64])
            winmask_s = cload("c_winmask_s", [32, WB + 8])
            ptfk = S.sb([128, NS * NPG], F32, "ptfk")
            cache_flat = V(cache.ap.rearrange("d r c -> (d r) c"))
            with S.scope():
                ptb = S.sb([128, NS * NPG], I32, "ptb")
                pidx = cload("c_pidx", [128, NS * NPG])
                S.dma("sp", ptb, V(pt_d.ap.partition_broadcast(128)))
                S.copy("dve", ptfk, ptb)
                S.ts("dve", ptfk, ptfk, 128.0, None, ALU.mult)
                S.tt("dve", ptfk, ptfk, pidx, ALU.add)

        def sample_layer(l, last):
            L = LS
            xsrc = x_s if l == 0 else xs_cur
            xdst = y_s if last else xs_cur
            with S.scope():
                idx = S.sb([128, NS * NPG], I32, "s_idx")
                ptf2 = S.sb([128, NS * NPG], F32, "s_ptf2")
                S.ts("dve", ptf2, ptfk, float(l * NPHYS * 128), None, ALU.add)
                S.copy("dve", idx, ptf2)
                xt = S.sb([128, 1024], F32, "s_xt")
                mix = S.sb([128, 1024], BF16, "s_mix")
                qT_s = S.sb([64, 4, L], BF16, "s_qT")
                knew = S.sb([64, 4, L], BF16, "s_knew")
                kvbf = S.sb([128, 384], BF16, "s_kvbf")
                gates = S.sb([128, 12], F32, "s_gates")
                xmid_s = S.sb([128, 1024], F32, "s_xmid")
                junkS = Obj()
                junkS.junk = Ring(S, 1, [128, 1024], BF16, "s_junk")
                junkS.xn = Ring(S, 1, [128, 1024], BF16, "s_xn")
                S.dma("sp", xt[:L], xsrc)
                with S.scope():
                    hT = S.sb([128, 8, L], BF16, "s_hT")
                    rmsnorm_T(xt[:L], L, gcol_pre, hT, B=junkS)
                    if SST < 0.2:
                        return
                    ext = S.sb([128, 8, NS, 11], F32, "s_ext")
                    pext = S.sb([64, 4, NS, 23], F32, "s_pext")
                    tl = S.sb([128, 1280], F32, "s_tail")
                    p = psf.next()
                    pv = p[:, 0:8 * L].rr("p (c l) -> p c l", c=8)
                    for c in range(8):
                        for k in range(8):
                            S.mm(pv[:, c, :], win_sb[:, k, OFF_XBC + c * 128: OFF_XBC + (c + 1) * 128], hT[:, k, :],
                                 start=(k == 0), stop=(k == 7))
                    S.copy("act", ext[:, :, :, 3:11], pv.rr("p c (b i) -> p c b i", b=NS))
                    p = psf.next()
                    pv = p[:, 0:4 * L].rr("p (c l) -> p c l", c=4)
                    for g in range(4):
                        for k in range(8):
                            S.mm(pv[0:64, g, :], win_sb[:, k, OFF_POOL + g * 64: OFF_POOL + (g + 1) * 64], hT[:, k, :],
                                 start=(k == 0), stop=(k == 7))
                    S.copy("act", pext[:, :, :, 15:23], pv[0:64].rr("p c (b i) -> p c b i", b=NS))
                    if SST < 0.4:
                        return
                    z_ps = psf.next()
                    for k in range(8):
                        S.mm(z_ps[:L], hT[:, k, :], win_sb[:, k, 0:512], start=(k == 0), stop=(k == 7))
                    zs = S.sb([128, 512], F32, "s_zs")
                    S.act(zs[:L], z_ps[:L], AF.Silu)
                    q_ps = psf.next()
                    for k in range(8):
                        S.mm(q_ps[:L, 0:256], hT[:, k, :], win_sb[:, k, OFF_Q:OFF_Q + 256], start=(k == 0), stop=(k == 7))
                    kv_ps = psf.next()
                    for k in range(8):
                        S.mm(kv_ps[:L, 0:396], hT[:, k, :], win_sb[:, k, OFF_KV:OFF_KV + 396], start=(k == 0), stop=(k == 7))
                    if SST < 0.5:
                        return
                    qf = S.sb([128, 256], F32, "s_qf")
                    rows = S.sb([128, 384], F32, "s_rows")
                    S.act(qf[:L], q_ps[:L, 0:256], AF.Copy, scale=0.125)
                    S.copy("act", rows[:L], kv_ps[:L, 0:384])
                    S.act(gates[:L], kv_ps[:L, 384:396], AF.Sigmoid)
                    for nh in range(2):
                        p = psf.next()
                        for k in range(8):
                            S.mm(p[:L], hT[:, k, :], win_sb[:, k, OFF_XBC + nh * 512: OFF_XBC + (nh + 1) * 512],
                                 start=(k == 0), stop=(k == 7))
                        S.copy("act", tl[:L, nh * 512:(nh + 1) * 512], p[:L])
                    p = psf.next()
                    for k in range(8):
                        S.mm(p[:L, 0:256], hT[:, k, :], win_sb[:, k, OFF_POOL:OFF_POOL + 256], start=(k == 0), stop=(k == 7))
                    S.copy("act", tl[:L, 1024:1280], p[:L, 0:256])
                    for b in range(NS):
                        S.dma("sp", conv_s[l, b * 3:(b + 1) * 3, :], tl[b * 8 + 5:b * 8 + 8, 0:1024])
                        S.dma("sp", pool_s[l, b * 15 + 7:b * 15 + 15, :], tl[b * 8:b * 8 + 8, 1024:1280])
                    if SST < 0.8:
                        return
                    rt = S.sb([128, 4, 4, 8], F32, "s_rt")
                    R.rt = Ring(S, 1, [128, 4, 4, 8], F32, "s_rt2")
                    rope(qf[:L].rr("p (h d) -> p h d", h=4), 4, cs_s, L)
                    rope(rows[:L].rr("p (j two d) -> p j two d", j=3, two=2)[:, :, 0, :], 3, cs_s, L)
                    if SST < 0.85:
                        return
                    S.dma("sp", kv_s[l], rows[:L, 0:256])
                    for b in range(NS):
                        S.dma("sp", win_s[l, b, WB - 8:WB, :], rows[b * 8:(b + 1) * 8, 256:384])
                    qbf = S.sb([128, 256], BF16, "s_qbf")
                    S.copy("pool", qbf[:L], qf[:L])
                    S.copy("pool", kvbf[:L], rows[:L])
                    if SST < 0.9:
                        return
                    pT = psb.next()
                    pTv = pT.rr("p (k l) -> p k l", k=8)[:, :, 0:L]
                    for h in range(4):
                        S.tr(pTv[0:64, h, :], qbf[:L, h * 64:(h + 1) * 64], identb[:L, :L])
                    for j, c0 in enumerate((0, 64, 128, 256)):
                        S.tr(pTv[0:64, 4 + j, :], kvbf[:L, c0:c0 + 64], identb[:L, :L])
                    if SST < 0.95:
                        return
                    if VAR == 1:
                        S.copy("dve", qT_s, pTv[0:64, 0:4, :])
                    elif VAR == 2:
                        for h in range(4):
                            S.copy("dve", qT_s[:, h, :], pTv[0:64, h, :])
                            S.copy("dve", knew[:, h, :], pTv[0:64, 4 + h, :])
                    elif VAR == 3:
                        S.memset("dve", qT_s, 0.0)
                    elif VAR == 4:
                        S.memset("dve", small()[:, 0:8], 0.0)
                    elif VAR == 5:
                        S.memset("pool", small()[:, 0:8], 0.0)
                    elif VAR == 6:
                        pass
                    else:
                        S.copy("dve", qT_s, pTv[0:64, 0:4, :])
                        S.copy("dve", knew, pTv[0:64, 4:8, :])
                    if SST < 1:
                        return
                    cst = S.sb([NS * 3, 1024], F32, "s_cst")
                    S.dma("sp", cst, st_conv[l])
                    p = psf.next()
                    pv = p[:, 0:8 * NS * 3].rr("p (c r) -> p c r", c=8)
                    for c in range(8):
                        S.mm(pv[:, c, :], cst[:, c * 128:(c + 1) * 128], identf[:NS * 3, :NS * 3])
                    S.copy("act", ext[:, :, :, 0:3], pv.rr("p c (b t) -> p c b t", b=NS))
                    if SST < 1.2:
                        return
                    acc = S.sb([128, 8, NS, 8], F32, "s_acc")
                    for c in range(8):
                        S.ts("dve", acc[:, c], ext[:, c, :, 0:8], convw[:, c, 0:1], convb[:, c:c + 1], ALU.mult, ALU.add)
                        for k in range(1, 4):
                            S.stt(acc[:, c], ext[:, c, :, k:k + 8], convw[:, c, k:k + 1], acc[:, c], ALU.mult, ALU.add)
                    abf = S.sb([128, 8, L], BF16, "s_abf")
                    S.act(abf, acc.rr("p c b i -> p c (b i)"), AF.Silu)
                    dbg("s_acc", acc.rr("p c b i -> p c (b i)")); dbg("s_ext", ext.rr("p c b t -> p c (b t)"))
                    pT = psb.next()
                    pTv = pT[:, 0:768].rr("p (k l) -> p k l", k=6)
                    for c in range(6):
                        S.tr(pTv[:L, c, :], abf[:, c, :], identb)
                    xsB = S.sb([128, 768], BF16, "s_xsB")
                    S.copy("dve", xsB[:L], pTv[:L].rr("p k l -> p (k l)"))
                    xs3 = xsB[:L, 0:512].rr("p (h d) -> p h d", h=8)
                    if SST < 1.4:
                        return
                    misc_ps = psf.next()
                    for k in range(8):
                        S.mm(misc_ps[:L, 0:8], hT[:, k, :], win_sb[:, k, OFF_DT:OFF_DT + 8], start=(k == 0), stop=(k == 7))
                    sm = small()
                    dtp = sm[:L, 0:8]
                    S.tt("dve", dtp, misc_ps[:L, 0:8], dtb_b[:L], ALU.add)
                    sm2 = small()
                    S.ts("dve", sm2[:L, 8:16], dtp, -1.0, None, ALU.mult)
                    S.tt("dve", sm2[:L, 0:8], dtp, sm2[:L, 8:16], ALU.max)
                    S.act(sm2[:L, 0:8], sm2[:L, 0:8], AF.Exp, scale=-1.0)
                    S.act(sm2[:L, 0:8], sm2[:L, 0:8], AF.Ln, bias=1.0)
                    S.ts("dve", sm2[:L, 8:16], dtp, 0.0, None, ALU.max)
                    dt = sm[:L, 8:16]
                    S.tt("dve", dt, sm2[:L, 0:8], sm2[:L, 8:16], ALU.add)
                    sm3 = small()
                    adt = sm3[:L, 0:8]
                    S.tt("dve", adt, dt, a_b[:L], ALU.mult)
                    xc = S.sb([128, 512], BF16, "s_xc")
                    S.tt("pool", xc[:L].rr("p (h d) -> p h d", h=8), xs3, dt.us(2).bc([L, 8, 64]), ALU.mult)
                    dbg("s_dt", dt); dbg("s_xsB", xsB[:L], BF16); dbg("s_xc", xc[:L], BF16)
                    S.mm(misc_ps[:L, 8:16], incl_s, adt)
                    S.mm(misc_ps[:L, 16:24], after_s, adt)
                    for b in range(NS):
                        S.mm(misc_ps[:, 24 + 8 * b:32 + 8 * b], seqmask_s[:, b, :], adt)
                    sm4 = small()
                    S.act(sm4[:L, 0:8], misc_ps[:L, 8:16], AF.Exp)
                    S.act(sm4[:L, 8:16], misc_ps[:L, 16:24], AF.Exp)
                    elast = S.sb([128, NS, 8], F32, "s_elast")
                    S.act(elast.rr("p b h -> p (b h)"), misc_ps[:, 24:24 + 8 * NS], AF.Exp)
                    eacum = sm4[:L, 0:8]
                    dte = sm4[:L, 8:16]
                    if SST < 1.6:
                        return
                    sc_ps = psf.next()
                    scv = sc_ps[:L, 0:2 * L].rr("p (g l) -> p g l", g=2)
                    for g in range(2):
                        S.mm(scv[:, g, :], abf[:, 4 + g, :], abf[:, 6 + g, :])
                    scm = S.sb([128, 2, L], F32, "s_scm")
                    S.tt("dve", scm[:L], scv, incl_s.us(1).bc([L, 2, L]), ALU.mult)
                    if SST < 1.65:
                        return
                    rhsD = S.sb([128, 8, L], F32, "s_rhsD")
                    S.tt("pool", rhsD[:L], incl_s.us(1).bc([L, 8, L]), adt.us(2).bc([L, 8, L]), ALU.mult)
                    p = psf.next()
                    S.mm(p[:L, 0:8 * L], after_s, rhsD[:L].rr("p h l -> p (h l)"))
                    Dexp = S.sb([128, 8, L], F32, "s_Dexp")
                    S.act(Dexp[:L].rr("p h l -> p (h l)"), p[:L, 0:8 * L], AF.Exp)
                    if SST < 1.7:
                        return
                    MT = S.sb([128, 8, L], BF16, "s_MT")
                    for g in range(2):
                        S.tt("dve", MT[:L, g * 4:(g + 1) * 4, :], Dexp[:L, g * 4:(g + 1) * 4, :],
                             scm[:L, g, :].us(1).bc([L, 4, L]), ALU.mult)
                    if SST < 1.75:
                        return
                    Y_ps = psf.next()
                    for h in range(8):
                        S.mm(Y_ps[:L, h * 64:(h + 1) * 64], MT[:L, h, :], xc[:L, h * 64:(h + 1) * 64])
                    if SST < 1.78:
                        return
                    y = S.sb([128, 512], F32, "s_y")
                    t1 = S.sb([128, 512], F32, "s_t1")
                    t2 = S.sb([128, 512], F32, "s_t2")
                    S.tt("pool", t1[:L].rr("p (h d) -> p h d", h=8), xs3, dsk_b[:L].us(2).bc([L, 8, 64]), ALU.mult)
                    if SST < 1.785:
                        return
                    S.tt("dve", y[:L], Y_ps[:L], t1[:L], ALU.add)
                    if SST < 1.8:
                        return
                    STb = S.sb([128, 512], BF16, "s_STb")
                    sA = S.sb([64, 8, 128], F32, "s_sA")
                    sB = S.sb([128, 4, 128], F32, "s_sB")
                    sBb = S.sb([128, 4, 128], BF16, "s_sBb")
                    snew = S.sb([64, 8, 128], F32, "s_snew")
                    stmp = S.sb([64, 4, 128], F32, "s_stmp")
                    xcd = S.sb([128, 512], BF16, "s_xcd")
                    for b in range(NS):
                        S.dma("sp", sA, st_ssm[l, b].rr("h p n -> p h n"))
                        S.dma("sp", sB, st_ssm[l, b].rr("h p n -> (h p) n").rr("(hp q) n -> q hp n", q=128))
                        S.copy("dve", sBb, sB)
                        pT = psb.next()
                        pTv = pT[:, 0:512].rr("p (k l) -> p k l", k=4)
                        for hp in range(4):
                            S.tr(pTv[:, hp, :], sBb[:, hp, :], identb)
                        S.copy("dve", STb, pT[:, 0:512])
                        if SST < 1.82:
                            return
                        Yo_ps = psf.next()
                        for g in range(2):
                            S.mm(Yo_ps[:L, g * 256:(g + 1) * 256], abf[:, 6 + g, :], STb[:, g * 256:(g + 1) * 256])
                        smb = small()
                        S.ts("dve", smb[:L, 0:8], eacum, rowmask_s[:, b:b + 1], None, ALU.mult)
                        S.ts("dve", smb[:L, 8:16], dte, rowmask_s[:, b:b + 1], None, ALU.mult)
                        S.tt("dve", t2[:L].rr("p (h d) -> p h d", h=8), Yo_ps[:L].rr("p (h d) -> p h d", h=8),
                             smb[:L, 0:8].us(2).bc([L, 8, 64]), ALU.mult)
                        S.tt("pool", y[:L], y[:L], t2[:L], ALU.add)
                        if SST < 1.84:
                            return
                        S.tt("pool", xcd[:L].rr("p (h d) -> p h d", h=8), xc[:L].rr("p (h d) -> p h d", h=8),
                             smb[:L, 8:16].us(2).bc([L, 8, 64]), ALU.mult)
                        for hh in range(2):
                            Sn_ps = psf.next()
                            for h4 in range(4):
                                h = hh * 4 + h4
                                S.mm(Sn_ps[0:64, h4 * 128:(h4 + 1) * 128], xcd[:L, h * 64:(h + 1) * 64],
                                     xsB[:L, 512 + hh * 128: 512 + (hh + 1) * 128])
                            S.tt("pool", stmp, sA[:, hh * 4:(hh + 1) * 4, :],
                                 elast[0:64, b, hh * 4:(hh + 1) * 4].us(2).bc([64, 4, 128]), ALU.mult)
                            S.tt("dve", snew[:, hh * 4:(hh + 1) * 4, :], stmp, Sn_ps[0:64].rr("p (h n) -> p h n", h=4), ALU.add)
                        S.dma("sp", ssm_s[l, b].rr("h p n -> p h n"), snew)
                    if SST < 2:
                        return
                    S.tt("dve", y[:L], y[:L], zs[:L], ALU.mult)
                    ss = small()
                    junk = junkS.junk.next()
                    S.act(junk[:L, 0:512], y[:L], AF.Square, accum=ss[:L, 0:1])
                    S.ts("dve", ss[:L, 1:2], ss[:L, 0:1], 1.0 / 512, RMS_EPS, ALU.mult, ALU.add)
                    S.act(ss[:L, 1:2], ss[:L, 1:2], AF.Sqrt)
                    S.recip(ss[:L, 2:3], ss[:L, 1:2])
                    S.act(mix[:L, 0:512], y[:L], AF.Copy, scale=ss[:L, 2:3])
                    pst = S.sb([NS * 15, 256], F32, "s_pst")
                    S.dma("sp", pst, st_pool[l])
                    for b in range(NS):
                        S.dma("sp", pool_s[l, b * 15:b * 15 + 7, :], pst[b * 15 + 8:b * 15 + 15, :])
                    p = psf.next()
                    pv = p[:, 0:4 * NS * 15].rr("p (c r) -> p c r", c=4)
                    for g in range(4):
                        S.mm(pv[0:64, g, :], pst[:, g * 64:(g + 1) * 64], identf[:NS * 15, :NS * 15])
                    S.copy("act", pext[:, :, :, 0:15], pv[0:64].rr("p c (b t) -> p c b t", b=NS))
                    s2 = S.sb([64, 4, NS, 22], F32, "s_s2")
                    s4 = S.sb([64, 3, NS, 20], F32, "s_s4")
                    s8 = S.sb([64, 2, NS, 16], F32, "s_s8")
                    s16 = S.sb([64, 1, NS, 8], F32, "s_s16")
                    S.tt("pool", s2, pext[:, :, :, 1:23], pext[:, :, :, 0:22], ALU.add)
                    S.tt("pool", s4, s2[:, 1:4, :, 2:22], s2[:, 1:4, :, 0:20], ALU.add)
                    S.tt("pool", s8, s4[:, 1:3, :, 4:20], s4[:, 1:3, :, 0:16], ALU.add)
                    S.tt("pool", s16, s8[:, 1:2, :, 8:16], s8[:, 1:2, :, 0:8], ALU.add)
                    pooled = S.sb([64, 4, NS, 8], BF16, "s_pooled")
                    srcs = [(s2, 0, 14), (s4, 0, 12), (s8, 0, 8), (s16, 0, 0)]
                    for g, (sx, gi, off) in enumerate(srcs):
                        S.stt(pooled[:, g], sx[:, gi, :, off:off + 8], 1.0 / (2 ** (g + 1)), pext[:, g, :, 15:23],
                              ALU.mult, ALU.subtract)
                    yp_ps = psf.next()
                    for g in range(4):
                        S.mm(yp_ps[:L, g * 64:(g + 1) * 64], pooled[:, g].rr("p b i -> p (b i)"), poolw_sb[:, g, :])
                    S.copy("act", mix[:L, 512:768], yp_ps[:L, 0:256])
                if SST < 3:
                    return
                with S.scope():
                    load_cmp_weights(l)
                    ksT = S.sb([64, NKS], BF16, "s_ksT")
                    vs = S.sb([128, NPG + 1, 64], BF16, "s_vs")
                    kwT = S.sb([64, WB + 8], BF16, "s_kwT")
                    vw = S.sb([128, NTW, 64], BF16, "s_vw")
                    gel_k = S.sb([128, NTC * 128], BF16, "s_gelk")
                    gel_v = S.sb([128, NTC * 128], BF16, "s_gelv")
                    ckT = S.sb([64, NTC * 128], BF16, "s_ckT")
                    cvs = S.sb([128, NTC, 64], BF16, "s_cv")
                    qTb = S.sb([64, 4, 8], BF16, "s_qTb")
                    g_rows = S.sb([32, 3], F32, "s_grows")
                    oacc = S.sb([32, 64], F32, "s_oacc")
                    oaccb = S.sb([32, 64], BF16, "s_oaccb")
                    for b in range(NS):
                        S.memset("dve", ksT[:, PAST:NKS], 0.0)
                        S.memset("dve", vs[:, NPG, :], 0.0)
                        S.memset("dve", vw[:, NTW - 1, :], 0.0)
                        with S.scope():
                            cT = S.sb([64, 2, 16, PAST // 16], BF16, "s_cT")
                            r_pg = Ring(S, 4, [128, 256], F32, "s_pg")
                            r_pgb = Ring(S, 2, [128, 256], BF16, "s_pgb")
                            pgts = {}

                            def issue_page(pg):
                                pgt = r_pg.next()
                                j = b * NPG + pg

                                def fn(e, pgt=pgt, j=j):
                                    return e.indirect_dma_start(
                                        out=pgt.ap, out_offset=None, in_=cache_flat.ap,
                                        in_offset=bass.IndirectOffsetOnAxis(ap=idx.ap[:, j:j + 1], axis=0))
                                S.op("pool", fn, [pgt], [idx], dma=True)
                                pgts[pg] = pgt
                            PF = 3
                            for pg in range(min(PF, NPG)):
                                issue_page(pg)
                            for pg in range(NPG):
                                pgt = pgts.pop(pg)
                                pgb = r_pgb.next()
                                S.copy("act", pgb, pgt)
                                if pg + PF < NPG:
                                    issue_page(pg + PF)
                                pT = psb.next()
                                pTv = pT[0:64, 0:384].rr("p (k l) -> p k l", k=3)
                                for jj in range(3):
                                    S.tr(pTv[:, jj, :], pgb[:, jj * 64:(jj + 1) * 64], identb)
                                S.copy("dve", cT[:, :, :, pg * 8:(pg + 1) * 8], pTv[:, 0:2, :].rr("p k (b r) -> p k r b", r=16))
                                S.copy("dve", ksT[:, pg * 128:(pg + 1) * 128], pTv[:, 2, :])
                                S.copy("act", vs[:, pg, :], pgb[:, 192:256])
                            for (ci, w1, cb, gel) in ((0, CW.w1k, CW.cbias_k, gel_k), (1, CW.w1v, CW.cbias_v, gel_v)):
                                n0 = 0
                                while n0 < NV:
                                    nb = min(512, NV - n0)
                                    p = psf.next()
                                    for li in range(32):
                                        S.mm(p[:, 0:nb], w1[:, li, :], cT[:, ci, li % 16, n0 + li // 16: n0 + li // 16 + nb],
                                             start=(li == 0), stop=(li == 31))
                                    S.act(gel[:, n0:n0 + nb], p[:, 0:nb], AF.Gelu_apprx_tanh, bias=cb)
                                    n0 += nb
                        if SST < 4:
                            return
                        n0 = 0
                        while n0 < NV:
                            nb = min(512, NV - n0)
                            p = psf.next()
                            S.mm(p[0:64, 0:nb], CW.w2k, gel_k[:, n0:n0 + nb])
                            S.copy("act", ckT[:, n0:n0 + nb], p[0:64, 0:nb])
                            n0 += nb
                        for t in range(NTC):
                            rws = min(128, NV - t * 128)
                            p = psf.next()
                            S.mm(p[0:rws, 0:64], gel_v[:, t * 128:t * 128 + rws], CW.w2v)
                            S.copy("act", cvs[0:rws, t, :], p[0:rws, 0:64])
                        S.copy("dve", ksT[:, PAST:PAST + 8], knew[:, 2, b * 8:(b + 1) * 8])
                        S.dma("sp", vs[0:8, NPG, :], kvbf[b * 8:(b + 1) * 8, 192:256])
                        with S.scope():
                            r_wt = Ring(S, 2, [128, 128], F32, "s_wt")
                            r_wtb = Ring(S, 2, [128, 128], BF16, "s_wtb")
                            for t in range(WB // 128):
                                wt = r_wt.next()
                                S.dma("sp", wt, st_win[l, b, t * 128:(t + 1) * 128, :])
                                r0 = 8 if t == 0 else 0
                                S.dma("sp", win_s[l, b, t * 128 - 8 + r0:t * 128 + 120, :], wt[r0:128, :])
                                wtb = r_wtb.next()
                                S.copy("act", wtb, wt)
                                pT = psb.next()
                                S.tr(pT[0:64, 0:128], wtb[:, 0:64], identb)
                                S.copy("dve", kwT[:, t * 128:(t + 1) * 128], pT[0:64, 0:128])
                                S.copy("act", vw[:, t, :], wtb[:, 64:128])
                        S.copy("dve", kwT[:, WB:WB + 8], knew[:, 3, b * 8:(b + 1) * 8])
                        S.dma("sp", vw[0:8, WB // 128, :], kvbf[b * 8:(b + 1) * 8, 320:384])
                        S.copy("dve", qTb, qT_s[:, :, b * 8:(b + 1) * 8])
                        for h in range(4):
                            S.dma("sp", g_rows[h * 8:(h + 1) * 8, :], gates[b * 8:(b + 1) * 8, 3 * h:3 * h + 3])
                        qTf = qTb.rr("p h i -> p (h i)")
                        with S.scope():
                            M = 32
                            ssb = S.sb([32, 512], F32, "s_ssb")
                            ebf = S.sb([32, NKS], BF16, "s_ebf")
                            pTs = S.sb([128, NPG + 1, 32], BF16, "s_pTs")
                            e32 = S.sb([32, NTC * 128], F32, "s_e32")
                            pn = S.sb([32, NTC * 128], BF16, "s_pn")
                            pnT = S.sb([128, NTC, 32], BF16, "s_pnT")
                            impr = S.sb([32, NSLC_S], F32, "s_impr")
                            imp = S.sb([32, NSLC_S], F32, "s_imp")
                            wk = S.sb([32, NSLC_S], F32, "s_wk")
                            selm = S.sb([32, NSLC_S], F32, "s_selm")
                            mxc = S.sb([32, 32], F32, "s_mxc")
                            n0 = 0
                            while n0 < NV:
                                nb = min(512, NV - n0)
                                p = psf.next()
                                S.mm(p[0:M, 0:nb], qTf, ckT[:, n0:n0 + nb])
                                S.copy("act", e32[:, n0:n0 + nb], p[0:M, 0:nb])
                                n0 += nb
                            sm = softmax_rows(e32[:, 0:NV], M, NV, e32[:, 0:NV], clamp=True)
                            S.ts("dve", sm[:M, 3:4], sm[:M, 2:3], 1e-30, None, ALU.max)
                            S.recip(sm[:M, 4:5], sm[:M, 3:4])
                            S.ts("dve", pn[:, 0:NV], e32[:, 0:NV], sm[:M, 4:5], None, ALU.mult)
                            pT = psb.next()
                            pTv = pT[:, 0:NTC * 32].rr("p (k l) -> p k l", k=NTC)
                            for t in range(NTC):
                                rws = min(128, NV - t * 128)
                                S.tr(pTv[0:rws, t, :], pn[:, t * 128:t * 128 + rws], identb[:M, :M])
                            if NV % 128 != 0:
                                S.memset("dve", pnT[:, NTC - 1, :], 0.0)
                            for t in range(NTC):
                                rws = min(128, NV - t * 128)
                                S.copy("dve", pnT[0:rws, t, :], pTv[0:rws, t, :])
                            o_ps = psf.next()
                            for t in range(NTC):
                                rws = min(128, NV - t * 128)
                                S.mm(o_ps[0:M, 0:64], pnT[0:rws, t, :], cvs[0:rws, t, :], start=(t == 0), stop=(t == NTC - 1))
                            S.ts("dve", oacc, o_ps[0:M, 0:64], g_rows[:, 0:1], None, ALU.mult)
                            if SST < 5:
                                return
                            i_ps = psf.next()
                            for t in range(NTC):
                                rws = min(128, NV - t * 128)
                                S.mm(i_ps[0:M, 0:NSLC_S], pnT[0:rws, t, :], ov_s[0:rws, t, :], start=(t == 0), stop=(t == NTC - 1))
                            S.copy("act", impr, i_ps[0:M, 0:NSLC_S])
                            i2_ps = psf.next()
                            S.mm(i2_ps[0:M, 0:NSLC_S], Rm, impr)
                            S.tt("dve", imp, i2_ps[0:M, 0:NSLC_S], addc_s, ALU.add)
                            m8 = small()
                            if NSLC_S > 16:
                                S.max8(m8[:M, 0:8], imp)
                                S.match_replace(wk, m8[:M, 0:8], imp, -3e38)
                                S.max8(m8[:M, 8:16], wk)
                                S.ts("dve", selm, imp, m8[:M, 15:16], NEG, ALU.is_lt, ALU.mult)
                            else:
                                S.memset("dve", selm, 0.0)
                            nch = (NKS + 511) // 512
                            for pas in range(2):
                                for c in range(nch):
                                    k0 = c * 512
                                    w = min(512, NKS - k0)
                                    p = psf.next()
                                    S.mm(p[0:M, 0:w], qTf, ksT[:, k0:k0 + w])
                                    S.tt("dve", ssb[:, 0:w].rr("p (j c) -> p j c", c=64), p[0:M, 0:w].rr("p (j c) -> p j c", c=64),
                                         selm[:, k0 // 64:(k0 + w) // 64].us(2).bc([M, w // 64, 64]), ALU.add)
                                    if c == nch - 1:
                                        S.tt("dve", ssb[:, w - 64:w], ssb[:, w - 64:w], tokmask_s, ALU.add)
                                    if pas == 0:
                                        S.reduce(mxc[:, c:c + 1], ssb[:, 0:w], ALU.max)
                                    else:
                                        S.act(ebf[:, k0:k0 + w], ssb[:, 0:w], AF.Exp, bias=sm1[:M, 1:2], accum=mxc[:, c:c + 1])
                                if pas == 0:
                                    sm1 = small()
                                    S.reduce(sm1[:M, 0:1], mxc[:, 0:nch], ALU.max)
                                    S.ts("dve", sm1[:M, 1:2], sm1[:M, 0:1], -1.0, None, ALU.mult)
                                else:
                                    S.reduce(sm1[:M, 2:3], mxc[:, 0:nch], ALU.add)
                            S.recip(sm1[:M, 3:4], sm1[:M, 2:3])
                            S.tt("dve", sm1[:M, 4:5], sm1[:M, 3:4], g_rows[:, 1:2], ALU.mult)
                            ntl = NPG + 1
                            for b0 in range(0, ntl, 32):
                                nb8 = min(32, ntl - b0)
                                pT = psb.next()
                                pTv = pT.rr("p (k l) -> p k l", k=32)
                                for jx in range(nb8):
                                    t = b0 + jx
                                    rws = min(128, NKS - t * 128)
                                    S.tr(pTv[0:rws, jx, :], ebf[:, t * 128:t * 128 + rws], identb[:M, :M])
                                nfull = nb8 if (b0 + nb8 < ntl) else nb8 - 1
                                if nfull > 0:
                                    S.copy("dve", pTs[:, b0:b0 + nfull, :], pTv[:, 0:nfull, :])
                                if nfull < nb8:
                                    S.copy("dve", pTs[0:64, ntl - 1, :], pTv[0:64, nb8 - 1, :])
                            o_ps = psf.next()
                            for t in range(ntl):
                                rws = min(128, NKS - t * 128)
                                S.mm(o_ps[0:M, 0:64], pTs[0:rws, t, :], vs[0:rws, t, :], start=(t == 0), stop=(t == ntl - 1))
                            S.stt(oacc, o_ps[0:M, 0:64], sm1[:M, 4:5], oacc, ALU.mult, ALU.add)
                            if SST < 6:
                                return
                            NKW = WB + 8
                            wsb = S.sb([32, NKW], F32, "s_wsb")
                            k0 = 0
                            while k0 < NKW:
                                w = min(512, NKW - k0)
                                p = psf.next()
                                S.mm(p[0:M, 0:w], qTf, kwT[:, k0:k0 + w])
                                S.tt("dve", wsb[:, k0:k0 + w], p[0:M, 0:w], winmask_s[:, k0:k0 + w], ALU.add)
                                k0 += w
                            sm = softmax_rows(wsb, M, NKW, ebf[:, 0:NKW])
                            S.recip(sm[:M, 3:4], sm[:M, 2:3])
                            S.tt("dve", sm[:M, 4:5], sm[:M, 3:4], g_rows[:, 2:3], ALU.mult)
                            pT = psb.next()
                            pTv = pT.rr("p (k l) -> p k l", k=32)
                            for t in range(NTW):
                                rws = min(128, NKW - t * 128)
                                S.tr(pTv[0:rws, t, :], ebf[:, t * 128:t * 128 + rws], identb[:M, :M])
                            for t in range(NTW):
                                rws = min(128, NKW - t * 128)
                                S.copy("dve", pTs[0:rws, t, :], pTv[0:rws, t, :])
                            o_ps = psf.next()
                            for t in range(NTW):
                                rws = min(128, NKW - t * 128)
                                S.mm(o_ps[0:M, 0:64], pTs[0:rws, t, :], vw[0:rws, t, :], start=(t == 0), stop=(t == NTW - 1))
                            S.stt(oacc, o_ps[0:M, 0:64], sm[:M, 4:5], oacc, ALU.mult, ALU.add)
                            S.copy("dve", oaccb, oacc)
                            for h in range(4):
                                S.dma("sp", mix[b * 8:(b + 1) * 8, 768 + h * 64:768 + (h + 1) * 64], oaccb[h * 8:(h + 1) * 8, :])
                if SST < 7:
                    return
                with S.scope():
                    pT = psb.next()
                    pTv = pT[:, 0:8 * L].rr("p (k l) -> p k l", k=8)
                    for k in range(8):
                        S.tr(pTv[:, k, :], mix[:L, k * 128:(k + 1) * 128], identb[:L, :L])
                    mixT = S.sb([128, 8, L], BF16, "s_mixT")
                    S.tt("dve", mixT, pTv, mixg.us(2).bc([128, 8, L]), ALU.mult)
                    mo = [psf.next(), psf.next()]
                    ss = small()
                    junk = junkS.junk.next()
                    for nh in range(2):
                        for k in range(8):
                            S.mm(mo[nh][:L], mixT[:, k, :], wout_sb[:, k, nh * 512:(nh + 1) * 512], start=(k == 0), stop=(k == 7))
                        S.act(junk[:L, nh * 512:(nh + 1) * 512], mo[nh][:L], AF.Square, accum=ss[:L, nh:nh + 1])
                    S.tt("dve", ss[:L, 2:3], ss[:L, 0:1], ss[:L, 1:2], ALU.add)
                    S.ts("dve", ss[:L, 3:4], ss[:L, 2:3], 1.0 / 1024, RMS_EPS, ALU.mult, ALU.add)
                    S.act(ss[:L, 3:4], ss[:L, 3:4], AF.Sqrt)
                    S.recip(ss[:L, 4:5], ss[:L, 3:4])
                    tmp = S.sb([128, 512], F32, "s_tmp")
                    for nh in range(2):
                        sl = slice(nh * 512, (nh + 1) * 512)
                        S.stt(tmp[:L], mo[nh][:L], ss[:L, 4:5], gpost_b[:L, sl], ALU.mult, ALU.mult)
                        S.tt("pool", xmid_s[:L, sl], tmp[:L], xt[:L, sl], ALU.add)
                if SST < 8:
                    return
                with S.scope():
                    alloc_ffn_bufs(128)
                    h2T, actT, f_sb = F.h2T, F.actT, F.f_sb
                    rmsnorm_T(xmid_s[:L], L, gcol_ffn, h2T[:, :, 0:L], B=F)
                    fst = S.sb([NS * 2, 4096], F32, "s_fst")
                    S.dma("sp", fst, st_ffn[l])
                    carry_s = S.sb([128, 32, NS, 2], F32, "s_carry")
                    for q4 in range(4):
                        p = psf.next()
                        pv = p[:, 0:8 * NS * 2].rr("p (c r) -> p c r", c=8)
                        for c8 in range(8):
                            c = q4 * 8 + c8
                            S.mm(pv[:, c8, :], fst[:, c * 128:(c + 1) * 128], identf[:NS * 2, :NS * 2])
                        S.copy("act", carry_s[:, q4 * 8:(q4 + 1) * 8], pv.rr("p c (b t) -> p c b t", b=NS))
                    gtl = S.sb([32, 4096], F32, "s_gtl")
                    r_gx = Ring(S, 2, [128, NS, 10], F32, "s_gx")
                    r_ga = Ring(S, 2, [128, NS, 8], F32, "s_ga")
                    for s in range(16):
                        wg = F.wg.next()
                        wv = F.wv.next()
                        S.dma("sp", wg, wg_s[s])
                        S.dma("sp", wv, wv_s[s])
                        p = psf.next()
                        for k in range(8):
                            S.mm(p[:L, 0:256], h2T[:, k, 0:L], wg[:, k, :], start=(k == 0), stop=(k == 7))
                        S.copy("act", gtl[:, s * 256:(s + 1) * 256], p[:L, 0:256])
                        for c2 in range(2):
                            c = s * 2 + c2
                            g_ps = psf.next()
                            v_ps = psf.next()
                            for k in range(8):
                                S.mm(g_ps[:, 0:L], wg[:, k, c2 * 128:(c2 + 1) * 128], h2T[:, k, 0:L], start=(k == 0), stop=(k == 7))
                            for k in range(8):
                                S.mm(v_ps[:, 0:L], wv[:, k, c2 * 128:(c2 + 1) * 128], h2T[:, k, 0:L], start=(k == 0), stop=(k == 7))
                            gx = r_gx.next()
                            S.copy("act", gx[:, :, 2:10], g_ps[:, 0:L].rr("p (b i) -> p b i", b=NS))
                            S.copy("pool", gx[:, :, 0:2], carry_s[:, c])
                            ga = r_ga.next()
                            S.ts("pool", ga, gx[:, :, 0:8], fconvw[:, c, 0:1], fconvb[:, c:c + 1], ALU.mult, ALU.add)
                            S.stt(ga, gx[:, :, 1:9], fconvw[:, c, 1:2], ga, ALU.mult, ALU.add)
                            S.stt(ga, gx[:, :, 2:10], fconvw[:, c, 2:3], ga, ALU.mult, ALU.add)
                            S.act(ga, ga, AF.Gelu_apprx_tanh)
                            S.tt("dve", actT[:, c, 0:L], ga.rr("p b i -> p (b i)"), v_ps[:, 0:L], ALU.mult)
                    for b in range(NS):
                        S.dma("sp", ffn_s[l, b * 2:(b + 1) * 2, :], gtl[b * 8 + 6:b * 8 + 8, :])
                    for s in range(16):
                        wd = F.wd.next()
                        S.dma("sp", wd, wd_s[s])
                        for nh in range(2):
                            p = psf.next()
                            for c2 in range(2):
                                S.mm(p[:L], actT[:, s * 2 + c2, 0:L], wd[:, c2, nh * 512:(nh + 1) * 512],
                                     start=(c2 == 0), stop=(c2 == 1))
                            dst = f_sb[:L, 0, nh * 512:(nh + 1) * 512]
                            if s == 0:
                                S.copy("act", dst, p[:L])
                            else:
                                S.tt("dve", dst, dst, p[:L], ALU.add)
                    junk = F.junk.next()
                    ss = small()
                    S.act(junk[:L], f_sb[:L, 0, :], AF.Square, accum=ss[:L, 0:1])
                    S.ts("dve", ss[:L, 1:2], ss[:L, 0:1], 1.0 / 1024, RMS_EPS, ALU.mult, ALU.add)
                    S.act(ss[:L, 1:2], ss[:L, 1:2], AF.Sqrt)
                    S.recip(ss[:L, 2:3], ss[:L, 1:2])
                    S.stt(f_sb[:L, 0, :], f_sb[:L, 0, :], ss[:L, 2:3], gfpost_b[:L], ALU.mult, ALU.mult)
                    S.tt("pool", f_sb[:L, 0, :], f_sb[:L, 0, :], xmid_s[:L], ALU.add)
                    S.dma("sp", xdst, f_sb[:L, 0, :])

        for l in range(DEPTH):
            if l == 0:
                precast_in(0)
            load_layer_weights(l)
            precast_ffn(l)
            if l + 1 < DEPTH:
                precast_in(l + 1)
            last = (l == DEPTH - 1)

            def xdst(i, last=last):
                if last:
                    return y_p[i * 128:(i + 1) * 128, :]
                return xcur_tile(i)
            with S.scope():
                alloc_prompt_persist()
                nblk_ = (NT + NTB - 1) // NTB if not cfg.get("skip_prompt") else 0
                for blk in range(nblk_):
                    t0 = blk * NTB
                    ntb = min(NTB, NT - t0)
                    with S.scope():
                        alloc_mixer_rings()
                        load_cmp_weights(l)
                        for ti in range(t0, t0 + ntb):
                            xsrc = x_p[ti * 128:(ti + 1) * 128, :] if l == 0 else xcur_tile(ti)
                            mixer_tile_prompt(l, ti, xsrc)
                    if STAGE >= 8:
                        with S.scope():
                            alloc_ffn_bufs(ntb * 128)
                            ffn_block_prompt(l, blk, ntb, xdst)
                if STAGE >= 3 and not cfg.get("skip_prompt"):
                    with S.scope():
                        ssm_out_prompt(l)
            if NS > 0 and STAGE >= 9:
                sample_layer(l, last)
        print("SBUF peak bytes", S.sb_peak, "instr", {k: len(v) for k, v in S.prog.items()})
        S.emit()
    return nc, outs, consts


_CACHE = {}


def kernel(**inp):
    x_prompt = np.asarray(inp["x_prompt"], np.float32)
    x_sample = np.asarray(inp["x_sample"], np.float32)
    B, T, D = x_prompt.shape
    BS, TS, _ = x_sample.shape
    DEPTH = inp["w_in"].shape[0]
    cache_kv = np.asarray(inp["cache_nsa_kv"], np.float32)
    NPHYS = cache_kv.shape[1]
    page_table = np.asarray(inp["page_table"], np.int32)
    NPG = page_table.shape[1]
    n_cores = NCORES
    assert B == n_cores and BS % n_cores == 0 and TS == 8
    NS = BS // n_cores
    cfg = dict(T=T, DEPTH=DEPTH, NS=NS, NPG=NPG, NPHYS=NPHYS)
    key = (T, DEPTH, NS, NPG, NPHYS)
    if key not in _CACHE:
        _CACHE[key] = build(cfg)
    nc, outs, consts = _CACHE[key]
    WB = inp["state_nsa_win"].shape[2]
    cache2 = np.ascontiguousarray(cache_kv.reshape(DEPTH, NPHYS * 128, 256))
    st_win = np.asarray(inp["state_nsa_win"], np.float32).reshape(DEPTH, BS, WB, 128)
    st_conv = np.asarray(inp["state_ssd_conv"], np.float32)
    st_ssm = np.asarray(inp["state_ssm"], np.float32)
    st_pool = np.asarray(inp["state_pool"], np.float32)
    st_ffn = np.asarray(inp["state_ffn_conv"], np.float32)
    wnames = ["norm_mix_pre", "w_in", "ssd_conv_w", "ssd_conv_b", "ssd_dt_bias", "ssd_a_log", "ssd_d", "ssd_norm",
              "pool_w", "pool_scale", "nsa_pe_k", "nsa_pe_v", "nsa_w1_k", "nsa_w1_v", "nsa_w2_k", "nsa_w2_v", "w_out",
              "norm_mix_post", "norm_ffn_pre", "ffn_w_gate", "ffn_w_val", "ffn_conv_w", "ffn_conv_b", "ffn_w_down",
              "norm_ffn_post"]
    shared = {n: np.ascontiguousarray(np.asarray(inp[n], np.float32)) for n in wnames}
    shared.update(consts)
    shared["cache"] = cache2
    in_maps = []
    for c in range(n_cores):
        sl = slice(c * NS, (c + 1) * NS)
        m = dict(shared)
        m["x_p"] = np.ascontiguousarray(x_prompt[c])
        m["x_s"] = np.ascontiguousarray(x_sample[sl].reshape(NS * 8, D))
        m["pt"] = np.ascontiguousarray(page_table[sl].reshape(-1))
        m["st_win"] = np.ascontiguousarray(st_win[:, sl])
        m["st_conv"] = np.ascontiguousarray(st_conv[:, sl].reshape(DEPTH, NS * 3, 1024))
        m["st_ssm"] = np.ascontiguousarray(st_ssm[:, sl])
        m["st_pool"] = np.ascontiguousarray(st_pool[:, sl].reshape(DEPTH, NS * 15, 256))
        m["st_ffn"] = np.ascontiguousarray(st_ffn[:, sl].reshape(DEPTH, NS * 2, 4096))
        in_maps.append(m)
    res = run_bass_kernel_spmd(nc, in_maps, core_ids=list(range(n_cores))).results
    WK = min(512, T)

    def g(name):
        return [np.asarray(r[name], np.float32) for r in res]

    y_p = np.stack(g("y_p"), 0)
    y_s = np.concatenate([a.reshape(NS, 8, D) for a in g("y_s")], 0)
    kv_p = np.stack([a.reshape(DEPTH, T, 4, 64) for a in g("kv_p")], 1)
    kv_s = np.concatenate([a.reshape(DEPTH, NS, 8, 4, 64) for a in g("kv_s")], 1)
    win_p = np.stack([a.reshape(DEPTH, WK, 2, 64) for a in g("win_p")], 1)
    win_s = np.concatenate([a.reshape(DEPTH, NS, WB, 2, 64) for a in g("win_s")], 1)
    conv_p = np.stack(g("conv_p"), 1)
    conv_s = np.concatenate([a.reshape(DEPTH, NS, 3, 1024) for a in g("conv_s")], 1)
    ssm_p = np.stack(g("ssm_p"), 1)
    ssm_s = np.concatenate(g("ssm_s"), 1)
    pool_p = np.stack(g("pool_p"), 1)
    pool_s = np.concatenate([a.reshape(DEPTH, NS, 15, 256) for a in g("pool_s")], 1)
    ffn_p = np.stack(g("ffn_p"), 1)
    ffn_s = np.concatenate([a.reshape(DEPTH, NS, 2, 4096) for a in g("ffn_s")], 1)
    return (y_p, y_s, kv_p, kv_s, win_p, win_s, conv_p, conv_s, ssm_p, ssm_s, pool_p, pool_s, ffn_p, ffn_s)
```

```python
import numpy as np
import contextlib
import concourse.bass as bass
import concourse.mybir as mybir
from concourse.bass_utils import run_bass_kernel_spmd

F32 = mybir.dt.float32
BF16 = mybir.dt.bfloat16
I32 = mybir.dt.int32
ALU = mybir.AluOpType
AF = mybir.ActivationFunctionType
AX = mybir.AxisListType

NDS = 8
SBUF_BYTES = 229376
SBUF_BASE = 16640


class Res:
    __slots__ = ("w", "r", "name")

    def __init__(self, name=""):
        self.w = None
        self.r = {}
        self.name = name


class V:
    __slots__ = ("ap", "res")

    def __init__(self, ap, res=()):
        self.ap = ap
        self.res = tuple(res)

    def __getitem__(self, key):
        return V(self.ap[key], self.res)

    def rr(self, s, **kw):
        return V(self.ap.rearrange(s, **kw), self.res)

    def bc(self, shape):
        return V(self.ap.to_broadcast(list(shape)), self.res)

    def us(self, axis):
        return V(self.ap.unsqueeze(axis), self.res)

    def bitcast(self, dt):
        return V(self.ap.bitcast(dt), self.res)

    def with_res(self, res):
        return V(self.ap, res)

    @property
    def shape(self):
        return self.ap.shape


def _resources(vs):
    out = []
    for v in vs:
        if isinstance(v, V):
            for r in v.res:
                if r not in out:
                    out.append(r)
    return out


class Sched:
    def __init__(self, nc, es, same_sync=True):
        self.nc = nc
        self.es = es
        self.eng = dict(pe=nc.tensor, act=nc.scalar, dve=nc.vector, pool=nc.gpsimd, sp=nc.sync)
        self.prog = {k: [] for k in self.eng}
        self.sem = {}
        self.val = {}
        for k in self.eng:
            self.sem[k] = es.enter_context(nc.semaphore("s_" + k))
            self.val[k] = 0
        self.dq = {}
        for q in ("sp", "pool", "act", "pre"):
            names = []
            for i in range(NDS):
                n = "d_%s%d" % (q, i)
                self.sem[n] = es.enter_context(nc.semaphore(n))
                self.val[n] = 0
                names.append(n)
            self.dq[q] = [names, 0]
        self.seen = {k: {} for k in self.eng}
        self.same_sync = same_sync
        self.ntens = 0
        self.sb_top = SBUF_BASE
        self.sb_peak = 0

    def sb(self, shape, dtype, name=None):
        self.ntens += 1
        name = "%s_%d" % (name or "t", self.ntens)
        esz = 2 if dtype == BF16 else 4
        nbytes = esz
        for d in shape[1:]:
            nbytes *= d
        off = (self.sb_top + 31) // 32 * 32
        self.sb_top = off + nbytes
        self.sb_peak = max(self.sb_peak, self.sb_top)
        assert self.sb_top <= SBUF_BYTES, "SBUF overflow: %s needs %d at %d" % (name, nbytes, off)
        t = self.nc.alloc_sbuf_tensor_at(name, list(shape), dtype, offset=off)
        return V(t[tuple(slice(None) for _ in shape)], (Res(name),))

    @contextlib.contextmanager
    def scope(self):
        self.barrier()
        top = self.sb_top
        try:
            yield
        finally:
            self.barrier()
            self.sb_top = top

    def ps(self, shape, dtype, name=None):
        self.ntens += 1
        name = "%s_%d" % (name or "p", self.ntens)
        t = self.es.enter_context(self.nc.psum_tensor(name, list(shape), dtype))
        return V(t[tuple(slice(None) for _ in shape)], (Res(name),))

    def op(self, e, fn, outs, ins, dma=False, nobar=False):
        need = {}
        seen = self.seen[e]

        def add(s, v):
            if s == e and not dma:
                if e == "pe" or (not self.same_sync and e != "pool"):
                    return
            if seen.get(s, 0) >= v:
                return
            if need.get(s, 0) < v:
                need[s] = v

        rin = _resources(ins)
        rout = _resources(outs)
        for r in rin:
            if r.w is not None:
                add(*r.w)
        for r in rout:
            if r.w is not None:
                add(*r.w)
            for s, v in r.r.items():
                add(s, v)
        if dma:
            qk = "pre" if nobar else e
            names, i = self.dq[qk]
            s = names[i % len(names)]
            self.dq[qk][1] += 1
            if self.val[s] > 0:
                add(s, self.val[s])
            self.val[s] += 16
            ev = (s, self.val[s])
            inc = 16
        else:
            self.val[e] += 1
            s = e
            ev = (e, self.val[e])
            inc = 1
        for s2, v in need.items():
            seen[s2] = v
        self.prog[e].append((list(need.items()), fn, s, inc))
        for r in rin:
            if r.r.get(ev[0], 0) < ev[1]:
                r.r[ev[0]] = ev[1]
        for r in rout:
            r.w = ev
            r.r = {}

    def barrier(self):
        for e in self.eng:
            waits = []
            for s, v in self.val.items():
                if v > 0 and self.seen[e].get(s, 0) < v and s != e and not s.startswith("d_pre"):
                    waits.append((s, v))
                    self.seen[e][s] = v
            if waits:
                self.prog[e].append((waits, None, None, 0))

    def emit(self):
        waits = [(s, v) for s, v in self.val.items() if v > 0 and s != "sp"]
        self.prog["sp"].append((waits, None, None, 0))
        sem = self.sem
        prog = self.prog

        def mk(ename):
            def body(eobj):
                for waits, fn, s, inc in prog[ename]:
                    for (ws, wv) in waits:
                        eobj.wait_ge(sem[ws], wv)
                    if fn is not None:
                        ins = fn(eobj)
                        ins.then_inc(sem[s], inc)
            return body

        with self.nc.Block() as block:
            block.tensor(mk("pe"))
            block.scalar(mk("act"))
            block.vector(mk("dve"))
            block.gpsimd(mk("pool"))
            block.sync(mk("sp"))

    def dma(self, q, out, in_, nobar=False, **kw):
        self.op(q, lambda e: e.dma_start(out=out.ap, in_=in_.ap, **kw), [out], [in_], dma=True, nobar=nobar)

    def mm(self, out, lhsT, rhs, start=True, stop=True):
        self.op("pe", lambda e: e.matmul(out.ap, lhsT.ap, rhs.ap, start=start, stop=stop), [out], [lhsT, rhs])

    def tr(self, out, in_, ident):
        self.op("pe", lambda e: e.transpose(out.ap, in_.ap, ident.ap), [out], [in_, ident])

    def act(self, out, in_, func, bias=None, scale=None, accum=None):
        ins = [in_] + [b for b in (bias, scale) if isinstance(b, V)]
        outs = [out] + ([accum] if accum is not None else [])

        def fn(e):
            kw = {}
            if bias is not None:
                kw["bias"] = bias.ap if isinstance(bias, V) else bias
            if scale is not None:
                kw["scale"] = scale.ap if isinstance(scale, V) else scale
            if accum is not None:
                kw["accum_out"] = accum.ap
            return e.activation(out=out.ap, in_=in_.ap, func=func, **kw)
        self.op("act", fn, outs, ins)

    def tt(self, e, out, in0, in1, op):
        self.op(e, lambda en: en.tensor_tensor(out.ap, in0.ap, in1.ap, op), [out], [in0, in1])

    def ts(self, e, out, in0, s1, s2=None, op0=ALU.mult, op1=None, accum=None):
        ins = [in0] + [b for b in (s1, s2) if isinstance(b, V)]
        outs = [out] + ([accum] if accum is not None else [])

        def fn(en):
            a1 = s1.ap if isinstance(s1, V) else s1
            a2 = s2.ap if isinstance(s2, V) else s2
            kw = {}
            if op1 is not None:
                kw["op1"] = op1
            if accum is not None:
                kw["accum_out"] = accum.ap
            return en.tensor_scalar(out.ap, in0.ap, a1, a2, op0, **kw)
        self.op(e, fn, outs, ins)

    def stt(self, out, in0, scalar, in1, op0, op1):
        ins = [in0, in1] + ([scalar] if isinstance(scalar, V) else [])

        def fn(en):
            sc = scalar.ap if isinstance(scalar, V) else scalar
            return en.scalar_tensor_tensor(out.ap, in0.ap, sc, in1.ap, op0, op1)
        self.op("dve", fn, [out], ins)

    def copy(self, e, out, in_):
        if e == "act":
            self.op("act", lambda en: en.copy(out.ap, in_.ap), [out], [in_])
        else:
            self.op(e, lambda en: en.tensor_copy(out.ap, in_.ap), [out], [in_])

    def memset(self, e, out, val):
        self.op(e, lambda en: en.memset(out.ap, val), [out], [])

    def reduce(self, out, in_, op, axis=AX.X):
        self.op("dve", lambda en: en.tensor_reduce(out.ap, in_.ap, axis, op), [out], [in_])

    def max8(self, out, in_):
        self.op("dve", lambda en: en.max(out.ap, in_.ap), [out], [in_])

    def match_replace(self, out, to_replace, values, imm):
        self.op("dve", lambda en: en.match_replace(out.ap, to_replace.ap, values.ap, imm), [out], [to_replace, values])

    def recip(self, out, in_):
        self.op("dve", lambda en: en.reciprocal(out.ap, in_.ap), [out], [in_])


class Ring:
    def __init__(self, S, n, shape, dtype, name, psum=False):
        self.t = [(S.ps if psum else S.sb)(shape, dtype, name) for _ in range(n)]
        self.i = 0

    def next(self):
        t = self.t[self.i % len(self.t)]
        self.i += 1
        return t

import ml_dtypes

D_MODEL = 1024
IN_W = 2452
OFF_XBC = 512
OFF_DT = 1536
OFF_POOL = 1544
OFF_Q = 1800
OFF_KV = 2056
NEG = -30000.0
RMS_EPS = 1e-6
NCORES = 8


def make_consts(cfg):
    T = cfg["T"]
    NT = T // 128
    NS = cfg["NS"]
    LS = NS * 8
    c = {}
    c["c_identb"] = np.eye(128).astype(ml_dtypes.bfloat16)
    c["c_identf"] = np.eye(128, dtype=np.float32)
    k = np.arange(128)
    c["c_incl"] = (k[:, None] <= k[None, :]).astype(np.float32)
    c["c_after"] = (k[:, None] > k[None, :]).astype(np.float32)
    c["c_diag"] = np.where(k[None, :] <= k[:, None], 0.0, NEG).astype(np.float32)
    c["c_far"] = np.where(k[None, :] > k[:, None], 0.0, NEG).astype(np.float32)
    half = 8
    inv = 1.0 / (500000.0 ** (np.arange(half, dtype=np.float32) / half))
    pos = np.arange(T, dtype=np.float32)
    ang = pos[:, None] * inv[None, :]
    cs = np.concatenate([np.cos(ang), np.sin(ang)], axis=1).astype(np.float32)
    c["c_cs_p"] = np.ascontiguousarray(cs.reshape(NT, 128, 16).transpose(1, 0, 2))
    r = np.arange(8)
    c["c_cmpdiag"] = np.where(k[:, None] >= 16 * (r[None, :] - 1) + 31, 0.0, NEG).astype(np.float32)
    n_slc = T // 64
    addc = np.zeros((128, NT, max(n_slc, 8)), np.float32)
    for ti in range(NT):
        tpos = 128 * ti + k
        cur = tpos // 64
        j = np.arange(n_slc)
        forced = (j[None, :] == 0) | (j[None, :] == cur[:, None]) | (j[None, :] == cur[:, None] - 1)
        causal = (j[None, :] * 64 <= tpos[:, None])
        a = np.where(forced, 1e30, 0.0)
        a = np.where(causal, a, -1e30)
        addc[:, ti, :n_slc] = a
    c["c_addc_p"] = addc
    n_cmp = T // 16 - 1
    nn = np.arange(128)
    jj = np.arange(max(n_slc, 8))
    ov = ((16 * nn[:, None] < 64 * jj[None, :] + 64) & (16 * nn[:, None] + 32 > 64 * jj[None, :]))
    ov = ov & (nn[:, None] < n_cmp)
    c["c_ov_p"] = ov.astype(ml_dtypes.bfloat16)
    rc = np.zeros((64, 4, 16), np.float32)
    for g, w in enumerate((2, 4, 8, 16)):
        rc[:, g, :] = 1.0 / np.minimum(np.arange(16) + 1, w)
    c["c_rc"] = rc
    c["c_ones"] = np.ones((128, 128), np.float32)
    if NS > 0:
        NPG = cfg["NPG"]
        past = NPG * 128
        WB = min(512, past)
        r = np.arange(LS)
        sq = r // 8
        same = sq[:, None] == sq[None, :]
        c["c_incl_s"] = ((r[:, None] <= r[None, :]) & same).astype(np.float32)
        c["c_after_s"] = ((r[:, None] > r[None, :]) & same).astype(np.float32)
        sm_ = np.zeros((LS, NS, 128), np.float32)
        rm_ = np.zeros((LS, NS), np.float32)
        for b in range(NS):
            sm_[b * 8:(b + 1) * 8, b, :] = 1.0
            rm_[b * 8:(b + 1) * 8, b] = 1.0
        c["c_seqmask_s"] = sm_
        c["c_rowmask_s"] = rm_
        pos_s = (past + (r % 8)).astype(np.float32)
        ang_s = pos_s[:, None] * inv[None, :]
        c["c_cs_s"] = np.concatenate([np.cos(ang_s), np.sin(ang_s)], axis=1).astype(np.float32)
        ri = np.arange(32) % 8
        c["c_R"] = (ri[:, None] == ri[None, :]).astype(np.float32)
        NSLC_S = past // 64 + 1
        NV = past // 16 - 1
        cur = past // 64
        j = np.arange(NSLC_S)
        forced = (j == 0) | (j == cur) | (j == cur - 1)
        c["c_addc_s"] = np.broadcast_to(np.where(forced, 1e30, 0.0).astype(np.float32)[None, :], (32, NSLC_S)).copy()
        NTC = (NV + 127) // 128
        nn2 = np.arange(NTC * 128)
        ov2 = ((16 * nn2[:, None] < 64 * j[None, :] + 64) & (16 * nn2[:, None] + 32 > 64 * j[None, :])) & (nn2[:, None] < NV)
        c["c_ov_s"] = np.ascontiguousarray(ov2.reshape(NTC, 128, NSLC_S).transpose(1, 0, 2)).astype(ml_dtypes.bfloat16)
        cc = np.arange(64)
        c["c_tokmask_s"] = np.where(cc[None, :] <= ri[:, None], 0.0, NEG).astype(np.float32)
        jw = np.arange(WB + 8)
        stored = jw[None, :] < WB
        valid = np.where(stored, jw[None, :] > ri[:, None] + WB - 512, (jw[None, :] - WB) <= ri[:, None])
        c["c_winmask_s"] = np.where(valid, 0.0, NEG).astype(np.float32)
        c["c_pidx"] = np.broadcast_to(np.arange(128, dtype=np.float32)[:, None], (128, NS * NPG)).copy()
    return c


class Obj:
    pass


def build(cfg):
    T = cfg["T"]
    NT = T // 128
    DEPTH = cfg["DEPTH"]
    NS = cfg["NS"]
    WK = min(512, T)
    NCMP = T // 16 - 1
    NSLC = T // 64
    consts = make_consts(cfg)
    STAGE = cfg.get("stage", 99)
    VAR = cfg.get("var", 0)
    SST = cfg.get("sst", 99)

    nc = bass.Bass("TRN2", target_bir_lowering=False)
    es = contextlib.ExitStack()
    outs = []
    with es:
        S = Sched(nc, es, same_sync=cfg.get("same_sync", True))

        def din(name, shape, dt=F32):
            return V(nc.dram_tensor(name, list(shape), dt, kind="ExternalInput").ap())

        def dout(name, shape, dt=F32):
            outs.append(name)
            return V(nc.dram_tensor(name, list(shape), dt, kind="ExternalOutput").ap())

        def dscr(name, shape, dt=F32):
            return nc.dram_tensor(name, list(shape), dt, kind="Internal").ap()

        x_p = din("x_p", [T, D_MODEL])
        W = {}
        for nm, shp in [("norm_mix_pre", [DEPTH, 1024]), ("w_in", [DEPTH, 1024, IN_W]), ("ssd_conv_w", [DEPTH, 4, 1024]),
                        ("ssd_conv_b", [DEPTH, 1024]), ("ssd_dt_bias", [DEPTH, 8]), ("ssd_a_log", [DEPTH, 8]),
                        ("ssd_d", [DEPTH, 8]), ("ssd_norm", [DEPTH, 512]), ("pool_w", [DEPTH, 4, 64, 64]),
                        ("pool_scale", [DEPTH, 256]), ("nsa_pe_k", [DEPTH, 32, 64]), ("nsa_pe_v", [DEPTH, 32, 64]),
                        ("nsa_w1_k", [DEPTH, 32, 64, 128]), ("nsa_w1_v", [DEPTH, 32, 64, 128]),
                        ("nsa_w2_k", [DEPTH, 128, 64]), ("nsa_w2_v", [DEPTH, 128, 64]), ("w_out", [DEPTH, 1024, 1024]),
                        ("norm_mix_post", [DEPTH, 1024]), ("norm_ffn_pre", [DEPTH, 1024]),
                        ("ffn_w_gate", [DEPTH, 1024, 4096]), ("ffn_w_val", [DEPTH, 1024, 4096]),
                        ("ffn_conv_w", [DEPTH, 3, 4096]), ("ffn_conv_b", [DEPTH, 4096]),
                        ("ffn_w_down", [DEPTH, 4096, 1024]), ("norm_ffn_post", [DEPTH, 1024])]:
            W[nm] = din(nm, shp)
        CD = {}
        for nm, arr in consts.items():
            CD[nm] = din(nm, list(arr.shape), BF16 if arr.dtype == ml_dtypes.bfloat16 else F32)

        y_p = dout("y_p", [T, D_MODEL])
        kv_p = dout("kv_p", [DEPTH, T, 256])
        win_p = dout("win_p", [DEPTH, WK, 128])
        conv_p = dout("conv_p", [DEPTH, 3, 1024])
        ssm_p = dout("ssm_p", [DEPTH, 8, 64, 128])
        pool_p = dout("pool_p", [DEPTH, 15, 256])
        ffn_p = dout("ffn_p", [DEPTH, 2, 4096])

        xcur_ap = dscr("xcur", [T, D_MODEL])
        xcur_res = [Res("xcur%d" % i) for i in range(NT)]

        def xcur_tile(i):
            return V(xcur_ap[i * 128:(i + 1) * 128, :], (xcur_res[i],))

        def cload(name, shape, dt=F32, src=None, q="sp"):
            t = S.sb(shape, dt, name)
            S.dma(q, t, src if src is not None else CD[name])
            return t

        identb = cload("c_identb", [128, 128], BF16)
        identf = cload("c_identf", [128, 128])
        incl = cload("c_incl", [128, 128])
        after = cload("c_after", [128, 128])
        diagm = cload("c_diag", [128, 128])
        farm = cload("c_far", [128, 128])
        cs_p = cload("c_cs_p", [128, NT, 16])
        cmpdiag = cload("c_cmpdiag", [128, 8])
        addc_p = cload("c_addc_p", [128, NT, max(NSLC, 8)])
        ov_p = cload("c_ov_p", [128, max(NSLC, 8)], BF16)
        rc_t = cload("c_rc", [64, 4, 16])
        ones = cload("c_ones", [128, 128])

        psf = Ring(S, 6, [128, 512], F32, "psf", psum=True)
        psb = Ring(S, 2, [128, 1024], BF16, "psb", psum=True)

        win_sb = S.sb([128, 8, IN_W], BF16, "win")
        wout_sb = S.sb([128, 8, 1024], BF16, "wout")
        gcol_pre = S.sb([128, 8], F32, "gpre")
        gcol_ffn = S.sb([128, 8], F32, "gffn")
        gpost_b = S.sb([128, 1024], F32, "gpost")
        gfpost_b = S.sb([128, 1024], F32, "gfpost")
        mixg = S.sb([128, 8], F32, "mixg")
        convw = S.sb([128, 8, 4], F32, "convw")
        convb = S.sb([128, 8], F32, "convb")
        dtb_b = S.sb([128, 8], F32, "dtb")
        a_b = S.sb([128, 8], F32, "ab")
        dsk_b = S.sb([128, 8], F32, "dsk")
        poolw_f = S.sb([64, 4, 64], F32, "poolwf")
        pscale_b = S.sb([64, 256], F32, "pscale")
        poolw_sb = S.sb([64, 4, 64], BF16, "poolw")
        CW = Obj()

        def load_cmp_weights(l):
            CW.peT_k = S.sb([64, 32], BF16, "pek")
            CW.peT_v = S.sb([64, 32], BF16, "pev")
            CW.w1k = S.sb([64, 32, 128], BF16, "w1k")
            CW.w1v = S.sb([64, 32, 128], BF16, "w1v")
            CW.w2k = S.sb([128, 64], BF16, "w2k")
            CW.w2v = S.sb([128, 64], BF16, "w2v")
            CW.cbias_k = S.sb([128, 1], F32, "cbk")
            CW.cbias_v = S.sb([128, 1], F32, "cbv")
            pef = small()
            S.dma("sp", pef[0:64, 0:32], W["nsa_pe_k"][l].rr("l d -> d l"), allow_slow_non_contiguous=True)
            S.copy("dve", CW.peT_k, pef[0:64, 0:32])
            pef = small()
            S.dma("sp", pef[0:64, 0:32], W["nsa_pe_v"][l].rr("l d -> d l"), allow_slow_non_contiguous=True)
            S.copy("dve", CW.peT_v, pef[0:64, 0:32])
            S.dma("pool", CW.w1k, W["nsa_w1_k"][l].rr("l d e -> d l e"))
            S.dma("pool", CW.w1v, W["nsa_w1_v"][l].rr("l d e -> d l e"))
            S.dma("pool", CW.w2k, W["nsa_w2_k"][l])
            S.dma("pool", CW.w2v, W["nsa_w2_v"][l])
            for (w1, peT, cb) in ((CW.w1k, CW.peT_k, CW.cbias_k), (CW.w1v, CW.peT_v, CW.cbias_v)):
                p = psf.next()
                for li in range(32):
                    S.mm(p[:, 0:1], w1[:, li, :], peT[:, li:li + 1], start=(li == 0), stop=(li == 31))
                S.copy("act", cb, p[:, 0:1])
        fconvw = S.sb([128, 32, 3], F32, "fconvw")
        fconvb = S.sb([128, 32], F32, "fconvb")

        def load_layer_weights(l):
            S.dma("sp", win_sb, winpre)
            S.dma("act", wout_sb, woutpre)
            S.dma("sp", gcol_pre, W["norm_mix_pre"][l].rr("(k p) -> p k", p=128), allow_slow_non_contiguous=True)
            S.dma("sp", gcol_ffn, W["norm_ffn_pre"][l].rr("(k p) -> p k", p=128), allow_slow_non_contiguous=True)
            S.dma("sp", gpost_b, V(W["norm_mix_post"].ap[l].partition_broadcast(128)))
            S.dma("sp", gfpost_b, V(W["norm_ffn_post"].ap[l].partition_broadcast(128)))
            S.memset("pool", mixg, 1.0)
            S.dma("sp", mixg[:, 0:4], W["ssd_norm"][l].rr("(k p) -> p k", p=128), allow_slow_non_contiguous=True)
            for t_ in range(4):
                S.dma("sp", convw[:, :, t_], W["ssd_conv_w"][l, t_].rr("(c p) -> p c", p=128), allow_slow_non_contiguous=True)
            S.dma("sp", convb, W["ssd_conv_b"][l].rr("(c p) -> p c", p=128), allow_slow_non_contiguous=True)
            S.dma("sp", dtb_b, V(W["ssd_dt_bias"].ap[l].partition_broadcast(128)))
            S.dma("sp", a_b, V(W["ssd_a_log"].ap[l].partition_broadcast(128)))
            S.act(a_b, a_b, AF.Exp)
            S.ts("dve", a_b, a_b, -1.0, None, ALU.mult)
            S.dma("sp", dsk_b, V(W["ssd_d"].ap[l].partition_broadcast(128)))
            S.dma("sp", poolw_f, W["pool_w"][l].rr("g c d -> c g d"))
            S.dma("sp", pscale_b, V(W["pool_scale"].ap[l].partition_broadcast(64)))
            S.tt("dve", poolw_sb, poolw_f, pscale_b.rr("p (g d) -> p g d", g=4), ALU.mult)
            for t_ in range(3):
                S.dma("sp", fconvw[:, :, t_], W["ffn_conv_w"][l, t_].rr("(c p) -> p c", p=128), allow_slow_non_contiguous=True)
            S.dma("sp", fconvb, W["ffn_conv_b"][l].rr("(c p) -> p c", p=128), allow_slow_non_contiguous=True)

        r_sm = Ring(S, 24, [128, 32], F32, "sm")
        NTB = min(4, NT)
        xmid_ap = dscr("xmid", [T, D_MODEL])
        xmid_res = [Res("xmid%d" % i) for i in range(NT)]

        def xmid_tile(i):
            return V(xmid_ap[i * 128:(i + 1) * 128, :], (xmid_res[i],))

        wg_s_ap = dscr("wg_s", [16, 128, 8, 256], BF16)
        wv_s_ap = dscr("wv_s", [16, 128, 8, 256], BF16)
        wd_s_ap = dscr("wd_s", [16, 128, 2, 1024], BF16)
        wg_s = [V(wg_s_ap[i], (Res("wgs%d" % i),)) for i in range(16)]
        wv_s = [V(wv_s_ap[i], (Res("wvs%d" % i),)) for i in range(16)]
        wd_s = [V(wd_s_ap[i], (Res("wds%d" % i),)) for i in range(16)]

        win_s_ap = dscr("winpre_s", [128, 8, IN_W], BF16)
        wout_s_ap = dscr("woutpre_s", [128, 8, 1024], BF16)
        winpre = V(win_s_ap, (Res("winpre"),))
        woutpre = V(wout_s_ap, (Res("woutpre"),))

        def precast_in(l):
            S.dma("pool", winpre, W["w_in"][l].rr("(k p) n -> p k n", p=128), nobar=True)
            S.dma("pool", woutpre, W["w_out"][l].rr("(k p) n -> p k n", p=128), nobar=True)

        def precast_ffn(l):
            for s_ in range(16):
                S.dma("pool", wg_s[s_], W["ffn_w_gate"][l, :, s_ * 256:(s_ + 1) * 256].rr("(k p) n -> p k n", p=128), nobar=True)
                S.dma("pool", wv_s[s_], W["ffn_w_val"][l, :, s_ * 256:(s_ + 1) * 256].rr("(k p) n -> p k n", p=128), nobar=True)
            for s_ in range(16):
                S.dma("pool", wd_s[s_], W["ffn_w_down"][l, s_ * 256:(s_ + 1) * 256, :].rr("(c p) n -> p c n", p=128), nobar=True)

        R = Obj()
        F = Obj()
        PP = Obj()

        def alloc_prompt_persist():
            PP.kcT = S.sb([64, T], BF16, "kcT")
            PP.vcT = S.sb([64, T], BF16, "vcT")
            PP.ksT = S.sb([64, T], BF16, "ksT")
            PP.kwT = S.sb([64, T], BF16, "kwT")
            PP.vs_tok = S.sb([128, NT, 64], BF16, "vstok")
            PP.vw_tok = S.sb([128, NT, 64], BF16, "vwtok")
            PP.geluT_k = S.sb([128, 128], BF16, "gelk")
            PP.geluT_v = S.sb([128, 128], BF16, "gelv")
            PP.ckT = S.sb([64, 128], BF16, "ckT")
            PP.cv = S.sb([128, 64], BF16, "cv")
            PP.ST = S.sb([128, 512], F32, "ST")
            PP.STbf = S.sb([128, 512], BF16, "STbf")
            PP.cprev = S.sb([128, 8, 3], F32, "cprev")
            PP.pprev = S.sb([64, 4, 15], F32, "pprev")
            PP.carry = S.sb([128, 32, 2], F32, "carry")

        def alloc_mixer_rings():
            R.x = Ring(S, 1, [128, 1024], F32, "xt")
            R.junk = Ring(S, 1, [128, 1024], BF16, "junk")
            R.xn = Ring(S, 1, [128, 1024], BF16, "xn")
            R.hT = Ring(S, 1, [128, 8, 128], BF16, "hT")
            R.ext = Ring(S, 1, [128, 8, 1, 131], F32, "ext")
            R.acc = Ring(S, 1, [128, 8, 128], F32, "cacc")
            R.actbf = Ring(S, 1, [128, 8, 128], BF16, "actbf")
            R.xsB = Ring(S, 1, [128, 768], BF16, "xsB")
            R.xc = Ring(S, 2, [128, 512], BF16, "xc")
            R.rhsD = Ring(S, 1, [128, 4, 128], F32, "rhsD")
            R.Dexp = Ring(S, 1, [128, 4, 128], F32, "Dexp")
            R.scm = Ring(S, 1, [128, 2, 128], F32, "scm")
            R.MT = Ring(S, 1, [128, 8, 128], BF16, "MT")
            R.y = Ring(S, 1, [128, 512], F32, "y")
            R.t512 = Ring(S, 3, [128, 512], F32, "t512")
            R.pext = Ring(S, 1, [64, 4, 1, 143], F32, "pext")
            R.ps2 = Ring(S, 1, [64, 4, 1, 142], F32, "ps2")
            R.ps4 = Ring(S, 1, [64, 3, 1, 140], F32, "ps4")
            R.ps8 = Ring(S, 1, [64, 2, 1, 136], F32, "ps8")
            R.ps16 = Ring(S, 1, [64, 1, 1, 128], F32, "ps16")
            R.pooled = Ring(S, 1, [64, 4, 128], BF16, "pooled")
            R.mix = Ring(S, 1, [128, 1024], BF16, "mix")
            R.mixT = Ring(S, 1, [128, 8, 128], BF16, "mixT")
            R.qf = Ring(S, 1, [128, 256], F32, "qf")
            R.rows = Ring(S, 1, [128, 384], F32, "rows")
            R.rt = Ring(S, 1, [128, 4, 4, 8], F32, "ropet")
            R.qbf = Ring(S, 1, [128, 256], BF16, "qbf")
            R.kvbf = Ring(S, 1, [128, 384], BF16, "kvbf")
            R.qT = Ring(S, 1, [64, 4, 128], BF16, "qT")
            R.gates = Ring(S, 2, [128, 12], F32, "gates")
            R.ssb = Ring(S, 1, [128, max(T, 1280)], F32, "ssb")
            R.ebf = Ring(S, 1, [128, max(T, 512)], BF16, "ebf")
            R.pT = Ring(S, 1, [128, max(NT, 5), 128], BF16, "pT")
            R.e32 = Ring(S, 2, [128, 128], F32, "e32")
            R.pn = Ring(S, 2, [128, 128], BF16, "pn")
            R.pnT = Ring(S, 2, [128, 128], BF16, "pnT")
            R.oacc = Ring(S, 1, [128, 256], F32, "oacc")
            R.imp = Ring(S, 3, [128, max(NSLC, 8)], F32, "imp")
            R.selm = Ring(S, 1, [128, max(NSLC, 8)], F32, "selm")
            R.xo = Ring(S, 1, [128, 1024], F32, "xo")

        def alloc_ffn_bufs(Ltot):
            F.h2T = S.sb([128, 8, Ltot], BF16, "h2T")
            F.actT = S.sb([128, 32, Ltot], BF16, "actT")
            F.wg = Ring(S, 2, [128, 8, 256], BF16, "wg")
            F.wv = Ring(S, 2, [128, 8, 256], BF16, "wv")
            F.wd = Ring(S, 2, [128, 2, 1024], BF16, "wd")
            F.gext = Ring(S, 2, [128, Ltot + 2], F32, "gext")
            F.gacc = Ring(S, 2, [128, Ltot], F32, "gacc")
            F.f_sb = S.sb([128, (Ltot + 127) // 128, 1024], F32, "fsb")
            F.xin = Ring(S, 1, [128, 1024], F32, "xin")
            F.junk = Ring(S, 1, [128, 1024], BF16, "fjunk")
            F.xn = Ring(S, 1, [128, 1024], BF16, "fxn")
            F.gts = Ring(S, 1, [128, 256], F32, "gts")

        def small():
            return r_sm.next()

        DBG = cfg.get("debug", False)

        def dbg(name, v, dt=F32):
            if not DBG:
                return
            shp = list(v.shape)
            o = dout("dbg_" + name, shp, dt)
            S.dma("sp", o, v)

        def rmsnorm_T(src, L, gcol, hT_dst, D=1024, B=None):
            B = B or R
            junk = B.junk.next()
            ss = small()
            S.act(junk[:L, :D], src, AF.Square, accum=ss[:L, 0:1])
            S.ts("dve", ss[:L, 1:2], ss[:L, 0:1], 1.0 / D, RMS_EPS, ALU.mult, ALU.add)
            S.act(ss[:L, 1:2], ss[:L, 1:2], AF.Sqrt)
            S.recip(ss[:L, 2:3], ss[:L, 1:2])
            xn = B.xn.next()
            S.act(xn[:L, :D], src, AF.Copy, scale=ss[:L, 2:3])
            nk = D // 128
            pT = psb.next()
            pTv = pT.rr("p (k l) -> p k l", k=8)
            for k in range(nk):
                S.tr(pTv[:, k, :L], xn[:L, k * 128:(k + 1) * 128], identb[:L, :L])
            S.tt("dve", hT_dst, pTv[:, 0:nk, :L], gcol.us(2).bc([128, nk, L]), ALU.mult)

        def rope(vw, H, cs, L):
            rt = R.rt.next()
            cosb = cs[:, 0:8].us(1).bc([L, H, 8])
            sinb = cs[:, 8:16].us(1).bc([L, H, 8])
            x1 = vw[:, :, 0:8]
            x2 = vw[:, :, 8:16]
            S.tt("pool", rt[:L, 0, 0:H, :], x1, cosb, ALU.mult)
            S.tt("pool", rt[:L, 1, 0:H, :], x2, sinb, ALU.mult)
            S.tt("pool", rt[:L, 2, 0:H, :], x2, cosb, ALU.mult)
            S.tt("pool", rt[:L, 3, 0:H, :], x1, sinb, ALU.mult)
            S.tt("pool", x1, rt[:L, 0, 0:H, :], rt[:L, 1, 0:H, :], ALU.subtract)
            S.tt("pool", x2, rt[:L, 2, 0:H, :], rt[:L, 3, 0:H, :], ALU.add)

        def softmax_rows(s_sb, L, nk, e_out, clamp=False):
            sm = small()
            S.reduce(sm[:L, 0:1], s_sb, ALU.max)
            if clamp:
                S.ts("dve", sm[:L, 1:2], sm[:L, 0:1], -1e4, -1.0, ALU.max, ALU.mult)
            else:
                S.ts("dve", sm[:L, 1:2], sm[:L, 0:1], -1.0, None, ALU.mult)
            S.act(e_out, s_sb, AF.Exp, bias=sm[:L, 1:2], accum=sm[:L, 2:3])
            return sm


        def mixer_tile_prompt(l, ti, xsrc):
            L = 128
            last_tile = (ti == NT - 1)
            if STAGE < 0:
                return
            xt = R.x.next()
            S.dma("sp", xt, xsrc)
            hT = R.hT.next()
            rmsnorm_T(xt, L, gcol_pre, hT)

            if STAGE < 1:
                return
            ext = R.ext.next()
            for half in range(2):
                p = psf.next()
                pv = p.rr("p (c l) -> p c l", c=4)
                for c4 in range(4):
                    c = half * 4 + c4
                    for k in range(8):
                        S.mm(pv[:, c4, :], win_sb[:, k, OFF_XBC + c * 128: OFF_XBC + (c + 1) * 128], hT[:, k, :],
                             start=(k == 0), stop=(k == 7))
                S.copy("act", ext[:, half * 4:(half + 1) * 4, 0, 3:131], pv)
            pext = R.pext.next()
            p = psf.next()
            pv = p.rr("p (c l) -> p c l", c=4)
            for g in range(4):
                for k in range(8):
                    S.mm(pv[0:64, g, :], win_sb[:, k, OFF_POOL + g * 64: OFF_POOL + (g + 1) * 64], hT[:, k, :],
                         start=(k == 0), stop=(k == 7))
            S.copy("act", pext[:, :, 0, 15:143], pv[0:64])
            z_ps = psf.next()
            for k in range(8):
                S.mm(z_ps, hT[:, k, :], win_sb[:, k, 0:512], start=(k == 0), stop=(k == 7))
            zs = R.t512.next()
            S.act(zs, z_ps, AF.Silu)
            q_ps = psf.next()
            for k in range(8):
                S.mm(q_ps[:, 0:256], hT[:, k, :], win_sb[:, k, OFF_Q:OFF_Q + 256], start=(k == 0), stop=(k == 7))
            kv_ps = psf.next()
            for k in range(8):
                S.mm(kv_ps[:, 0:396], hT[:, k, :], win_sb[:, k, OFF_KV:OFF_KV + 396], start=(k == 0), stop=(k == 7))
            if last_tile:
                tl = R.ssb.next()
                for nh in range(2):
                    p = psf.next()
                    for k in range(8):
                        S.mm(p, hT[:, k, :], win_sb[:, k, OFF_XBC + nh * 512: OFF_XBC + (nh + 1) * 512],
                             start=(k == 0), stop=(k == 7))
                    S.copy("act", tl[:, nh * 512:(nh + 1) * 512], p)
                p = psf.next()
                for k in range(8):
                    S.mm(p[:, 0:256], hT[:, k, :], win_sb[:, k, OFF_POOL:OFF_POOL + 256], start=(k == 0), stop=(k == 7))
                S.copy("act", tl[:, 1024:1280], p[:, 0:256])
                S.dma("sp", conv_p[l], tl[125:128, 0:1024])
                S.dma("sp", pool_p[l], tl[113:128, 1024:1280])

            if STAGE < 2:
                return
            qf = R.qf.next()
            S.act(qf, q_ps[:, 0:256], AF.Copy, scale=0.125)
            rows = R.rows.next()
            S.copy("act", rows, kv_ps[:, 0:384])
            gates = R.gates.next()
            S.act(gates, kv_ps[:, 384:396], AF.Sigmoid)
            if STAGE < 2.2:
                return
            cs = cs_p[:, ti, :]
            rope(qf.rr("p (h d) -> p h d", h=4), 4, cs, L)
            rope(rows.rr("p (j two d) -> p j two d", j=3, two=2)[:, :, 0, :], 3, cs, L)
            if STAGE < 2.4:
                return
            S.dma("sp", kv_p[l, ti * 128:(ti + 1) * 128, :], rows[:, 0:256])
            if (ti + 1) * 128 > T - WK:
                o0 = ti * 128 - (T - WK)
                S.dma("sp", win_p[l, o0:o0 + 128, :], rows[:, 256:384])
            qbf = R.qbf.next()
            S.copy("pool", qbf, qf)
            kvbf = R.kvbf.next()
            S.copy("pool", kvbf, rows)
            if STAGE < 2.6:
                return
            pT = psb.next()
            pTv = pT.rr("p (k l) -> p k l", k=8)
            for h in range(4):
                S.tr(pTv[0:64, h, :], qbf[:, h * 64:(h + 1) * 64], identb)
            for j, c0 in enumerate((0, 64, 128, 256)):
                S.tr(pTv[0:64, 4 + j, :], kvbf[:, c0:c0 + 64], identb)
            if STAGE < 2.8:
                return
            qT = R.qT.next()
            S.copy("dve", qT, pTv[0:64, 0:4, :])
            if STAGE < 2.85:
                return
            tsl = slice(ti * 128, (ti + 1) * 128)
            S.copy("dve", PP.kcT[:, tsl], pTv[0:64, 4, :])
            S.copy("dve", PP.vcT[:, tsl], pTv[0:64, 5, :])
            if STAGE < 2.9:
                return
            S.copy("dve", PP.ksT[:, tsl], pTv[0:64, 6, :])
            S.copy("dve", PP.kwT[:, tsl], pTv[0:64, 7, :])
            if STAGE < 2.95:
                return
            if VAR == 1:
                S.memset("dve", PP.vs_tok[:, ti, :], 0.0)
            elif VAR == 2:
                S.copy("dve", PP.vs_tok[:, ti, :], kvbf[:, 192:256])
            elif VAR == 3:
                S.copy("dve", PP.vs_tok[:, ti, :], kvbf[:, 128:192])
            elif VAR == 4:
                S.copy("dve", R.mix.next()[:, 0:64], kvbf[:, 192:256])
            elif VAR == 6:
                S.memset("dve", small()[:, 0:8], 0.0)
            elif VAR == 7:
                S.copy("dve", R.mix.next()[:, 0:128], kvbf[:, 128:256])
            elif VAR == 8:
                pass
            elif VAR == 5:
                S.copy("dve", PP.vs_tok[:, ti, :], qbf[:, 192:256])
            else:
                S.copy("dve", PP.vs_tok[:, ti, :], kvbf[:, 192:256])
                S.copy("dve", PP.vw_tok[:, ti, :], kvbf[:, 320:384])

            if STAGE < 3:
                return
            if ti == 0:
                S.memset("pool", ext[:, :, 0, 0:3], 0.0)
            else:
                S.copy("pool", ext[:, :, 0, 0:3], PP.cprev)
            S.copy("pool", PP.cprev, ext[:, :, 0, 128:131])
            acc = R.acc.next()
            for c in range(8):
                S.ts("dve", acc[:, c, :], ext[:, c, 0, 0:128], convw[:, c, 0:1], convb[:, c:c + 1], ALU.mult, ALU.add)
                for k in range(1, 4):
                    S.stt(acc[:, c, :], ext[:, c, 0, k:k + 128], convw[:, c, k:k + 1], acc[:, c, :], ALU.mult, ALU.add)
            abf = R.actbf.next()
            S.act(abf, acc, AF.Silu)
            pT = psb.next()
            pTv = pT.rr("p (k l) -> p k l", k=8)
            for c in range(6):
                S.tr(pTv[:, c, :], abf[:, c, :], identb)
            xsB = R.xsB.next()
            S.copy("dve", xsB, pTv[:, 0:6, :].rr("p k l -> p (k l)"))
            xs3 = xsB[:, 0:512].rr("p (h d) -> p h d", h=8)

            misc_ps = psf.next()
            for k in range(8):
                S.mm(misc_ps[:, 0:8], hT[:, k, :], win_sb[:, k, OFF_DT:OFF_DT + 8], start=(k == 0), stop=(k == 7))
            sm = small()
            dtp = sm[:, 0:8]
            S.tt("dve", dtp, misc_ps[:, 0:8], dtb_b, ALU.add)
            sm2 = small()
            S.ts("dve", sm2[:, 8:16], dtp, -1.0, None, ALU.mult)
            S.tt("dve", sm2[:, 0:8], dtp, sm2[:, 8:16], ALU.max)
            S.act(sm2[:, 0:8], sm2[:, 0:8], AF.Exp, scale=-1.0)
            S.act(sm2[:, 0:8], sm2[:, 0:8], AF.Ln, bias=1.0)
            S.ts("dve", sm2[:, 8:16], dtp, 0.0, None, ALU.max)
            dt = sm[:, 8:16]
            S.tt("dve", dt, sm2[:, 0:8], sm2[:, 8:16], ALU.add)
            sm3 = small()
            adt = sm3[:, 0:8]
            S.tt("dve", adt, dt, a_b, ALU.mult)
            xc = R.xc.next()
            S.tt("pool", xc.rr("p (h d) -> p h d", h=8), xs3, dt.us(2).bc([128, 8, 64]), ALU.mult)
            if ti == 0:
                dbg("dt", dt); dbg("adt", adt); dbg("xsB", xsB, BF16); dbg("xc", xc, BF16); dbg("acc", acc)
            S.mm(misc_ps[:, 8:16], incl, adt)
            S.mm(misc_ps[:, 16:24], after, adt)
            S.mm(misc_ps[:, 24:32], ones, adt)
            sm4 = small()
            S.act(sm4[:, 0:8], misc_ps[:, 8:16], AF.Exp)
            S.act(sm4[:, 8:16], misc_ps[:, 16:24], AF.Exp)
            S.act(sm3[:, 8:16], misc_ps[:, 24:32], AF.Exp)
            eacum = sm4[:, 0:8]
            dte = sm4[:, 8:16]
            elast = sm3[:, 8:16]
            sc_ps = psf.next()
            scv = sc_ps[:, 0:256].rr("p (g l) -> p g l", g=2)
            for g in range(2):
                S.mm(scv[:, g, :], abf[:, 4 + g, :], abf[:, 6 + g, :])
            scm = R.scm.next()
            S.tt("dve", scm, scv, incl.us(1).bc([128, 2, 128]), ALU.mult)
            MT = R.MT.next()
            for g in range(2):
                rhsD = R.rhsD.next()
                S.tt("pool", rhsD, incl.us(1).bc([128, 4, 128]), adt[:, g * 4:(g + 1) * 4].us(2).bc([128, 4, 128]), ALU.mult)
                p = psf.next()
                S.mm(p, after, rhsD.rr("p h l -> p (h l)"))
                Dexp = R.Dexp.next()
                S.act(Dexp.rr("p h l -> p (h l)"), p, AF.Exp)
                S.tt("pool" if g == 0 else "dve", MT[:, g * 4:(g + 1) * 4, :], Dexp,
                     scm[:, g, :].us(1).bc([128, 4, 128]), ALU.mult)
            if ti == 0:
                dbg("eacum", eacum); dbg("dte", dte); dbg("elast", elast); dbg("MT", MT, BF16); dbg("scm", scm)
            Y_ps = psf.next()
            for h in range(8):
                S.mm(Y_ps[:, h * 64:(h + 1) * 64], MT[:, h, :], xc[:, h * 64:(h + 1) * 64])
            y = R.y.next()
            t1 = R.t512.next()
            S.tt("pool", t1.rr("p (h d) -> p h d", h=8), xs3, dsk_b.us(2).bc([128, 8, 64]), ALU.mult)
            S.tt("dve", y, Y_ps, t1, ALU.add)
            if ti > 0:
                Yo_ps = psf.next()
                for g in range(2):
                    S.mm(Yo_ps[:, g * 256:(g + 1) * 256], abf[:, 6 + g, :], PP.STbf[:, g * 256:(g + 1) * 256])
                t2 = R.t512.next()
                S.tt("dve", t2.rr("p (h d) -> p h d", h=8), Yo_ps.rr("p (h d) -> p h d", h=8),
                     eacum.us(2).bc([128, 8, 64]), ALU.mult)
                S.tt("pool", y, y, t2, ALU.add)
            S.tt("dve", y, y, zs, ALU.mult)
            xcd = R.xc.next()
            S.tt("pool", xcd.rr("p (h d) -> p h d", h=8), xc.rr("p (h d) -> p h d", h=8),
                 dte.us(2).bc([128, 8, 64]), ALU.mult)
            Sn_ps = psf.next()
            for g in range(2):
                S.mm(Sn_ps[:, g * 256:(g + 1) * 256], xsB[:, 512 + g * 128: 512 + (g + 1) * 128], xcd[:, g * 256:(g + 1) * 256])
            if ti == 0:
                S.copy("act", PP.ST, Sn_ps)
            else:
                S.tt("pool", PP.ST.rr("p (h d) -> p h d", h=8), PP.ST.rr("p (h d) -> p h d", h=8),
                     elast.us(2).bc([128, 8, 64]), ALU.mult)
                S.tt("dve", PP.ST, PP.ST, Sn_ps, ALU.add)
            S.copy("pool", PP.STbf, PP.ST)
            if ti == 0:
                dbg("y", y); dbg("ST0", PP.ST); dbg("zs", zs)
            mix = R.mix.next()
            junk = R.junk.next()
            ss = small()
            S.act(junk[:, 0:512], y, AF.Square, accum=ss[:, 0:1])
            S.ts("dve", ss[:, 1:2], ss[:, 0:1], 1.0 / 512, RMS_EPS, ALU.mult, ALU.add)
            S.act(ss[:, 1:2], ss[:, 1:2], AF.Sqrt)
            S.recip(ss[:, 2:3], ss[:, 1:2])
            S.act(mix[:, 0:512], y, AF.Copy, scale=ss[:, 2:3])

            if STAGE < 4:
                return
            if ti == 0:
                S.memset("pool", pext[:, :, 0, 0:15], 0.0)
            else:
                S.copy("pool", pext[:, :, 0, 0:15], PP.pprev)
            S.copy("pool", PP.pprev, pext[:, :, 0, 128:143])
            s2 = R.ps2.next()
            s4 = R.ps4.next()
            s8 = R.ps8.next()
            s16 = R.ps16.next()
            S.tt("pool", s2, pext[:, :, :, 1:143], pext[:, :, :, 0:142], ALU.add)
            S.tt("pool", s4, s2[:, 1:4, :, 2:142], s2[:, 1:4, :, 0:140], ALU.add)
            S.tt("pool", s8, s4[:, 1:3, :, 4:140], s4[:, 1:3, :, 0:136], ALU.add)
            S.tt("pool", s16, s8[:, 1:2, :, 8:136], s8[:, 1:2, :, 0:128], ALU.add)
            pooled = R.pooled.next()
            srcs = [(s2, 0, 14), (s4, 0, 12), (s8, 0, 8), (s16, 0, 0)]
            for g, (sx, gi, off) in enumerate(srcs):
                S.stt(pooled[:, g, :], sx[:, gi, 0, off:off + 128], 1.0 / (2 ** (g + 1)), pext[:, g, 0, 15:143],
                      ALU.mult, ALU.subtract)
            if ti == 0:
                for g, (sx, gi, off) in enumerate(srcs):
                    tmp = small()
                    S.tt("dve", tmp[0:64, 0:16], sx[:, gi, 0, off:off + 16], rc_t[:, g, :], ALU.mult)
                    S.tt("dve", pooled[:, g, 0:16], tmp[0:64, 0:16], pext[:, g, 0, 15:31], ALU.subtract)
            yp_ps = psf.next()
            for g in range(4):
                S.mm(yp_ps[:, g * 64:(g + 1) * 64], pooled[:, g, :], poolw_sb[:, g, :])
            S.copy("act", mix[:, 512:768], yp_ps[:, 0:256])

            if STAGE < 5:
                return
            n0 = max(0, 8 * ti - 1)
            n1 = min(8 * ti + 6, NCMP - 1)
            nb = n1 - n0 + 1
            ncur = n1 + 1
            if nb > 0:
                for (srcT, w1, cb, gel) in ((PP.kcT, CW.w1k, CW.cbias_k, PP.geluT_k), (PP.vcT, CW.w1v, CW.cbias_v, PP.geluT_v)):
                    p = psf.next()
                    for li in range(32):
                        S.mm(p[:, 0:nb], w1[:, li, :], srcT[:, 16 * n0 + li: 16 * n1 + li + 1: 16],
                             start=(li == 0), stop=(li == 31))
                    S.act(gel[:, n0:n1 + 1], p[:, 0:nb], AF.Gelu_apprx_tanh, bias=cb)
                p = psf.next()
                S.mm(p[0:64, 0:nb], CW.w2k, PP.geluT_k[:, n0:n1 + 1])
                S.copy("act", PP.ckT[:, n0:n1 + 1], p[0:64, 0:nb])
                p = psf.next()
                S.mm(p[0:ncur, 0:64], PP.geluT_v[:, 0:ncur], CW.w2v)
                S.copy("act", PP.cv[0:ncur, :], p[0:ncur, 0:64])

            if STAGE < 6:
                return
            oacc = R.oacc.next()
            use_sel = (ti >= 8)
            nblk = 2 * ti + 2
            impacc = R.imp.next() if use_sel else None
            for h in range(4):
                s_ps = psf.next()
                S.mm(s_ps[:, 0:ncur], qT[:, h, :], PP.ckT[:, 0:ncur])
                ssb = R.ssb.next()
                c0 = max(0, 8 * ti - 1)
                r0 = c0 - (8 * ti - 1)
                if c0 > 0:
                    S.copy("act", ssb[:, 0:c0], s_ps[:, 0:c0])
                S.tt("dve", ssb[:, c0:ncur], s_ps[:, c0:ncur], cmpdiag[:, r0:r0 + (ncur - c0)], ALU.add)
                e32 = R.e32.next()
                sm = softmax_rows(ssb[:, 0:ncur], L, ncur, e32[:, 0:ncur], clamp=True)
                S.ts("dve", sm[:, 3:4], sm[:, 2:3], 1e-30, None, ALU.max)
                S.recip(sm[:, 4:5], sm[:, 3:4])
                pn = R.pn.next()
                S.ts("dve", pn[:, 0:ncur], e32[:, 0:ncur], sm[:, 4:5], None, ALU.mult)
                pT = psb.next()
                S.tr(pT[0:ncur, 0:128], pn[:, 0:ncur], identb)
                pnT = R.pnT.next()
                S.copy("dve", pnT[0:ncur, :], pT[0:ncur, 0:128])
                o_ps = psf.next()
                S.mm(o_ps[:, 0:64], pnT[0:ncur, :], PP.cv[0:ncur, :])
                S.ts("dve", oacc[:, h * 64:(h + 1) * 64], o_ps[:, 0:64], gates[:, 3 * h:3 * h + 1], None, ALU.mult)
                if use_sel:
                    i_ps = psf.next()
                    S.mm(i_ps[:, 0:nblk], pnT[0:ncur, :], ov_p[0:ncur, 0:nblk])
                    if h == 0:
                        S.tt("dve", impacc[:, 0:nblk], i_ps[:, 0:nblk], addc_p[:, ti, 0:nblk], ALU.add)
                    else:
                        S.tt("dve", impacc[:, 0:nblk], impacc[:, 0:nblk], i_ps[:, 0:nblk], ALU.add)
            selm = None
            if use_sel:
                imp = impacc
                m8 = small()
                wk = R.imp.next()
                S.max8(m8[:, 0:8], imp[:, 0:nblk])
                S.match_replace(wk[:, 0:nblk], m8[:, 0:8], imp[:, 0:nblk], -3e38)
                S.max8(m8[:, 8:16], wk[:, 0:nblk])
                selm = R.selm.next()
                S.ts("dve", selm[:, 0:nblk], imp[:, 0:nblk], m8[:, 15:16], NEG, ALU.is_lt, ALU.mult)
            for h in range(4):
                for br in (1, 2):
                    if br == 1:
                        kt0 = 0
                        kT_, v_ = PP.ksT, PP.vs_tok
                    else:
                        kt0 = max(0, ti - 4)
                        kT_, v_ = PP.kwT, PP.vw_tok
                    ntl = ti - kt0 + 1
                    nk = ntl * 128
                    ssb = R.ssb.next()
                    k0 = 0
                    while k0 < nk:
                        w = min(512, nk - k0)
                        s_ps = psf.next()
                        S.mm(s_ps[:, 0:w], qT[:, h, :], kT_[:, kt0 * 128 + k0: kt0 * 128 + k0 + w])
                        if br == 1 and use_sel:
                            S.tt("dve", ssb[:, k0:k0 + w].rr("p (j c) -> p j c", c=64),
                                 s_ps[:, 0:w].rr("p (j c) -> p j c", c=64),
                                 selm[:, k0 // 64:(k0 + w) // 64].us(2).bc([128, w // 64, 64]), ALU.add)
                        else:
                            S.copy("act", ssb[:, k0:k0 + w], s_ps[:, 0:w])
                        k0 += w
                    S.tt("pool", ssb[:, nk - 128:nk], ssb[:, nk - 128:nk], diagm, ALU.add)
                    if br == 2 and ti - 4 >= 0:
                        S.tt("pool", ssb[:, 0:128], ssb[:, 0:128], farm, ALU.add)
                    ebf = R.ebf.next()
                    sm = softmax_rows(ssb[:, 0:nk], L, nk, ebf[:, 0:nk])
                    S.recip(sm[:, 3:4], sm[:, 2:3])
                    S.tt("dve", sm[:, 4:5], sm[:, 3:4], gates[:, 3 * h + br:3 * h + br + 1], ALU.mult)
                    pTs = R.pT.next()
                    for b0 in range(0, ntl, 8):
                        nb8 = min(8, ntl - b0)
                        pT = psb.next()
                        pTv = pT.rr("p (k l) -> p k l", k=8)
                        for j in range(nb8):
                            S.tr(pTv[:, j, :], ebf[:, (b0 + j) * 128:(b0 + j + 1) * 128], identb)
                        S.copy("dve", pTs[:, b0:b0 + nb8, :], pTv[:, 0:nb8, :])
                    o_ps = psf.next()
                    for j in range(ntl):
                        S.mm(o_ps[:, 0:64], pTs[:, j, :], v_[:, kt0 + j, :], start=(j == 0), stop=(j == ntl - 1))
                    S.stt(oacc[:, h * 64:(h + 1) * 64], o_ps[:, 0:64], sm[:, 4:5], oacc[:, h * 64:(h + 1) * 64],
                          ALU.mult, ALU.add)
            S.copy("pool", mix[:, 768:1024], oacc)

            if STAGE < 7:
                return
            pT = psb.next()
            pTv = pT.rr("p (k l) -> p k l", k=8)
            for k in range(8):
                S.tr(pTv[:, k, :], mix[:, k * 128:(k + 1) * 128], identb)
            mixT = R.mixT.next()
            S.tt("dve", mixT, pTv, mixg.us(2).bc([128, 8, 128]), ALU.mult)
            mo = [psf.next(), psf.next()]
            ss = small()
            junk = R.junk.next()
            for nh in range(2):
                for k in range(8):
                    S.mm(mo[nh], mixT[:, k, :], wout_sb[:, k, nh * 512:(nh + 1) * 512], start=(k == 0), stop=(k == 7))
                S.act(junk[:, nh * 512:(nh + 1) * 512], mo[nh], AF.Square, accum=ss[:, nh:nh + 1])
            S.tt("dve", ss[:, 2:3], ss[:, 0:1], ss[:, 1:2], ALU.add)
            S.ts("dve", ss[:, 3:4], ss[:, 2:3], 1.0 / 1024, RMS_EPS, ALU.mult, ALU.add)
            S.act(ss[:, 3:4], ss[:, 3:4], AF.Sqrt)
            S.recip(ss[:, 4:5], ss[:, 3:4])
            xm = R.xo.next()
            for nh in range(2):
                sl = slice(nh * 512, (nh + 1) * 512)
                tmp = R.t512.next()
                S.stt(tmp, mo[nh], ss[:, 4:5], gpost_b[:, sl], ALU.mult, ALU.mult)
                S.tt("pool", xm[:, sl], tmp, xt[:, sl], ALU.add)
            S.dma("sp", xmid_tile(ti), xm)

        def ffn_block_prompt(l, blk, ntb, xdst_fn):
            Ltot = ntb * 128
            first_blk = (blk == 0)
            last_blk = (blk * NTB + ntb == NT)
            h2T, actT, f_sb = F.h2T, F.actT, F.f_sb
            carry = PP.carry
            for j in range(ntb):
                xin = F.xin.next()
                S.dma("sp", xin, xmid_tile(blk * NTB + j))
                rmsnorm_T(xin, 128, gcol_ffn, h2T[:, :, j * 128:(j + 1) * 128], B=F)
            for s in range(16):
                wg = F.wg.next()
                wv = F.wv.next()
                S.dma("sp", wg, wg_s[s])
                S.dma("sp", wv, wv_s[s])
                if last_blk:
                    p = psf.next()
                    for k in range(8):
                        S.mm(p[:, 0:256], h2T[:, k, Ltot - 128:Ltot], wg[:, k, :], start=(k == 0), stop=(k == 7))
                    gts = F.gts.next()
                    S.copy("act", gts, p[:, 0:256])
                    S.dma("sp", ffn_p[l, :, s * 256:(s + 1) * 256], gts[126:128, :])
                for c2 in range(2):
                    c = s * 2 + c2
                    g_ps = psf.next()
                    v_ps = psf.next()
                    for k in range(8):
                        S.mm(g_ps[:, 0:Ltot], wg[:, k, c2 * 128:(c2 + 1) * 128], h2T[:, k, 0:Ltot], start=(k == 0), stop=(k == 7))
                    for k in range(8):
                        S.mm(v_ps[:, 0:Ltot], wv[:, k, c2 * 128:(c2 + 1) * 128], h2T[:, k, 0:Ltot], start=(k == 0), stop=(k == 7))
                    gext = F.gext.next()
                    S.copy("act", gext[:, 2:2 + Ltot], g_ps[:, 0:Ltot])
                    if first_blk:
                        S.memset("pool", gext[:, 0:2], 0.0)
                    else:
                        S.copy("pool", gext[:, 0:2], carry[:, c, :])
                    S.copy("pool", carry[:, c, :], gext[:, Ltot:Ltot + 2])
                    gacc = F.gacc.next()
                    S.ts("pool", gacc[:, 0:Ltot], gext[:, 0:Ltot], fconvw[:, c, 0:1], fconvb[:, c:c + 1], ALU.mult, ALU.add)
                    S.stt(gacc[:, 0:Ltot], gext[:, 1:1 + Ltot], fconvw[:, c, 1:2], gacc[:, 0:Ltot], ALU.mult, ALU.add)
                    S.stt(gacc[:, 0:Ltot], gext[:, 2:2 + Ltot], fconvw[:, c, 2:3], gacc[:, 0:Ltot], ALU.mult, ALU.add)
                    S.act(gacc[:, 0:Ltot], gacc[:, 0:Ltot], AF.Gelu_apprx_tanh)
                    S.tt("dve", actT[:, c, 0:Ltot], gacc[:, 0:Ltot], v_ps[:, 0:Ltot], ALU.mult)
            for s in range(16):
                wd = F.wd.next()
                S.dma("sp", wd, wd_s[s])
                for j in range(ntb):
                    for nh in range(2):
                        p = psf.next()
                        for c2 in range(2):
                            S.mm(p, actT[:, s * 2 + c2, j * 128:(j + 1) * 128], wd[:, c2, nh * 512:(nh + 1) * 512],
                                 start=(c2 == 0), stop=(c2 == 1))
                        dst = f_sb[:, j, nh * 512:(nh + 1) * 512]
                        if s == 0:
                            S.copy("act", dst, p)
                        else:
                            S.tt("dve", dst, dst, p, ALU.add)
            for j in range(ntb):
                junk = F.junk.next()
                ss = small()
                S.act(junk, f_sb[:, j, :], AF.Square, accum=ss[:, 0:1])
                S.ts("dve", ss[:, 1:2], ss[:, 0:1], 1.0 / 1024, RMS_EPS, ALU.mult, ALU.add)
                S.act(ss[:, 1:2], ss[:, 1:2], AF.Sqrt)
                S.recip(ss[:, 2:3], ss[:, 1:2])
                xin = F.xin.next()
                S.dma("sp", xin, xmid_tile(blk * NTB + j))
                S.stt(f_sb[:, j, :], f_sb[:, j, :], ss[:, 2:3], gfpost_b, ALU.mult, ALU.mult)
                S.tt("pool", f_sb[:, j, :], f_sb[:, j, :], xin, ALU.add)
                S.dma("sp", xdst_fn(blk * NTB + j), f_sb[:, j, :])

        def ssm_out_prompt(l):
            so = S.sb([64, 1024], F32, "ssmo")
            for hh in range(2):
                p = psf.next()
                pv = p.rr("p (h n) -> p h n", h=4)
                for h4 in range(4):
                    h = hh * 4 + h4
                    S.tr(pv[0:64, h4, :], PP.ST[:, h * 64:(h + 1) * 64], identf)
                S.copy("act", so[:, hh * 512:(hh + 1) * 512], p[0:64, :])
            S.dma("sp", ssm_p[l].rr("h p n -> p h n"), so.rr("p (h n) -> p h n", h=8))

        if NS > 0:
            NPG = cfg["NPG"]
            NPHYS = cfg["NPHYS"]
            PAST = NPG * 128
            WB = min(512, PAST)
            LS = NS * 8
            NV = PAST // 16 - 1
            NTC = (NV + 127) // 128
            NSLC_S = PAST // 64 + 1
            NKS = PAST + 64
            NTW = (WB + 8 + 127) // 128
            x_s = din("x_s", [LS, D_MODEL])
            cache = din("cache", [DEPTH, NPHYS * 128, 256])
            pt_d = din("pt", [NS * NPG], I32)
            st_win = din("st_win", [DEPTH, NS, WB, 128])
            st_conv = din("st_conv", [DEPTH, NS * 3, 1024])
            st_ssm = din("st_ssm", [DEPTH, NS, 8, 64, 128])
            st_pool = din("st_pool", [DEPTH, NS * 15, 256])
            st_ffn = din("st_ffn", [DEPTH, NS * 2, 4096])
            y_s = dout("y_s", [LS, D_MODEL])
            kv_s = dout("kv_s", [DEPTH, LS, 256])
            win_s = dout("win_s", [DEPTH, NS, WB, 128])
            conv_s = dout("conv_s", [DEPTH, NS * 3, 1024])
            ssm_s = dout("ssm_s", [DEPTH, NS, 8, 64, 128])
            pool_s = dout("pool_s", [DEPTH, NS * 15, 256])
            ffn_s = dout("ffn_s", [DEPTH, NS * 2, 4096])
            xs_ap = dscr("xs_cur", [LS, D_MODEL])
            xs_res = Res("xs_cur")
            xs_cur = V(xs_ap[:, :], (xs_res,))

            incl_s = cload("c_incl_s", [LS, LS])
            after_s = cload("c_after_s", [LS, LS])
            seqmask_s = cload("c_seqmask_s", [LS, NS, 128])
            rowmask_s = cload("c_rowmask_s", [LS, NS])
            cs_s = cload("c_cs_s", [LS, 16])
            Rm = cload("c_R", [32, 32])
            addc_s = cload("c_addc_s", [32, NSLC_S])
            ov_s = cload("c_ov_s", [128, NTC, NSLC_S], BF16)
            tokmask_s = cload("c_tokmask_s", [32, 64])
            winmask_s = cload("c_winmask_s", [32, WB + 8])
            ptfk = S.sb([128, NS * NPG], F32, "ptfk")
            cache_flat = V(cache.ap.rearrange("d r c -> (d r) c"))
            with S.scope():
                ptb = S.sb([128, NS * NPG], I32, "ptb")
                pidx = cload("c_pidx", [128, NS * NPG])
                S.dma("sp", ptb, V(pt_d.ap.partition_broadcast(128)))
                S.copy("dve", ptfk, ptb)
                S.ts("dve", ptfk, ptfk, 128.0, None, ALU.mult)
                S.tt("dve", ptfk, ptfk, pidx, ALU.add)

        def sample_layer(l, last):
            L = LS
            xsrc = x_s if l == 0 else xs_cur
            xdst = y_s if last else xs_cur
            with S.scope():
                idx = S.sb([128, NS * NPG], I32, "s_idx")
                ptf2 = S.sb([128, NS * NPG], F32, "s_ptf2")
                S.ts("dve", ptf2, ptfk, float(l * NPHYS * 128), None, ALU.add)
                S.copy("dve", idx, ptf2)
                xt = S.sb([128, 1024], F32, "s_xt")
                mix = S.sb([128, 1024], BF16, "s_mix")
                qT_s = S.sb([64, 4, L], BF16, "s_qT")
                knew = S.sb([64, 4, L], BF16, "s_knew")
                kvbf = S.sb([128, 384], BF16, "s_kvbf")
                gates = S.sb([128, 12], F32, "s_gates")
                xmid_s = S.sb([128, 1024], F32, "s_xmid")
                junkS = Obj()
                junkS.junk = Ring(S, 1, [128, 1024], BF16, "s_junk")
                junkS.xn = Ring(S, 1, [128, 1024], BF16, "s_xn")
                S.dma("sp", xt[:L], xsrc)
                with S.scope():
                    hT = S.sb([128, 8, L], BF16, "s_hT")
                    rmsnorm_T(xt[:L], L, gcol_pre, hT, B=junkS)
                    if SST < 0.2:
                        return
                    ext = S.sb([128, 8, NS, 11], F32, "s_ext")
                    pext = S.sb([64, 4, NS, 23], F32, "s_pext")
                    tl = S.sb([128, 1280], F32, "s_tail")
                    p = psf.next()
                    pv = p[:, 0:8 * L].rr("p (c l) -> p c l", c=8)
                    for c in range(8):
                        for k in range(8):
                            S.mm(pv[:, c, :], win_sb[:, k, OFF_XBC + c * 128: OFF_XBC + (c + 1) * 128], hT[:, k, :],
                                 start=(k == 0), stop=(k == 7))
                    S.copy("act", ext[:, :, :, 3:11], pv.rr("p c (b i) -> p c b i", b=NS))
                    p = psf.next()
                    pv = p[:, 0:4 * L].rr("p (c l) -> p c l", c=4)
                    for g in range(4):
                        for k in range(8):
                            S.mm(pv[0:64, g, :], win_sb[:, k, OFF_POOL + g * 64: OFF_POOL + (g + 1) * 64], hT[:, k, :],
                                 start=(k == 0), stop=(k == 7))
                    S.copy("act", pext[:, :, :, 15:23], pv[0:64].rr("p c (b i) -> p c b i", b=NS))
                    if SST < 0.4:
                        return
                    z_ps = psf.next()
                    for k in range(8):
                        S.mm(z_ps[:L], hT[:, k, :], win_sb[:, k, 0:512], start=(k == 0), stop=(k == 7))
                    zs = S.sb([128, 512], F32, "s_zs")
                    S.act(zs[:L], z_ps[:L], AF.Silu)
                    q_ps = psf.next()
                    for k in range(8):
                        S.mm(q_ps[:L, 0:256], hT[:, k, :], win_sb[:, k, OFF_Q:OFF_Q + 256], start=(k == 0), stop=(k == 7))
                    kv_ps = psf.next()
                    for k in range(8):
                        S.mm(kv_ps[:L, 0:396], hT[:, k, :], win_sb[:, k, OFF_KV:OFF_KV + 396], start=(k == 0), stop=(k == 7))
                    if SST < 0.5:
                        return
                    qf = S.sb([128, 256], F32, "s_qf")
                    rows = S.sb([128, 384], F32, "s_rows")
                    S.act(qf[:L], q_ps[:L, 0:256], AF.Copy, scale=0.125)
                    S.copy("act", rows[:L], kv_ps[:L, 0:384])
                    S.act(gates[:L], kv_ps[:L, 384:396], AF.Sigmoid)
                    for nh in range(2):
                        p = psf.next()
                        for k in range(8):
                            S.mm(p[:L], hT[:, k, :], win_sb[:, k, OFF_XBC + nh * 512: OFF_XBC + (nh + 1) * 512],
                                 start=(k == 0), stop=(k == 7))
                        S.copy("act", tl[:L, nh * 512:(nh + 1) * 512], p[:L])
                    p = psf.next()
                    for k in range(8):
                        S.mm(p[:L, 0:256], hT[:, k, :], win_sb[:, k, OFF_POOL:OFF_POOL + 256], start=(k == 0), stop=(k == 7))
                    S.copy("act", tl[:L, 1024:1280], p[:L, 0:256])
                    for b in range(NS):
                        S.dma("sp", conv_s[l, b * 3:(b + 1) * 3, :], tl[b * 8 + 5:b * 8 + 8, 0:1024])
                        S.dma("sp", pool_s[l, b * 15 + 7:b * 15 + 15, :], tl[b * 8:b * 8 + 8, 1024:1280])
                    if SST < 0.8:
                        return
                    rt = S.sb([128, 4, 4, 8], F32, "s_rt")
                    R.rt = Ring(S, 1, [128, 4, 4, 8], F32, "s_rt2")
                    rope(qf[:L].rr("p (h d) -> p h d", h=4), 4, cs_s, L)
                    rope(rows[:L].rr("p (j two d) -> p j two d", j=3, two=2)[:, :, 0, :], 3, cs_s, L)
                    if SST < 0.85:
                        return
                    S.dma("sp", kv_s[l], rows[:L, 0:256])
                    for b in range(NS):
                        S.dma("sp", win_s[l, b, WB - 8:WB, :], rows[b * 8:(b + 1) * 8, 256:384])
                    qbf = S.sb([128, 256], BF16, "s_qbf")
                    S.copy("pool", qbf[:L], qf[:L])
                    S.copy("pool", kvbf[:L], rows[:L])
                    if SST < 0.9:
                        return
                    pT = psb.next()
                    pTv = pT.rr("p (k l) -> p k l", k=8)[:, :, 0:L]
                    for h in range(4):
                        S.tr(pTv[0:64, h, :], qbf[:L, h * 64:(h + 1) * 64], identb[:L, :L])
                    for j, c0 in enumerate((0, 64, 128, 256)):
                        S.tr(pTv[0:64, 4 + j, :], kvbf[:L, c0:c0 + 64], identb[:L, :L])
                    if SST < 0.95:
                        return
                    if VAR == 1:
                        S.copy("dve", qT_s, pTv[0:64, 0:4, :])
                    elif VAR == 2:
                        for h in range(4):
                            S.copy("dve", qT_s[:, h, :], pTv[0:64, h, :])
                            S.copy("dve", knew[:, h, :], pTv[0:64, 4 + h, :])
                    elif VAR == 3:
                        S.memset("dve", qT_s, 0.0)
                    elif VAR == 4:
                        S.memset("dve", small()[:, 0:8], 0.0)
                    elif VAR == 5:
                        S.memset("pool", small()[:, 0:8], 0.0)
                    elif VAR == 6:
                        pass
                    else:
                        S.copy("dve", qT_s, pTv[0:64, 0:4, :])
                        S.copy("dve", knew, pTv[0:64, 4:8, :])
                    if SST < 1:
                        return
                    cst = S.sb([NS * 3, 1024], F32, "s_cst")
                    S.dma("sp", cst, st_conv[l])
                    p = psf.next()
                    pv = p[:, 0:8 * NS * 3].rr("p (c r) -> p c r", c=8)
                    for c in range(8):
                        S.mm(pv[:, c, :], cst[:, c * 128:(c + 1) * 128], identf[:NS * 3, :NS * 3])
                    S.copy("act", ext[:, :, :, 0:3], pv.rr("p c (b t) -> p c b t", b=NS))
                    if SST < 1.2:
                        return
                    acc = S.sb([128, 8, NS, 8], F32, "s_acc")
                    for c in range(8):
                        S.ts("dve", acc[:, c], ext[:, c, :, 0:8], convw[:, c, 0:1], convb[:, c:c + 1], ALU.mult, ALU.add)
                        for k in range(1, 4):
                            S.stt(acc[:, c], ext[:, c, :, k:k + 8], convw[:, c, k:k + 1], acc[:, c], ALU.mult, ALU.add)
                    abf = S.sb([128, 8, L], BF16, "s_abf")
                    S.act(abf, acc.rr("p c b i -> p c (b i)"), AF.Silu)
                    dbg("s_acc", acc.rr("p c b i -> p c (b i)")); dbg("s_ext", ext.rr("p c b t -> p c (b t)"))
                    pT = psb.next()
                    pTv = pT[:, 0:768].rr("p (k l) -> p k l", k=6)
                    for c in range(6):
                        S.tr(pTv[:L, c, :], abf[:, c, :], identb)
                    xsB = S.sb([128, 768], BF16, "s_xsB")
                    S.copy("dve", xsB[:L], pTv[:L].rr("p k l -> p (k l)"))
                    xs3 = xsB[:L, 0:512].rr("p (h d) -> p h d", h=8)
                    if SST < 1.4:
                        return
                    misc_ps = psf.next()
                    for k in range(8):
                        S.mm(misc_ps[:L, 0:8], hT[:, k, :], win_sb[:, k, OFF_DT:OFF_DT + 8], start=(k == 0), stop=(k == 7))
                    sm = small()
                    dtp = sm[:L, 0:8]
                    S.tt("dve", dtp, misc_ps[:L, 0:8], dtb_b[:L], ALU.add)
                    sm2 = small()
                    S.ts("dve", sm2[:L, 8:16], dtp, -1.0, None, ALU.mult)
                    S.tt("dve", sm2[:L, 0:8], dtp, sm2[:L, 8:16], ALU.max)
                    S.act(sm2[:L, 0:8], sm2[:L, 0:8], AF.Exp, scale=-1.0)
                    S.act(sm2[:L, 0:8], sm2[:L, 0:8], AF.Ln, bias=1.0)
                    S.ts("dve", sm2[:L, 8:16], dtp, 0.0, None, ALU.max)
                    dt = sm[:L, 8:16]
                    S.tt("dve", dt, sm2[:L, 0:8], sm2[:L, 8:16], ALU.add)
                    sm3 = small()
                    adt = sm3[:L, 0:8]
                    S.tt("dve", adt, dt, a_b[:L], ALU.mult)
                    xc = S.sb([128, 512], BF16, "s_xc")
                    S.tt("pool", xc[:L].rr("p (h d) -> p h d", h=8), xs3, dt.us(2).bc([L, 8, 64]), ALU.mult)
                    dbg("s_dt", dt); dbg("s_xsB", xsB[:L], BF16); dbg("s_xc", xc[:L], BF16)
                    S.mm(misc_ps[:L, 8:16], incl_s, adt)
                    S.mm(misc_ps[:L, 16:24], after_s, adt)
                    for b in range(NS):
                        S.mm(misc_ps[:, 24 + 8 * b:32 + 8 * b], seqmask_s[:, b, :], adt)
                    sm4 = small()
                    S.act(sm4[:L, 0:8], misc_ps[:L, 8:16], AF.Exp)
                    S.act(sm4[:L, 8:16], misc_ps[:L, 16:24], AF.Exp)
                    elast = S.sb([128, NS, 8], F32, "s_elast")
                    S.act(elast.rr("p b h -> p (b h)"), misc_ps[:, 24:24 + 8 * NS], AF.Exp)
                    eacum = sm4[:L, 0:8]
                    dte = sm4[:L, 8:16]
                    if SST < 1.6:
                        return
                    sc_ps = psf.next()
                    scv = sc_ps[:L, 0:2 * L].rr("p (g l) -> p g l", g=2)
                    for g in range(2):
                        S.mm(scv[:, g, :], abf[:, 4 + g, :], abf[:, 6 + g, :])
                    scm = S.sb([128, 2, L], F32, "s_scm")
                    S.tt("dve", scm[:L], scv, incl_s.us(1).bc([L, 2, L]), ALU.mult)
                    if SST < 1.65:
                        return
                    rhsD = S.sb([128, 8, L], F32, "s_rhsD")
                    S.tt("pool", rhsD[:L], incl_s.us(1).bc([L, 8, L]), adt.us(2).bc([L, 8, L]), ALU.mult)
                    p = psf.next()
                    S.mm(p[:L, 0:8 * L], after_s, rhsD[:L].rr("p h l -> p (h l)"))
                    Dexp = S.sb([128, 8, L], F32, "s_Dexp")
                    S.act(Dexp[:L].rr("p h l -> p (h l)"), p[:L, 0:8 * L], AF.Exp)
                    if SST < 1.7:
                        return
                    MT = S.sb([128, 8, L], BF16, "s_MT")
                    for g in range(2):
                        S.tt("dve", MT[:L, g * 4:(g + 1) * 4, :], Dexp[:L, g * 4:(g + 1) * 4, :],
                             scm[:L, g, :].us(1).bc([L, 4, L]), ALU.mult)
                    if SST < 1.75:
                        return
                    Y_ps = psf.next()
                    for h in range(8):
                        S.mm(Y_ps[:L, h * 64:(h + 1) * 64], MT[:L, h, :], xc[:L, h * 64:(h + 1) * 64])
                    if SST < 1.78:
                        return
                    y = S.sb([128, 512], F32, "s_y")
                    t1 = S.sb([128, 512], F32, "s_t1")
                    t2 = S.sb([128, 512], F32, "s_t2")
                    S.tt("pool", t1[:L].rr("p (h d) -> p h d", h=8), xs3, dsk_b[:L].us(2).bc([L, 8, 64]), ALU.mult)
                    if SST < 1.785:
                        return
                    S.tt("dve", y[:L], Y_ps[:L], t1[:L], ALU.add)
                    if SST < 1.8:
                        return
                    STb = S.sb([128, 512], BF16, "s_STb")
                    sA = S.sb([64, 8, 128], F32, "s_sA")
                    sB = S.sb([128, 4, 128], F32, "s_sB")
                    sBb = S.sb([128, 4, 128], BF16, "s_sBb")
                    snew = S.sb([64, 8, 128], F32, "s_snew")
                    stmp = S.sb([64, 4, 128], F32, "s_stmp")
                    xcd = S.sb([128, 512], BF16, "s_xcd")
                    for b in range(NS):
                        S.dma("sp", sA, st_ssm[l, b].rr("h p n -> p h n"))
                        S.dma("sp", sB, st_ssm[l, b].rr("h p n -> (h p) n").rr("(hp q) n -> q hp n", q=128))
                        S.copy("dve", sBb, sB)
                        pT = psb.next()
                        pTv = pT[:, 0:512].rr("p (k l) -> p k l", k=4)
                        for hp in range(4):
                            S.tr(pTv[:, hp, :], sBb[:, hp, :], identb)
                        S.copy("dve", STb, pT[:, 0:512])
                        if SST < 1.82:
                            return
                        Yo_ps = psf.next()
                        for g in range(2):
                            S.mm(Yo_ps[:L, g * 256:(g + 1) * 256], abf[:, 6 + g, :], STb[:, g * 256:(g + 1) * 256])
                        smb = small()
                        S.ts("dve", smb[:L, 0:8], eacum, rowmask_s[:, b:b + 1], None, ALU.mult)
                        S.ts("dve", smb[:L, 8:16], dte, rowmask_s[:, b:b + 1], None, ALU.mult)
                        S.tt("dve", t2[:L].rr("p (h d) -> p h d", h=8), Yo_ps[:L].rr("p (h d) -> p h d", h=8),
                             smb[:L, 0:8].us(2).bc([L, 8, 64]), ALU.mult)
                        S.tt("pool", y[:L], y[:L], t2[:L], ALU.add)
                        if SST < 1.84:
                            return
                        S.tt("pool", xcd[:L].rr("p (h d) -> p h d", h=8), xc[:L].rr("p (h d) -> p h d", h=8),
                             smb[:L, 8:16].us(2).bc([L, 8, 64]), ALU.mult)
                        for hh in range(2):
                            Sn_ps = psf.next()
                            for h4 in range(4):
                                h = hh * 4 + h4
                                S.mm(Sn_ps[0:64, h4 * 128:(h4 + 1) * 128], xcd[:L, h * 64:(h + 1) * 64],
                                     xsB[:L, 512 + hh * 128: 512 + (hh + 1) * 128])
                            S.tt("pool", stmp, sA[:, hh * 4:(hh + 1) * 4, :],
                                 elast[0:64, b, hh * 4:(hh + 1) * 4].us(2).bc([64, 4, 128]), ALU.mult)
                            S.tt("dve", snew[:, hh * 4:(hh + 1) * 4, :], stmp, Sn_ps[0:64].rr("p (h n) -> p h n", h=4), ALU.add)
                        S.dma("sp", ssm_s[l, b].rr("h p n -> p h n"), snew)
                    if SST < 2:
                        return
                    S.tt("dve", y[:L], y[:L], zs[:L], ALU.mult)
                    ss = small()
                    junk = junkS.junk.next()
                    S.act(junk[:L, 0:512], y[:L], AF.Square, accum=ss[:L, 0:1])
                    S.ts("dve", ss[:L, 1:2], ss[:L, 0:1], 1.0 / 512, RMS_EPS, ALU.mult, ALU.add)
                    S.act(ss[:L, 1:2], ss[:L, 1:2], AF.Sqrt)
                    S.recip(ss[:L, 2:3], ss[:L, 1:2])
                    S.act(mix[:L, 0:512], y[:L], AF.Copy, scale=ss[:L, 2:3])
                    pst = S.sb([NS * 15, 256], F32, "s_pst")
                    S.dma("sp", pst, st_pool[l])
                    for b in range(NS):
                        S.dma("sp", pool_s[l, b * 15:b * 15 + 7, :], pst[b * 15 + 8:b * 15 + 15, :])
                    p = psf.next()
                    pv = p[:, 0:4 * NS * 15].rr("p (c r) -> p c r", c=4)
                    for g in range(4):
                        S.mm(pv[0:64, g, :], pst[:, g * 64:(g + 1) * 64], identf[:NS * 15, :NS * 15])
                    S.copy("act", pext[:, :, :, 0:15], pv[0:64].rr("p c (b t) -> p c b t", b=NS))
                    s2 = S.sb([64, 4, NS, 22], F32, "s_s2")
                    s4 = S.sb([64, 3, NS, 20], F32, "s_s4")
                    s8 = S.sb([64, 2, NS, 16], F32, "s_s8")
                    s16 = S.sb([64, 1, NS, 8], F32, "s_s16")
                    S.tt("pool", s2, pext[:, :, :, 1:23], pext[:, :, :, 0:22], ALU.add)
                    S.tt("pool", s4, s2[:, 1:4, :, 2:22], s2[:, 1:4, :, 0:20], ALU.add)
                    S.tt("pool", s8, s4[:, 1:3, :, 4:20], s4[:, 1:3, :, 0:16], ALU.add)
                    S.tt("pool", s16, s8[:, 1:2, :, 8:16], s8[:, 1:2, :, 0:8], ALU.add)
                    pooled = S.sb([64, 4, NS, 8], BF16, "s_pooled")
                    srcs = [(s2, 0, 14), (s4, 0, 12), (s8, 0, 8), (s16, 0, 0)]
                    for g, (sx, gi, off) in enumerate(srcs):
                        S.stt(pooled[:, g], sx[:, gi, :, off:off + 8], 1.0 / (2 ** (g + 1)), pext[:, g, :, 15:23],
                              ALU.mult, ALU.subtract)
                    yp_ps = psf.next()
                    for g in range(4):
                        S.mm(yp_ps[:L, g * 64:(g + 1) * 64], pooled[:, g].rr("p b i -> p (b i)"), poolw_sb[:, g, :])
                    S.copy("act", mix[:L, 512:768], yp_ps[:L, 0:256])
                if SST < 3:
                    return
                with S.scope():
                    load_cmp_weights(l)
                    ksT = S.sb([64, NKS], BF16, "s_ksT")
                    vs = S.sb([128, NPG + 1, 64], BF16, "s_vs")
                    kwT = S.sb([64, WB + 8], BF16, "s_kwT")
                    vw = S.sb([128, NTW, 64], BF16, "s_vw")
                    gel_k = S.sb([128, NTC * 128], BF16, "s_gelk")
                    gel_v = S.sb([128, NTC * 128], BF16, "s_gelv")
                    ckT = S.sb([64, NTC * 128], BF16, "s_ckT")
                    cvs = S.sb([128, NTC, 64], BF16, "s_cv")
                    qTb = S.sb([64, 4, 8], BF16, "s_qTb")
                    g_rows = S.sb([32, 3], F32, "s_grows")
                    oacc = S.sb([32, 64], F32, "s_oacc")
                    oaccb = S.sb([32, 64], BF16, "s_oaccb")
                    for b in range(NS):
                        S.memset("dve", ksT[:, PAST:NKS], 0.0)
                        S.memset("dve", vs[:, NPG, :], 0.0)
                        S.memset("dve", vw[:, NTW - 1, :], 0.0)
                        with S.scope():
                            cT = S.sb([64, 2, 16, PAST // 16], BF16, "s_cT")
                            r_pg = Ring(S, 4, [128, 256], F32, "s_pg")
                            r_pgb = Ring(S, 2, [128, 256], BF16, "s_pgb")
                            pgts = {}

                            def issue_page(pg):
                                pgt = r_pg.next()
                                j = b * NPG + pg

                                def fn(e, pgt=pgt, j=j):
                                    return e.indirect_dma_start(
                                        out=pgt.ap, out_offset=None, in_=cache_flat.ap,
                                        in_offset=bass.IndirectOffsetOnAxis(ap=idx.ap[:, j:j + 1], axis=0))
                                S.op("pool", fn, [pgt], [idx], dma=True)
                                pgts[pg] = pgt
                            PF = 3
                            for pg in range(min(PF, NPG)):
                                issue_page(pg)
                            for pg in range(NPG):
                                pgt = pgts.pop(pg)
                                pgb = r_pgb.next()
                                S.copy("act", pgb, pgt)
                                if pg + PF < NPG:
                                    issue_page(pg + PF)
                                pT = psb.next()
                                pTv = pT[0:64, 0:384].rr("p (k l) -> p k l", k=3)
                                for jj in range(3):
                                    S.tr(pTv[:, jj, :], pgb[:, jj * 64:(jj + 1) * 64], identb)
                                S.copy("dve", cT[:, :, :, pg * 8:(pg + 1) * 8], pTv[:, 0:2, :].rr("p k (b r) -> p k r b", r=16))
                                S.copy("dve", ksT[:, pg * 128:(pg + 1) * 128], pTv[:, 2, :])
                                S.copy("act", vs[:, pg, :], pgb[:, 192:256])
                            for (ci, w1, cb, gel) in ((0, CW.w1k, CW.cbias_k, gel_k), (1, CW.w1v, CW.cbias_v, gel_v)):
                                n0 = 0
                                while n0 < NV:
                                    nb = min(512, NV - n0)
                                    p = psf.next()
                                    for li in range(32):
                                        S.mm(p[:, 0:nb], w1[:, li, :], cT[:, ci, li % 16, n0 + li // 16: n0 + li // 16 + nb],
                                             start=(li == 0), stop=(li == 31))
                                    S.act(gel[:, n0:n0 + nb], p[:, 0:nb], AF.Gelu_apprx_tanh, bias=cb)
                                    n0 += nb
                        if SST < 4:
                            return
                        n0 = 0
                        while n0 < NV:
                            nb = min(512, NV - n0)
                            p = psf.next()
                            S.mm(p[0:64, 0:nb], CW.w2k, gel_k[:, n0:n0 + nb])
                            S.copy("act", ckT[:, n0:n0 + nb], p[0:64, 0:nb])
                            n0 += nb
                        for t in range(NTC):
                            rws = min(128, NV - t * 128)
                            p = psf.next()
                            S.mm(p[0:rws, 0:64], gel_v[:, t * 128:t * 128 + rws], CW.w2v)
                            S.copy("act", cvs[0:rws, t, :], p[0:rws, 0:64])
                        S.copy("dve", ksT[:, PAST:PAST + 8], knew[:, 2, b * 8:(b + 1) * 8])
                        S.dma("sp", vs[0:8, NPG, :], kvbf[b * 8:(b + 1) * 8, 192:256])
                        with S.scope():
                            r_wt = Ring(S, 2, [128, 128], F32, "s_wt")
                            r_wtb = Ring(S, 2, [128, 128], BF16, "s_wtb")
                            for t in range(WB // 128):
                                wt = r_wt.next()
                                S.dma("sp", wt, st_win[l, b, t * 128:(t + 1) * 128, :])
                                r0 = 8 if t == 0 else 0
                                S.dma("sp", win_s[l, b, t * 128 - 8 + r0:t * 128 + 120, :], wt[r0:128, :])
                                wtb = r_wtb.next()
                                S.copy("act", wtb, wt)
                                pT = psb.next()
                                S.tr(pT[0:64, 0:128], wtb[:, 0:64], identb)
                                S.copy("dve", kwT[:, t * 128:(t + 1) * 128], pT[0:64, 0:128])
                                S.copy("act", vw[:, t, :], wtb[:, 64:128])
                        S.copy("dve", kwT[:, WB:WB + 8], knew[:, 3, b * 8:(b + 1) * 8])
                        S.dma("sp", vw[0:8, WB // 128, :], kvbf[b * 8:(b + 1) * 8, 320:384])
                        S.copy("dve", qTb, qT_s[:, :, b * 8:(b + 1) * 8])
                        for h in range(4):
                            S.dma("sp", g_rows[h * 8:(h + 1) * 8, :], gates[b * 8:(b + 1) * 8, 3 * h:3 * h + 3])
                        qTf = qTb.rr("p h i -> p (h i)")
                        with S.scope():
                            M = 32
                            ssb = S.sb([32, 512], F32, "s_ssb")
                            ebf = S.sb([32, NKS], BF16, "s_ebf")
                            pTs = S.sb([128, NPG + 1, 32], BF16, "s_pTs")
                            e32 = S.sb([32, NTC * 128], F32, "s_e32")
                            pn = S.sb([32, NTC * 128], BF16, "s_pn")
                            pnT = S.sb([128, NTC, 32], BF16, "s_pnT")
                            impr = S.sb([32, NSLC_S], F32, "s_impr")
                            imp = S.sb([32, NSLC_S], F32, "s_imp")
                            wk = S.sb([32, NSLC_S], F32, "s_wk")
                            selm = S.sb([32, NSLC_S], F32, "s_selm")
                            mxc = S.sb([32, 32], F32, "s_mxc")
                            n0 = 0
                            while n0 < NV:
                                nb = min(512, NV - n0)
                                p = psf.next()
                                S.mm(p[0:M, 0:nb], qTf, ckT[:, n0:n0 + nb])
                                S.copy("act", e32[:, n0:n0 + nb], p[0:M, 0:nb])
                                n0 += nb
                            sm = softmax_rows(e32[:, 0:NV], M, NV, e32[:, 0:NV], clamp=True)
                            S.ts("dve", sm[:M, 3:4], sm[:M, 2:3], 1e-30, None, ALU.max)
                            S.recip(sm[:M, 4:5], sm[:M, 3:4])
                            S.ts("dve", pn[:, 0:NV], e32[:, 0:NV], sm[:M, 4:5], None, ALU.mult)
                            pT = psb.next()
                            pTv = pT[:, 0:NTC * 32].rr("p (k l) -> p k l", k=NTC)
                            for t in range(NTC):
                                rws = min(128, NV - t * 128)
                                S.tr(pTv[0:rws, t, :], pn[:, t * 128:t * 128 + rws], identb[:M, :M])
                            if NV % 128 != 0:
                                S.memset("dve", pnT[:, NTC - 1, :], 0.0)
                            for t in range(NTC):
                                rws = min(128, NV - t * 128)
                                S.copy("dve", pnT[0:rws, t, :], pTv[0:rws, t, :])
                            o_ps = psf.next()
                            for t in range(NTC):
                                rws = min(128, NV - t * 128)
                                S.mm(o_ps[0:M, 0:64], pnT[0:rws, t, :], cvs[0:rws, t, :], start=(t == 0), stop=(t == NTC - 1))
                            S.ts("dve", oacc, o_ps[0:M, 0:64], g_rows[:, 0:1], None, ALU.mult)
                            if SST < 5:
                                return
                            i_ps = psf.next()
                            for t in range(NTC):
                                rws = min(128, NV - t * 128)
                                S.mm(i_ps[0:M, 0:NSLC_S], pnT[0:rws, t, :], ov_s[0:rws, t, :], start=(t == 0), stop=(t == NTC - 1))
                            S.copy("act", impr, i_ps[0:M, 0:NSLC_S])
                            i2_ps = psf.next()
                            S.mm(i2_ps[0:M, 0:NSLC_S], Rm, impr)
                            S.tt("dve", imp, i2_ps[0:M, 0:NSLC_S], addc_s, ALU.add)
                            m8 = small()
                            if NSLC_S > 16:
                                S.max8(m8[:M, 0:8], imp)
                                S.match_replace(wk, m8[:M, 0:8], imp, -3e38)
                                S.max8(m8[:M, 8:16], wk)
                                S.ts("dve", selm, imp, m8[:M, 15:16], NEG, ALU.is_lt, ALU.mult)
                            else:
                                S.memset("dve", selm, 0.0)
                            nch = (NKS + 511) // 512
                            for pas in range(2):
                                for c in range(nch):
                                    k0 = c * 512
                                    w = min(512, NKS - k0)
                                    p = psf.next()
                                    S.mm(p[0:M, 0:w], qTf, ksT[:, k0:k0 + w])
                                    S.tt("dve", ssb[:, 0:w].rr("p (j c) -> p j c", c=64), p[0:M, 0:w].rr("p (j c) -> p j c", c=64),
                                         selm[:, k0 // 64:(k0 + w) // 64].us(2).bc([M, w // 64, 64]), ALU.add)
                                    if c == nch - 1:
                                        S.tt("dve", ssb[:, w - 64:w], ssb[:, w - 64:w], tokmask_s, ALU.add)
                                    if pas == 0:
                                        S.reduce(mxc[:, c:c + 1], ssb[:, 0:w], ALU.max)
                                    else:
                                        S.act(ebf[:, k0:k0 + w], ssb[:, 0:w], AF.Exp, bias=sm1[:M, 1:2], accum=mxc[:, c:c + 1])
                                if pas == 0:
                                    sm1 = small()
                                    S.reduce(sm1[:M, 0:1], mxc[:, 0:nch], ALU.max)
                                    S.ts("dve", sm1[:M, 1:2], sm1[:M, 0:1], -1.0, None, ALU.mult)
                                else:
                                    S.reduce(sm1[:M, 2:3], mxc[:, 0:nch], ALU.add)
                            S.recip(sm1[:M, 3:4], sm1[:M, 2:3])
                            S.tt("dve", sm1[:M, 4:5], sm1[:M, 3:4], g_rows[:, 1:2], ALU.mult)
                            ntl = NPG + 1
                            for b0 in range(0, ntl, 32):
                                nb8 = min(32, ntl - b0)
                                pT = psb.next()
                                pTv = pT.rr("p (k l) -> p k l", k=32)
                                for jx in range(nb8):
                                    t = b0 + jx
                                    rws = min(128, NKS - t * 128)
                                    S.tr(pTv[0:rws, jx, :], ebf[:, t * 128:t * 128 + rws], identb[:M, :M])
                                nfull = nb8 if (b0 + nb8 < ntl) else nb8 - 1
                                if nfull > 0:
                                    S.copy("dve", pTs[:, b0:b0 + nfull, :], pTv[:, 0:nfull, :])
                                if nfull < nb8:
                                    S.copy("dve", pTs[0:64, ntl - 1, :], pTv[0:64, nb8 - 1, :])
                            o_ps = psf.next()
                            for t in range(ntl):
                                rws = min(128, NKS - t * 128)
                                S.mm(o_ps[0:M, 0:64], pTs[0:rws, t, :], vs[0:rws, t, :], start=(t == 0), stop=(t == ntl - 1))
                            S.stt(oacc, o_ps[0:M, 0:64], sm1[:M, 4:5], oacc, ALU.mult, ALU.add)
                            if SST < 6:
                                return
                            NKW = WB + 8
                            wsb = S.sb([32, NKW], F32, "s_wsb")
                            k0 = 0
                            while k0 < NKW:
                                w = min(512, NKW - k0)
                                p = psf.next()
                                S.mm(p[0:M, 0:w], qTf, kwT[:, k0:k0 + w])
                                S.tt("dve", wsb[:, k0:k0 + w], p[0:M, 0:w], winmask_s[:, k0:k0 + w], ALU.add)
                                k0 += w
                            sm = softmax_rows(wsb, M, NKW, ebf[:, 0:NKW])
                            S.recip(sm[:M, 3:4], sm[:M, 2:3])
                            S.tt("dve", sm[:M, 4:5], sm[:M, 3:4], g_rows[:, 2:3], ALU.mult)
                            pT = psb.next()
                            pTv = pT.rr("p (k l) -> p k l", k=32)
                            for t in range(NTW):
                                rws = min(128, NKW - t * 128)
                                S.tr(pTv[0:rws, t, :], ebf[:, t * 128:t * 128 + rws], identb[:M, :M])
                            for t in range(NTW):
                                rws = min(128, NKW - t * 128)
                                S.copy("dve", pTs[0:rws, t, :], pTv[0:rws, t, :])
                            o_ps = psf.next()
                            for t in range(NTW):
                                rws = min(128, NKW - t * 128)
                                S.mm(o_ps[0:M, 0:64], pTs[0:rws, t, :], vw[0:rws, t, :], start=(t == 0), stop=(t == NTW - 1))
                            S.stt(oacc, o_ps[0:M, 0:64], sm[:M, 4:5], oacc, ALU.mult, ALU.add)
                            S.copy("dve", oaccb, oacc)
                            for h in range(4):
                                S.dma("sp", mix[b * 8:(b + 1) * 8, 768 + h * 64:768 + (h + 1) * 64], oaccb[h * 8:(h + 1) * 8, :])
                if SST < 7:
                    return
                with S.scope():
                    pT = psb.next()
                    pTv = pT[:, 0:8 * L].rr("p (k l) -> p k l", k=8)
                    for k in range(8):
                        S.tr(pTv[:, k, :], mix[:L, k * 128:(k + 1) * 128], identb[:L, :L])
                    mixT = S.sb([128, 8, L], BF16, "s_mixT")
                    S.tt("dve", mixT, pTv, mixg.us(2).bc([128, 8, L]), ALU.mult)
                    mo = [psf.next(), psf.next()]
                    ss = small()
                    junk = junkS.junk.next()
                    for nh in range(2):
                        for k in range(8):
                            S.mm(mo[nh][:L], mixT[:, k, :], wout_sb[:, k, nh * 512:(nh + 1) * 512], start=(k == 0), stop=(k == 7))
                        S.act(junk[:L, nh * 512:(nh + 1) * 512], mo[nh][:L], AF.Square, accum=ss[:L, nh:nh + 1])
                    S.tt("dve", ss[:L, 2:3], ss[:L, 0:1], ss[:L, 1:2], ALU.add)
                    S.ts("dve", ss[:L, 3:4], ss[:L, 2:3], 1.0 / 1024, RMS_EPS, ALU.mult, ALU.add)
                    S.act(ss[:L, 3:4], ss[:L, 3:4], AF.Sqrt)
                    S.recip(ss[:L, 4:5], ss[:L, 3:4])
                    tmp = S.sb([128, 512], F32, "s_tmp")
                    for nh in range(2):
                        sl = slice(nh * 512, (nh + 1) * 512)
                        S.stt(tmp[:L], mo[nh][:L], ss[:L, 4:5], gpost_b[:L, sl], ALU.mult, ALU.mult)
                        S.tt("pool", xmid_s[:L, sl], tmp[:L], xt[:L, sl], ALU.add)
                if SST < 8:
                    return
                with S.scope():
                    alloc_ffn_bufs(128)
                    h2T, actT, f_sb = F.h2T, F.actT, F.f_sb
                    rmsnorm_T(xmid_s[:L], L, gcol_ffn, h2T[:, :, 0:L], B=F)
                    fst = S.sb([NS * 2, 4096], F32, "s_fst")
                    S.dma("sp", fst, st_ffn[l])
                    carry_s = S.sb([128, 32, NS, 2], F32, "s_carry")
                    for q4 in range(4):
                        p = psf.next()
                        pv = p[:, 0:8 * NS * 2].rr("p (c r) -> p c r", c=8)
                        for c8 in range(8):
                            c = q4 * 8 + c8
                            S.mm(pv[:, c8, :], fst[:, c * 128:(c + 1) * 128], identf[:NS * 2, :NS * 2])
                        S.copy("act", carry_s[:, q4 * 8:(q4 + 1) * 8], pv.rr("p c (b t) -> p c b t", b=NS))
                    gtl = S.sb([32, 4096], F32, "s_gtl")
                    r_gx = Ring(S, 2, [128, NS, 10], F32, "s_gx")
                    r_ga = Ring(S, 2, [128, NS, 8], F32, "s_ga")
                    for s in range(16):
                        wg = F.wg.next()
                        wv = F.wv.next()
                        S.dma("sp", wg, wg_s[s])
                        S.dma("sp", wv, wv_s[s])
                        p = psf.next()
                        for k in range(8):
                            S.mm(p[:L, 0:256], h2T[:, k, 0:L], wg[:, k, :], start=(k == 0), stop=(k == 7))
                        S.copy("act", gtl[:, s * 256:(s + 1) * 256], p[:L, 0:256])
                        for c2 in range(2):
                            c = s * 2 + c2
                            g_ps = psf.next()
                            v_ps = psf.next()
                            for k in range(8):
                                S.mm(g_ps[:, 0:L], wg[:, k, c2 * 128:(c2 + 1) * 128], h2T[:, k, 0:L], start=(k == 0), stop=(k == 7))
                            for k in range(8):
                                S.mm(v_ps[:, 0:L], wv[:, k, c2 * 128:(c2 + 1) * 128], h2T[:, k, 0:L], start=(k == 0), stop=(k == 7))
                            gx = r_gx.next()
                            S.copy("act", gx[:, :, 2:10], g_ps[:, 0:L].rr("p (b i) -> p b i", b=NS))
                            S.copy("pool", gx[:, :, 0:2], carry_s[:, c])
                            ga = r_ga.next()
                            S.ts("pool", ga, gx[:, :, 0:8], fconvw[:, c, 0:1], fconvb[:, c:c + 1], ALU.mult, ALU.add)
                            S.stt(ga, gx[:, :, 1:9], fconvw[:, c, 1:2], ga, ALU.mult, ALU.add)
                            S.stt(ga, gx[:, :, 2:10], fconvw[:, c, 2:3], ga, ALU.mult, ALU.add)
                            S.act(ga, ga, AF.Gelu_apprx_tanh)
                            S.tt("dve", actT[:, c, 0:L], ga.rr("p b i -> p (b i)"), v_ps[:, 0:L], ALU.mult)
                    for b in range(NS):
                        S.dma("sp", ffn_s[l, b * 2:(b + 1) * 2, :], gtl[b * 8 + 6:b * 8 + 8, :])
                    for s in range(16):
                        wd = F.wd.next()
                        S.dma("sp", wd, wd_s[s])
                        for nh in range(2):
                            p = psf.next()
                            for c2 in range(2):
                                S.mm(p[:L], actT[:, s * 2 + c2, 0:L], wd[:, c2, nh * 512:(nh + 1) * 512],
                                     start=(c2 == 0), stop=(c2 == 1))
                            dst = f_sb[:L, 0, nh * 512:(nh + 1) * 512]
                            if s == 0:
                                S.copy("act", dst, p[:L])
                            else:
                                S.tt("dve", dst, dst, p[:L], ALU.add)
                    junk = F.junk.next()
                    ss = small()
                    S.act(junk[:L], f_sb[:L, 0, :], AF.Square, accum=ss[:L, 0:1])
                    S.ts("dve", ss[:L, 1:2], ss[:L, 0:1], 1.0 / 1024, RMS_EPS, ALU.mult, ALU.add)
                    S.act(ss[:L, 1:2], ss[:L, 1:2], AF.Sqrt)
                    S.recip(ss[:L, 2:3], ss[:L, 1:2])
                    S.stt(f_sb[:L, 0, :], f_sb[:L, 0, :], ss[:L, 2:3], gfpost_b[:L], ALU.mult, ALU.mult)
                    S.tt("pool", f_sb[:L, 0, :], f_sb[:L, 0, :], xmid_s[:L], ALU.add)
                    S.dma("sp", xdst, f_sb[:L, 0, :])

        for l in range(DEPTH):
            if l == 0:
                precast_in(0)
            load_layer_weights(l)
            precast_ffn(l)
            if l + 1 < DEPTH:
                precast_in(l + 1)
            last = (l == DEPTH - 1)

            def xdst(i, last=last):
                if last:
                    return y_p[i * 128:(i + 1) * 128, :]
                return xcur_tile(i)
            with S.scope():
                alloc_prompt_persist()
                nblk_ = (NT + NTB - 1) // NTB if not cfg.get("skip_prompt") else 0
                for blk in range(nblk_):
                    t0 = blk * NTB
                    ntb = min(NTB, NT - t0)
                    with S.scope():
                        alloc_mixer_rings()
                        load_cmp_weights(l)
                        for ti in range(t0, t0 + ntb):
                            xsrc = x_p[ti * 128:(ti + 1) * 128, :] if l == 0 else xcur_tile(ti)
                            mixer_tile_prompt(l, ti, xsrc)
                    if STAGE >= 8:
                        with S.scope():
                            alloc_ffn_bufs(ntb * 128)
                            ffn_block_prompt(l, blk, ntb, xdst)
                if STAGE >= 3 and not cfg.get("skip_prompt"):
                    with S.scope():
                        ssm_out_prompt(l)
            if NS > 0 and STAGE >= 9:
                sample_layer(l, last)
        print("SBUF peak bytes", S.sb_peak, "instr", {k: len(v) for k, v in S.prog.items()})
        S.emit()
    return nc, outs, consts


_CACHE = {}


def kernel(**inp):
    x_prompt = np.asarray(inp["x_prompt"], np.float32)
    x_sample = np.asarray(inp["x_sample"], np.float32)
    B, T, D = x_prompt.shape
    BS, TS, _ = x_sample.shape
    DEPTH = inp["w_in"].shape[0]
    cache_kv = np.asarray(inp["cache_nsa_kv"], np.float32)
    NPHYS = cache_kv.shape[1]
    page_table = np.asarray(inp["page_table"], np.int32)
    NPG = page_table.shape[1]
    n_cores = NCORES
    assert B == n_cores and BS % n_cores == 0 and TS == 8
    NS = BS // n_cores
    cfg = dict(T=T, DEPTH=DEPTH, NS=NS, NPG=NPG, NPHYS=NPHYS)
    key = (T, DEPTH, NS, NPG, NPHYS)
    if key not in _CACHE:
        _CACHE[key] = build(cfg)
    nc, outs, consts = _CACHE[key]
    WB = inp["state_nsa_win"].shape[2]
    cache2 = np.ascontiguousarray(cache_kv.reshape(DEPTH, NPHYS * 128, 256))
    st_win = np.asarray(inp["state_nsa_win"], np.float32).reshape(DEPTH, BS, WB, 128)
    st_conv = np.asarray(inp["state_ssd_conv"], np.float32)
    st_ssm = np.asarray(inp["state_ssm"], np.float32)
    st_pool = np.asarray(inp["state_pool"], np.float32)
    st_ffn = np.asarray(inp["state_ffn_conv"], np.float32)
    wnames = ["norm_mix_pre", "w_in", "ssd_conv_w", "ssd_conv_b", "ssd_dt_bias", "ssd_a_log", "ssd_d", "ssd_norm",
              "pool_w", "pool_scale", "nsa_pe_k", "nsa_pe_v", "nsa_w1_k", "nsa_w1_v", "nsa_w2_k", "nsa_w2_v", "w_out",
              "norm_mix_post", "norm_ffn_pre", "ffn_w_gate", "ffn_w_val", "ffn_conv_w", "ffn_conv_b", "ffn_w_down",
              "norm_ffn_post"]
    shared = {n: np.ascontiguousarray(np.asarray(inp[n], np.float32)) for n in wnames}
    shared.update(consts)
    shared["cache"] = cache2
    in_maps = []
    for c in range(n_cores):
        sl = slice(c * NS, (c + 1) * NS)
        m = dict(shared)
        m["x_p"] = np.ascontiguousarray(x_prompt[c])
        m["x_s"] = np.ascontiguousarray(x_sample[sl].reshape(NS * 8, D))
        m["pt"] = np.ascontiguousarray(page_table[sl].reshape(-1))
        m["st_win"] = np.ascontiguousarray(st_win[:, sl])
        m["st_conv"] = np.ascontiguousarray(st_conv[:, sl].reshape(DEPTH, NS * 3, 1024))
        m["st_ssm"] = np.ascontiguousarray(st_ssm[:, sl])
        m["st_pool"] = np.ascontiguousarray(st_pool[:, sl].reshape(DEPTH, NS * 15, 256))
        m["st_ffn"] = np.ascontiguousarray(st_ffn[:, sl].reshape(DEPTH, NS * 2, 4096))
        in_maps.append(m)
    res = run_bass_kernel_spmd(nc, in_maps, core_ids=list(range(n_cores))).results
    WK = min(512, T)

    def g(name):
        return [np.asarray(r[name], np.float32) for r in res]

    y_p = np.stack(g("y_p"), 0)
    y_s = np.concatenate([a.reshape(NS, 8, D) for a in g("y_s")], 0)
    kv_p = np.stack([a.reshape(DEPTH, T, 4, 64) for a in g("kv_p")], 1)
    kv_s = np.concatenate([a.reshape(DEPTH, NS, 8, 4, 64) for a in g("kv_s")], 1)
    win_p = np.stack([a.reshape(DEPTH, WK, 2, 64) for a in g("win_p")], 1)
    win_s = np.concatenate([a.reshape(DEPTH, NS, WB, 2, 64) for a in g("win_s")], 1)
    conv_p = np.stack(g("conv_p"), 1)
    conv_s = np.concatenate([a.reshape(DEPTH, NS, 3, 1024) for a in g("conv_s")], 1)
    ssm_p = np.stack(g("ssm_p"), 1)
    ssm_s = np.concatenate(g("ssm_s"), 1)
    pool_p = np.stack(g("pool_p"), 1)
    pool_s = np.concatenate([a.reshape(DEPTH, NS, 15, 256) for a in g("pool_s")], 1)
    ffn_p = np.stack(g("ffn_p"), 1)
    ffn_s = np.concatenate([a.reshape(DEPTH, NS, 2, 4096) for a in g("ffn_s")], 1)
    return (y_p, y_s, kv_p, kv_s, win_p, win_s, conv_p, conv_s, ssm_p, ssm_s, pool_p, pool_s, ffn_p, ffn_s)
```
